# Optimizing a Trainium2 kernel written in Bass

```python
import math
import jax, jax.numpy as jnp
from jax import lax
import numpy as np

D_MODEL = 1024
BATCH = 16
SEQ = 2048
DEPTH = 1

N_META = 16
POOL_WIDTH = D_MODEL // 2
POOL_WINDOWS = (2, 4, 8, 16)
POOL_GROUP = POOL_WIDTH // len(POOL_WINDOWS)
ATTN_WIDTH = D_MODEL - POOL_WIDTH
DIFF_HEAD_DIM = 64
N_ATTN_HEADS = ATTN_WIDTH // (2 * DIFF_HEAD_DIM)
V_HEAD_DIM = 2 * DIFF_HEAD_DIM
IN_WIDTH = POOL_WIDTH + 3 * ATTN_WIDTH
MIX_WIDTH = POOL_WIDTH + N_ATTN_HEADS * V_HEAD_DIM
REL_BUCKETS = 32
REL_MAX_DIST = 128
D_FF = int(math.ceil(math.ceil(8 * D_MODEL / 3) / 256) * 256)
Q_BLOCK = 128
NORM_EPS = 1e-6
SUBLN_EPS = 1e-5

kernel_name = "hymba_pool_diffattn_block"


def _rmsnorm(x, g, eps=NORM_EPS):
    xf = x.astype(jnp.float32)
    y = xf * lax.rsqrt(jnp.mean(xf * xf, axis=-1, keepdims=True) + eps)
    return (y * g.astype(jnp.float32)).astype(x.dtype)


def _t5_bucket(rel):
    nb = REL_BUCKETS // 2
    ret = jnp.where(rel > 0, nb, 0)
    n = jnp.abs(rel)
    max_exact = nb // 2
    nf = jnp.maximum(n, 1).astype(jnp.float32)
    large = max_exact + (jnp.log(nf / max_exact) / math.log(REL_MAX_DIST / max_exact)
                         * (nb - max_exact)).astype(jnp.int32)
    large = jnp.minimum(large, nb - 1)
    return ret + jnp.where(n < max_exact, n, large)


def _multiscale_pool(u, pool_w, pool_scale):
    B, L, _ = u.shape
    uf = u.astype(jnp.float32)
    csum = jnp.concatenate([jnp.zeros((B, 1, POOL_WIDTH), jnp.float32),
                            jnp.cumsum(uf, axis=1)], axis=1)
    pos = jnp.arange(L)
    diffs = []
    for g, w in enumerate(POOL_WINDOWS):
        left = w // 2
        right = w - 1 - left
        lo = jnp.clip(pos - left, 0, L - 1)
        hi = jnp.clip(pos + right, 0, L - 1)
        cg = csum[..., g * POOL_GROUP:(g + 1) * POOL_GROUP]
        window_sum = cg[:, hi + 1] - cg[:, lo]
        cnt = (hi - lo + 1).astype(jnp.float32)[None, :, None]
        diffs.append(window_sum / cnt - uf[..., g * POOL_GROUP:(g + 1) * POOL_GROUP])
    d = jnp.stack(diffs, axis=2)
    y = jnp.einsum('blgc,gcd->blgd', d, pool_w.astype(jnp.float32)).reshape(B, L, POOL_WIDTH)
    return (y * pool_scale.astype(jnp.float32)).astype(u.dtype)


def _diff_attention(q, k, v, rel_bias, lam, subln_g, lambda_init):
    B, L = q.shape[0], q.shape[1]
    n_blocks = -(-L // Q_BLOCK)
    Lp = n_blocks * Q_BLOCK
    qp = jnp.pad(q, ((0, 0), (0, Lp - L), (0, 0), (0, 0), (0, 0)))
    qb = qp.reshape(B, n_blocks, Q_BLOCK, N_ATTN_HEADS, 2, DIFF_HEAD_DIM).transpose(1, 0, 2, 3, 4, 5)
    kpos = jnp.arange(L)
    scale = DIFF_HEAD_DIM ** -0.5

    def block(args):
        qblk, start = args
        qpos = start + jnp.arange(Q_BLOCK)
        bucket = _t5_bucket(kpos[None, :] - qpos[:, None])
        bias = jnp.transpose(rel_bias[bucket], (2, 0, 1)).astype(jnp.float32)
        s = jnp.einsum('bqhcd,bkhcd->bchqk', qblk, k).astype(jnp.float32) * scale + bias[None, None]
        p = jax.nn.softmax(s, axis=-1)
        a = p[:, 0] - lam * p[:, 1]
        return jnp.einsum('bhqk,bkhe->bqhe', a.astype(v.dtype), v)

    starts = jnp.arange(n_blocks, dtype=jnp.int32) * Q_BLOCK
    o = lax.map(block, (qb, starts))
    o = o.transpose(1, 0, 2, 3, 4).reshape(B, Lp, N_ATTN_HEADS, V_HEAD_DIM)[:, :L]
    o = _rmsnorm(o, subln_g, eps=SUBLN_EPS) * (1.0 - lambda_init)
    return o.reshape(B, L, N_ATTN_HEADS * V_HEAD_DIM).astype(v.dtype)


def setup_inputs(seed: int = 0) -> dict:
    key = jax.random.key(seed)
    ks = jax.random.split(key, 20)
    f32 = jnp.float32
    nrm = lambda k, shape, s: jax.random.normal(k, shape, f32) * s
    return {
        "x": nrm(ks[0], (BATCH, SEQ, D_MODEL), 1.0),
        "meta_tokens": nrm(ks[1], (N_META, D_MODEL), 1.0),
        "rel_bias": nrm(ks[2], (REL_BUCKETS, N_ATTN_HEADS), 0.5),
        "norm1_g": 1.0 + nrm(ks[3], (DEPTH, D_MODEL), 0.02),
        "w_in": nrm(ks[4], (DEPTH, D_MODEL, IN_WIDTH), D_MODEL ** -0.5),
        "pool_w": nrm(ks[5], (DEPTH, len(POOL_WINDOWS), POOL_GROUP, POOL_GROUP), POOL_GROUP ** -0.5),
        "pool_scale": 1.0 + nrm(ks[6], (DEPTH, POOL_WIDTH), 0.1),
        "lambda_q1": nrm(ks[7], (DEPTH, DIFF_HEAD_DIM), 0.1),
        "lambda_k1": nrm(ks[8], (DEPTH, DIFF_HEAD_DIM), 0.1),
        "lambda_q2": nrm(ks[9], (DEPTH, DIFF_HEAD_DIM), 0.1),
        "lambda_k2": nrm(ks[10], (DEPTH, DIFF_HEAD_DIM), 0.1),
        "subln_g": 1.0 + nrm(ks[11], (DEPTH, V_HEAD_DIM), 0.02),
        "w_o": nrm(ks[12], (DEPTH, MIX_WIDTH, D_MODEL), MIX_WIDTH ** -0.5),
        "norm2_g": 1.0 + nrm(ks[13], (DEPTH, D_MODEL), 0.02),
        "w_gate": nrm(ks[14], (DEPTH, D_MODEL, D_FF), D_MODEL ** -0.5),
        "w_up": nrm(ks[15], (DEPTH, D_MODEL, D_FF), D_MODEL ** -0.5),
        "w_down": nrm(ks[16], (DEPTH, D_FF, D_MODEL), D_FF ** -0.5),
        "final_g": 1.0 + nrm(ks[17], (D_MODEL,), 0.02),
    }


def reference(x, meta_tokens, rel_bias, norm1_g, w_in, pool_w, pool_scale, lambda_q1, lambda_k1,
              lambda_q2, lambda_k2, subln_g, w_o, norm2_g, w_gate, w_up, w_down, final_g):
    B = x.shape[0]
    meta = jnp.broadcast_to(meta_tokens[None].astype(x.dtype), (B, N_META, D_MODEL))
    h = jnp.concatenate([meta, x], axis=1)
    L = h.shape[1]
    o_q = POOL_WIDTH
    o_k = o_q + ATTN_WIDTH
    o_v = o_k + ATTN_WIDTH
    for layer in range(DEPTH):
        lambda_init = 0.8 - 0.6 * math.exp(-0.3 * layer)
        lam = (jnp.exp(jnp.sum(lambda_q1[layer].astype(jnp.float32) * lambda_k1[layer].astype(jnp.float32)))
               - jnp.exp(jnp.sum(lambda_q2[layer].astype(jnp.float32) * lambda_k2[layer].astype(jnp.float32)))
               + lambda_init)
        u = _rmsnorm(h, norm1_g[layer])
        z = u @ w_in[layer]
        y_pool = _multiscale_pool(z[..., :o_q], pool_w[layer], pool_scale[layer])
        q = z[..., o_q:o_k].reshape(B, L, N_ATTN_HEADS, 2, DIFF_HEAD_DIM)
        k = z[..., o_k:o_v].reshape(B, L, N_ATTN_HEADS, 2, DIFF_HEAD_DIM)
        v = z[..., o_v:].reshape(B, L, N_ATTN_HEADS, V_HEAD_DIM)
        y_attn = _diff_attention(q, k, v, rel_bias, lam, subln_g[layer], lambda_init)
        h = h + jnp.concatenate([y_pool, y_attn], axis=-1) @ w_o[layer]
        f = _rmsnorm(h, norm2_g[layer])
        h = h + (jax.nn.silu(f @ w_gate[layer]) * (f @ w_up[layer])) @ w_down[layer]
    return _rmsnorm(h, final_g)[:, N_META:]
```

```python
import math
import numpy as np
import ml_dtypes
import concourse.bass as bass
import concourse.mybir as mybir
from concourse.bass_utils import run_bass_kernel_spmd

F32 = mybir.dt.float32
BF16 = mybir.dt.bfloat16
AF = mybir.ActivationFunctionType
ALU = mybir.AluOpType

NCORES = 8
SEQ = 2048
D = 1024
NMETA = 16
LPOS = SEQ + NMETA
DFF = 2816
NKT = 17
QC = 256
NQC = SEQ // QC
TW = 768
NLUT = 896
LAMBDA_INIT = 0.8 - 0.6 * math.exp(-0.3 * 0)
NS_DMA = 8
FILLER = 0


class Buf:
    __slots__ = ("name", "psum", "last_w", "readers")

    def __init__(self, name, psum=False):
        self.name = name
        self.psum = psum
        self.last_w = None
        self.readers = {}


class Op:
    __slots__ = ("eng", "fn", "deps", "inc", "seq", "dma", "semkey", "val", "prev")

    def __init__(self, eng, fn, deps, dma):
        self.eng = eng
        self.fn = fn
        self.deps = deps
        self.inc = False
        self.seq = 0
        self.dma = dma
        self.semkey = None
        self.val = 0
        self.prev = None


class Tracker:
    ENGS = ("pe", "act", "dve", "pool", "sp")

    def __init__(self):
        self.ops = {e: [] for e in self.ENGS}
        self.dma_cnt = {"sp": 0, "pool": 0}
        self.dma_uses = {}
        self.dma_last = {}
        self.pending = {e: set() for e in self.ENGS}
        self.recent_dma = []
        self.all_dma = []

    def emit(self, eng, fn, reads=(), writes=(), dma=False):
        deps = set()
        for b in reads:
            if b.psum:
                if b.last_w is not None:
                    deps.add(b.last_w)
                deps.update(b.readers.values())
            elif b.last_w is not None:
                deps.add(b.last_w)
        for b in writes:
            if b.last_w is not None:
                deps.add(b.last_w)
            deps.update(b.readers.values())
        if self.pending[eng]:
            deps.update(self.pending[eng])
            self.pending[eng] = set()
        if eng == "pe":
            deps = {d for d in deps if not (d.eng == "pe" and not d.dma)}
        op = Op(eng, fn, deps, dma)
        if dma:
            k = self.dma_cnt[eng] % NS_DMA
            self.dma_cnt[eng] += 1
            key = (eng, k)
            n = self.dma_uses.get(key, 0) + 1
            self.dma_uses[key] = n
            op.semkey = key
            op.val = 16 * n
            op.prev = self.dma_last.get(key)
            self.dma_last[key] = op
            self.recent_dma.append(op)
            self.all_dma.append(op)
        self.ops[eng].append(op)
        for b in reads:
            if b.psum:
                b.last_w = op
                b.readers = {}
            else:
                b.readers[(eng, id(op)) if dma else eng] = op
        for b in writes:
            b.last_w = op
            b.readers = {}
        return op

    def barrier(self, exclude=()):
        deps = set(self.recent_dma) - set(exclude)
        self.recent_dma = []
        for e in self.ENGS:
            if self.ops[e]:
                last = self.ops[e][-1]
                if not last.dma:
                    deps.add(last)
                else:
                    for o in reversed(self.ops[e]):
                        if not o.dma:
                            deps.add(o)
                            break
        for e in self.ENGS:
            self.pending[e] = set(deps) | self.pending[e]

    def finalize(self):
        for e in self.ENGS:
            for op in self.ops[e]:
                for d in op.deps:
                    if not d.dma:
                        d.inc = True
        for e in self.ENGS:
            c = 0
            for op in self.ops[e]:
                if op.inc and not op.dma:
                    c += 1
                    op.seq = c

    def play(self, eng, handle, sems, dsems, final_wait=False):
        waited = {}

        def wait(key, semobj, val):
            if waited.get(key, 0) >= val:
                return
            handle.wait_ge(semobj, val)
            waited[key] = val

        for op in self.ops[eng]:
            for d in op.deps:
                if d.dma:
                    wait(d.semkey, dsems[d.semkey], d.val)
                else:
                    wait(d.eng, sems[d.eng], d.seq)
            if op.dma and op.prev is not None:
                wait(op.prev.semkey, dsems[op.prev.semkey], op.prev.val)
            inst = op.fn(handle)
            if op.dma:
                inst.then_inc(dsems[op.semkey], 16)
            elif op.inc:
                inst.then_inc(sems[eng], 1)
        if final_wait:
            for key, op in self.dma_last.items():
                wait(key, dsems[key], op.val)


def _t5_bucket_np(rel):
    rel = np.asarray(rel, dtype=np.int64)
    nb = 16
    ret = np.where(rel > 0, nb, 0)
    n = np.abs(rel)
    max_exact = 8
    nf = np.maximum(n, 1).astype(np.float32)
    large = max_exact + (np.log(nf / np.float32(max_exact)) / np.float32(math.log(128 / max_exact))
                         * np.float32(nb - max_exact)).astype(np.int32)
    large = np.minimum(large, nb - 1)
    return ret + np.where(n < max_exact, n, large)


def build_program():
    nc = bass.Bass("TRN2", target_bir_lowering=False)
    T = Tracker()

    def din(name, shape, dt=F32):
        return nc.dram_tensor(name, list(shape), dt, kind="ExternalInput")

    x_d = din("x", [2, SEQ, D])
    meta_d = din("meta", [NMETA, D])
    win_l = din("w_in_l", [1024, 2048])
    wo_l = din("w_o_l", [1024, 1024])
    wg_l = din("w_gate_l", [1408, 2048])
    wu_l = din("w_up_l", [1408, 2048])
    wd_l = din("w_down_l", [DFF, 1024])
    pw_l = din("pool_w_l", [128, 512])
    colpack_d = din("colpack", [128, 32])
    fgb_d = din("fgb", [128, D])
    lam_d = din("lamv", [128, 256])
    relb_d = din("relb", [32, 4])
    oh_d = din("onehot", [32, NLUT])
    ident_d = din("ident", [128, 128])
    aident_d = din("aident", [128, 128])
    out_d = nc.dram_tensor("out", [2, SEQ, D], F32, kind="ExternalOutput")

    win_b = nc.dram_tensor("win_b", [1024, 2048], BF16, kind="Internal")
    wo_b = nc.dram_tensor("wo_b", [1024, 1024], BF16, kind="Internal")
    wg_b = nc.dram_tensor("wg_b", [1408, 2048], BF16, kind="Internal")
    wu_b = nc.dram_tensor("wu_b", [1408, 2048], BF16, kind="Internal")
    wd_b = nc.dram_tensor("wd_b", [DFF, 1024], BF16, kind="Internal")
    lut_d = nc.dram_tensor("lut_d", [4, NLUT], F32, kind="Internal")

    ARENA = 131072
    ctxs = dict(
        arena=nc.sbuf_tensor("arena", [128, ARENA // 2], BF16),
        h1=nc.sbuf_tensor("h1", [128, 16 * D], F32),
        ident=nc.sbuf_tensor("ident_sb", [128, 128], F32),
        gbc=nc.sbuf_tensor("gbc", [128, D], F32),
        fgb=nc.sbuf_tensor("fgb_sb", [128, D], F32),
        pw=nc.sbuf_tensor("pw_sb", [128, 512], BF16),
        identb=nc.sbuf_tensor("identb_sb", [128, 128], BF16),
        jb=nc.sbuf_tensor("jb_sb", [128, 128], BF16),
        cols=nc.sbuf_tensor("cols", [128, 64], F32),
        lamv=nc.sbuf_tensor("lamv_sb", [128, 256], F32),
        sm=nc.sbuf_tensor("sm", [128, 128], F32),
        relb=nc.sbuf_tensor("relb_sb", [32, 4], F32),
        oh=nc.sbuf_tensor("oh_sb", [32, NLUT], F32),
        ps=nc.psum_tensor("ps", [128, 8 * 512], F32),
    )
    sem_names = ["pe", "act", "dve", "pool", "sp"]
    from contextlib import ExitStack
    with ExitStack() as es:
        tens = {k: es.enter_context(v) for k, v in ctxs.items()}
        sems = {e: es.enter_context(nc.semaphore("s_" + e)) for e in sem_names}
        dsems = {}
        for e in ("sp", "pool"):
            for k in range(NS_DMA):
                dsems[(e, k)] = es.enter_context(nc.semaphore(f"d_{e}{k}"))

        arena_bf = tens["arena"]
        arena_f = arena_bf.bitcast(F32)
        h1 = tens["h1"]
        ident = tens["ident"]
        gbc = tens["gbc"]
        fgb = tens["fgb"]
        pw = tens["pw"]
        cols = tens["cols"]
        lamv = tens["lamv"]
        sm = tens["sm"]
        relb = tens["relb"]
        ohsb = tens["oh"]
        ps = tens["ps"]
        psb = ps.bitcast(BF16)
        identb = tens["identb"]
        jb = tens["jb"]

        def bfv(off, n):
            return arena_bf[:, off // 2: off // 2 + n]

        def f32v(off, n):
            return arena_f[:, off // 4: off // 4 + n]

        def bank(i):
            return ps[:, i * 512:(i + 1) * 512]

        def bankb(i):
            return psb[:, i * 1024:(i + 1) * 1024]

        g1col = cols[:, 0:8]
        g2col = cols[:, 8:16]
        pscol = cols[:, 16:20]
        sgcol = cols[:, 20:21]
        sg08 = cols[:, 21:22]
        farb = cols[:, 24:32]
        neglam = cols[:, 32:33]
        eps6 = cols[:, 33:34]
        eps5 = cols[:, 34:35]
        lam_s = cols[:, 36:38]
        lam_e = cols[:, 38:40]
        ssA = sm[:, 0:2]
        lnA = sm[:, 2:4]
        rsA = sm[:, 4:6]
        ss2 = sm[:, 8:24]
        ln2 = sm[:, 24:32]
        rs2 = sm[:, 32:40]
        ss3 = sm[:, 40:48]
        ln3 = sm[:, 48:56]
        rs3 = sm[:, 56:64]
        rz = sm[:, 64:72]
        rz1n = sm[:, 72:76]
        ssb = sm[:, 76:80]
        lnb = sm[:, 80:84]
        rsb = sm[:, 84:88]

        B = lambda name, psum=False: Buf(name, psum)
        b_banks = [B(f"bank{i}", True) for i in range(8)]
        b_h1 = [B(f"h1_{i}") for i in range(16)]
        b_ident = B("ident")
        b_gbc = B("gbc")
        b_fgb = B("fgb")
        b_pw = B("pw")
        b_cols = B("cols")
        b_lamv = B("lamv")
        b_lams = B("lams")
        b_relb = B("relb")
        b_oh = B("oh")
        b_winb, b_wob, b_wgb, b_wub, b_wdb = B("winb"), B("wob"), B("wgb"), B("wub"), B("wdb")
        b_lutd = B("lutd")
        b_ss2 = [B(f"ss2_{i}") for i in range(16)]

        emit = T.emit

        def dma_sp(out, in_, reads, writes):
            return emit("sp", lambda e, o=out, i=in_: e.dma_start(out=o, in_=i), reads, writes, dma=True)

        def dma_pool(out, in_, reads, writes):
            return emit("pool", lambda e, o=out, i=in_: e.dma_start(out=o, in_=i), reads, writes, dma=True)

        def mm(out, lhsT, rhs, start, stop, reads, writes, skip=False):
            return emit("pe", lambda e, o=out, l=lhsT, r=rhs, s=start, t=stop, k=skip:
                        e.matmul(o, lhsT=l, rhs=r, start=s, stop=t, skip_group_check=k), reads, writes)

        def tr(out, in_, npart, reads, writes, bf=False):
            idn = (identb if bf else ident)[0:npart, 0:npart]
            return emit("pe", lambda e, o=out, i=in_, d=idn: e.transpose(o, i, d), list(reads) + [b_ident], writes)

        def act(out, in_, func, reads, writes, bias=None, scale=None, accum=None):
            def fn(e, o=out, i=in_, f=func, b=bias, s=scale, a=accum):
                kw = {}
                if b is not None:
                    kw["bias"] = b
                if s is not None:
                    kw["scale"] = s
                if a is not None:
                    kw["accum_out"] = a
                return e.activation(out=o, in_=i, func=f, **kw)
            return emit("act", fn, reads, writes)

        def dve(fn, reads, writes):
            return emit("dve", fn, reads, writes)

        def pool(fn, reads, writes):
            return emit("pool", fn, reads, writes)

        dma_sp(cols[:, 0:32], colpack_d.ap(), [], [b_cols])
        dma_sp(ident[:], ident_d.ap(), [], [b_ident])
        dma_sp(lamv[:], lam_d.ap(), [], [b_lamv])
        dma_sp(relb[:], relb_d.ap(), [], [b_relb])
        dma_sp(ohsb[:], oh_d.ap(), [], [b_oh])
        win0 = bfv(67600, 8 * 2048).rearrange("p (c n) -> p c n", c=8)
        b_win0 = B("win0")
        b_stg = [B(f"wstg{c}") for c in range(8)]
        for c in range(8):
            dma_sp(h1[:, c * 2048:(c + 1) * 2048],
                   win_l.ap().rearrange("(p a) n -> p a n", p=128)[:, c, :], [], [b_stg[c]])
        dma_sp(fgb[:], fgb_d.ap(), [], [b_fgb])
        b_win0c = [B(f"win0_{c}") for c in range(8)]
        for c in range(8):
            dve(lambda e, c=c: e.tensor_copy(out=win0[:, c, :], in_=h1[:, c * 2048:(c + 1) * 2048]),
                [b_stg[c]], [b_win0c[c]])
        early_ex = [dma_sp(win_b.ap().rearrange("(p a) n -> p (a n)", p=128), win0.rearrange("p c n -> p (c n)"),
                           b_win0c, [b_winb])]

        dve(lambda e: e.tensor_copy(out=identb[:], in_=ident[:]), [b_ident], [b_ident])
        b_jb = B("jb")
        jstage = f32v(8192, 128)
        b_jst = B("jstage")
        dma_sp(jstage, aident_d.ap(), [], [b_jst])
        dve(lambda e: e.tensor_copy(out=jb[:], in_=jstage), [b_jst], [b_jb])
        dve(lambda e: e.memset(cols[:, 33:34], 1e-6), [], [b_cols])
        dve(lambda e: e.memset(cols[:, 34:35], 1e-5), [], [b_cols])
        dve(lambda e: e.tensor_scalar(out=sg08, in0=sgcol, scalar1=float(1.0 - LAMBDA_INIT), scalar2=None,
                                      op0=ALU.mult), [b_cols], [b_cols])
        negbb = cols[:, 40:44]
        ea = cols[:, 44:48]
        dve(lambda e: e.tensor_scalar(out=negbb, in0=farb[:, 0:4], scalar1=-1.0, scalar2=None, op0=ALU.mult),
            [b_cols], [b_cols])
        dve(lambda e: e.tensor_tensor(out=ea, in0=farb[:, 4:8], in1=farb[:, 0:4], op=ALU.subtract),
            [b_cols], [b_cols])
        b_lamp = B("lamp")
        lamp = f32v(0, 128)
        lamj = bfv(1024, 128)
        dve(lambda e: e.tensor_tensor(out=lamp[:, 0:64], in0=lamv[:, 0:64], in1=lamv[:, 64:128], op=ALU.mult),
            [b_lamv], [b_lamp])
        dve(lambda e: e.tensor_tensor(out=lamp[:, 64:128], in0=lamv[:, 128:192], in1=lamv[:, 192:256], op=ALU.mult),
            [b_lamv], [b_lamp])
        b_lamj = B("lamj")
        act(lamj[:, 0:64], lamp[:, 0:64], AF.Copy, [b_lamp], [b_lamj, b_lams], accum=lam_s[:, 0:1])
        act(lamj[:, 0:64], lamp[:, 64:128], AF.Copy, [b_lamp], [b_lamj, b_lams], accum=lam_s[:, 1:2])
        act(lam_e, lam_s, AF.Exp, [b_lams], [b_lams])
        dve(lambda e: e.tensor_tensor(out=neglam, in0=lam_e[:, 1:2], in1=lam_e[:, 0:1], op=ALU.subtract),
            [b_lams], [b_cols])
        dve(lambda e: e.tensor_scalar(out=neglam, in0=neglam, scalar1=float(-LAMBDA_INIT), scalar2=None,
                                      op0=ALU.add), [b_cols], [b_cols])
        lutsb = f32v(4096, NLUT)
        b_lutsb = B("lutsb")
        for half in range(2):
            mm(bank(0)[0:4, 0:448], relb[:, :], ohsb[:, half * 448:(half + 1) * 448], True, True,
               [b_relb, b_oh], [b_banks[0]])
            dve(lambda e, hf=half: e.tensor_copy(out=lutsb[0:4, hf * 448:(hf + 1) * 448], in_=bank(0)[0:4, 0:448]),
                [b_banks[0]], [b_lutsb])
        dma_sp(lut_d.ap(), lutsb[0:4, :], [b_lutsb], [b_lutd])
        T.barrier(exclude=early_ex)

        O_QT, O_KT, O_U, O_VX, O_X = 0, 16640, 33280, 49920, 67600
        PW_ = 2080
        QT = bfv(O_QT, 4 * PW_).rearrange("p (h n) -> p h n", h=4)
        KT = bfv(O_KT, 4 * PW_).rearrange("p (h n) -> p h n", h=4)
        UU = bfv(O_U, 4 * PW_).rearrange("p (h n) -> p h n", h=4)
        VX = bfv(O_VX, NKT * 4 * 130).rearrange("p (k h n) -> p k h n", k=NKT, h=4)

        def gain_tile(gcol, ones_ap, b_ones):
            for c in range(8):
                dve(lambda e, c=c: e.tensor_scalar(out=gbc[:, c * 128:(c + 1) * 128], in0=ones_ap,
                                                   scalar1=gcol[:, c:c + 1], scalar2=None, op0=ALU.mult),
                    [b_cols, b_ones], [b_gbc])

        for s in range(2):
            b_QT = [B(f"QT{i}") for i in range(5)]
            b_KT = [B(f"KT{i}") for i in range(NKT)]
            b_U = [B(f"U{i}") for i in range(5)]
            b_Upad = B("Upad")
            b_VX = [B(f"VX{i}") for i in range(NKT)]
            win = bfv(O_X, 8 * 2048).rearrange("p (c n) -> p c n", c=8)
            uT = [bfv(O_X + 32768 + i * 8192, 8 * 512).rearrange("p (c n) -> p c n", c=8) for i in range(2)]
            xs = [f32v(O_X + 49152 + i * 4096, 1024) for i in range(2)]
            xb = [bfv(O_X + 57344 + i * 2048, 1024) for i in range(2)]
            b_win = B("win")
            b_uT = [B("uT0"), B("uT1")]
            b_xs = [B("xs0"), B("xs1")]
            b_xb = [B("xb0"), B("xb1")]
            b_ssA = [B("ssA0"), B("ssA1")]

            rd_win = b_win0c if s == 0 else [b_win]
            if s == 0:
                pass
            else:
                dma_sp(win.rearrange("p c n -> p (c n)"), win_b.ap().rearrange("(p a) n -> p (a n)", p=128),
                       [b_winb], [b_win])
            dve(lambda e: e.memset(xs[1][:, 0:128], 1.0), [], [b_xs[1]])
            gain_tile(g1col, xs[1][:, 0:128], b_xs[1])
            pool(lambda e: e.memset(UU[:, :, 2064:2080], 0.0), [], [b_Upad])
            pool(lambda e: e.memset(VX[:, :, :, 128:130], 1.0), [], b_VX)

            def norm_tile(ti, slot):
                np_ = NMETA if ti < 0 else 128
                src = meta_d.ap() if ti < 0 else x_d.ap()[s, ti * 128:(ti + 1) * 128, :]
                xt = xs[slot]
                xbt = xb[slot]
                dma_sp(xt[0:np_, :], src, [], [b_xs[slot]])
                act(xbt[0:np_, :], xt[0:np_, :], AF.Square, [b_xs[slot]], [b_xb[slot], b_ssA[slot]],
                    accum=ssA[0:np_, slot:slot + 1])
                act(lnA[0:np_, slot:slot + 1], ssA[0:np_, slot:slot + 1], AF.Ln, [b_ssA[slot], b_cols], [b_ssA[slot]],
                    bias=eps6[0:np_, :], scale=1.0 / D)
                act(rsA[0:np_, slot:slot + 1], lnA[0:np_, slot:slot + 1], AF.Exp, [b_ssA[slot]], [b_ssA[slot]],
                    scale=-0.5)
                act(xbt[0:np_, :], xt[0:np_, :], AF.Copy, [b_ssA[slot], b_xs[slot]], [b_xb[slot]],
                    scale=rsA[0:np_, slot:slot + 1])
                pb = slot
                for c in range(8):
                    tr(bankb(pb)[:, c * 128:c * 128 + np_], xbt[0:np_, c * 128:(c + 1) * 128], np_,
                       [b_xb[slot]], [b_banks[pb]], bf=True)
                p0 = 0 if ti < 0 else NMETA + ti * 128
                a = p0
                while a < p0 + np_:
                    pc = a // 512
                    bnd = min(p0 + np_, (pc + 1) * 512)
                    n = bnd - a
                    off = a - p0
                    src_ps = bankb(pb).rearrange("p (c n) -> p c n", c=8)[:, :, off:off + n]
                    dst = uT[pc % 2][:, :, a - pc * 512:a - pc * 512 + n]
                    g = gbc[:, :].rearrange("p (c n) -> p c n", c=8)[:, :, 0:n]
                    dve(lambda e, o=dst, i=src_ps, g=g: e.tensor_tensor(out=o, in0=i, in1=g, op=ALU.mult),
                        [b_banks[pb], b_gbc], [b_uT[pc % 2]])
                    a = bnd

            kindc = [0]

            def inproj_groups(pc):
                n = 512 if pc < 4 else 16
                u = uT[pc % 2]
                bu = b_uT[pc % 2]
                pos0 = pc * 512
                groups = []
                for grp in range(3):
                    for h in range(4):
                        def g_(grp=grp, h=h):
                            bi = 2 + (kindc[0] % 6)
                            kindc[0] += 1
                            colbase = {0: 512, 1: 1024, 2: 0}[grp] + h * 128
                            for c in range(8):
                                mm(bank(bi)[:, 0:n], win[:, c, colbase:colbase + 128], u[:, c, 0:n], c == 0, c == 7,
                                   rd_win + [bu], [b_banks[bi]])
                            if grp == 0:
                                dve(lambda e, o=QT[:, h, pos0:pos0 + n], i=bank(bi)[:, 0:n]:
                                    e.tensor_scalar(out=o, in0=i, scalar1=0.125, scalar2=None, op0=ALU.mult),
                                    [b_banks[bi]], [b_QT[pc]])
                            elif grp == 1:
                                wr = [b_KT[k] for k in range(pos0 // 128, (pos0 + n - 1) // 128 + 1)]
                                dve(lambda e, o=KT[:, h, pos0:pos0 + n], i=bank(bi)[:, 0:n]:
                                    e.tensor_copy(out=o, in_=i), [b_banks[bi]], wr)
                            else:
                                act(UU[:, h, pos0:pos0 + n], bank(bi)[:, 0:n], AF.Copy, [b_banks[bi]], [b_U[pc]])
                        groups.append(g_)
                nt = (n + 127) // 128
                for kk in range(nt):
                    def gv(kk=kk):
                        m = min(128, n - kk * 128)
                        kt = pos0 // 128 + kk
                        bi = 2 + (kindc[0] % 6)
                        kindc[0] += 1
                        for c in range(8):
                            mm(bank(bi)[0:m, :], u[:, c, kk * 128:kk * 128 + m], win[:, c, 1536:2048], c == 0, c == 7,
                               rd_win + [bu], [b_banks[bi]])
                        src_ps = bank(bi)[0:m, :].rearrange("p (h n) -> p h n", h=4)
                        if kk % 2 == 0:
                            dve(lambda e, o=VX[0:m, kt, :, 0:128], sp_=src_ps: e.tensor_copy(out=o, in_=sp_),
                                [b_banks[bi]], [b_VX[kt]])
                        else:
                            act(VX[0:m, kt, :, 0:128], src_ps, AF.Copy, [b_banks[bi]], [b_VX[kt]])
                    groups.append(gv)
                return groups

            tidx = 0
            order = [-1] + list(range(16))
            for ti in order[0:5]:
                norm_tile(ti, tidx % 2)
                tidx += 1
            nxt_tile = 5
            if s == 0:
                gate = [b_uT[0], b_uT[1]]
                dma_pool(wo_b.ap(), wo_l.ap(), gate, [b_wob])
                dma_pool(pw[:], pw_l.ap(), gate, [b_pw])
            for pc in range(5):
                groups = inproj_groups(pc)
                ng = len(groups)
                for gi, g_ in enumerate(groups):
                    g_()
                    if pc < 3 and gi % 4 == 3 and nxt_tile < len(order) and nxt_tile < 5 + 4 * (pc + 1):
                        norm_tile(order[nxt_tile], tidx % 2)
                        tidx += 1
                        nxt_tile += 1
                while pc < 3 and nxt_tile < 5 + 4 * (pc + 1):
                    norm_tile(order[nxt_tile], tidx % 2)
                    tidx += 1
                    nxt_tile += 1
            T.barrier()

            from collections import deque
            wo = bfv(O_X, 8 * 1024).rearrange("p (c n) -> p c n", c=8)
            Tst = f32v(O_X + 16384, 4 * TW).rearrange("p (h n) -> p h n", h=4)
            o2 = O_X + 16384 + 12288
            NE = 4
            Et = [bfv(o2 + i * 1024, 512) for i in range(NE)]
            o2 += NE * 1024
            Tb = bfv(o2, 4 * TW).rearrange("p (h n) -> p h n", h=4)
            o2 += 4 * TW * 2
            yT_off = o2
            yT = [bfv(o2 + i * 4096, 8 * QC).rearrange("p (c n) -> p c n", c=8) for i in range(2)]
            o2 += 8192
            ot = [f32v(o2 + i * 1024, 256).rearrange("p (q n) -> p q n", q=2) for i in range(2)]
            o2 += 2048
            yt = [f32v(o2 + i * 1024, 256).rearrange("p (q n) -> p q n", q=2) for i in range(2)]
            o2 += 2048
            tA = f32v(o2, 272)
            tB = f32v(o2 + 1088, 272)
            o2 += 2176
            dT = [bfv(o2 + i * 2048, 4 * QC).rearrange("p (g n) -> p g n", g=4) for i in range(2)]
            o2 += 4096
            junkB = bfv(o2, 1024)
            o2 += 2048
            Qb = [bfv(o2 + i * 1024, 512) for i in range(2)]
            o2 += 2048
            assert o2 <= ARENA, o2
            b_wo = B("wo")
            b_E = [B(f"E{i}") for i in range(NE)]
            b_sb = [B(f"sb{i}") for i in range(2)]
            b_yT = [B(f"yT{i}") for i in range(2)]
            b_ot = [B(f"ot{i}") for i in range(2)]
            b_yt = [B(f"yt{i}") for i in range(2)]
            b_tA, b_tB = B("tA"), B("tB")
            b_dT = [B(f"dT{i}") for i in range(2)]
            b_junkB = B("junkB")
            b_rz = [B("rz0"), B("rz1")]
            b_ssb = [B("ssb0"), B("ssb1")]
            b_Qb = [B("Qb0"), B("Qb1")]

            dma_sp(wo.rearrange("p c n -> p (c n)"), wo_b.ap().rearrange("(p a) n -> p (a n)", p=128),
                   [b_wob], [b_wo])
            for i in range(2):
                pool(lambda e, i=i: e.memset(Qb[i][:, :], 0.0), [], [b_Qb[i]])
            b_Tst = [B(f"Tst{h}") for h in range(4)]
            b_Tb = B("Tb")
            Hb = bfv(yT_off, 4 * TW)
            for h in range(4):
                src = bass.AP(lut_d, h * NLUT, [[1, 128], [1, TW]])
                dma_sp(Tst[:, h, :], src, [b_lutd], [b_Tst[h]])
                act(Hb[:, h * TW:(h + 1) * TW], Tst[:, h, :], AF.Exp, [b_Tst[h], b_cols], b_yT, bias=negbb[:, h:h + 1])
            tb_banks = [7, 3, 4, 5]
            k = 0
            for h in range(4):
                for half in range(2):
                    bk = tb_banks[k % 4]
                    k += 1
                    mm(bank(bk)[:, 0:384], jb[:, :], Hb[:, h * TW + half * 384:h * TW + (half + 1) * 384], True, True,
                       b_yT + [b_jb], [b_banks[bk]])
                    dve(lambda e, o=Tb[:, h, half * 384:(half + 1) * 384], i=bank(bk)[:, 0:384]:
                        e.tensor_copy(out=o, in_=i), [b_banks[bk]], [b_Tb])

            import heapq
            deferred = []
            dseq = [0]

            def defer(at, fn):
                dseq[0] += 1
                heapq.heappush(deferred, (at, dseq[0], fn))

            def run_deferred(cur, nmax):
                k = 0
                while deferred and k < nmax and deferred[0][0] <= cur:
                    heapq.heappop(deferred)[2]()
                    k += 1

            def pool_stage(cq, g, w):
                q0 = NMETA + cq * QC
                ysl = cq % 2

                def zz(lo, n):
                    return UU[:, g, q0 + lo:q0 + lo + n]
                rdU = [b_U[min(4, (q0 - 8) // 512)], b_U[min(4, (q0 + 263) // 512)], b_Upad]

                def padd(o, a, b, reads, writes):
                    pool(lambda e, o=o, a=a, b=b: e.tensor_tensor(out=o, in0=a, in1=b, op=ALU.add), reads, writes)
                if w == 2:
                    padd(tA[:, 0:256], zz(0, 256), zz(-1, 256), rdU, [b_tA])
                    ws, wb, wsl = tA, b_tA, 0
                else:
                    padd(tA[:, 1:272], zz(-7, 271), zz(-8, 271), rdU, [b_tA])
                    padd(tB[:, 2:271], tA[:, 3:272], tA[:, 1:270], [b_tA], [b_tB])
                    ws, wb, wsl = tB, b_tB, 8
                    if w >= 8:
                        padd(tA[:, 4:269], tB[:, 6:271], tB[:, 2:267], [b_tB], [b_tA])
                        ws, wb = tA, b_tA
                    if w == 16:
                        padd(tB[:, 8:265], tA[:, 12:269], tA[:, 4:261], [b_tA], [b_tB])
                        ws, wb = tB, b_tB
                def dpart():
                    dve(lambda e, o=dT[ysl][:, g, :], a=ws[:, wsl:wsl + 256], sc_=1.0 / w, b=zz(0, 256):
                        e.scalar_tensor_tensor(out=o, in0=a, scalar=sc_, in1=b, op0=ALU.mult, op1=ALU.subtract),
                        [wb] + rdU, [b_dT[ysl]])
                    if cq == NQC - 1:
                        right = w - 1 - w // 2
                        for r in range(right):
                            cnt = w - (right - r)
                            col = 255 - r
                            dve(lambda e, o=dT[ysl][:, g, col:col + 1], a=ws[:, wsl + col:wsl + col + 1], sc_=1.0 / cnt,
                                b=zz(col, 1):
                                e.scalar_tensor_tensor(out=o, in0=a, scalar=sc_, in1=b, op0=ALU.mult, op1=ALU.subtract),
                                [wb] + rdU, [b_dT[ysl]])
                return dpart

            def poolmm_stage(cq, g):
                ysl = cq % 2
                mb = 7
                mm(bank(mb)[:, 0:QC], pw[:, g * 128:(g + 1) * 128], dT[ysl][:, g, :], True, True,
                   [b_pw, b_dT[ysl]], [b_banks[mb]])
                dve(lambda e, o=yT[ysl][:, g, :], i=bank(mb)[:, 0:QC], sc_=pscol[:, g:g + 1]:
                    e.tensor_scalar(out=o, in0=i, scalar1=sc_, scalar2=None, op0=ALU.mult),
                    [b_banks[mb], b_cols], [b_yT[ysl]])

            def epilogue_stages(cq, h, osl):
                ysl = cq % 2
                ba, bb = 3 + 2 * osl, 4 + 2 * osl
                rzs = rz[:, osl * 4:(osl + 1) * 4]
                r1n = rz1n[:, osl * 2:(osl + 1) * 2]
                sss = ssb[:, osl * 2:(osl + 1) * 2]
                lns = lnb[:, osl * 2:(osl + 1) * 2]
                rss = rsb[:, osl * 2:(osl + 1) * 2]
                mb = 7

                def st1():
                    for c, bk in ((0, ba), (1, bb)):
                        zsrc = bank(bk)[:, 0:258].rearrange("p (q n) -> p q n", n=129)[:, :, 128:129]
                        zdst = rzs[:, c * 2:(c + 1) * 2].rearrange("p (q o) -> p q o", o=1)
                        dve(lambda e, o=zdst, i=zsrc: e.reciprocal(out=o, in_=i), [b_banks[bk]], [b_rz[osl]])
                    dve(lambda e, o=r1n, i=rzs[:, 2:4]: e.tensor_scalar(out=o, in0=i, scalar1=neglam, scalar2=None,
                                                                         op0=ALU.mult), [b_rz[osl], b_cols], [b_rz[osl]])

                def st2():
                    for qs in range(2):
                        dve(lambda e, o=ot[osl][:, qs, :], i=bank(ba)[:, qs * 129:qs * 129 + 128], sc_=rzs[:, qs:qs + 1]:
                            e.tensor_scalar(out=o, in0=i, scalar1=sc_, scalar2=None, op0=ALU.mult),
                            [b_banks[ba], b_rz[osl]], [b_ot[osl]])

                def st3():
                    for qs in range(2):
                        dve(lambda e, o=ot[osl][:, qs, :], i=bank(bb)[:, qs * 129:qs * 129 + 128], sc_=r1n[:, qs:qs + 1]:
                            e.scalar_tensor_tensor(out=o, in0=i, scalar=sc_, in1=o, op0=ALU.mult, op1=ALU.add),
                            [b_banks[bb], b_rz[osl], b_ot[osl]], [b_ot[osl]])

                def st4():
                    for qs in range(2):
                        dve(lambda e, o=junkB[:, 0:128], i=ot[osl][:, qs, :], a=sss[:, qs:qs + 1]:
                            e.scalar_tensor_tensor(out=o, in0=i, scalar=1.0, in1=i, op0=ALU.mult, op1=ALU.mult,
                                                   accum_out=a),
                            [b_ot[osl]], [b_junkB, b_ssb[osl]])

                def st5():
                    act(lns, sss, AF.Ln, [b_ssb[osl], b_cols], [b_ssb[osl]], bias=eps5, scale=1.0 / 128)
                    act(rss, lns, AF.Exp, [b_ssb[osl]], [b_ssb[osl]], scale=-0.5)

                def st6():
                    for qs in range(2):
                        dve(lambda e, o=yt[osl][:, qs, :], i=ot[osl][:, qs, :], sc_=rss[:, qs:qs + 1]:
                            e.tensor_scalar(out=o, in0=i, scalar1=sc_, scalar2=None, op0=ALU.mult),
                            [b_ot[osl], b_ssb[osl]], [b_yt[osl]])

                def st7():
                    for qs in range(2):
                        tr(bank(mb)[:, qs * 128:(qs + 1) * 128], yt[osl][:, qs, :], 128, [b_yt[osl]], [b_banks[mb]])

                def st8():
                    dve(lambda e, o=yT[ysl][:, 4 + h, :], i=bank(mb)[:, 0:QC]:
                        e.tensor_scalar(out=o, in0=i, scalar1=sg08, scalar2=None, op0=ALU.mult),
                        [b_banks[mb], b_cols], [b_yT[ysl]])
                def st78():
                    st7()
                    st8()
                return [(0, st1), (1, st2), (2, st3), (3, st4), (6, st5), (8, st6), (10, st78)]

            def wo_stages(cq):
                ysl = cq % 2
                sts = []
                k = 0
                for t2 in range(2):
                    tile_i = cq * 2 + t2
                    for half in range(4):
                        def st(t2=t2, half=half, tile_i=tile_i):
                            mb = 7
                            for c in range(8):
                                mm(bank(mb)[:, 0:256], yT[ysl][:, c, t2 * 128:(t2 + 1) * 128],
                                   wo[:, c, half * 256:(half + 1) * 256],
                                   c == 0, c == 7, [b_yT[ysl], b_wo], [b_banks[mb]])
                            hsl = h1[:, tile_i * D + half * 256:tile_i * D + (half + 1) * 256]
                            dve(lambda e, hsl=hsl, mb=mb: e.tensor_tensor(out=hsl, in0=bank(mb)[:, 0:256], in1=hsl, op=ALU.add),
                                [b_banks[mb], b_h1[tile_i]], [b_h1[tile_i]])
                        sts.append((14 + 2 * k, st))
                        k += 1

                    def stq(tile_i=tile_i):
                        dve(lambda e, o=junkB[:, :], i=h1[:, tile_i * D:(tile_i + 1) * D], a=ss2[:, tile_i:tile_i + 1]:
                            e.scalar_tensor_tensor(out=o, in0=i, scalar=1.0, in1=i, op0=ALU.mult, op1=ALU.mult,
                                                   accum_out=a),
                            [b_h1[tile_i]], [b_junkB, b_ss2[tile_i]])
                    sts.append((15 + 2 * (k - 1), stq))
                return sts

            unit = 0

            def av(cqh, j, esl, kn, osl, h):
                ba, bb = 3 + 2 * osl, 4 + 2 * osl
                for c, bk in ((0, ba), (1, bb)):
                    for qs in range(2):
                        mm(bank(bk)[:, qs * 129:(qs + 1) * 129],
                           Et[esl][0:kn, c * QC + qs * 128:c * QC + (qs + 1) * 128],
                           VX[0:kn, j, h, 0:129], (j == 0 and qs == 0), j == NKT - 1,
                           [b_E[esl], b_VX[j]], [b_banks[bk]], skip=True)
                if FILLER:
                    mm(bank(bb)[:, 258:258 + FILLER], Et[esl][0:kn, QC + 128:QC + 256], Qb[osl][0:kn, 0:FILLER],
                       False, False, [b_E[esl], b_Qb[osl]], [b_banks[bb]], skip=True)
                if j == NKT - 1:
                    for off, st in epilogue_stages(cqh, h, osl):
                        defer(unit + off, st)
                    if h == 3:
                        for off, st in wo_stages(cqh):
                            defer(unit + off, st)

            def emit_qb(cq_, h_, slot):
                q0_ = NMETA + cq_ * QC
                for c in range(2):
                    pool(lambda e, o=Qb[slot][c * 64:(c + 1) * 64, c * QC:(c + 1) * QC],
                         i=QT[c * 64:(c + 1) * 64, h_, q0_:q0_ + QC]: e.tensor_copy(out=o, in_=i),
                         [b_QT[min(4, q0_ // 512)], b_QT[min(4, (q0_ + QC - 1) // 512)]], [b_Qb[slot]])

            units = []
            hq_ = 0
            for cq in range(NQC):
                for h in range(4):
                    for j in range(NKT):
                        units.append(dict(cq=cq, h=h, j=j, sl=hq_ % 2, kn=128 if j < NKT - 1 else LPOS - 128 * (NKT - 1)))
                    hq_ += 1
            NU = len(units)

            def emit_qk(u):
                U_ = units[u]
                cq_, h_, j_, kn_, sl_ = U_["cq"], U_["h"], U_["j"], U_["kn"], U_["sl"]
                ssl_ = u % 3
                k0_ = j_ * 128
                mm(bank(ssl_)[0:kn_, :], KT[:, h_, k0_:k0_ + kn_], Qb[sl_][:, :], True, True,
                   [b_KT[j_], b_Qb[sl_]], [b_banks[ssl_]])

            emit_qb(0, 0, 0)
            emit_qk(0)
            emit_qk(1)
            if s == 0:
                dma_pool(wg_b.ap(), wg_l.ap(), [], [b_wgb])
                dma_pool(wu_b.ap(), wu_l.ap(), [], [b_wub])
                dma_pool(wd_b.ap(), wd_l.ap(), [], [b_wdb])
            for u in range(NU):
                U_ = units[u]
                cq, h, j, kn, sl = U_["cq"], U_["h"], U_["j"], U_["kn"], U_["sl"]
                q0 = NMETA + cq * QC
                unit = u + 1
                if h == 0 and j == 0:
                    for t2 in range(2):
                        tile_i = cq * 2 + t2
                        dma_sp(h1[:, tile_i * D:(tile_i + 1) * D], x_d.ap()[s, tile_i * 128:(tile_i + 1) * 128, :],
                               [], [b_h1[tile_i]])
                    for g, w in enumerate((2, 4, 8, 16)):
                        def pst(cq=cq, g=g, w=w, at=unit + 1 + 5 * g + 4):
                            dpart = pool_stage(cq, g, w)
                            defer(at, dpart)
                        defer(unit + 1 + 5 * g, pst)
                if h == 2 and j == 0:
                    for g in range(4):
                        defer(unit + 3 * g, lambda cq=cq, g=g: poolmm_stage(cq, g))
                ssl = u % 3
                esl = u % NE
                k0 = j * 128
                Dd = k0 - q0
                s_ps = bank(ssl)[0:kn, :]
                e_out = Et[esl][0:kn, :]
                maxrel = Dd + kn - 1
                minrel = Dd - (QC - 1)
                if minrel >= 128:
                    act(e_out, s_ps, AF.Exp, [b_banks[ssl], b_cols], [b_E[esl]], bias=ea[0:kn, h:h + 1])
                else:
                    act(e_out, s_ps, AF.Exp, [b_banks[ssl]], [b_E[esl]])
                if maxrel <= -128 or minrel >= 128:
                    pass
                else:
                    i0 = 368 - Dd
                    assert 0 <= i0 <= TW - QC, (i0, Dd)
                    for c in range(2):
                        dve(lambda e, o=Et[esl][0:kn, c * QC:(c + 1) * QC], b=Tb[0:kn, h, i0:i0 + QC]:
                            e.tensor_tensor(out=o, in0=o, in1=b, op=ALU.mult),
                            [b_Tb, b_E[esl]], [b_E[esl]])
                if j == 8 and u + 9 < NU:
                    nU = units[u + 9]
                    emit_qb(nU["cq"], nU["h"], nU["sl"])
                if u + 2 < NU:
                    emit_qk(u + 2)
                if u >= 1:
                    pU = units[u - 1]
                    av(pU["cq"], pU["j"], (u - 1) % NE, pU["kn"], pU["sl"], pU["h"])
                run_deferred(unit, 1)
            pU = units[NU - 1]
            unit = NU + 1
            av(pU["cq"], pU["j"], (NU - 1) % NE, pU["kn"], pU["sl"], pU["h"])
            while deferred:
                run_deferred(10 ** 9, 10 ** 6)
            T.barrier()

            fT = bfv(0, 8 * 1024).rearrange("p (c n) -> p c n", c=8)
            aT = bfv(16384, 22 * 1024).rearrange("p (k n) -> p k n", k=22)
            Wdh = [bfv(61440 + i * 22528, 22 * 512).rearrange("p (k n) -> p k n", k=22) for i in range(2)]
            wgs = [bfv(106496 + i * 8192, 2048).rearrange("p (c n) -> p c n", c=8) for i in range(2)]
            wus = [bfv(106496 + i * 8192 + 4096, 2048).rearrange("p (c n) -> p c n", c=8) for i in range(2)]
            hn2 = [bfv(122880 + i * 2048, 1024) for i in range(2)]
            sgt = [bfv(126976 + i * 1024, 512) for i in range(2)]
            junkC = bfv(129024, 1024)
            b_fT = [B(f"fT{i}") for i in range(8)]
            b_aT = [B(f"aT{i}") for i in range(4)]
            b_Wdh = [B("Wdh0"), B("Wdh1")]
            b_wgs = [B("wgs0"), B("wgs1")]
            b_hn2 = [B("hn2_0"), B("hn2_1")]
            b_sgt = [B("sgt0"), B("sgt1")]
            b_junkC = B("junkC")
            b_st = B("stats")
            dve(lambda e: e.memset(junkC[:, 0:128], 1.0), [], [b_junkC])
            gain_tile(g2col, junkC[:, 0:128], b_junkC)
            b_rs2 = [B("rs2_0"), B("rs2_1")]
            b_st3t = [B(f"st3_{i}") for i in range(8)]

            def c0_stats(sc):
                rd_ss = [b_ss2[t] for t in range(sc * 8, sc * 8 + 8)]
                act(ln2[:, :], ss2[:, sc * 8:(sc + 1) * 8], AF.Ln, rd_ss + [b_cols], [b_rs2[sc]], bias=eps6, scale=1.0 / D)
                act(rs2x[:, sc * 8:(sc + 1) * 8], ln2[:, :], AF.Exp, [b_rs2[sc]], [b_rs2[sc]], scale=-0.5)

            def c0_copy(sc, i):
                t = sc * 8 + i
                hs = i % 2
                act(hn2[hs][:, :], h1[:, t * D:(t + 1) * D], AF.Copy, [b_h1[t], b_rs2[sc]], [b_hn2[hs]],
                    scale=rs2x[:, sc * 8 + i:sc * 8 + i + 1])

            def c0_tr(sc, i, pbase):
                hs = i % 2
                pb = pbase + hs
                for c in range(8):
                    tr(bankb(pb)[:, c * 128:(c + 1) * 128], hn2[hs][:, c * 128:(c + 1) * 128], 128,
                       [b_hn2[hs]], [b_banks[pb]], bf=True)
                src_ps = bankb(pb).rearrange("p (c n) -> p c n", c=8)
                dst = fT[:, :, i * 128:(i + 1) * 128]
                g = gbc[:, :].rearrange("p (c n) -> p c n", c=8)
                dve(lambda e, o=dst, i_=src_ps, g=g: e.tensor_tensor(out=o, in0=i_, in1=g, op=ALU.mult),
                    [b_banks[pb], b_gbc], [b_fT[i]])

            rs2x = sm[:, 88:104]
            c0_stats(0)
            c0_copy(0, 0)
            for i in range(8):
                if i + 1 < 8:
                    c0_copy(0, i + 1)
                c0_tr(0, i, 0)
            for sc in range(2):
                tiles = list(range(sc * 8, sc * 8 + 8))
                dma_sp(Wdh[0], wd_b.ap().rearrange("(k p) n -> p k n", p=128)[:, :, 0:512], [b_wdb], [b_Wdh[0]])
                gu = 0
                for j in range(11):
                    wsl = j % 2
                    dma_sp(wgs[wsl].rearrange("p c n -> p (c n)"), wg_b.ap()[j * 128:(j + 1) * 128, :],
                           [b_wgb], [b_wgs[wsl]])
                    dma_sp(wus[wsl].rearrange("p c n -> p (c n)"), wu_b.ap()[j * 128:(j + 1) * 128, :],
                           [b_wub], [b_wgs[wsl]])
                    for tc in range(2):
                        rdf = [b_fT[tc * 4 + k] for k in range(4)]
                        for sub in range(2):
                            gsl = gu % 2
                            gu += 1
                            bg, bu_ = 4 + 2 * gsl, 5 + 2 * gsl
                            for c in range(8):
                                mm(bank(bg), wgs[wsl][:, c, sub * 128:(sub + 1) * 128], fT[:, c, tc * 512:(tc + 1) * 512],
                                   c == 0, c == 7, [b_wgs[wsl]] + rdf, [b_banks[bg]])
                            for c in range(8):
                                mm(bank(bu_), wus[wsl][:, c, sub * 128:(sub + 1) * 128], fT[:, c, tc * 512:(tc + 1) * 512],
                                   c == 0, c == 7, [b_wgs[wsl]] + rdf, [b_banks[bu_]])
                            act(sgt[gsl][:, :], bank(bg), AF.Silu, [b_banks[bg]], [b_sgt[gsl]])
                            kf = 2 * j + sub
                            dve(lambda e, gsl=gsl, bu_=bu_, kf=kf, tc=tc:
                                e.tensor_tensor(out=aT[:, kf, tc * 512:(tc + 1) * 512], in0=bank(bu_), in1=sgt[gsl][:, :],
                                                op=ALU.mult), [b_banks[bu_], b_sgt[gsl]], [b_aT[tc * 2 + (kf % 2)]])
                dma_sp(Wdh[1], wd_b.ap().rearrange("(k p) n -> p k n", p=128)[:, :, 512:1024], [b_wdb], [b_Wdh[1]])
                dn = 0
                nxt = sc + 1 if sc + 1 < 2 else None
                if nxt is not None:
                    c0_stats(nxt)
                    c0_copy(nxt, 0)
                for half in range(2):
                    for i, t in enumerate(tiles):
                        bi = dn % 4
                        dn += 1
                        tc = i // 4
                        for k in range(22):
                            mm(bank(bi), aT[:, k, i * 128:(i + 1) * 128], Wdh[half][:, k, :], k == 0, k == 21,
                               [b_aT[tc * 2], b_aT[tc * 2 + 1], b_Wdh[half]], [b_banks[bi]])
                        hsl = h1[:, t * D + half * 512:t * D + (half + 1) * 512]
                        dve(lambda e, hsl=hsl, bi=bi: e.tensor_tensor(out=hsl, in0=bank(bi), in1=hsl, op=ALU.add),
                            [b_banks[bi], b_h1[t]], [b_h1[t]])
                        if nxt is not None and half == 0:
                            if i + 1 < 8:
                                c0_copy(nxt, i + 1)
                            c0_tr(nxt, i, 4)
                        if half == 1:
                            hfull = h1[:, t * D:(t + 1) * D]
                            act(junkC[:, :], hfull, AF.Square, [b_h1[t]], [b_junkC, b_st3t[i]], accum=ss3[:, i:i + 1])
                            act(ln3[:, i:i + 1], ss3[:, i:i + 1], AF.Ln, [b_st3t[i], b_cols], [b_st3t[i]], bias=eps6,
                                scale=1.0 / D)
                            act(rs3[:, i:i + 1], ln3[:, i:i + 1], AF.Exp, [b_st3t[i]], [b_st3t[i]], scale=-0.5)
                            dve(lambda e, hsl=hfull, i=i: e.scalar_tensor_tensor(out=hsl, in0=hsl, scalar=rs3[:, i:i + 1],
                                                                                in1=fgb[:, :], op0=ALU.mult, op1=ALU.mult),
                                [b_h1[t], b_st3t[i], b_fgb], [b_h1[t]])
                            dma_pool(out_d.ap()[s, t * 128:(t + 1) * 128, :], hfull, [b_h1[t]], [])
            T.barrier()

        T.finalize()
        with nc.Block() as block:
            @block.tensor
            def _(e):
                T.play("pe", e, sems, dsems)

            @block.scalar
            def _(e):
                T.play("act", e, sems, dsems)

            @block.vector
            def _(e):
                T.play("dve", e, sems, dsems)

            @block.gpsimd
            def _(e):
                T.play("pool", e, sems, dsems)

            @block.sync
            def _(e):
                T.play("sp", e, sems, dsems, final_wait=True)
    return nc


_CACHE = {}


def _host_layouts(inp):
    f = lambda a: np.ascontiguousarray(np.asarray(a, dtype=np.float32))
    w_in = f(inp["w_in"])[0]
    w_o = f(inp["w_o"])[0]
    w_gate = f(inp["w_gate"])[0]
    w_up = f(inp["w_up"])[0]
    w_down = f(inp["w_down"])[0]
    rel_bias = f(inp["rel_bias"])
    n = np.arange(NLUT)
    bucket = _t5_bucket_np(495 - n)
    onehot = np.zeros((32, NLUT), np.float32)
    onehot[bucket, n] = 1.0
    shared = {
        "meta": f(inp["meta_tokens"]),
        "w_in_l": np.ascontiguousarray(w_in.reshape(8, 128, 2048).transpose(1, 0, 2)).reshape(1024, 2048),
        "w_o_l": np.ascontiguousarray(w_o.reshape(8, 128, 1024).transpose(1, 0, 2)).reshape(1024, 1024),
        "w_gate_l": np.ascontiguousarray(w_gate.reshape(8, 128, 11, 256).transpose(2, 1, 0, 3)).reshape(1408, 2048),
        "w_up_l": np.ascontiguousarray(w_up.reshape(8, 128, 11, 256).transpose(2, 1, 0, 3)).reshape(1408, 2048),
        "w_down_l": w_down,
        "pool_w_l": np.ascontiguousarray(f(inp["pool_w"])[0].transpose(1, 0, 2)).reshape(128, 512),
        "colpack": np.ascontiguousarray(np.concatenate([
            f(inp["norm1_g"])[0].reshape(8, 128).T,
            f(inp["norm2_g"])[0].reshape(8, 128).T,
            f(inp["pool_scale"])[0].reshape(4, 128).T,
            f(inp["subln_g"])[0].reshape(128, 1),
            np.zeros((128, 3), np.float32),
            np.broadcast_to(np.concatenate([rel_bias[15], rel_bias[31]])[None, :], (128, 8)),
        ], axis=1)),
        "fgb": np.ascontiguousarray(np.broadcast_to(f(inp["final_g"])[None, :], (128, D))),
        "lamv": np.ascontiguousarray(np.broadcast_to(np.concatenate(
            [f(inp["lambda_q1"])[0], f(inp["lambda_k1"])[0], f(inp["lambda_q2"])[0], f(inp["lambda_k2"])[0]])[None, :],
            (128, 256))),
        "relb": rel_bias,
        "onehot": onehot,
        "ident": np.eye(128, dtype=np.float32),
        "aident": np.ascontiguousarray(np.eye(128, dtype=np.float32)[::-1]),
    }
    return shared


def kernel(**inputs):
    x = np.ascontiguousarray(np.asarray(inputs["x"], dtype=np.float32))
    shared = _host_layouts(inputs)
    if "nc" not in _CACHE:
        _CACHE["nc"] = build_program()
    nc = _CACHE["nc"]
    in_maps = []
    for c in range(NCORES):
        m = dict(shared)
        m["x"] = x[2 * c:2 * c + 2]
        in_maps.append(m)
    res = run_bass_kernel_spmd(nc, in_maps, core_ids=list(range(NCORES)))
    out = np.concatenate([np.asarray(r["out"], dtype=np.float32) for r in res.results], axis=0)
    return out
```

```python
import math
import numpy as np
import ml_dtypes
import concourse.bass as bass
import concourse.mybir as mybir
from concourse.bass_utils import run_bass_kernel_spmd

F32 = mybir.dt.float32
BF16 = mybir.dt.bfloat16
AF = mybir.ActivationFunctionType
ALU = mybir.AluOpType

NCORES = 8
SEQ = 2048
D = 1024
NMETA = 16
LPOS = SEQ + NMETA
DFF = 2816
NKT = 17
QC = 256
NQC = SEQ // QC
TW = 768
NLUT = 896
LAMBDA_INIT = 0.8 - 0.6 * math.exp(-0.3 * 0)
NS_DMA = 8
FILLER = 0


class Buf:
    __slots__ = ("name", "psum", "last_w", "readers")

    def __init__(self, name, psum=False):
        self.name = name
        self.psum = psum
        self.last_w = None
        self.readers = {}


class Op:
    __slots__ = ("eng", "fn", "deps", "inc", "seq", "dma", "semkey", "val", "prev")

    def __init__(self, eng, fn, deps, dma):
        self.eng = eng
        self.fn = fn
        self.deps = deps
        self.inc = False
        self.seq = 0
        self.dma = dma
        self.semkey = None
        self.val = 0
        self.prev = None


class Tracker:
    ENGS = ("pe", "act", "dve", "pool", "sp")

    def __init__(self):
        self.ops = {e: [] for e in self.ENGS}
        self.dma_cnt = {"sp": 0, "pool": 0}
        self.dma_uses = {}
        self.dma_last = {}
        self.pending = {e: set() for e in self.ENGS}
        self.recent_dma = []
        self.all_dma = []

    def emit(self, eng, fn, reads=(), writes=(), dma=False):
        deps = set()
        for b in reads:
            if b.psum:
                if b.last_w is not None:
                    deps.add(b.last_w)
                deps.update(b.readers.values())
            elif b.last_w is not None:
                deps.add(b.last_w)
        for b in writes:
            if b.last_w is not None:
                deps.add(b.last_w)
            deps.update(b.readers.values())
        if self.pending[eng]:
            deps.update(self.pending[eng])
            self.pending[eng] = set()
        if eng == "pe":
            deps = {d for d in deps if not (d.eng == "pe" and not d.dma)}
        op = Op(eng, fn, deps, dma)
        if dma:
            k = self.dma_cnt[eng] % NS_DMA
            self.dma_cnt[eng] += 1
            key = (eng, k)
            n = self.dma_uses.get(key, 0) + 1
            self.dma_uses[key] = n
            op.semkey = key
            op.val = 16 * n
            op.prev = self.dma_last.get(key)
            self.dma_last[key] = op
            self.recent_dma.append(op)
            self.all_dma.append(op)
        self.ops[eng].append(op)
        for b in reads:
            if b.psum:
                b.last_w = op
                b.readers = {}
            else:
                b.readers[(eng, id(op)) if dma else eng] = op
        for b in writes:
            b.last_w = op
            b.readers = {}
        return op

    def barrier(self, exclude=()):
        deps = set(self.recent_dma) - set(exclude)
        self.recent_dma = []
        for e in self.ENGS:
            if self.ops[e]:
                last = self.ops[e][-1]
                if not last.dma:
                    deps.add(last)
                else:
                    for o in reversed(self.ops[e]):
                        if not o.dma:
                            deps.add(o)
                            break
        for e in self.ENGS:
            self.pending[e] = set(deps) | self.pending[e]

    def finalize(self):
        for e in self.ENGS:
            for op in self.ops[e]:
                for d in op.deps:
                    if not d.dma:
                        d.inc = True
        for e in self.ENGS:
            c = 0
            for op in self.ops[e]:
                if op.inc and not op.dma:
                    c += 1
                    op.seq = c

    def play(self, eng, handle, sems, dsems, final_wait=False):
        waited = {}

        def wait(key, semobj, val):
            if waited.get(key, 0) >= val:
                return
            handle.wait_ge(semobj, val)
            waited[key] = val

        for op in self.ops[eng]:
            for d in op.deps:
                if d.dma:
                    wait(d.semkey, dsems[d.semkey], d.val)
                else:
                    wait(d.eng, sems[d.eng], d.seq)
            if op.dma and op.prev is not None:
                wait(op.prev.semkey, dsems[op.prev.semkey], op.prev.val)
            inst = op.fn(handle)
            if op.dma:
                inst.then_inc(dsems[op.semkey], 16)
            elif op.inc:
                inst.then_inc(sems[eng], 1)
        if final_wait:
            for key, op in self.dma_last.items():
                wait(key, dsems[key], op.val)


def _t5_bucket_np(rel):
    rel = np.asarray(rel, dtype=np.int64)
    nb = 16
    ret = np.where(rel > 0, nb, 0)
    n = np.abs(rel)
    max_exact = 8
    nf = np.maximum(n, 1).astype(np.float32)
    large = max_exact + (np.log(nf / np.float32(max_exact)) / np.float32(math.log(128 / max_exact))
                         * np.float32(nb - max_exact)).astype(np.int32)
    large = np.minimum(large, nb - 1)
    return ret + np.where(n < max_exact, n, large)


def build_program():
    nc = bass.Bass("TRN2", target_bir_lowering=False)
    T = Tracker()

    def din(name, shape, dt=F32):
        return nc.dram_tensor(name, list(shape), dt, kind="ExternalInput")

    x_d = din("x", [2, SEQ, D])
    meta_d = din("meta", [NMETA, D])
    win_l = din("w_in_l", [1024, 2048])
    wo_l = din("w_o_l", [1024, 1024])
    wg_l = din("w_gate_l", [1408, 2048])
    wu_l = din("w_up_l", [1408, 2048])
    wd_l = din("w_down_l", [DFF, 1024])
    pw_l = din("pool_w_l", [128, 512])
    colpack_d = din("colpack", [128, 32])
    fgb_d = din("fgb", [128, D])
    lam_d = din("lamv", [128, 256])
    relb_d = din("relb", [32, 4])
    oh_d = din("onehot", [32, NLUT])
    ident_d = din("ident", [128, 128])
    aident_d = din("aident", [128, 128])
    out_d = nc.dram_tensor("out", [2, SEQ, D], F32, kind="ExternalOutput")

    win_b = nc.dram_tensor("win_b", [1024, 2048], BF16, kind="Internal")
    wo_b = nc.dram_tensor("wo_b", [1024, 1024], BF16, kind="Internal")
    wg_b = nc.dram_tensor("wg_b", [1408, 2048], BF16, kind="Internal")
    wu_b = nc.dram_tensor("wu_b", [1408, 2048], BF16, kind="Internal")
    wd_b = nc.dram_tensor("wd_b", [DFF, 1024], BF16, kind="Internal")
    lut_d = nc.dram_tensor("lut_d", [4, NLUT], F32, kind="Internal")

    ARENA = 131072
    ctxs = dict(
        arena=nc.sbuf_tensor("arena", [128, ARENA // 2], BF16),
        h1=nc.sbuf_tensor("h1", [128, 16 * D], F32),
        ident=nc.sbuf_tensor("ident_sb", [128, 128], F32),
        gbc=nc.sbuf_tensor("gbc", [128, D], F32),
        fgb=nc.sbuf_tensor("fgb_sb", [128, D], F32),
        pw=nc.sbuf_tensor("pw_sb", [128, 512], BF16),
        identb=nc.sbuf_tensor("identb_sb", [128, 128], BF16),
        jb=nc.sbuf_tensor("jb_sb", [128, 128], BF16),
        cols=nc.sbuf_tensor("cols", [128, 64], F32),
        lamv=nc.sbuf_tensor("lamv_sb", [128, 256], F32),
        sm=nc.sbuf_tensor("sm", [128, 128], F32),
        relb=nc.sbuf_tensor("relb_sb", [32, 4], F32),
        oh=nc.sbuf_tensor("oh_sb", [32, NLUT], F32),
        ps=nc.psum_tensor("ps", [128, 8 * 512], F32),
    )
    sem_names = ["pe", "act", "dve", "pool", "sp"]
    from contextlib import ExitStack
    with ExitStack() as es:
        tens = {k: es.enter_context(v) for k, v in ctxs.items()}
        sems = {e: es.enter_context(nc.semaphore("s_" + e)) for e in sem_names}
        dsems = {}
        for e in ("sp", "pool"):
            for k in range(NS_DMA):
                dsems[(e, k)] = es.enter_context(nc.semaphore(f"d_{e}{k}"))

        arena_bf = tens["arena"]
        arena_f = arena_bf.bitcast(F32)
        h1 = tens["h1"]
        ident = tens["ident"]
        gbc = tens["gbc"]
        fgb = tens["fgb"]
        pw = tens["pw"]
        cols = tens["cols"]
        lamv = tens["lamv"]
        sm = tens["sm"]
        relb = tens["relb"]
        ohsb = tens["oh"]
        ps = tens["ps"]
        psb = ps.bitcast(BF16)
        identb = tens["identb"]
        jb = tens["jb"]

        def bfv(off, n):
            return arena_bf[:, off // 2: off // 2 + n]

        def f32v(off, n):
            return arena_f[:, off // 4: off // 4 + n]

        def bank(i):
            return ps[:, i * 512:(i + 1) * 512]

        def bankb(i):
            return psb[:, i * 1024:(i + 1) * 1024]

        g1col = cols[:, 0:8]
        g2col = cols[:, 8:16]
        pscol = cols[:, 16:20]
        sgcol = cols[:, 20:21]
        sg08 = cols[:, 21:22]
        farb = cols[:, 24:32]
        neglam = cols[:, 32:33]
        eps6 = cols[:, 33:34]
        eps5 = cols[:, 34:35]
        lam_s = cols[:, 36:38]
        lam_e = cols[:, 38:40]
        ssA = sm[:, 0:2]
        lnA = sm[:, 2:4]
        rsA = sm[:, 4:6]
        ss2 = sm[:, 8:24]
        ln2 = sm[:, 24:32]
        rs2 = sm[:, 32:40]
        ss3 = sm[:, 40:48]
        ln3 = sm[:, 48:56]
        rs3 = sm[:, 56:64]
        rz = sm[:, 64:72]
        rz1n = sm[:, 72:76]
        ssb = sm[:, 76:80]
        lnb = sm[:, 80:84]
        rsb = sm[:, 84:88]

        B = lambda name, psum=False: Buf(name, psum)
        b_banks = [B(f"bank{i}", True) for i in range(8)]
        b_h1 = [B(f"h1_{i}") for i in range(16)]
        b_ident = B("ident")
        b_gbc = B("gbc")
        b_fgb = B("fgb")
        b_pw = B("pw")
        b_cols = B("cols")
        b_lamv = B("lamv")
        b_lams = B("lams")
        b_relb = B("relb")
        b_oh = B("oh")
        b_winb, b_wob, b_wgb, b_wub, b_wdb = B("winb"), B("wob"), B("wgb"), B("wub"), B("wdb")
        b_lutd = B("lutd")
        b_ss2 = [B(f"ss2_{i}") for i in range(16)]

        emit = T.emit

        def dma_sp(out, in_, reads, writes):
            return emit("sp", lambda e, o=out, i=in_: e.dma_start(out=o, in_=i), reads, writes, dma=True)

        def dma_pool(out, in_, reads, writes):
            return emit("pool", lambda e, o=out, i=in_: e.dma_start(out=o, in_=i), reads, writes, dma=True)

        def mm(out, lhsT, rhs, start, stop, reads, writes, skip=False):
            return emit("pe", lambda e, o=out, l=lhsT, r=rhs, s=start, t=stop, k=skip:
                        e.matmul(o, lhsT=l, rhs=r, start=s, stop=t, skip_group_check=k), reads, writes)

        def tr(out, in_, npart, reads, writes, bf=False):
            idn = (identb if bf else ident)[0:npart, 0:npart]
            return emit("pe", lambda e, o=out, i=in_, d=idn: e.transpose(o, i, d), list(reads) + [b_ident], writes)

        def act(out, in_, func, reads, writes, bias=None, scale=None, accum=None):
            def fn(e, o=out, i=in_, f=func, b=bias, s=scale, a=accum):
                kw = {}
                if b is not None:
                    kw["bias"] = b
                if s is not None:
                    kw["scale"] = s
                if a is not None:
                    kw["accum_out"] = a
                return e.activation(out=o, in_=i, func=f, **kw)
            return emit("act", fn, reads, writes)

        def dve(fn, reads, writes):
            return emit("dve", fn, reads, writes)

        def pool(fn, reads, writes):
            return emit("pool", fn, reads, writes)

        dma_sp(cols[:, 0:32], colpack_d.ap(), [], [b_cols])
        dma_sp(ident[:], ident_d.ap(), [], [b_ident])
        dma_sp(lamv[:], lam_d.ap(), [], [b_lamv])
        dma_sp(relb[:], relb_d.ap(), [], [b_relb])
        dma_sp(ohsb[:], oh_d.ap(), [], [b_oh])
        win0 = bfv(67600, 8 * 2048).rearrange("p (c n) -> p c n", c=8)
        b_win0 = B("win0")
        b_stg = [B(f"wstg{c}") for c in range(8)]
        for c in range(8):
            dma_sp(h1[:, c * 2048:(c + 1) * 2048],
                   win_l.ap().rearrange("(p a) n -> p a n", p=128)[:, c, :], [], [b_stg[c]])
        dma_sp(fgb[:], fgb_d.ap(), [], [b_fgb])
        b_win0c = [B(f"win0_{c}") for c in range(8)]
        for c in range(8):
            dve(lambda e, c=c: e.tensor_copy(out=win0[:, c, :], in_=h1[:, c * 2048:(c + 1) * 2048]),
                [b_stg[c]], [b_win0c[c]])
        early_ex = [dma_sp(win_b.ap().rearrange("(p a) n -> p (a n)", p=128), win0.rearrange("p c n -> p (c n)"),
                           b_win0c, [b_winb])]

        dve(lambda e: e.tensor_copy(out=identb[:], in_=ident[:]), [b_ident], [b_ident])
        b_jb = B("jb")
        jstage = f32v(8192, 128)
        b_jst = B("jstage")
        dma_sp(jstage, aident_d.ap(), [], [b_jst])
        dve(lambda e: e.tensor_copy(out=jb[:], in_=jstage), [b_jst], [b_jb])
        dve(lambda e: e.memset(cols[:, 33:34], 1e-6), [], [b_cols])
        dve(lambda e: e.memset(cols[:, 34:35], 1e-5), [], [b_cols])
        dve(lambda e: e.tensor_scalar(out=sg08, in0=sgcol, scalar1=float(1.0 - LAMBDA_INIT), scalar2=None,
                                      op0=ALU.mult), [b_cols], [b_cols])
        negbb = cols[:, 40:44]
        ea = cols[:, 44:48]
        dve(lambda e: e.tensor_scalar(out=negbb, in0=farb[:, 0:4], scalar1=-1.0, scalar2=None, op0=ALU.mult),
            [b_cols], [b_cols])
        dve(lambda e: e.tensor_tensor(out=ea, in0=farb[:, 4:8], in1=farb[:, 0:4], op=ALU.subtract),
            [b_cols], [b_cols])
        b_lamp = B("lamp")
        lamp = f32v(0, 128)
        lamj = bfv(1024, 128)
        dve(lambda e: e.tensor_tensor(out=lamp[:, 0:64], in0=lamv[:, 0:64], in1=lamv[:, 64:128], op=ALU.mult),
            [b_lamv], [b_lamp])
        dve(lambda e: e.tensor_tensor(out=lamp[:, 64:128], in0=lamv[:, 128:192], in1=lamv[:, 192:256], op=ALU.mult),
            [b_lamv], [b_lamp])
        b_lamj = B("lamj")
        act(lamj[:, 0:64], lamp[:, 0:64], AF.Copy, [b_lamp], [b_lamj, b_lams], accum=lam_s[:, 0:1])
        act(lamj[:, 0:64], lamp[:, 64:128], AF.Copy, [b_lamp], [b_lamj, b_lams], accum=lam_s[:, 1:2])
        act(lam_e, lam_s, AF.Exp, [b_lams], [b_lams])
        dve(lambda e: e.tensor_tensor(out=neglam, in0=lam_e[:, 1:2], in1=lam_e[:, 0:1], op=ALU.subtract),
            [b_lams], [b_cols])
        dve(lambda e: e.tensor_scalar(out=neglam, in0=neglam, scalar1=float(-LAMBDA_INIT), scalar2=None,
                                      op0=ALU.add), [b_cols], [b_cols])
        lutsb = f32v(4096, NLUT)
        b_lutsb = B("lutsb")
        for half in range(2):
            mm(bank(0)[0:4, 0:448], relb[:, :], ohsb[:, half * 448:(half + 1) * 448], True, True,
               [b_relb, b_oh], [b_banks[0]])
            dve(lambda e, hf=half: e.tensor_copy(out=lutsb[0:4, hf * 448:(hf + 1) * 448], in_=bank(0)[0:4, 0:448]),
                [b_banks[0]], [b_lutsb])
        dma_sp(lut_d.ap(), lutsb[0:4, :], [b_lutsb], [b_lutd])
        T.barrier(exclude=early_ex)

        O_QT, O_KT, O_U, O_VX, O_X = 0, 16640, 33280, 49920, 67600
        PW_ = 2080
        QT = bfv(O_QT, 4 * PW_).rearrange("p (h n) -> p h n", h=4)
        KT = bfv(O_KT, 4 * PW_).rearrange("p (h n) -> p h n", h=4)
        UU = bfv(O_U, 4 * PW_).rearrange("p (h n) -> p h n", h=4)
        VX = bfv(O_VX, NKT * 4 * 130).rearrange("p (k h n) -> p k h n", k=NKT, h=4)

        def gain_tile(gcol, ones_ap, b_ones):
            for c in range(8):
                dve(lambda e, c=c: e.tensor_scalar(out=gbc[:, c * 128:(c + 1) * 128], in0=ones_ap,
                                                   scalar1=gcol[:, c:c + 1], scalar2=None, op0=ALU.mult),
                    [b_cols, b_ones], [b_gbc])

        for s in range(2):
            b_QT = [B(f"QT{i}") for i in range(5)]
            b_KT = [B(f"KT{i}") for i in range(NKT)]
            b_U = [B(f"U{i}") for i in range(5)]
            b_Upad = B("Upad")
            b_VX = [B(f"VX{i}") for i in range(NKT)]
            win = bfv(O_X, 8 * 2048).rearrange("p (c n) -> p c n", c=8)
            uT = [bfv(O_X + 32768 + i * 8192, 8 * 512).rearrange("p (c n) -> p c n", c=8) for i in range(2)]
            xs = [f32v(O_X + 49152 + i * 4096, 1024) for i in range(2)]
            xb = [bfv(O_X + 57344 + i * 2048, 1024) for i in range(2)]
            b_win = B("win")
            b_uT = [B("uT0"), B("uT1")]
            b_xs = [B("xs0"), B("xs1")]
            b_xb = [B("xb0"), B("xb1")]
            b_ssA = [B("ssA0"), B("ssA1")]

            rd_win = b_win0c if s == 0 else [b_win]
            if s == 0:
                pass
            else:
                dma_sp(win.rearrange("p c n -> p (c n)"), win_b.ap().rearrange("(p a) n -> p (a n)", p=128),
                       [b_winb], [b_win])
            dve(lambda e: e.memset(xs[1][:, 0:128], 1.0), [], [b_xs[1]])
            gain_tile(g1col, xs[1][:, 0:128], b_xs[1])
            pool(lambda e: e.memset(UU[:, :, 2064:2080], 0.0), [], [b_Upad])
            pool(lambda e: e.memset(VX[:, :, :, 128:130], 1.0), [], b_VX)

            def norm_tile(ti, slot):
                np_ = NMETA if ti < 0 else 128
                src = meta_d.ap() if ti < 0 else x_d.ap()[s, ti * 128:(ti + 1) * 128, :]
                xt = xs[slot]
                xbt = xb[slot]
                dma_sp(xt[0:np_, :], src, [], [b_xs[slot]])
                act(xbt[0:np_, :], xt[0:np_, :], AF.Square, [b_xs[slot]], [b_xb[slot], b_ssA[slot]],
                    accum=ssA[0:np_, slot:slot + 1])
                act(lnA[0:np_, slot:slot + 1], ssA[0:np_, slot:slot + 1], AF.Ln, [b_ssA[slot], b_cols], [b_ssA[slot]],
                    bias=eps6[0:np_, :], scale=1.0 / D)
                act(rsA[0:np_, slot:slot + 1], lnA[0:np_, slot:slot + 1], AF.Exp, [b_ssA[slot]], [b_ssA[slot]],
                    scale=-0.5)
                act(xbt[0:np_, :], xt[0:np_, :], AF.Copy, [b_ssA[slot], b_xs[slot]], [b_xb[slot]],
                    scale=rsA[0:np_, slot:slot + 1])
                pb = slot
                for c in range(8):
                    tr(bankb(pb)[:, c * 128:c * 128 + np_], xbt[0:np_, c * 128:(c + 1) * 128], np_,
                       [b_xb[slot]], [b_banks[pb]], bf=True)
                p0 = 0 if ti < 0 else NMETA + ti * 128
                a = p0
                while a < p0 + np_:
                    pc = a // 512
                    bnd = min(p0 + np_, (pc + 1) * 512)
                    n = bnd - a
                    off = a - p0
                    src_ps = bankb(pb).rearrange("p (c n) -> p c n", c=8)[:, :, off:off + n]
                    dst = uT[pc % 2][:, :, a - pc * 512:a - pc * 512 + n]
                    g = gbc[:, :].rearrange("p (c n) -> p c n", c=8)[:, :, 0:n]
                    dve(lambda e, o=dst, i=src_ps, g=g: e.tensor_tensor(out=o, in0=i, in1=g, op=ALU.mult),
                        [b_banks[pb], b_gbc], [b_uT[pc % 2]])
                    a = bnd

            kindc = [0]

            def inproj_groups(pc):
                n = 512 if pc < 4 else 16
                u = uT[pc % 2]
                bu = b_uT[pc % 2]
                pos0 = pc * 512
                groups = []
                for grp in range(3):
                    for h in range(4):
                        def g_(grp=grp, h=h):
                            bi = 2 + (kindc[0] % 6)
                            kindc[0] += 1
                            colbase = {0: 512, 1: 1024, 2: 0}[grp] + h * 128
                            for c in range(8):
                                mm(bank(bi)[:, 0:n], win[:, c, colbase:colbase + 128], u[:, c, 0:n], c == 0, c == 7,
                                   rd_win + [bu], [b_banks[bi]])
                            if grp == 0:
                                dve(lambda e, o=QT[:, h, pos0:pos0 + n], i=bank(bi)[:, 0:n]:
                                    e.tensor_scalar(out=o, in0=i, scalar1=0.125, scalar2=None, op0=ALU.mult),
                                    [b_banks[bi]], [b_QT[pc]])
                            elif grp == 1:
                                wr = [b_KT[k] for k in range(pos0 // 128, (pos0 + n - 1) // 128 + 1)]
                                dve(lambda e, o=KT[:, h, pos0:pos0 + n], i=bank(bi)[:, 0:n]:
                                    e.tensor_copy(out=o, in_=i), [b_banks[bi]], wr)
                            else:
                                act(UU[:, h, pos0:pos0 + n], bank(bi)[:, 0:n], AF.Copy, [b_banks[bi]], [b_U[pc]])
                        groups.append(g_)
                nt = (n + 127) // 128
                for kk in range(nt):
                    def gv(kk=kk):
                        m = min(128, n - kk * 128)
                        kt = pos0 // 128 + kk
                        bi = 2 + (kindc[0] % 6)
                        kindc[0] += 1
                        for c in range(8):
                            mm(bank(bi)[0:m, :], u[:, c, kk * 128:kk * 128 + m], win[:, c, 1536:2048], c == 0, c == 7,
                               rd_win + [bu], [b_banks[bi]])
                        src_ps = bank(bi)[0:m, :].rearrange("p (h n) -> p h n", h=4)
                        if kk % 2 == 0:
                            dve(lambda e, o=VX[0:m, kt, :, 0:128], sp_=src_ps: e.tensor_copy(out=o, in_=sp_),
                                [b_banks[bi]], [b_VX[kt]])
                        else:
                            act(VX[0:m, kt, :, 0:128], src_ps, AF.Copy, [b_banks[bi]], [b_VX[kt]])
                    groups.append(gv)
                return groups

            tidx = 0
            order = [-1] + list(range(16))
            for ti in order[0:5]:
                norm_tile(ti, tidx % 2)
                tidx += 1
            nxt_tile = 5
            if s == 0:
                gate = [b_uT[0], b_uT[1]]
                dma_pool(wo_b.ap(), wo_l.ap(), gate, [b_wob])
                dma_pool(pw[:], pw_l.ap(), gate, [b_pw])
            for pc in range(5):
                groups = inproj_groups(pc)
                ng = len(groups)
                for gi, g_ in enumerate(groups):
                    g_()
                    if pc < 3 and gi % 4 == 3 and nxt_tile < len(order) and nxt_tile < 5 + 4 * (pc + 1):
                        norm_tile(order[nxt_tile], tidx % 2)
                        tidx += 1
                        nxt_tile += 1
                while pc < 3 and nxt_tile < 5 + 4 * (pc + 1):
                    norm_tile(order[nxt_tile], tidx % 2)
                    tidx += 1
                    nxt_tile += 1
            T.barrier()

            from collections import deque
            wo = bfv(O_X, 8 * 1024).rearrange("p (c n) -> p c n", c=8)
            Tst = f32v(O_X + 16384, 4 * TW).rearrange("p (h n) -> p h n", h=4)
            o2 = O_X + 16384 + 12288
            NE = 4
            Et = [bfv(o2 + i * 1024, 512) for i in range(NE)]
            o2 += NE * 1024
            Tb = bfv(o2, 4 * TW).rearrange("p (h n) -> p h n", h=4)
            o2 += 4 * TW * 2
            yT_off = o2
            yT = [bfv(o2 + i * 4096, 8 * QC).rearrange("p (c n) -> p c n", c=8) for i in range(2)]
            o2 += 8192
            ot = [f32v(o2 + i * 1024, 256).rearrange("p (q n) -> p q n", q=2) for i in range(2)]
            o2 += 2048
            yt = [f32v(o2 + i * 1024, 256).rearrange("p (q n) -> p q n", q=2) for i in range(2)]
            o2 += 2048
            tA = f32v(o2, 272)
            tB = f32v(o2 + 1088, 272)
            o2 += 2176
            dT = [bfv(o2 + i * 2048, 4 * QC).rearrange("p (g n) -> p g n", g=4) for i in range(2)]
            o2 += 4096
            junkB = bfv(o2, 1024)
            o2 += 2048
            Qb = [bfv(o2 + i * 1024, 512) for i in range(2)]
            o2 += 2048
            assert o2 <= ARENA, o2
            b_wo = B("wo")
            b_E = [B(f"E{i}") for i in range(NE)]
            b_sb = [B(f"sb{i}") for i in range(2)]
            b_yT = [B(f"yT{i}") for i in range(2)]
            b_ot = [B(f"ot{i}") for i in range(2)]
            b_yt = [B(f"yt{i}") for i in range(2)]
            b_tA, b_tB = B("tA"), B("tB")
            b_dT = [B(f"dT{i}") for i in range(2)]
            b_junkB = B("junkB")
            b_rz = [B("rz0"), B("rz1")]
            b_ssb = [B("ssb0"), B("ssb1")]
            b_Qb = [B("Qb0"), B("Qb1")]

            dma_sp(wo.rearrange("p c n -> p (c n)"), wo_b.ap().rearrange("(p a) n -> p (a n)", p=128),
                   [b_wob], [b_wo])
            for i in range(2):
                pool(lambda e, i=i: e.memset(Qb[i][:, :], 0.0), [], [b_Qb[i]])
            b_Tst = [B(f"Tst{h}") for h in range(4)]
            b_Tb = B("Tb")
            Hb = bfv(yT_off, 4 * TW)
            for h in range(4):
                src = bass.AP(lut_d, h * NLUT, [[1, 128], [1, TW]])
                dma_sp(Tst[:, h, :], src, [b_lutd], [b_Tst[h]])
                act(Hb[:, h * TW:(h + 1) * TW], Tst[:, h, :], AF.Exp, [b_Tst[h], b_cols], b_yT, bias=negbb[:, h:h + 1])
            tb_banks = [7, 3, 4, 5]
            k = 0
            for h in range(4):
                for half in range(2):
                    bk = tb_banks[k % 4]
                    k += 1
                    mm(bank(bk)[:, 0:384], jb[:, :], Hb[:, h * TW + half * 384:h * TW + (half + 1) * 384], True, True,
                       b_yT + [b_jb], [b_banks[bk]])
                    dve(lambda e, o=Tb[:, h, half * 384:(half + 1) * 384], i=bank(bk)[:, 0:384]:
                        e.tensor_copy(out=o, in_=i), [b_banks[bk]], [b_Tb])

            import heapq
            deferred = []
            dseq = [0]

            def defer(at, fn):
                dseq[0] += 1
                heapq.heappush(deferred, (at, dseq[0], fn))

            def run_deferred(cur, nmax):
                k = 0
                while deferred and k < nmax and deferred[0][0] <= cur:
                    heapq.heappop(deferred)[2]()
                    k += 1

            def pool_stage(cq, g, w):
                q0 = NMETA + cq * QC
                ysl = cq % 2

                def zz(lo, n):
                    return UU[:, g, q0 + lo:q0 + lo + n]
                rdU = [b_U[min(4, (q0 - 8) // 512)], b_U[min(4, (q0 + 263) // 512)], b_Upad]

                def padd(o, a, b, reads, writes):
                    pool(lambda e, o=o, a=a, b=b: e.tensor_tensor(out=o, in0=a, in1=b, op=ALU.add), reads, writes)
                if w == 2:
                    padd(tA[:, 0:256], zz(0, 256), zz(-1, 256), rdU, [b_tA])
                    ws, wb, wsl = tA, b_tA, 0
                else:
                    padd(tA[:, 1:272], zz(-7, 271), zz(-8, 271), rdU, [b_tA])
                    padd(tB[:, 2:271], tA[:, 3:272], tA[:, 1:270], [b_tA], [b_tB])
                    ws, wb, wsl = tB, b_tB, 8
                    if w >= 8:
                        padd(tA[:, 4:269], tB[:, 6:271], tB[:, 2:267], [b_tB], [b_tA])
                        ws, wb = tA, b_tA
                    if w == 16:
                        padd(tB[:, 8:265], tA[:, 12:269], tA[:, 4:261], [b_tA], [b_tB])
                        ws, wb = tB, b_tB
                def dpart():
                    dve(lambda e, o=dT[ysl][:, g, :], a=ws[:, wsl:wsl + 256], sc_=1.0 / w, b=zz(0, 256):
                        e.scalar_tensor_tensor(out=o, in0=a, scalar=sc_, in1=b, op0=ALU.mult, op1=ALU.subtract),
                        [wb] + rdU, [b_dT[ysl]])
                    if cq == NQC - 1:
                        right = w - 1 - w // 2
                        for r in range(right):
                            cnt = w - (right - r)
                            col = 255 - r
                            dve(lambda e, o=dT[ysl][:, g, col:col + 1], a=ws[:, wsl + col:wsl + col + 1], sc_=1.0 / cnt,
                                b=zz(col, 1):
                                e.scalar_tensor_tensor(out=o, in0=a, scalar=sc_, in1=b, op0=ALU.mult, op1=ALU.subtract),
                                [wb] + rdU, [b_dT[ysl]])
                return dpart

            def poolmm_stage(cq, g):
                ysl = cq % 2
                mb = 7
                mm(bank(mb)[:, 0:QC], pw[:, g * 128:(g + 1) * 128], dT[ysl][:, g, :], True, True,
                   [b_pw, b_dT[ysl]], [b_banks[mb]])
                dve(lambda e, o=yT[ysl][:, g, :], i=bank(mb)[:, 0:QC], sc_=pscol[:, g:g + 1]:
                    e.tensor_scalar(out=o, in0=i, scalar1=sc_, scalar2=None, op0=ALU.mult),
                    [b_banks[mb], b_cols], [b_yT[ysl]])

            def epilogue_stages(cq, h, osl):
                ysl = cq % 2
                ba, bb = 3 + 2 * osl, 4 + 2 * osl
                rzs = rz[:, osl * 4:(osl + 1) * 4]
                r1n = rz1n[:, osl * 2:(osl + 1) * 2]
                sss = ssb[:, osl * 2:(osl + 1) * 2]
                lns = lnb[:, osl * 2:(osl + 1) * 2]
                rss = rsb[:, osl * 2:(osl + 1) * 2]
                mb = 7

                def st1():
                    for c, bk in ((0, ba), (1, bb)):
                        zsrc = bank(bk)[:, 0:258].rearrange("p (q n) -> p q n", n=129)[:, :, 128:129]
                        zdst = rzs[:, c * 2:(c + 1) * 2].rearrange("p (q o) -> p q o", o=1)
                        dve(lambda e, o=zdst, i=zsrc: e.reciprocal(out=o, in_=i), [b_banks[bk]], [b_rz[osl]])
                    dve(lambda e, o=r1n, i=rzs[:, 2:4]: e.tensor_scalar(out=o, in0=i, scalar1=neglam, scalar2=None,
                                                                         op0=ALU.mult), [b_rz[osl], b_cols], [b_rz[osl]])

                def st2():
                    for qs in range(2):
                        dve(lambda e, o=ot[osl][:, qs, :], i=bank(ba)[:, qs * 129:qs * 129 + 128], sc_=rzs[:, qs:qs + 1]:
                            e.tensor_scalar(out=o, in0=i, scalar1=sc_, scalar2=None, op0=ALU.mult),
                            [b_banks[ba], b_rz[osl]], [b_ot[osl]])

                def st3():
                    for qs in range(2):
                        dve(lambda e, o=ot[osl][:, qs, :], i=bank(bb)[:, qs * 129:qs * 129 + 128], sc_=r1n[:, qs:qs + 1]:
                            e.scalar_tensor_tensor(out=o, in0=i, scalar=sc_, in1=o, op0=ALU.mult, op1=ALU.add),
                            [b_banks[bb], b_rz[osl], b_ot[osl]], [b_ot[osl]])

                def st4():
                    for qs in range(2):
                        dve(lambda e, o=junkB[:, 0:128], i=ot[osl][:, qs, :], a=sss[:, qs:qs + 1]:
                            e.scalar_tensor_tensor(out=o, in0=i, scalar=1.0, in1=i, op0=ALU.mult, op1=ALU.mult,
                                                   accum_out=a),
                            [b_ot[osl]], [b_junkB, b_ssb[osl]])

                def st5():
                    act(lns, sss, AF.Ln, [b_ssb[osl], b_cols], [b_ssb[osl]], bias=eps5, scale=1.0 / 128)
                    act(rss, lns, AF.Exp, [b_ssb[osl]], [b_ssb[osl]], scale=-0.5)

                def st6():
                    for qs in range(2):
                        dve(lambda e, o=yt[osl][:, qs, :], i=ot[osl][:, qs, :], sc_=rss[:, qs:qs + 1]:
                            e.tensor_scalar(out=o, in0=i, scalar1=sc_, scalar2=None, op0=ALU.mult),
                            [b_ot[osl], b_ssb[osl]], [b_yt[osl]])

                def st7():
                    for qs in range(2):
                        tr(bank(mb)[:, qs * 128:(qs + 1) * 128], yt[osl][:, qs, :], 128, [b_yt[osl]], [b_banks[mb]])

                def st8():
                    dve(lambda e, o=yT[ysl][:, 4 + h, :], i=bank(mb)[:, 0:QC]:
                        e.tensor_scalar(out=o, in0=i, scalar1=sg08, scalar2=None, op0=ALU.mult),
                        [b_banks[mb], b_cols], [b_yT[ysl]])
                def st78():
                    st7()
                    st8()
                return [(0, st1), (1, st2), (2, st3), (3, st4), (6, st5), (8, st6), (10, st78)]

            def wo_stages(cq):
                ysl = cq % 2
                sts = []
                k = 0
                for t2 in range(2):
                    tile_i = cq * 2 + t2
                    for half in range(4):
                        def st(t2=t2, half=half, tile_i=tile_i):
                            mb = 7
                            for c in range(8):
                                mm(bank(mb)[:, 0:256], yT[ysl][:, c, t2 * 128:(t2 + 1) * 128],
                                   wo[:, c, half * 256:(half + 1) * 256],
                                   c == 0, c == 7, [b_yT[ysl], b_wo], [b_banks[mb]])
                            hsl = h1[:, tile_i * D + half * 256:tile_i * D + (half + 1) * 256]
                            dve(lambda e, hsl=hsl, mb=mb: e.tensor_tensor(out=hsl, in0=bank(mb)[:, 0:256], in1=hsl, op=ALU.add),
                                [b_banks[mb], b_h1[tile_i]], [b_h1[tile_i]])
                        sts.append((14 + 2 * k, st))
                        k += 1

                    def stq(tile_i=tile_i):
                        dve(lambda e, o=junkB[:, :], i=h1[:, tile_i * D:(tile_i + 1) * D], a=ss2[:, tile_i:tile_i + 1]:
                            e.scalar_tensor_tensor(out=o, in0=i, scalar=1.0, in1=i, op0=ALU.mult, op1=ALU.mult,
                                                   accum_out=a),
                            [b_h1[tile_i]], [b_junkB, b_ss2[tile_i]])
                    sts.append((15 + 2 * (k - 1), stq))
                return sts

            unit = 0

            def av(cqh, j, esl, kn, osl, h):
                ba, bb = 3 + 2 * osl, 4 + 2 * osl
                for c, bk in ((0, ba), (1, bb)):
                    for qs in range(2):
                        mm(bank(bk)[:, qs * 129:(qs + 1) * 129],
                           Et[esl][0:kn, c * QC + qs * 128:c * QC + (qs + 1) * 128],
                           VX[0:kn, j, h, 0:129], (j == 0 and qs == 0), j == NKT - 1,
                           [b_E[esl], b_VX[j]], [b_banks[bk]], skip=True)
                if FILLER:
                    mm(bank(bb)[:, 258:258 + FILLER], Et[esl][0:kn, QC + 128:QC + 256], Qb[osl][0:kn, 0:FILLER],
                       False, False, [b_E[esl], b_Qb[osl]], [b_banks[bb]], skip=True)
                if j == NKT - 1:
                    for off, st in epilogue_stages(cqh, h, osl):
                        defer(unit + off, st)
                    if h == 3:
                        for off, st in wo_stages(cqh):
                            defer(unit + off, st)

            def emit_qb(cq_, h_, slot):
                q0_ = NMETA + cq_ * QC
                for c in range(2):
                    pool(lambda e, o=Qb[slot][c * 64:(c + 1) * 64, c * QC:(c + 1) * QC],
                         i=QT[c * 64:(c + 1) * 64, h_, q0_:q0_ + QC]: e.tensor_copy(out=o, in_=i),
                         [b_QT[min(4, q0_ // 512)], b_QT[min(4, (q0_ + QC - 1) // 512)]], [b_Qb[slot]])

            units = []
            hq_ = 0
            for cq in range(NQC):
                for h in range(4):
                    for j in range(NKT):
                        units.append(dict(cq=cq, h=h, j=j, sl=hq_ % 2, kn=128 if j < NKT - 1 else LPOS - 128 * (NKT - 1)))
                    hq_ += 1
            NU = len(units)

            def emit_qk(u):
                U_ = units[u]
                cq_, h_, j_, kn_, sl_ = U_["cq"], U_["h"], U_["j"], U_["kn"], U_["sl"]
                ssl_ = u % 3
                k0_ = j_ * 128
                mm(bank(ssl_)[0:kn_, :], KT[:, h_, k0_:k0_ + kn_], Qb[sl_][:, :], True, True,
                   [b_KT[j_], b_Qb[sl_]], [b_banks[ssl_]])

            emit_qb(0, 0, 0)
            emit_qk(0)
            emit_qk(1)
            cast_pieces = []
            if s == 0:
                for r in range(11):
                    cast_pieces.append((wg_b.ap()[r * 128:(r + 1) * 128, :], wg_l.ap()[r * 128:(r + 1) * 128, :], b_wgb))
                    cast_pieces.append((wu_b.ap()[r * 128:(r + 1) * 128, :], wu_l.ap()[r * 128:(r + 1) * 128, :], b_wub))
                for r in range(11):
                    cast_pieces.append((wd_b.ap()[r * 256:(r + 1) * 256, :], wd_l.ap()[r * 256:(r + 1) * 256, :], b_wdb))
            for u in range(NU):
                U_ = units[u]
                cq, h, j, kn, sl = U_["cq"], U_["h"], U_["j"], U_["kn"], U_["sl"]
                q0 = NMETA + cq * QC
                unit = u + 1
                if h == 0 and j == 0:
                    for t2 in range(2):
                        tile_i = cq * 2 + t2
                        dma_sp(h1[:, tile_i * D:(tile_i + 1) * D], x_d.ap()[s, tile_i * 128:(tile_i + 1) * 128, :],
                               [], [b_h1[tile_i]])
                    for g, w in enumerate((2, 4, 8, 16)):
                        def pst(cq=cq, g=g, w=w, at=unit + 1 + 5 * g + 4):
                            dpart = pool_stage(cq, g, w)
                            defer(at, dpart)
                        defer(unit + 1 + 5 * g, pst)
                if h == 2 and j == 0:
                    for g in range(4):
                        defer(unit + 3 * g, lambda cq=cq, g=g: poolmm_stage(cq, g))
                ssl = u % 3
                esl = u % NE
                k0 = j * 128
                Dd = k0 - q0
                s_ps = bank(ssl)[0:kn, :]
                e_out = Et[esl][0:kn, :]
                maxrel = Dd + kn - 1
                minrel = Dd - (QC - 1)
                if minrel >= 128:
                    act(e_out, s_ps, AF.Exp, [b_banks[ssl], b_cols], [b_E[esl]], bias=ea[0:kn, h:h + 1])
                else:
                    act(e_out, s_ps, AF.Exp, [b_banks[ssl]], [b_E[esl]])
                if maxrel <= -128 or minrel >= 128:
                    pass
                else:
                    i0 = 368 - Dd
                    assert 0 <= i0 <= TW - QC, (i0, Dd)
                    for c in range(2):
                        dve(lambda e, o=Et[esl][0:kn, c * QC:(c + 1) * QC], b=Tb[0:kn, h, i0:i0 + QC]:
                            e.tensor_tensor(out=o, in0=o, in1=b, op=ALU.mult),
                            [b_Tb, b_E[esl]], [b_E[esl]])
                if j == 4 and cast_pieces:
                    o_, i_, bb_ = cast_pieces.pop(0)
                    dma_pool(o_, i_, [], [bb_])
                if j == 8 and u + 9 < NU:
                    nU = units[u + 9]
                    emit_qb(nU["cq"], nU["h"], nU["sl"])
                if u + 2 < NU:
                    emit_qk(u + 2)
                if u >= 1:
                    pU = units[u - 1]
                    av(pU["cq"], pU["j"], (u - 1) % NE, pU["kn"], pU["sl"], pU["h"])
                run_deferred(unit, 1)
            pU = units[NU - 1]
            unit = NU + 1
            av(pU["cq"], pU["j"], (NU - 1) % NE, pU["kn"], pU["sl"], pU["h"])
            while deferred:
                run_deferred(10 ** 9, 10 ** 6)
            while cast_pieces:
                o_, i_, bb_ = cast_pieces.pop(0)
                dma_pool(o_, i_, [], [bb_])
            T.barrier()

            fT = bfv(0, 8 * 1024).rearrange("p (c n) -> p c n", c=8)
            aT = bfv(16384, 22 * 1024).rearrange("p (k n) -> p k n", k=22)
            Wdh = [bfv(61440 + i * 22528, 22 * 512).rearrange("p (k n) -> p k n", k=22) for i in range(2)]
            wgs = [bfv(106496 + i * 8192, 2048).rearrange("p (c n) -> p c n", c=8) for i in range(2)]
            wus = [bfv(106496 + i * 8192 + 4096, 2048).rearrange("p (c n) -> p c n", c=8) for i in range(2)]
            hn2 = [bfv(122880 + i * 2048, 1024) for i in range(2)]
            sgt = [bfv(126976 + i * 1024, 512) for i in range(2)]
            junkC = bfv(129024, 1024)
            b_fT = [B(f"fT{i}") for i in range(8)]
            b_aT = [B(f"aT{i}") for i in range(4)]
            b_Wdh = [B("Wdh0"), B("Wdh1")]
            b_wgs = [B("wgs0"), B("wgs1")]
            b_hn2 = [B("hn2_0"), B("hn2_1")]
            b_sgt = [B("sgt0"), B("sgt1")]
            b_junkC = B("junkC")
            b_st = B("stats")
            dve(lambda e: e.memset(junkC[:, 0:128], 1.0), [], [b_junkC])
            gain_tile(g2col, junkC[:, 0:128], b_junkC)
            b_rs2 = [B("rs2_0"), B("rs2_1")]
            b_st3t = [B(f"st3_{i}") for i in range(8)]

            def c0_stats(sc):
                rd_ss = [b_ss2[t] for t in range(sc * 8, sc * 8 + 8)]
                act(ln2[:, :], ss2[:, sc * 8:(sc + 1) * 8], AF.Ln, rd_ss + [b_cols], [b_rs2[sc]], bias=eps6, scale=1.0 / D)
                act(rs2x[:, sc * 8:(sc + 1) * 8], ln2[:, :], AF.Exp, [b_rs2[sc]], [b_rs2[sc]], scale=-0.5)

            def c0_copy(sc, i):
                t = sc * 8 + i
                hs = i % 2
                act(hn2[hs][:, :], h1[:, t * D:(t + 1) * D], AF.Copy, [b_h1[t], b_rs2[sc]], [b_hn2[hs]],
                    scale=rs2x[:, sc * 8 + i:sc * 8 + i + 1])

            def c0_tr(sc, i, pbase):
                hs = i % 2
                pb = pbase + hs
                for c in range(8):
                    tr(bankb(pb)[:, c * 128:(c + 1) * 128], hn2[hs][:, c * 128:(c + 1) * 128], 128,
                       [b_hn2[hs]], [b_banks[pb]], bf=True)
                src_ps = bankb(pb).rearrange("p (c n) -> p c n", c=8)
                dst = fT[:, :, i * 128:(i + 1) * 128]
                g = gbc[:, :].rearrange("p (c n) -> p c n", c=8)
                dve(lambda e, o=dst, i_=src_ps, g=g: e.tensor_tensor(out=o, in0=i_, in1=g, op=ALU.mult),
                    [b_banks[pb], b_gbc], [b_fT[i]])

            rs2x = sm[:, 88:104]
            c0_stats(0)
            c0_copy(0, 0)
            for i in range(8):
                if i + 1 < 8:
                    c0_copy(0, i + 1)
                c0_tr(0, i, 0)
            for sc in range(2):
                tiles = list(range(sc * 8, sc * 8 + 8))
                dma_sp(Wdh[0], wd_b.ap().rearrange("(k p) n -> p k n", p=128)[:, :, 0:512], [b_wdb], [b_Wdh[0]])
                gu = 0
                for j in range(11):
                    wsl = j % 2
                    dma_sp(wgs[wsl].rearrange("p c n -> p (c n)"), wg_b.ap()[j * 128:(j + 1) * 128, :],
                           [b_wgb], [b_wgs[wsl]])
                    dma_sp(wus[wsl].rearrange("p c n -> p (c n)"), wu_b.ap()[j * 128:(j + 1) * 128, :],
                           [b_wub], [b_wgs[wsl]])
                    for tc in range(2):
                        rdf = [b_fT[tc * 4 + k] for k in range(4)]
                        for sub in range(2):
                            gsl = gu % 2
                            gu += 1
                            bg, bu_ = 4 + 2 * gsl, 5 + 2 * gsl
                            for c in range(8):
                                mm(bank(bg), wgs[wsl][:, c, sub * 128:(sub + 1) * 128], fT[:, c, tc * 512:(tc + 1) * 512],
                                   c == 0, c == 7, [b_wgs[wsl]] + rdf, [b_banks[bg]])
                            for c in range(8):
                                mm(bank(bu_), wus[wsl][:, c, sub * 128:(sub + 1) * 128], fT[:, c, tc * 512:(tc + 1) * 512],
                                   c == 0, c == 7, [b_wgs[wsl]] + rdf, [b_banks[bu_]])
                            act(sgt[gsl][:, :], bank(bg), AF.Silu, [b_banks[bg]], [b_sgt[gsl]])
                            kf = 2 * j + sub
                            dve(lambda e, gsl=gsl, bu_=bu_, kf=kf, tc=tc:
                                e.tensor_tensor(out=aT[:, kf, tc * 512:(tc + 1) * 512], in0=bank(bu_), in1=sgt[gsl][:, :],
                                                op=ALU.mult), [b_banks[bu_], b_sgt[gsl]], [b_aT[tc * 2 + (kf % 2)]])
                dma_sp(Wdh[1], wd_b.ap().rearrange("(k p) n -> p k n", p=128)[:, :, 512:1024], [b_wdb], [b_Wdh[1]])
                dn = 0
                nxt = sc + 1 if sc + 1 < 2 else None
                if nxt is not None:
                    c0_stats(nxt)
                    c0_copy(nxt, 0)
                for half in range(2):
                    for i, t in enumerate(tiles):
                        bi = dn % 4
                        dn += 1
                        tc = i // 4
                        for k in range(22):
                            mm(bank(bi), aT[:, k, i * 128:(i + 1) * 128], Wdh[half][:, k, :], k == 0, k == 21,
                               [b_aT[tc * 2], b_aT[tc * 2 + 1], b_Wdh[half]], [b_banks[bi]])
                        hsl = h1[:, t * D + half * 512:t * D + (half + 1) * 512]
                        dve(lambda e, hsl=hsl, bi=bi: e.tensor_tensor(out=hsl, in0=bank(bi), in1=hsl, op=ALU.add),
                            [b_banks[bi], b_h1[t]], [b_h1[t]])
                        if nxt is not None and half == 0:
                            if i + 1 < 8:
                                c0_copy(nxt, i + 1)
                            c0_tr(nxt, i, 4)
                        if half == 1:
                            hfull = h1[:, t * D:(t + 1) * D]
                            act(junkC[:, :], hfull, AF.Square, [b_h1[t]], [b_junkC, b_st3t[i]], accum=ss3[:, i:i + 1])
                            act(ln3[:, i:i + 1], ss3[:, i:i + 1], AF.Ln, [b_st3t[i], b_cols], [b_st3t[i]], bias=eps6,
                                scale=1.0 / D)
                            act(rs3[:, i:i + 1], ln3[:, i:i + 1], AF.Exp, [b_st3t[i]], [b_st3t[i]], scale=-0.5)
                            dve(lambda e, hsl=hfull, i=i: e.scalar_tensor_tensor(out=hsl, in0=hsl, scalar=rs3[:, i:i + 1],
                                                                                in1=fgb[:, :], op0=ALU.mult, op1=ALU.mult),
                                [b_h1[t], b_st3t[i], b_fgb], [b_h1[t]])
                            dma_pool(out_d.ap()[s, t * 128:(t + 1) * 128, :], hfull, [b_h1[t]], [])
            T.barrier()

        T.finalize()
        with nc.Block() as block:
            @block.tensor
            def _(e):
                T.play("pe", e, sems, dsems)

            @block.scalar
            def _(e):
                T.play("act", e, sems, dsems)

            @block.vector
            def _(e):
                T.play("dve", e, sems, dsems)

            @block.gpsimd
            def _(e):
                T.play("pool", e, sems, dsems)

            @block.sync
            def _(e):
                T.play("sp", e, sems, dsems, final_wait=True)
    return nc


_CACHE = {}


def _host_layouts(inp):
    f = lambda a: np.ascontiguousarray(np.asarray(a, dtype=np.float32))
    w_in = f(inp["w_in"])[0]
    w_o = f(inp["w_o"])[0]
    w_gate = f(inp["w_gate"])[0]
    w_up = f(inp["w_up"])[0]
    w_down = f(inp["w_down"])[0]
    rel_bias = f(inp["rel_bias"])
    n = np.arange(NLUT)
    bucket = _t5_bucket_np(495 - n)
    onehot = np.zeros((32, NLUT), np.float32)
    onehot[bucket, n] = 1.0
    shared = {
        "meta": f(inp["meta_tokens"]),
        "w_in_l": np.ascontiguousarray(w_in.reshape(8, 128, 2048).transpose(1, 0, 2)).reshape(1024, 2048),
        "w_o_l": np.ascontiguousarray(w_o.reshape(8, 128, 1024).transpose(1, 0, 2)).reshape(1024, 1024),
        "w_gate_l": np.ascontiguousarray(w_gate.reshape(8, 128, 11, 256).transpose(2, 1, 0, 3)).reshape(1408, 2048),
        "w_up_l": np.ascontiguousarray(w_up.reshape(8, 128, 11, 256).transpose(2, 1, 0, 3)).reshape(1408, 2048),
        "w_down_l": w_down,
        "pool_w_l": np.ascontiguousarray(f(inp["pool_w"])[0].transpose(1, 0, 2)).reshape(128, 512),
        "colpack": np.ascontiguousarray(np.concatenate([
            f(inp["norm1_g"])[0].reshape(8, 128).T,
            f(inp["norm2_g"])[0].reshape(8, 128).T,
            f(inp["pool_scale"])[0].reshape(4, 128).T,
            f(inp["subln_g"])[0].reshape(128, 1),
            np.zeros((128, 3), np.float32),
            np.broadcast_to(np.concatenate([rel_bias[15], rel_bias[31]])[None, :], (128, 8)),
        ], axis=1)),
        "fgb": np.ascontiguousarray(np.broadcast_to(f(inp["final_g"])[None, :], (128, D))),
        "lamv": np.ascontiguousarray(np.broadcast_to(np.concatenate(
            [f(inp["lambda_q1"])[0], f(inp["lambda_k1"])[0], f(inp["lambda_q2"])[0], f(inp["lambda_k2"])[0]])[None, :],
            (128, 256))),
        "relb": rel_bias,
        "onehot": onehot,
        "ident": np.eye(128, dtype=np.float32),
        "aident": np.ascontiguousarray(np.eye(128, dtype=np.float32)[::-1]),
    }
    return shared


def kernel(**inputs):
    x = np.ascontiguousarray(np.asarray(inputs["x"], dtype=np.float32))
    shared = _host_layouts(inputs)
    if "nc" not in _CACHE:
        _CACHE["nc"] = build_program()
    nc = _CACHE["nc"]
    in_maps = []
    for c in range(NCORES):
        m = dict(shared)
        m["x"] = x[2 * c:2 * c + 2]
        in_maps.append(m)
    res = run_bass_kernel_spmd(nc, in_maps, core_ids=list(range(NCORES)))
    out = np.concatenate([np.asarray(r["out"], dtype=np.float32) for r in res.results], axis=0)
    return out
```

```python
import math
import numpy as np
import ml_dtypes
import concourse.bass as bass
import concourse.mybir as mybir
from concourse.bass_utils import run_bass_kernel_spmd

F32 = mybir.dt.float32
BF16 = mybir.dt.bfloat16
AF = mybir.ActivationFunctionType
ALU = mybir.AluOpType

NCORES = 8
SEQ = 2048
D = 1024
NMETA = 16
LPOS = SEQ + NMETA
DFF = 2816
NKT = 17
QC = 256
NQC = SEQ // QC
TW = 768
NLUT = 896
LAMBDA_INIT = 0.8 - 0.6 * math.exp(-0.3 * 0)
NS_DMA = 8
FILLER = 0


class Buf:
    __slots__ = ("name", "psum", "last_w", "readers")

    def __init__(self, name, psum=False):
        self.name = name
        self.psum = psum
        self.last_w = None
        self.readers = {}


class Op:
    __slots__ = ("eng", "fn", "deps", "inc", "seq", "dma", "semkey", "val", "prev")

    def __init__(self, eng, fn, deps, dma):
        self.eng = eng
        self.fn = fn
        self.deps = deps
        self.inc = False
        self.seq = 0
        self.dma = dma
        self.semkey = None
        self.val = 0
        self.prev = None


class Tracker:
    ENGS = ("pe", "act", "dve", "pool", "sp")

    def __init__(self):
        self.ops = {e: [] for e in self.ENGS}
        self.dma_cnt = {"sp": 0, "pool": 0}
        self.dma_uses = {}
        self.dma_last = {}
        self.pending = {e: set() for e in self.ENGS}
        self.recent_dma = []
        self.all_dma = []

    def emit(self, eng, fn, reads=(), writes=(), dma=False):
        deps = set()
        for b in reads:
            if b.psum:
                if b.last_w is not None:
                    deps.add(b.last_w)
                deps.update(b.readers.values())
            elif b.last_w is not None:
                deps.add(b.last_w)
        for b in writes:
            if b.last_w is not None:
                deps.add(b.last_w)
            deps.update(b.readers.values())
        if self.pending[eng]:
            deps.update(self.pending[eng])
            self.pending[eng] = set()
        if eng == "pe":
            deps = {d for d in deps if not (d.eng == "pe" and not d.dma)}
        op = Op(eng, fn, deps, dma)
        if dma:
            k = self.dma_cnt[eng] % NS_DMA
            self.dma_cnt[eng] += 1
            key = (eng, k)
            n = self.dma_uses.get(key, 0) + 1
            self.dma_uses[key] = n
            op.semkey = key
            op.val = 16 * n
            op.prev = self.dma_last.get(key)
            self.dma_last[key] = op
            self.recent_dma.append(op)
            self.all_dma.append(op)
        self.ops[eng].append(op)
        for b in reads:
            if b.psum:
                b.last_w = op
                b.readers = {}
            else:
                b.readers[(eng, id(op)) if dma else eng] = op
        for b in writes:
            b.last_w = op
            b.readers = {}
        return op

    def barrier(self, exclude=()):
        deps = set(self.recent_dma) - set(exclude)
        self.recent_dma = []
        for e in self.ENGS:
            if self.ops[e]:
                last = self.ops[e][-1]
                if not last.dma:
                    deps.add(last)
                else:
                    for o in reversed(self.ops[e]):
                        if not o.dma:
                            deps.add(o)
                            break
        for e in self.ENGS:
            self.pending[e] = set(deps) | self.pending[e]

    def finalize(self):
        for e in self.ENGS:
            for op in self.ops[e]:
                for d in op.deps:
                    if not d.dma:
                        d.inc = True
        for e in self.ENGS:
            c = 0
            for op in self.ops[e]:
                if op.inc and not op.dma:
                    c += 1
                    op.seq = c

    def play(self, eng, handle, sems, dsems, final_wait=False):
        waited = {}

        def wait(key, semobj, val):
            if waited.get(key, 0) >= val:
                return
            handle.wait_ge(semobj, val)
            waited[key] = val

        for op in self.ops[eng]:
            for d in op.deps:
                if d.dma:
                    wait(d.semkey, dsems[d.semkey], d.val)
                else:
                    wait(d.eng, sems[d.eng], d.seq)
            if op.dma and op.prev is not None:
                wait(op.prev.semkey, dsems[op.prev.semkey], op.prev.val)
            inst = op.fn(handle)
            if op.dma:
                inst.then_inc(dsems[op.semkey], 16)
            elif op.inc:
                inst.then_inc(sems[eng], 1)
        if final_wait:
            for key, op in self.dma_last.items():
                wait(key, dsems[key], op.val)


def _t5_bucket_np(rel):
    rel = np.asarray(rel, dtype=np.int64)
    nb = 16
    ret = np.where(rel > 0, nb, 0)
    n = np.abs(rel)
    max_exact = 8
    nf = np.maximum(n, 1).astype(np.float32)
    large = max_exact + (np.log(nf / np.float32(max_exact)) / np.float32(math.log(128 / max_exact))
                         * np.float32(nb - max_exact)).astype(np.int32)
    large = np.minimum(large, nb - 1)
    return ret + np.where(n < max_exact, n, large)


def build_program():
    nc = bass.Bass("TRN2", target_bir_lowering=False)
    T = Tracker()

    def din(name, shape, dt=F32):
        return nc.dram_tensor(name, list(shape), dt, kind="ExternalInput")

    x_d = din("x", [2, SEQ, D])
    meta_d = din("meta", [NMETA, D])
    win_l = din("w_in_l", [1024, 2048])
    wo_l = din("w_o_l", [1024, 1024])
    wg_l = din("w_gate_l", [1408, 2048])
    wu_l = din("w_up_l", [1408, 2048])
    wd_l = din("w_down_l", [DFF, 1024])
    pw_l = din("pool_w_l", [128, 512])
    colpack_d = din("colpack", [128, 32])
    fgb_d = din("fgb", [128, D])
    lam_d = din("lamv", [128, 256])
    relb_d = din("relb", [32, 4])
    oh_d = din("onehot", [32, NLUT])
    ident_d = din("ident", [128, 128])
    aident_d = din("aident", [128, 128])
    out_d = nc.dram_tensor("out", [2, SEQ, D], F32, kind="ExternalOutput")

    win_b = nc.dram_tensor("win_b", [1024, 2048], BF16, kind="Internal")
    wo_b = nc.dram_tensor("wo_b", [1024, 1024], BF16, kind="Internal")
    wg_b = nc.dram_tensor("wg_b", [1408, 2048], BF16, kind="Internal")
    wu_b = nc.dram_tensor("wu_b", [1408, 2048], BF16, kind="Internal")
    wd_b = nc.dram_tensor("wd_b", [DFF, 1024], BF16, kind="Internal")
    lut_d = nc.dram_tensor("lut_d", [4, NLUT], F32, kind="Internal")

    ARENA = 131072
    ctxs = dict(
        arena=nc.sbuf_tensor("arena", [128, ARENA // 2], BF16),
        h1=nc.sbuf_tensor("h1", [128, 16 * D], F32),
        ident=nc.sbuf_tensor("ident_sb", [128, 128], F32),
        gbc=nc.sbuf_tensor("gbc", [128, D], F32),
        fgb=nc.sbuf_tensor("fgb_sb", [128, D], F32),
        pw=nc.sbuf_tensor("pw_sb", [128, 512], BF16),
        identb=nc.sbuf_tensor("identb_sb", [128, 128], BF16),
        jb=nc.sbuf_tensor("jb_sb", [128, 128], BF16),
        cols=nc.sbuf_tensor("cols", [128, 64], F32),
        lamv=nc.sbuf_tensor("lamv_sb", [128, 256], F32),
        sm=nc.sbuf_tensor("sm", [128, 128], F32),
        relb=nc.sbuf_tensor("relb_sb", [32, 4], F32),
        oh=nc.sbuf_tensor("oh_sb", [32, NLUT], F32),
        ps=nc.psum_tensor("ps", [128, 8 * 512], F32),
    )
    sem_names = ["pe", "act", "dve", "pool", "sp"]
    from contextlib import ExitStack
    with ExitStack() as es:
        tens = {k: es.enter_context(v) for k, v in ctxs.items()}
        sems = {e: es.enter_context(nc.semaphore("s_" + e)) for e in sem_names}
        dsems = {}
        for e in ("sp", "pool"):
            for k in range(NS_DMA):
                dsems[(e, k)] = es.enter_context(nc.semaphore(f"d_{e}{k}"))

        arena_bf = tens["arena"]
        arena_f = arena_bf.bitcast(F32)
        h1 = tens["h1"]
        ident = tens["ident"]
        gbc = tens["gbc"]
        fgb = tens["fgb"]
        pw = tens["pw"]
        cols = tens["cols"]
        lamv = tens["lamv"]
        sm = tens["sm"]
        relb = tens["relb"]
        ohsb = tens["oh"]
        ps = tens["ps"]
        psb = ps.bitcast(BF16)
        identb = tens["identb"]
        jb = tens["jb"]

        def bfv(off, n):
            return arena_bf[:, off // 2: off // 2 + n]

        def f32v(off, n):
            return arena_f[:, off // 4: off // 4 + n]

        def bank(i):
            return ps[:, i * 512:(i + 1) * 512]

        def bankb(i):
            return psb[:, i * 1024:(i + 1) * 1024]

        g1col = cols[:, 0:8]
        g2col = cols[:, 8:16]
        pscol = cols[:, 16:20]
        sgcol = cols[:, 20:21]
        sg08 = cols[:, 21:22]
        farb = cols[:, 24:32]
        neglam = cols[:, 32:33]
        eps6 = cols[:, 33:34]
        eps5 = cols[:, 34:35]
        lam_s = cols[:, 36:38]
        lam_e = cols[:, 38:40]
        ssA = sm[:, 0:2]
        lnA = sm[:, 2:4]
        rsA = sm[:, 4:6]
        ss2 = sm[:, 8:24]
        ln2 = sm[:, 24:32]
        rs2 = sm[:, 32:40]
        ss3 = sm[:, 40:48]
        ln3 = sm[:, 48:56]
        rs3 = sm[:, 56:64]
        rz = sm[:, 64:72]
        rz1n = sm[:, 72:76]
        ssb = sm[:, 76:80]
        lnb = sm[:, 80:84]
        rsb = sm[:, 84:88]

        B = lambda name, psum=False: Buf(name, psum)
        b_banks = [B(f"bank{i}", True) for i in range(8)]
        b_h1 = [B(f"h1_{i}") for i in range(16)]
        b_ident = B("ident")
        b_gbc = B("gbc")
        b_fgb = B("fgb")
        b_pw = B("pw")
        b_cols = B("cols")
        b_lamv = B("lamv")
        b_lams = B("lams")
        b_relb = B("relb")
        b_oh = B("oh")
        b_winb, b_wob, b_wgb, b_wub, b_wdb = B("winb"), B("wob"), B("wgb"), B("wub"), B("wdb")
        b_lutd = B("lutd")
        b_ss2 = [B(f"ss2_{i}") for i in range(16)]

        emit = T.emit

        def dma_sp(out, in_, reads, writes):
            return emit("sp", lambda e, o=out, i=in_: e.dma_start(out=o, in_=i), reads, writes, dma=True)

        def dma_pool(out, in_, reads, writes):
            return emit("pool", lambda e, o=out, i=in_: e.dma_start(out=o, in_=i), reads, writes, dma=True)

        def mm(out, lhsT, rhs, start, stop, reads, writes, skip=False):
            return emit("pe", lambda e, o=out, l=lhsT, r=rhs, s=start, t=stop, k=skip:
                        e.matmul(o, lhsT=l, rhs=r, start=s, stop=t, skip_group_check=k), reads, writes)

        def tr(out, in_, npart, reads, writes, bf=False):
            idn = (identb if bf else ident)[0:npart, 0:npart]
            return emit("pe", lambda e, o=out, i=in_, d=idn: e.transpose(o, i, d), list(reads) + [b_ident], writes)

        def act(out, in_, func, reads, writes, bias=None, scale=None, accum=None):
            def fn(e, o=out, i=in_, f=func, b=bias, s=scale, a=accum):
                kw = {}
                if b is not None:
                    kw["bias"] = b
                if s is not None:
                    kw["scale"] = s
                if a is not None:
                    kw["accum_out"] = a
                return e.activation(out=o, in_=i, func=f, **kw)
            return emit("act", fn, reads, writes)

        def dve(fn, reads, writes):
            return emit("dve", fn, reads, writes)

        def pool(fn, reads, writes):
            return emit("pool", fn, reads, writes)

        dma_sp(cols[:, 0:32], colpack_d.ap(), [], [b_cols])
        dma_sp(ident[:], ident_d.ap(), [], [b_ident])
        dma_sp(lamv[:], lam_d.ap(), [], [b_lamv])
        dma_sp(relb[:], relb_d.ap(), [], [b_relb])
        dma_sp(ohsb[:], oh_d.ap(), [], [b_oh])
        win0 = bfv(67600, 8 * 2048).rearrange("p (c n) -> p c n", c=8)
        b_win0 = B("win0")
        b_stg = [B(f"wstg{c}") for c in range(8)]
        for c in range(8):
            dma_sp(h1[:, c * 2048:(c + 1) * 2048],
                   win_l.ap().rearrange("(p a) n -> p a n", p=128)[:, c, :], [], [b_stg[c]])
        dma_sp(fgb[:], fgb_d.ap(), [], [b_fgb])
        b_win0c = [B(f"win0_{c}") for c in range(8)]
        for c in range(8):
            dve(lambda e, c=c: e.tensor_copy(out=win0[:, c, :], in_=h1[:, c * 2048:(c + 1) * 2048]),
                [b_stg[c]], [b_win0c[c]])
        early_ex = [dma_sp(win_b.ap().rearrange("(p a) n -> p (a n)", p=128), win0.rearrange("p c n -> p (c n)"),
                           b_win0c, [b_winb])]

        dve(lambda e: e.tensor_copy(out=identb[:], in_=ident[:]), [b_ident], [b_ident])
        b_jb = B("jb")
        jstage = f32v(8192, 128)
        b_jst = B("jstage")
        dma_sp(jstage, aident_d.ap(), [], [b_jst])
        dve(lambda e: e.tensor_copy(out=jb[:], in_=jstage), [b_jst], [b_jb])
        dve(lambda e: e.memset(cols[:, 33:34], 1e-6), [], [b_cols])
        dve(lambda e: e.memset(cols[:, 34:35], 1e-5), [], [b_cols])
        dve(lambda e: e.tensor_scalar(out=sg08, in0=sgcol, scalar1=float(1.0 - LAMBDA_INIT), scalar2=None,
                                      op0=ALU.mult), [b_cols], [b_cols])
        negbb = cols[:, 40:44]
        ea = cols[:, 44:48]
        dve(lambda e: e.tensor_scalar(out=negbb, in0=farb[:, 0:4], scalar1=-1.0, scalar2=None, op0=ALU.mult),
            [b_cols], [b_cols])
        dve(lambda e: e.tensor_tensor(out=ea, in0=farb[:, 4:8], in1=farb[:, 0:4], op=ALU.subtract),
            [b_cols], [b_cols])
        b_lamp = B("lamp")
        lamp = f32v(0, 128)
        lamj = bfv(1024, 128)
        dve(lambda e: e.tensor_tensor(out=lamp[:, 0:64], in0=lamv[:, 0:64], in1=lamv[:, 64:128], op=ALU.mult),
            [b_lamv], [b_lamp])
        dve(lambda e: e.tensor_tensor(out=lamp[:, 64:128], in0=lamv[:, 128:192], in1=lamv[:, 192:256], op=ALU.mult),
            [b_lamv], [b_lamp])
        b_lamj = B("lamj")
        act(lamj[:, 0:64], lamp[:, 0:64], AF.Copy, [b_lamp], [b_lamj, b_lams], accum=lam_s[:, 0:1])
        act(lamj[:, 0:64], lamp[:, 64:128], AF.Copy, [b_lamp], [b_lamj, b_lams], accum=lam_s[:, 1:2])
        act(lam_e, lam_s, AF.Exp, [b_lams], [b_lams])
        dve(lambda e: e.tensor_tensor(out=neglam, in0=lam_e[:, 1:2], in1=lam_e[:, 0:1], op=ALU.subtract),
            [b_lams], [b_cols])
        dve(lambda e: e.tensor_scalar(out=neglam, in0=neglam, scalar1=float(-LAMBDA_INIT), scalar2=None,
                                      op0=ALU.add), [b_cols], [b_cols])
        lutsb = f32v(4096, NLUT)
        b_lutsb = B("lutsb")
        for half in range(2):
            mm(bank(0)[0:4, 0:448], relb[:, :], ohsb[:, half * 448:(half + 1) * 448], True, True,
               [b_relb, b_oh], [b_banks[0]])
            dve(lambda e, hf=half: e.tensor_copy(out=lutsb[0:4, hf * 448:(hf + 1) * 448], in_=bank(0)[0:4, 0:448]),
                [b_banks[0]], [b_lutsb])
        dma_sp(lut_d.ap(), lutsb[0:4, :], [b_lutsb], [b_lutd])
        T.barrier(exclude=early_ex)

        O_QT, O_KT, O_U, O_VX, O_X = 0, 16640, 33280, 49920, 67600
        PW_ = 2080
        QT = bfv(O_QT, 4 * PW_).rearrange("p (h n) -> p h n", h=4)
        KT = bfv(O_KT, 4 * PW_).rearrange("p (h n) -> p h n", h=4)
        UU = bfv(O_U, 4 * PW_).rearrange("p (h n) -> p h n", h=4)
        VX = bfv(O_VX, NKT * 4 * 130).rearrange("p (k h n) -> p k h n", k=NKT, h=4)

        def gain_tile(gcol, ones_ap, b_ones):
            for c in range(8):
                dve(lambda e, c=c: e.tensor_scalar(out=gbc[:, c * 128:(c + 1) * 128], in0=ones_ap,
                                                   scalar1=gcol[:, c:c + 1], scalar2=None, op0=ALU.mult),
                    [b_cols, b_ones], [b_gbc])

        for s in range(2):
            b_QT = [B(f"QT{i}") for i in range(5)]
            b_KT = [B(f"KT{i}") for i in range(NKT)]
            b_U = [B(f"U{i}") for i in range(5)]
            b_Upad = B("Upad")
            b_VX = [B(f"VX{i}") for i in range(NKT)]
            win = bfv(O_X, 8 * 2048).rearrange("p (c n) -> p c n", c=8)
            uT = [bfv(O_X + 32768 + i * 8192, 8 * 512).rearrange("p (c n) -> p c n", c=8) for i in range(2)]
            xs = [f32v(O_X + 49152 + i * 4096, 1024) for i in range(2)]
            xb = [bfv(O_X + 57344 + i * 2048, 1024) for i in range(2)]
            b_win = B("win")
            b_uT = [B("uT0"), B("uT1")]
            b_xs = [B("xs0"), B("xs1")]
            b_xb = [B("xb0"), B("xb1")]
            b_ssA = [B("ssA0"), B("ssA1")]

            rd_win = b_win0c if s == 0 else [b_win]
            if s == 0:
                pass
            else:
                dma_sp(win.rearrange("p c n -> p (c n)"), win_b.ap().rearrange("(p a) n -> p (a n)", p=128),
                       [b_winb], [b_win])
            dve(lambda e: e.memset(xs[1][:, 0:128], 1.0), [], [b_xs[1]])
            gain_tile(g1col, xs[1][:, 0:128], b_xs[1])
            pool(lambda e: e.memset(UU[:, :, 2064:2080], 0.0), [], [b_Upad])
            pool(lambda e: e.memset(VX[:, :, :, 128:130], 1.0), [], b_VX)

            def norm_tile(ti, slot):
                np_ = NMETA if ti < 0 else 128
                src = meta_d.ap() if ti < 0 else x_d.ap()[s, ti * 128:(ti + 1) * 128, :]
                xt = xs[slot]
                xbt = xb[slot]
                dma_sp(xt[0:np_, :], src, [], [b_xs[slot]])
                act(xbt[0:np_, :], xt[0:np_, :], AF.Square, [b_xs[slot]], [b_xb[slot], b_ssA[slot]],
                    accum=ssA[0:np_, slot:slot + 1])
                act(lnA[0:np_, slot:slot + 1], ssA[0:np_, slot:slot + 1], AF.Ln, [b_ssA[slot], b_cols], [b_ssA[slot]],
                    bias=eps6[0:np_, :], scale=1.0 / D)
                act(rsA[0:np_, slot:slot + 1], lnA[0:np_, slot:slot + 1], AF.Exp, [b_ssA[slot]], [b_ssA[slot]],
                    scale=-0.5)
                act(xbt[0:np_, :], xt[0:np_, :], AF.Copy, [b_ssA[slot], b_xs[slot]], [b_xb[slot]],
                    scale=rsA[0:np_, slot:slot + 1])
                pb = slot
                for c in range(8):
                    tr(bankb(pb)[:, c * 128:c * 128 + np_], xbt[0:np_, c * 128:(c + 1) * 128], np_,
                       [b_xb[slot]], [b_banks[pb]], bf=True)
                p0 = 0 if ti < 0 else NMETA + ti * 128
                a = p0
                while a < p0 + np_:
                    pc = a // 512
                    bnd = min(p0 + np_, (pc + 1) * 512)
                    n = bnd - a
                    off = a - p0
                    src_ps = bankb(pb).rearrange("p (c n) -> p c n", c=8)[:, :, off:off + n]
                    dst = uT[pc % 2][:, :, a - pc * 512:a - pc * 512 + n]
                    g = gbc[:, :].rearrange("p (c n) -> p c n", c=8)[:, :, 0:n]
                    dve(lambda e, o=dst, i=src_ps, g=g: e.tensor_tensor(out=o, in0=i, in1=g, op=ALU.mult),
                        [b_banks[pb], b_gbc], [b_uT[pc % 2]])
                    a = bnd

            kindc = [0]

            def inproj_groups(pc):
                n = 512 if pc < 4 else 16
                u = uT[pc % 2]
                bu = b_uT[pc % 2]
                pos0 = pc * 512
                groups = []
                for grp in range(3):
                    for h in range(4):
                        def g_(grp=grp, h=h):
                            bi = 2 + (kindc[0] % 6)
                            kindc[0] += 1
                            colbase = {0: 512, 1: 1024, 2: 0}[grp] + h * 128
                            for c in range(8):
                                mm(bank(bi)[:, 0:n], win[:, c, colbase:colbase + 128], u[:, c, 0:n], c == 0, c == 7,
                                   rd_win + [bu], [b_banks[bi]])
                            if grp == 0:
                                dve(lambda e, o=QT[:, h, pos0:pos0 + n], i=bank(bi)[:, 0:n]:
                                    e.tensor_scalar(out=o, in0=i, scalar1=0.125, scalar2=None, op0=ALU.mult),
                                    [b_banks[bi]], [b_QT[pc]])
                            elif grp == 1:
                                wr = [b_KT[k] for k in range(pos0 // 128, (pos0 + n - 1) // 128 + 1)]
                                dve(lambda e, o=KT[:, h, pos0:pos0 + n], i=bank(bi)[:, 0:n]:
                                    e.tensor_copy(out=o, in_=i), [b_banks[bi]], wr)
                            else:
                                act(UU[:, h, pos0:pos0 + n], bank(bi)[:, 0:n], AF.Copy, [b_banks[bi]], [b_U[pc]])
                        groups.append(g_)
                nt = (n + 127) // 128
                for kk in range(nt):
                    def gv(kk=kk):
                        m = min(128, n - kk * 128)
                        kt = pos0 // 128 + kk
                        bi = 2 + (kindc[0] % 6)
                        kindc[0] += 1
                        for c in range(8):
                            mm(bank(bi)[0:m, :], u[:, c, kk * 128:kk * 128 + m], win[:, c, 1536:2048], c == 0, c == 7,
                               rd_win + [bu], [b_banks[bi]])
                        src_ps = bank(bi)[0:m, :].rearrange("p (h n) -> p h n", h=4)
                        if kk % 2 == 0:
                            dve(lambda e, o=VX[0:m, kt, :, 0:128], sp_=src_ps: e.tensor_copy(out=o, in_=sp_),
                                [b_banks[bi]], [b_VX[kt]])
                        else:
                            act(VX[0:m, kt, :, 0:128], src_ps, AF.Copy, [b_banks[bi]], [b_VX[kt]])
                    groups.append(gv)
                return groups

            tidx = 0
            order = [-1] + list(range(16))
            for ti in order[0:5]:
                norm_tile(ti, tidx % 2)
                tidx += 1
            nxt_tile = 5
            if s == 0:
                gate = [b_uT[0], b_uT[1]]
                dma_pool(wo_b.ap(), wo_l.ap(), gate, [b_wob])
                dma_pool(pw[:], pw_l.ap(), gate, [b_pw])
            for pc in range(5):
                groups = inproj_groups(pc)
                ng = len(groups)
                for gi, g_ in enumerate(groups):
                    g_()
                    if pc < 3 and gi % 4 == 3 and nxt_tile < len(order) and nxt_tile < 5 + 4 * (pc + 1):
                        norm_tile(order[nxt_tile], tidx % 2)
                        tidx += 1
                        nxt_tile += 1
                while pc < 3 and nxt_tile < 5 + 4 * (pc + 1):
                    norm_tile(order[nxt_tile], tidx % 2)
                    tidx += 1
                    nxt_tile += 1
            T.barrier()

            from collections import deque
            wo = bfv(O_X, 8 * 1024).rearrange("p (c n) -> p c n", c=8)
            Tst = f32v(O_X + 16384, 4 * TW).rearrange("p (h n) -> p h n", h=4)
            o2 = O_X + 16384 + 12288
            NE = 4
            Et = [bfv(o2 + i * 1024, 512) for i in range(NE)]
            o2 += NE * 1024
            Tb = bfv(o2, 4 * TW).rearrange("p (h n) -> p h n", h=4)
            o2 += 4 * TW * 2
            yT_off = o2
            yT = [bfv(o2 + i * 4096, 8 * QC).rearrange("p (c n) -> p c n", c=8) for i in range(2)]
            o2 += 8192
            ot = [f32v(o2 + i * 1024, 256).rearrange("p (q n) -> p q n", q=2) for i in range(2)]
            o2 += 2048
            yt = [f32v(o2 + i * 1024, 256).rearrange("p (q n) -> p q n", q=2) for i in range(2)]
            o2 += 2048
            tA = f32v(o2, 272)
            tB = f32v(o2 + 1088, 272)
            o2 += 2176
            dT = [bfv(o2 + i * 2048, 4 * QC).rearrange("p (g n) -> p g n", g=4) for i in range(2)]
            o2 += 4096
            junkB = bfv(o2, 1024)
            o2 += 2048
            Qb = [bfv(o2 + i * 1024, 512) for i in range(2)]
            o2 += 2048
            assert o2 <= ARENA, o2
            b_wo = B("wo")
            b_E = [B(f"E{i}") for i in range(NE)]
            b_sb = [B(f"sb{i}") for i in range(2)]
            b_yT = [B(f"yT{i}") for i in range(2)]
            b_ot = [B(f"ot{i}") for i in range(2)]
            b_yt = [B(f"yt{i}") for i in range(2)]
            b_tA, b_tB = B("tA"), B("tB")
            b_dT = [B(f"dT{i}") for i in range(2)]
            b_junkB = B("junkB")
            b_rz = [B("rz0"), B("rz1")]
            b_ssb = [B("ssb0"), B("ssb1")]
            b_Qb = [B("Qb0"), B("Qb1")]

            dma_sp(wo.rearrange("p c n -> p (c n)"), wo_b.ap().rearrange("(p a) n -> p (a n)", p=128),
                   [b_wob], [b_wo])
            for i in range(2):
                pool(lambda e, i=i: e.memset(Qb[i][:, :], 0.0), [], [b_Qb[i]])
            b_Tst = [B(f"Tst{h}") for h in range(4)]
            b_Tb = B("Tb")
            Hb = bfv(yT_off, 4 * TW)
            for h in range(4):
                src = bass.AP(lut_d, h * NLUT, [[1, 128], [1, TW]])
                dma_sp(Tst[:, h, :], src, [b_lutd], [b_Tst[h]])
                act(Hb[:, h * TW:(h + 1) * TW], Tst[:, h, :], AF.Exp, [b_Tst[h], b_cols], b_yT, bias=negbb[:, h:h + 1])
            tb_banks = [7, 3, 4, 5]
            k = 0
            for h in range(4):
                for half in range(2):
                    bk = tb_banks[k % 4]
                    k += 1
                    mm(bank(bk)[:, 0:384], jb[:, :], Hb[:, h * TW + half * 384:h * TW + (half + 1) * 384], True, True,
                       b_yT + [b_jb], [b_banks[bk]])
                    dve(lambda e, o=Tb[:, h, half * 384:(half + 1) * 384], i=bank(bk)[:, 0:384]:
                        e.tensor_copy(out=o, in_=i), [b_banks[bk]], [b_Tb])

            import heapq
            deferred = []
            dseq = [0]

            def defer(at, fn):
                dseq[0] += 1
                heapq.heappush(deferred, (at, dseq[0], fn))

            def run_deferred(cur, nmax):
                k = 0
                while deferred and k < nmax and deferred[0][0] <= cur:
                    heapq.heappop(deferred)[2]()
                    k += 1

            def pool_stage(cq, g, w):
                q0 = NMETA + cq * QC
                ysl = cq % 2

                def zz(lo, n):
                    return UU[:, g, q0 + lo:q0 + lo + n]
                rdU = [b_U[min(4, (q0 - 8) // 512)], b_U[min(4, (q0 + 263) // 512)], b_Upad]

                def padd(o, a, b, reads, writes):
                    pool(lambda e, o=o, a=a, b=b: e.tensor_tensor(out=o, in0=a, in1=b, op=ALU.add), reads, writes)
                if w == 2:
                    padd(tA[:, 0:256], zz(0, 256), zz(-1, 256), rdU, [b_tA])
                    ws, wb, wsl = tA, b_tA, 0
                else:
                    padd(tA[:, 1:272], zz(-7, 271), zz(-8, 271), rdU, [b_tA])
                    padd(tB[:, 2:271], tA[:, 3:272], tA[:, 1:270], [b_tA], [b_tB])
                    ws, wb, wsl = tB, b_tB, 8
                    if w >= 8:
                        padd(tA[:, 4:269], tB[:, 6:271], tB[:, 2:267], [b_tB], [b_tA])
                        ws, wb = tA, b_tA
                    if w == 16:
                        padd(tB[:, 8:265], tA[:, 12:269], tA[:, 4:261], [b_tA], [b_tB])
                        ws, wb = tB, b_tB
                def dpart():
                    dve(lambda e, o=dT[ysl][:, g, :], a=ws[:, wsl:wsl + 256], sc_=1.0 / w, b=zz(0, 256):
                        e.scalar_tensor_tensor(out=o, in0=a, scalar=sc_, in1=b, op0=ALU.mult, op1=ALU.subtract),
                        [wb] + rdU, [b_dT[ysl]])
                    if cq == NQC - 1:
                        right = w - 1 - w // 2
                        for r in range(right):
                            cnt = w - (right - r)
                            col = 255 - r
                            dve(lambda e, o=dT[ysl][:, g, col:col + 1], a=ws[:, wsl + col:wsl + col + 1], sc_=1.0 / cnt,
                                b=zz(col, 1):
                                e.scalar_tensor_tensor(out=o, in0=a, scalar=sc_, in1=b, op0=ALU.mult, op1=ALU.subtract),
                                [wb] + rdU, [b_dT[ysl]])
                return dpart

            def poolmm_stage(cq, g):
                ysl = cq % 2
                mb = 7
                mm(bank(mb)[:, 0:QC], pw[:, g * 128:(g + 1) * 128], dT[ysl][:, g, :], True, True,
                   [b_pw, b_dT[ysl]], [b_banks[mb]])
                dve(lambda e, o=yT[ysl][:, g, :], i=bank(mb)[:, 0:QC], sc_=pscol[:, g:g + 1]:
                    e.tensor_scalar(out=o, in0=i, scalar1=sc_, scalar2=None, op0=ALU.mult),
                    [b_banks[mb], b_cols], [b_yT[ysl]])

            def epilogue_stages(cq, h, osl):
                ysl = cq % 2
                ba, bb = 3 + 2 * osl, 4 + 2 * osl
                rzs = rz[:, osl * 4:(osl + 1) * 4]
                r1n = rz1n[:, osl * 2:(osl + 1) * 2]
                sss = ssb[:, osl * 2:(osl + 1) * 2]
                lns = lnb[:, osl * 2:(osl + 1) * 2]
                rss = rsb[:, osl * 2:(osl + 1) * 2]
                mb = 7

                def st1():
                    for c, bk in ((0, ba), (1, bb)):
                        zsrc = bank(bk)[:, 0:258].rearrange("p (q n) -> p q n", n=129)[:, :, 128:129]
                        zdst = rzs[:, c * 2:(c + 1) * 2].rearrange("p (q o) -> p q o", o=1)
                        dve(lambda e, o=zdst, i=zsrc: e.reciprocal(out=o, in_=i), [b_banks[bk]], [b_rz[osl]])
                    dve(lambda e, o=r1n, i=rzs[:, 2:4]: e.tensor_scalar(out=o, in0=i, scalar1=neglam, scalar2=None,
                                                                         op0=ALU.mult), [b_rz[osl], b_cols], [b_rz[osl]])

                def st2():
                    for qs in range(2):
                        dve(lambda e, o=ot[osl][:, qs, :], i=bank(ba)[:, qs * 129:qs * 129 + 128], sc_=rzs[:, qs:qs + 1]:
                            e.tensor_scalar(out=o, in0=i, scalar1=sc_, scalar2=None, op0=ALU.mult),
                            [b_banks[ba], b_rz[osl]], [b_ot[osl]])

                def st3():
                    for qs in range(2):
                        dve(lambda e, o=ot[osl][:, qs, :], i=bank(bb)[:, qs * 129:qs * 129 + 128], sc_=r1n[:, qs:qs + 1]:
                            e.scalar_tensor_tensor(out=o, in0=i, scalar=sc_, in1=o, op0=ALU.mult, op1=ALU.add),
                            [b_banks[bb], b_rz[osl], b_ot[osl]], [b_ot[osl]])

                def st4():
                    for qs in range(2):
                        dve(lambda e, o=junkB[:, 0:128], i=ot[osl][:, qs, :], a=sss[:, qs:qs + 1]:
                            e.scalar_tensor_tensor(out=o, in0=i, scalar=1.0, in1=i, op0=ALU.mult, op1=ALU.mult,
                                                   accum_out=a),
                            [b_ot[osl]], [b_junkB, b_ssb[osl]])

                def st5():
                    act(lns, sss, AF.Ln, [b_ssb[osl], b_cols], [b_ssb[osl]], bias=eps5, scale=1.0 / 128)
                    act(rss, lns, AF.Exp, [b_ssb[osl]], [b_ssb[osl]], scale=-0.5)

                def st6():
                    for qs in range(2):
                        dve(lambda e, o=yt[osl][:, qs, :], i=ot[osl][:, qs, :], sc_=rss[:, qs:qs + 1]:
                            e.tensor_scalar(out=o, in0=i, scalar1=sc_, scalar2=None, op0=ALU.mult),
                            [b_ot[osl], b_ssb[osl]], [b_yt[osl]])

                def st7():
                    for qs in range(2):
                        tr(bank(mb)[:, qs * 128:(qs + 1) * 128], yt[osl][:, qs, :], 128, [b_yt[osl]], [b_banks[mb]])

                def st8():
                    dve(lambda e, o=yT[ysl][:, 4 + h, :], i=bank(mb)[:, 0:QC]:
                        e.tensor_scalar(out=o, in0=i, scalar1=sg08, scalar2=None, op0=ALU.mult),
                        [b_banks[mb], b_cols], [b_yT[ysl]])
                def st78():
                    st7()
                    st8()
                return [(0, st1), (2, st2), (4, st3), (6, st4), (11, st5), (15, st6), (19, st78)]

            def wo_stages(cq):
                ysl = cq % 2
                sts = []
                k = 0
                for t2 in range(2):
                    tile_i = cq * 2 + t2
                    for half in range(4):
                        def st(t2=t2, half=half, tile_i=tile_i):
                            mb = 7
                            for c in range(8):
                                mm(bank(mb)[:, 0:256], yT[ysl][:, c, t2 * 128:(t2 + 1) * 128],
                                   wo[:, c, half * 256:(half + 1) * 256],
                                   c == 0, c == 7, [b_yT[ysl], b_wo], [b_banks[mb]])
                            hsl = h1[:, tile_i * D + half * 256:tile_i * D + (half + 1) * 256]
                            dve(lambda e, hsl=hsl, mb=mb: e.tensor_tensor(out=hsl, in0=bank(mb)[:, 0:256], in1=hsl, op=ALU.add),
                                [b_banks[mb], b_h1[tile_i]], [b_h1[tile_i]])
                        sts.append((24 + 2 * k, st))
                        k += 1

                    def stq(tile_i=tile_i):
                        dve(lambda e, o=junkB[:, :], i=h1[:, tile_i * D:(tile_i + 1) * D], a=ss2[:, tile_i:tile_i + 1]:
                            e.scalar_tensor_tensor(out=o, in0=i, scalar=1.0, in1=i, op0=ALU.mult, op1=ALU.mult,
                                                   accum_out=a),
                            [b_h1[tile_i]], [b_junkB, b_ss2[tile_i]])
                    sts.append((25 + 2 * (k - 1), stq))
                return sts

            unit = 0

            def av(cqh, j, esl, kn, osl, h):
                ba, bb = 3 + 2 * osl, 4 + 2 * osl
                for c, bk in ((0, ba), (1, bb)):
                    for qs in range(2):
                        mm(bank(bk)[:, qs * 129:(qs + 1) * 129],
                           Et[esl][0:kn, c * QC + qs * 128:c * QC + (qs + 1) * 128],
                           VX[0:kn, j, h, 0:129], (j == 0 and qs == 0), j == NKT - 1,
                           [b_E[esl], b_VX[j]], [b_banks[bk]], skip=True)
                if FILLER:
                    mm(bank(bb)[:, 258:258 + FILLER], Et[esl][0:kn, QC + 128:QC + 256], Qb[osl][0:kn, 0:FILLER],
                       False, False, [b_E[esl], b_Qb[osl]], [b_banks[bb]], skip=True)
                if j == NKT - 1:
                    for off, st in epilogue_stages(cqh, h, osl):
                        defer(unit + off, st)
                    if h == 3:
                        for off, st in wo_stages(cqh):
                            defer(unit + off, st)

            def emit_qb(cq_, h_, slot):
                q0_ = NMETA + cq_ * QC
                for c in range(2):
                    pool(lambda e, o=Qb[slot][c * 64:(c + 1) * 64, c * QC:(c + 1) * QC],
                         i=QT[c * 64:(c + 1) * 64, h_, q0_:q0_ + QC]: e.tensor_copy(out=o, in_=i),
                         [b_QT[min(4, q0_ // 512)], b_QT[min(4, (q0_ + QC - 1) // 512)]], [b_Qb[slot]])

            units = []
            hq_ = 0
            for cq in range(NQC):
                for h in range(4):
                    for j in range(NKT):
                        units.append(dict(cq=cq, h=h, j=j, sl=hq_ % 2, kn=128 if j < NKT - 1 else LPOS - 128 * (NKT - 1)))
                    hq_ += 1
            NU = len(units)

            def emit_qk(u):
                U_ = units[u]
                cq_, h_, j_, kn_, sl_ = U_["cq"], U_["h"], U_["j"], U_["kn"], U_["sl"]
                ssl_ = u % 3
                k0_ = j_ * 128
                mm(bank(ssl_)[0:kn_, :], KT[:, h_, k0_:k0_ + kn_], Qb[sl_][:, :], True, True,
                   [b_KT[j_], b_Qb[sl_]], [b_banks[ssl_]])

            emit_qb(0, 0, 0)
            emit_qk(0)
            emit_qk(1)
            cast_pieces = []
            if s == 0:
                for r in range(11):
                    cast_pieces.append((wg_b.ap()[r * 128:(r + 1) * 128, :], wg_l.ap()[r * 128:(r + 1) * 128, :], b_wgb))
                    cast_pieces.append((wu_b.ap()[r * 128:(r + 1) * 128, :], wu_l.ap()[r * 128:(r + 1) * 128, :], b_wub))
                for r in range(11):
                    cast_pieces.append((wd_b.ap()[r * 256:(r + 1) * 256, :], wd_l.ap()[r * 256:(r + 1) * 256, :], b_wdb))
            for u in range(NU):
                U_ = units[u]
                cq, h, j, kn, sl = U_["cq"], U_["h"], U_["j"], U_["kn"], U_["sl"]
                q0 = NMETA + cq * QC
                unit = u + 1
                if h == 0 and j == 0:
                    for t2 in range(2):
                        tile_i = cq * 2 + t2
                        dma_sp(h1[:, tile_i * D:(tile_i + 1) * D], x_d.ap()[s, tile_i * 128:(tile_i + 1) * 128, :],
                               [], [b_h1[tile_i]])
                    for g, w in enumerate((2, 4, 8, 16)):
                        def pst(cq=cq, g=g, w=w, at=unit + 1 + 8 * g + 6):
                            dpart = pool_stage(cq, g, w)
                            defer(at, dpart)
                        defer(unit + 1 + 8 * g, pst)
                if h == 2 and j == 0:
                    for g in range(4):
                        defer(unit + 3 * g, lambda cq=cq, g=g: poolmm_stage(cq, g))
                ssl = u % 3
                esl = u % NE
                k0 = j * 128
                Dd = k0 - q0
                s_ps = bank(ssl)[0:kn, :]
                e_out = Et[esl][0:kn, :]
                maxrel = Dd + kn - 1
                minrel = Dd - (QC - 1)
                if minrel >= 128:
                    act(e_out, s_ps, AF.Exp, [b_banks[ssl], b_cols], [b_E[esl]], bias=ea[0:kn, h:h + 1])
                else:
                    act(e_out, s_ps, AF.Exp, [b_banks[ssl]], [b_E[esl]])
                if maxrel <= -128 or minrel >= 128:
                    pass
                else:
                    i0 = 368 - Dd
                    assert 0 <= i0 <= TW - QC, (i0, Dd)
                    for c in range(2):
                        dve(lambda e, o=Et[esl][0:kn, c * QC:(c + 1) * QC], b=Tb[0:kn, h, i0:i0 + QC]:
                            e.tensor_tensor(out=o, in0=o, in1=b, op=ALU.mult),
                            [b_Tb, b_E[esl]], [b_E[esl]])
                if j == 4 and cast_pieces:
                    o_, i_, bb_ = cast_pieces.pop(0)
                    dma_pool(o_, i_, [], [bb_])
                if j == 8 and u + 9 < NU:
                    nU = units[u + 9]
                    emit_qb(nU["cq"], nU["h"], nU["sl"])
                if u + 2 < NU:
                    emit_qk(u + 2)
                if u >= 1:
                    pU = units[u - 1]
                    av(pU["cq"], pU["j"], (u - 1) % NE, pU["kn"], pU["sl"], pU["h"])
                run_deferred(unit, 1)
            pU = units[NU - 1]
            unit = NU + 1
            av(pU["cq"], pU["j"], (NU - 1) % NE, pU["kn"], pU["sl"], pU["h"])
            while deferred:
                run_deferred(10 ** 9, 10 ** 6)
            while cast_pieces:
                o_, i_, bb_ = cast_pieces.pop(0)
                dma_pool(o_, i_, [], [bb_])
            T.barrier()

            fT = bfv(0, 8 * 1024).rearrange("p (c n) -> p c n", c=8)
            aT = bfv(16384, 22 * 1024).rearrange("p (k n) -> p k n", k=22)
            Wdh = [bfv(61440 + i * 22528, 22 * 512).rearrange("p (k n) -> p k n", k=22) for i in range(2)]
            wgs = [bfv(106496 + i * 8192, 2048).rearrange("p (c n) -> p c n", c=8) for i in range(2)]
            wus = [bfv(106496 + i * 8192 + 4096, 2048).rearrange("p (c n) -> p c n", c=8) for i in range(2)]
            hn2 = [bfv(122880 + i * 2048, 1024) for i in range(2)]
            sgt = [bfv(126976 + i * 1024, 512) for i in range(2)]
            junkC = bfv(129024, 1024)
            b_fT = [B(f"fT{i}") for i in range(8)]
            b_aT = [B(f"aT{i}") for i in range(4)]
            b_Wdh = [B("Wdh0"), B("Wdh1")]
            b_wgs = [B("wgs0"), B("wgs1")]
            b_hn2 = [B("hn2_0"), B("hn2_1")]
            b_sgt = [B("sgt0"), B("sgt1")]
            b_junkC = B("junkC")
            b_st = B("stats")
            dve(lambda e: e.memset(junkC[:, 0:128], 1.0), [], [b_junkC])
            gain_tile(g2col, junkC[:, 0:128], b_junkC)
            b_rs2 = [B("rs2_0"), B("rs2_1")]
            b_st3t = [B(f"st3_{i}") for i in range(8)]

            def c0_stats(sc):
                rd_ss = [b_ss2[t] for t in range(sc * 8, sc * 8 + 8)]
                act(ln2[:, :], ss2[:, sc * 8:(sc + 1) * 8], AF.Ln, rd_ss + [b_cols], [b_rs2[sc]], bias=eps6, scale=1.0 / D)
                act(rs2x[:, sc * 8:(sc + 1) * 8], ln2[:, :], AF.Exp, [b_rs2[sc]], [b_rs2[sc]], scale=-0.5)

            def c0_copy(sc, i):
                t = sc * 8 + i
                hs = i % 2
                act(hn2[hs][:, :], h1[:, t * D:(t + 1) * D], AF.Copy, [b_h1[t], b_rs2[sc]], [b_hn2[hs]],
                    scale=rs2x[:, sc * 8 + i:sc * 8 + i + 1])

            def c0_tr(sc, i, pbase):
                hs = i % 2
                pb = pbase + hs
                for c in range(8):
                    tr(bankb(pb)[:, c * 128:(c + 1) * 128], hn2[hs][:, c * 128:(c + 1) * 128], 128,
                       [b_hn2[hs]], [b_banks[pb]], bf=True)
                src_ps = bankb(pb).rearrange("p (c n) -> p c n", c=8)
                dst = fT[:, :, i * 128:(i + 1) * 128]
                g = gbc[:, :].rearrange("p (c n) -> p c n", c=8)
                dve(lambda e, o=dst, i_=src_ps, g=g: e.tensor_tensor(out=o, in0=i_, in1=g, op=ALU.mult),
                    [b_banks[pb], b_gbc], [b_fT[i]])

            rs2x = sm[:, 88:104]
            c0_stats(0)
            c0_copy(0, 0)
            for i in range(8):
                if i + 1 < 8:
                    c0_copy(0, i + 1)
                c0_tr(0, i, 0)
            for sc in range(2):
                tiles = list(range(sc * 8, sc * 8 + 8))
                dma_sp(Wdh[0], wd_b.ap().rearrange("(k p) n -> p k n", p=128)[:, :, 0:512], [b_wdb], [b_Wdh[0]])
                gu = 0
                for j in range(11):
                    wsl = j % 2
                    dma_sp(wgs[wsl].rearrange("p c n -> p (c n)"), wg_b.ap()[j * 128:(j + 1) * 128, :],
                           [b_wgb], [b_wgs[wsl]])
                    dma_sp(wus[wsl].rearrange("p c n -> p (c n)"), wu_b.ap()[j * 128:(j + 1) * 128, :],
                           [b_wub], [b_wgs[wsl]])
                    for tc in range(2):
                        rdf = [b_fT[tc * 4 + k] for k in range(4)]
                        for sub in range(2):
                            gsl = gu % 2
                            gu += 1
                            bg, bu_ = 4 + 2 * gsl, 5 + 2 * gsl
                            for c in range(8):
                                mm(bank(bg), wgs[wsl][:, c, sub * 128:(sub + 1) * 128], fT[:, c, tc * 512:(tc + 1) * 512],
                                   c == 0, c == 7, [b_wgs[wsl]] + rdf, [b_banks[bg]])
                            for c in range(8):
                                mm(bank(bu_), wus[wsl][:, c, sub * 128:(sub + 1) * 128], fT[:, c, tc * 512:(tc + 1) * 512],
                                   c == 0, c == 7, [b_wgs[wsl]] + rdf, [b_banks[bu_]])
                            act(sgt[gsl][:, :], bank(bg), AF.Silu, [b_banks[bg]], [b_sgt[gsl]])
                            kf = 2 * j + sub
                            dve(lambda e, gsl=gsl, bu_=bu_, kf=kf, tc=tc:
                                e.tensor_tensor(out=aT[:, kf, tc * 512:(tc + 1) * 512], in0=bank(bu_), in1=sgt[gsl][:, :],
                                                op=ALU.mult), [b_banks[bu_], b_sgt[gsl]], [b_aT[tc * 2 + (kf % 2)]])
                dma_sp(Wdh[1], wd_b.ap().rearrange("(k p) n -> p k n", p=128)[:, :, 512:1024], [b_wdb], [b_Wdh[1]])
                dn = 0
                nxt = sc + 1 if sc + 1 < 2 else None
                if nxt is not None:
                    c0_stats(nxt)
                    c0_copy(nxt, 0)
                for half in range(2):
                    for i, t in enumerate(tiles):
                        bi = dn % 4
                        dn += 1
                        tc = i // 4
                        for k in range(22):
                            mm(bank(bi), aT[:, k, i * 128:(i + 1) * 128], Wdh[half][:, k, :], k == 0, k == 21,
                               [b_aT[tc * 2], b_aT[tc * 2 + 1], b_Wdh[half]], [b_banks[bi]])
                        hsl = h1[:, t * D + half * 512:t * D + (half + 1) * 512]
                        dve(lambda e, hsl=hsl, bi=bi: e.tensor_tensor(out=hsl, in0=bank(bi), in1=hsl, op=ALU.add),
                            [b_banks[bi], b_h1[t]], [b_h1[t]])
                        if nxt is not None and half == 0:
                            if i + 1 < 8:
                                c0_copy(nxt, i + 1)
                            c0_tr(nxt, i, 4)
                        if half == 1:
                            hfull = h1[:, t * D:(t + 1) * D]
                            act(junkC[:, :], hfull, AF.Square, [b_h1[t]], [b_junkC, b_st3t[i]], accum=ss3[:, i:i + 1])
                            act(ln3[:, i:i + 1], ss3[:, i:i + 1], AF.Ln, [b_st3t[i], b_cols], [b_st3t[i]], bias=eps6,
                                scale=1.0 / D)
                            act(rs3[:, i:i + 1], ln3[:, i:i + 1], AF.Exp, [b_st3t[i]], [b_st3t[i]], scale=-0.5)
                            dve(lambda e, hsl=hfull, i=i: e.scalar_tensor_tensor(out=hsl, in0=hsl, scalar=rs3[:, i:i + 1],
                                                                                in1=fgb[:, :], op0=ALU.mult, op1=ALU.mult),
                                [b_h1[t], b_st3t[i], b_fgb], [b_h1[t]])
                            dma_pool(out_d.ap()[s, t * 128:(t + 1) * 128, :], hfull, [b_h1[t]], [])
            T.barrier()

        T.finalize()
        with nc.Block() as block:
            @block.tensor
            def _(e):
                T.play("pe", e, sems, dsems)

            @block.scalar
            def _(e):
                T.play("act", e, sems, dsems)

            @block.vector
            def _(e):
                T.play("dve", e, sems, dsems)

            @block.gpsimd
            def _(e):
                T.play("pool", e, sems, dsems)

            @block.sync
            def _(e):
                T.play("sp", e, sems, dsems, final_wait=True)
    return nc


_CACHE = {}


def _host_layouts(inp):
    f = lambda a: np.ascontiguousarray(np.asarray(a, dtype=np.float32))
    w_in = f(inp["w_in"])[0]
    w_o = f(inp["w_o"])[0]
    w_gate = f(inp["w_gate"])[0]
    w_up = f(inp["w_up"])[0]
    w_down = f(inp["w_down"])[0]
    rel_bias = f(inp["rel_bias"])
    n = np.arange(NLUT)
    bucket = _t5_bucket_np(495 - n)
    onehot = np.zeros((32, NLUT), np.float32)
    onehot[bucket, n] = 1.0
    shared = {
        "meta": f(inp["meta_tokens"]),
        "w_in_l": np.ascontiguousarray(w_in.reshape(8, 128, 2048).transpose(1, 0, 2)).reshape(1024, 2048),
        "w_o_l": np.ascontiguousarray(w_o.reshape(8, 128, 1024).transpose(1, 0, 2)).reshape(1024, 1024),
        "w_gate_l": np.ascontiguousarray(w_gate.reshape(8, 128, 11, 256).transpose(2, 1, 0, 3)).reshape(1408, 2048),
        "w_up_l": np.ascontiguousarray(w_up.reshape(8, 128, 11, 256).transpose(2, 1, 0, 3)).reshape(1408, 2048),
        "w_down_l": w_down,
        "pool_w_l": np.ascontiguousarray(f(inp["pool_w"])[0].transpose(1, 0, 2)).reshape(128, 512),
        "colpack": np.ascontiguousarray(np.concatenate([
            f(inp["norm1_g"])[0].reshape(8, 128).T,
            f(inp["norm2_g"])[0].reshape(8, 128).T,
            f(inp["pool_scale"])[0].reshape(4, 128).T,
            f(inp["subln_g"])[0].reshape(128, 1),
            np.zeros((128, 3), np.float32),
            np.broadcast_to(np.concatenate([rel_bias[15], rel_bias[31]])[None, :], (128, 8)),
        ], axis=1)),
        "fgb": np.ascontiguousarray(np.broadcast_to(f(inp["final_g"])[None, :], (128, D))),
        "lamv": np.ascontiguousarray(np.broadcast_to(np.concatenate(
            [f(inp["lambda_q1"])[0], f(inp["lambda_k1"])[0], f(inp["lambda_q2"])[0], f(inp["lambda_k2"])[0]])[None, :],
            (128, 256))),
        "relb": rel_bias,
        "onehot": onehot,
        "ident": np.eye(128, dtype=np.float32),
        "aident": np.ascontiguousarray(np.eye(128, dtype=np.float32)[::-1]),
    }
    return shared


def kernel(**inputs):
    x = np.ascontiguousarray(np.asarray(inputs["x"], dtype=np.float32))
    shared = _host_layouts(inputs)
    if "nc" not in _CACHE:
        _CACHE["nc"] = build_program()
    nc = _CACHE["nc"]
    in_maps = []
    for c in range(NCORES):
        m = dict(shared)
        m["x"] = x[2 * c:2 * c + 2]
        in_maps.append(m)
    res = run_bass_kernel_spmd(nc, in_maps, core_ids=list(range(NCORES)))
    out = np.concatenate([np.asarray(r["out"], dtype=np.float32) for r in res.results], axis=0)
    return out
```

```python
import math
import numpy as np
import ml_dtypes
import concourse.bass as bass
import concourse.mybir as mybir
from concourse.bass_utils import run_bass_kernel_spmd

F32 = mybir.dt.float32
BF16 = mybir.dt.bfloat16
AF = mybir.ActivationFunctionType
ALU = mybir.AluOpType

NCORES = 8
SEQ = 2048
D = 1024
NMETA = 16
LPOS = SEQ + NMETA
DFF = 2816
NKT = 17
QC = 256
NQC = SEQ // QC
TW = 768
NLUT = 896
LAMBDA_INIT = 0.8 - 0.6 * math.exp(-0.3 * 0)
NS_DMA = 8
FILLER = 0


class Buf:
    __slots__ = ("name", "psum", "last_w", "readers")

    def __init__(self, name, psum=False):
        self.name = name
        self.psum = psum
        self.last_w = None
        self.readers = {}


class Op:
    __slots__ = ("eng", "fn", "deps", "inc", "seq", "dma", "semkey", "val", "prev")

    def __init__(self, eng, fn, deps, dma):
        self.eng = eng
        self.fn = fn
        self.deps = deps
        self.inc = False
        self.seq = 0
        self.dma = dma
        self.semkey = None
        self.val = 0
        self.prev = None


class Tracker:
    ENGS = ("pe", "act", "dve", "pool", "sp")

    def __init__(self):
        self.ops = {e: [] for e in self.ENGS}
        self.dma_cnt = {"sp": 0, "pool": 0}
        self.dma_uses = {}
        self.dma_last = {}
        self.pending = {e: set() for e in self.ENGS}
        self.recent_dma = []
        self.all_dma = []

    def emit(self, eng, fn, reads=(), writes=(), dma=False):
        deps = set()
        for b in reads:
            if b.psum:
                if b.last_w is not None:
                    deps.add(b.last_w)
                deps.update(b.readers.values())
            elif b.last_w is not None:
                deps.add(b.last_w)
        for b in writes:
            if b.last_w is not None:
                deps.add(b.last_w)
            deps.update(b.readers.values())
        if self.pending[eng]:
            deps.update(self.pending[eng])
            self.pending[eng] = set()
        if eng == "pe":
            deps = {d for d in deps if not (d.eng == "pe" and not d.dma)}
        op = Op(eng, fn, deps, dma)
        if dma:
            k = self.dma_cnt[eng] % NS_DMA
            self.dma_cnt[eng] += 1
            key = (eng, k)
            n = self.dma_uses.get(key, 0) + 1
            self.dma_uses[key] = n
            op.semkey = key
            op.val = 16 * n
            op.prev = self.dma_last.get(key)
            self.dma_last[key] = op
            self.recent_dma.append(op)
            self.all_dma.append(op)
        self.ops[eng].append(op)
        for b in reads:
            if b.psum:
                b.last_w = op
                b.readers = {}
            else:
                b.readers[(eng, id(op)) if dma else eng] = op
        for b in writes:
            b.last_w = op
            b.readers = {}
        return op

    def barrier(self, exclude=()):
        deps = set(self.recent_dma) - set(exclude)
        self.recent_dma = []
        for e in self.ENGS:
            if self.ops[e]:
                last = self.ops[e][-1]
                if not last.dma:
                    deps.add(last)
                else:
                    for o in reversed(self.ops[e]):
                        if not o.dma:
                            deps.add(o)
                            break
        for e in self.ENGS:
            self.pending[e] = set(deps) | self.pending[e]

    def finalize(self):
        for e in self.ENGS:
            for op in self.ops[e]:
                for d in op.deps:
                    if not d.dma:
                        d.inc = True
        for e in self.ENGS:
            c = 0
            for op in self.ops[e]:
                if op.inc and not op.dma:
                    c += 1
                    op.seq = c

    def play(self, eng, handle, sems, dsems, final_wait=False):
        waited = {}

        def wait(key, semobj, val):
            if waited.get(key, 0) >= val:
                return
            handle.wait_ge(semobj, val)
            waited[key] = val

        for op in self.ops[eng]:
            for d in op.deps:
                if d.dma:
                    wait(d.semkey, dsems[d.semkey], d.val)
                else:
                    wait(d.eng, sems[d.eng], d.seq)
            if op.dma and op.prev is not None:
                wait(op.prev.semkey, dsems[op.prev.semkey], op.prev.val)
            inst = op.fn(handle)
            if op.dma:
                inst.then_inc(dsems[op.semkey], 16)
            elif op.inc:
                inst.then_inc(sems[eng], 1)
        if final_wait:
            for key, op in self.dma_last.items():
                wait(key, dsems[key], op.val)


def _t5_bucket_np(rel):
    rel = np.asarray(rel, dtype=np.int64)
    nb = 16
    ret = np.where(rel > 0, nb, 0)
    n = np.abs(rel)
    max_exact = 8
    nf = np.maximum(n, 1).astype(np.float32)
    large = max_exact + (np.log(nf / np.float32(max_exact)) / np.float32(math.log(128 / max_exact))
                         * np.float32(nb - max_exact)).astype(np.int32)
    large = np.minimum(large, nb - 1)
    return ret + np.where(n < max_exact, n, large)


def build_program():
    nc = bass.Bass("TRN2", target_bir_lowering=False)
    T = Tracker()

    def din(name, shape, dt=F32):
        return nc.dram_tensor(name, list(shape), dt, kind="ExternalInput")

    x_d = din("x", [2, SEQ, D])
    meta_d = din("meta", [NMETA, D])
    win_l = din("w_in_l", [1024, 2048])
    wo_l = din("w_o_l", [1024, 1024])
    wg_l = din("w_gate_l", [1408, 2048])
    wu_l = din("w_up_l", [1408, 2048])
    wd_l = din("w_down_l", [DFF, 1024])
    pw_l = din("pool_w_l", [128, 512])
    colpack_d = din("colpack", [128, 32])
    fgb_d = din("fgb", [128, D])
    lam_d = din("lamv", [128, 256])
    relb_d = din("relb", [32, 4])
    oh_d = din("onehot", [32, NLUT])
    ident_d = din("ident", [128, 128])
    aident_d = din("aident", [128, 128])
    out_d = nc.dram_tensor("out", [2, SEQ, D], F32, kind="ExternalOutput")

    win_b = nc.dram_tensor("win_b", [1024, 2048], BF16, kind="Internal")
    wo_b = nc.dram_tensor("wo_b", [1024, 1024], BF16, kind="Internal")
    wg_b = nc.dram_tensor("wg_b", [1408, 2048], BF16, kind="Internal")
    wu_b = nc.dram_tensor("wu_b", [1408, 2048], BF16, kind="Internal")
    wd_b = nc.dram_tensor("wd_b", [DFF, 1024], BF16, kind="Internal")
    lut_d = nc.dram_tensor("lut_d", [4, NLUT], F32, kind="Internal")

    ARENA = 131072
    ctxs = dict(
        arena=nc.sbuf_tensor("arena", [128, ARENA // 2], BF16),
        h1=nc.sbuf_tensor("h1", [128, 16 * D], F32),
        ident=nc.sbuf_tensor("ident_sb", [128, 128], F32),
        gbc=nc.sbuf_tensor("gbc", [128, D], F32),
        fgb=nc.sbuf_tensor("fgb_sb", [128, D], F32),
        pw=nc.sbuf_tensor("pw_sb", [128, 512], BF16),
        identb=nc.sbuf_tensor("identb_sb", [128, 128], BF16),
        jb=nc.sbuf_tensor("jb_sb", [128, 128], BF16),
        cols=nc.sbuf_tensor("cols", [128, 64], F32),
        lamv=nc.sbuf_tensor("lamv_sb", [128, 256], F32),
        sm=nc.sbuf_tensor("sm", [128, 128], F32),
        relb=nc.sbuf_tensor("relb_sb", [32, 4], F32),
        oh=nc.sbuf_tensor("oh_sb", [32, NLUT], F32),
        ps=nc.psum_tensor("ps", [128, 8 * 512], F32),
    )
    sem_names = ["pe", "act", "dve", "pool", "sp"]
    from contextlib import ExitStack
    with ExitStack() as es:
        tens = {k: es.enter_context(v) for k, v in ctxs.items()}
        sems = {e: es.enter_context(nc.semaphore("s_" + e)) for e in sem_names}
        dsems = {}
        for e in ("sp", "pool"):
            for k in range(NS_DMA):
                dsems[(e, k)] = es.enter_context(nc.semaphore(f"d_{e}{k}"))

        arena_bf = tens["arena"]
        arena_f = arena_bf.bitcast(F32)
        h1 = tens["h1"]
        ident = tens["ident"]
        gbc = tens["gbc"]
        fgb = tens["fgb"]
        pw = tens["pw"]
        cols = tens["cols"]
        lamv = tens["lamv"]
        sm = tens["sm"]
        relb = tens["relb"]
        ohsb = tens["oh"]
        ps = tens["ps"]
        psb = ps.bitcast(BF16)
        identb = tens["identb"]
        jb = tens["jb"]

        def bfv(off, n):
            return arena_bf[:, off // 2: off // 2 + n]

        def f32v(off, n):
            return arena_f[:, off // 4: off // 4 + n]

        def bank(i):
            return ps[:, i * 512:(i + 1) * 512]

        def bankb(i):
            return psb[:, i * 1024:(i + 1) * 1024]

        g1col = cols[:, 0:8]
        g2col = cols[:, 8:16]
        pscol = cols[:, 16:20]
        sgcol = cols[:, 20:21]
        sg08 = cols[:, 21:22]
        farb = cols[:, 24:32]
        neglam = cols[:, 32:33]
        eps6 = cols[:, 33:34]
        eps5 = cols[:, 34:35]
        lam_s = cols[:, 36:38]
        lam_e = cols[:, 38:40]
        ssA = sm[:, 0:2]
        lnA = sm[:, 2:4]
        rsA = sm[:, 4:6]
        ss2 = sm[:, 8:24]
        ln2 = sm[:, 24:32]
        rs2 = sm[:, 32:40]
        ss3 = sm[:, 40:48]
        ln3 = sm[:, 48:56]
        rs3 = sm[:, 56:64]
        rz = sm[:, 64:72]
        rz1n = sm[:, 72:76]
        ssb = sm[:, 76:80]
        lnb = sm[:, 80:84]
        rsb = sm[:, 84:88]

        B = lambda name, psum=False: Buf(name, psum)
        b_banks = [B(f"bank{i}", True) for i in range(8)]
        b_h1 = [B(f"h1_{i}") for i in range(16)]
        b_ident = B("ident")
        b_gbc = B("gbc")
        b_fgb = B("fgb")
        b_pw = B("pw")
        b_cols = B("cols")
        b_lamv = B("lamv")
        b_lams = B("lams")
        b_relb = B("relb")
        b_oh = B("oh")
        b_winb, b_wob, b_wgb, b_wub, b_wdb = B("winb"), B("wob"), B("wgb"), B("wub"), B("wdb")
        b_lutd = B("lutd")
        b_ss2 = [B(f"ss2_{i}") for i in range(16)]

        emit = T.emit

        def dma_sp(out, in_, reads, writes):
            return emit("sp", lambda e, o=out, i=in_: e.dma_start(out=o, in_=i), reads, writes, dma=True)

        def dma_pool(out, in_, reads, writes):
            return emit("pool", lambda e, o=out, i=in_: e.dma_start(out=o, in_=i), reads, writes, dma=True)

        def mm(out, lhsT, rhs, start, stop, reads, writes, skip=False):
            return emit("pe", lambda e, o=out, l=lhsT, r=rhs, s=start, t=stop, k=skip:
                        e.matmul(o, lhsT=l, rhs=r, start=s, stop=t, skip_group_check=k), reads, writes)

        def tr(out, in_, npart, reads, writes, bf=False):
            idn = (identb if bf else ident)[0:npart, 0:npart]
            return emit("pe", lambda e, o=out, i=in_, d=idn: e.transpose(o, i, d), list(reads) + [b_ident], writes)

        def act(out, in_, func, reads, writes, bias=None, scale=None, accum=None):
            def fn(e, o=out, i=in_, f=func, b=bias, s=scale, a=accum):
                kw = {}
                if b is not None:
                    kw["bias"] = b
                if s is not None:
                    kw["scale"] = s
                if a is not None:
                    kw["accum_out"] = a
                return e.activation(out=o, in_=i, func=f, **kw)
            return emit("act", fn, reads, writes)

        def dve(fn, reads, writes):
            return emit("dve", fn, reads, writes)

        def pool(fn, reads, writes):
            return emit("pool", fn, reads, writes)

        dma_sp(cols[:, 0:32], colpack_d.ap(), [], [b_cols])
        dma_sp(ident[:], ident_d.ap(), [], [b_ident])
        dma_sp(lamv[:], lam_d.ap(), [], [b_lamv])
        dma_sp(relb[:], relb_d.ap(), [], [b_relb])
        dma_sp(ohsb[:], oh_d.ap(), [], [b_oh])
        win0 = bfv(67600, 8 * 2048).rearrange("p (c n) -> p c n", c=8)
        b_win0 = B("win0")
        b_stg = [B(f"wstg{c}") for c in range(8)]
        for c in range(8):
            dma_sp(h1[:, c * 2048:(c + 1) * 2048],
                   win_l.ap().rearrange("(p a) n -> p a n", p=128)[:, c, :], [], [b_stg[c]])
        dma_sp(fgb[:], fgb_d.ap(), [], [b_fgb])
        b_win0c = [B(f"win0_{c}") for c in range(8)]
        for c in range(8):
            dve(lambda e, c=c: e.tensor_copy(out=win0[:, c, :], in_=h1[:, c * 2048:(c + 1) * 2048]),
                [b_stg[c]], [b_win0c[c]])
        early_ex = [dma_sp(win_b.ap().rearrange("(p a) n -> p (a n)", p=128), win0.rearrange("p c n -> p (c n)"),
                           b_win0c, [b_winb])]

        dve(lambda e: e.tensor_copy(out=identb[:], in_=ident[:]), [b_ident], [b_ident])
        b_jb = B("jb")
        jstage = f32v(8192, 128)
        b_jst = B("jstage")
        dma_sp(jstage, aident_d.ap(), [], [b_jst])
        dve(lambda e: e.tensor_copy(out=jb[:], in_=jstage), [b_jst], [b_jb])
        dve(lambda e: e.memset(cols[:, 33:34], 1e-6), [], [b_cols])
        dve(lambda e: e.memset(cols[:, 34:35], 1e-5), [], [b_cols])
        dve(lambda e: e.tensor_scalar(out=sg08, in0=sgcol, scalar1=float(1.0 - LAMBDA_INIT), scalar2=None,
                                      op0=ALU.mult), [b_cols], [b_cols])
        negbb = cols[:, 40:44]
        ea = cols[:, 44:48]
        dve(lambda e: e.tensor_scalar(out=negbb, in0=farb[:, 0:4], scalar1=-1.0, scalar2=None, op0=ALU.mult),
            [b_cols], [b_cols])
        dve(lambda e: e.tensor_tensor(out=ea, in0=farb[:, 4:8], in1=farb[:, 0:4], op=ALU.subtract),
            [b_cols], [b_cols])
        b_lamp = B("lamp")
        lamp = f32v(0, 128)
        lamj = bfv(1024, 128)
        dve(lambda e: e.tensor_tensor(out=lamp[:, 0:64], in0=lamv[:, 0:64], in1=lamv[:, 64:128], op=ALU.mult),
            [b_lamv], [b_lamp])
        dve(lambda e: e.tensor_tensor(out=lamp[:, 64:128], in0=lamv[:, 128:192], in1=lamv[:, 192:256], op=ALU.mult),
            [b_lamv], [b_lamp])
        b_lamj = B("lamj")
        act(lamj[:, 0:64], lamp[:, 0:64], AF.Copy, [b_lamp], [b_lamj, b_lams], accum=lam_s[:, 0:1])
        act(lamj[:, 0:64], lamp[:, 64:128], AF.Copy, [b_lamp], [b_lamj, b_lams], accum=lam_s[:, 1:2])
        act(lam_e, lam_s, AF.Exp, [b_lams], [b_lams])
        dve(lambda e: e.tensor_tensor(out=neglam, in0=lam_e[:, 1:2], in1=lam_e[:, 0:1], op=ALU.subtract),
            [b_lams], [b_cols])
        dve(lambda e: e.tensor_scalar(out=neglam, in0=neglam, scalar1=float(-LAMBDA_INIT), scalar2=None,
                                      op0=ALU.add), [b_cols], [b_cols])
        lutsb = f32v(4096, NLUT)
        b_lutsb = B("lutsb")
        for half in range(2):
            mm(bank(0)[0:4, 0:448], relb[:, :], ohsb[:, half * 448:(half + 1) * 448], True, True,
               [b_relb, b_oh], [b_banks[0]])
            dve(lambda e, hf=half: e.tensor_copy(out=lutsb[0:4, hf * 448:(hf + 1) * 448], in_=bank(0)[0:4, 0:448]),
                [b_banks[0]], [b_lutsb])
        dma_sp(lut_d.ap(), lutsb[0:4, :], [b_lutsb], [b_lutd])
        T.barrier(exclude=early_ex)

        O_QT, O_KT, O_U, O_VX, O_X = 0, 16640, 33280, 49920, 67600
        PW_ = 2080
        QT = bfv(O_QT, 4 * PW_).rearrange("p (h n) -> p h n", h=4)
        KT = bfv(O_KT, 4 * PW_).rearrange("p (h n) -> p h n", h=4)
        UU = bfv(O_U, 4 * PW_).rearrange("p (h n) -> p h n", h=4)
        VX = bfv(O_VX, NKT * 4 * 130).rearrange("p (k h n) -> p k h n", k=NKT, h=4)

        def gain_tile(gcol, ones_ap, b_ones):
            for c in range(8):
                dve(lambda e, c=c: e.tensor_scalar(out=gbc[:, c * 128:(c + 1) * 128], in0=ones_ap,
                                                   scalar1=gcol[:, c:c + 1], scalar2=None, op0=ALU.mult),
                    [b_cols, b_ones], [b_gbc])

        for s in range(2):
            b_QT = [B(f"QT{i}") for i in range(5)]
            b_KT = [B(f"KT{i}") for i in range(NKT)]
            b_U = [B(f"U{i}") for i in range(5)]
            b_Upad = B("Upad")
            b_VX = [B(f"VX{i}") for i in range(NKT)]
            win = bfv(O_X, 8 * 2048).rearrange("p (c n) -> p c n", c=8)
            uT = [bfv(O_X + 32768 + i * 8192, 8 * 512).rearrange("p (c n) -> p c n", c=8) for i in range(2)]
            xs = [f32v(O_X + 49152 + i * 4096, 1024) for i in range(2)]
            xb = [bfv(O_X + 57344 + i * 2048, 1024) for i in range(2)]
            b_win = B("win")
            b_uT = [B("uT0"), B("uT1")]
            b_xs = [B("xs0"), B("xs1")]
            b_xb = [B("xb0"), B("xb1")]
            b_ssA = [B("ssA0"), B("ssA1")]

            rd_win = b_win0c if s == 0 else [b_win]
            if s == 0:
                pass
            else:
                dma_sp(win.rearrange("p c n -> p (c n)"), win_b.ap().rearrange("(p a) n -> p (a n)", p=128),
                       [b_winb], [b_win])
            dve(lambda e: e.memset(xs[1][:, 0:128], 1.0), [], [b_xs[1]])
            gain_tile(g1col, xs[1][:, 0:128], b_xs[1])
            pool(lambda e: e.memset(UU[:, :, 2064:2080], 0.0), [], [b_Upad])
            pool(lambda e: e.memset(VX[:, :, :, 128:130], 1.0), [], b_VX)

            def load_tile(ti, slot):
                np_ = NMETA if ti < 0 else 128
                src = meta_d.ap() if ti < 0 else x_d.ap()[s, ti * 128:(ti + 1) * 128, :]
                dma_sp(xs[slot][0:np_, :], src, [], [b_xs[slot]])

            def norm_tile(ti, slot):
                np_ = NMETA if ti < 0 else 128
                xt = xs[slot]
                xbt = xb[slot]
                act(xbt[0:np_, :], xt[0:np_, :], AF.Square, [b_xs[slot]], [b_xb[slot], b_ssA[slot]],
                    accum=ssA[0:np_, slot:slot + 1])
                act(lnA[0:np_, slot:slot + 1], ssA[0:np_, slot:slot + 1], AF.Ln, [b_ssA[slot], b_cols], [b_ssA[slot]],
                    bias=eps6[0:np_, :], scale=1.0 / D)
                act(rsA[0:np_, slot:slot + 1], lnA[0:np_, slot:slot + 1], AF.Exp, [b_ssA[slot]], [b_ssA[slot]],
                    scale=-0.5)
                act(xbt[0:np_, :], xt[0:np_, :], AF.Copy, [b_ssA[slot], b_xs[slot]], [b_xb[slot]],
                    scale=rsA[0:np_, slot:slot + 1])
                pb = slot
                for c in range(8):
                    tr(bankb(pb)[:, c * 128:c * 128 + np_], xbt[0:np_, c * 128:(c + 1) * 128], np_,
                       [b_xb[slot]], [b_banks[pb]], bf=True)
                p0 = 0 if ti < 0 else NMETA + ti * 128
                a = p0
                while a < p0 + np_:
                    pc = a // 512
                    bnd = min(p0 + np_, (pc + 1) * 512)
                    n = bnd - a
                    off = a - p0
                    src_ps = bankb(pb).rearrange("p (c n) -> p c n", c=8)[:, :, off:off + n]
                    dst = uT[pc % 2][:, :, a - pc * 512:a - pc * 512 + n]
                    g = gbc[:, :].rearrange("p (c n) -> p c n", c=8)[:, :, 0:n]
                    dve(lambda e, o=dst, i=src_ps, g=g: e.tensor_tensor(out=o, in0=i, in1=g, op=ALU.mult),
                        [b_banks[pb], b_gbc], [b_uT[pc % 2]])
                    a = bnd

            kindc = [0]

            def inproj_groups(pc):
                n = 512 if pc < 4 else 16
                u = uT[pc % 2]
                bu = b_uT[pc % 2]
                pos0 = pc * 512
                groups = []
                for grp in range(3):
                    for h in range(4):
                        def g_(grp=grp, h=h):
                            bi = 2 + (kindc[0] % 6)
                            kindc[0] += 1
                            colbase = {0: 512, 1: 1024, 2: 0}[grp] + h * 128
                            for c in range(8):
                                mm(bank(bi)[:, 0:n], win[:, c, colbase:colbase + 128], u[:, c, 0:n], c == 0, c == 7,
                                   rd_win + [bu], [b_banks[bi]])
                            if grp == 0:
                                dve(lambda e, o=QT[:, h, pos0:pos0 + n], i=bank(bi)[:, 0:n]:
                                    e.tensor_scalar(out=o, in0=i, scalar1=0.125, scalar2=None, op0=ALU.mult),
                                    [b_banks[bi]], [b_QT[pc]])
                            elif grp == 1:
                                wr = [b_KT[k] for k in range(pos0 // 128, (pos0 + n - 1) // 128 + 1)]
                                dve(lambda e, o=KT[:, h, pos0:pos0 + n], i=bank(bi)[:, 0:n]:
                                    e.tensor_copy(out=o, in_=i), [b_banks[bi]], wr)
                            else:
                                act(UU[:, h, pos0:pos0 + n], bank(bi)[:, 0:n], AF.Copy, [b_banks[bi]], [b_U[pc]])
                        groups.append(g_)
                nt = (n + 127) // 128
                for kk in range(nt):
                    def gv(kk=kk):
                        m = min(128, n - kk * 128)
                        kt = pos0 // 128 + kk
                        bi = 2 + (kindc[0] % 6)
                        kindc[0] += 1
                        for c in range(8):
                            mm(bank(bi)[0:m, :], u[:, c, kk * 128:kk * 128 + m], win[:, c, 1536:2048], c == 0, c == 7,
                               rd_win + [bu], [b_banks[bi]])
                        src_ps = bank(bi)[0:m, :].rearrange("p (h n) -> p h n", h=4)
                        if kk % 2 == 0:
                            dve(lambda e, o=VX[0:m, kt, :, 0:128], sp_=src_ps: e.tensor_copy(out=o, in_=sp_),
                                [b_banks[bi]], [b_VX[kt]])
                        else:
                            act(VX[0:m, kt, :, 0:128], src_ps, AF.Copy, [b_banks[bi]], [b_VX[kt]])
                    groups.append(gv)
                return groups

            tidx = 0
            order = [-1] + list(range(16))
            load_tile(order[0], 0)
            load_tile(order[1], 1)
            _nt = norm_tile

            def norm_tile(ti, slot):
                _nt(ti, slot)
                k = order.index(ti)
                if k + 2 < len(order):
                    load_tile(order[k + 2], k % 2)
            for ti in order[0:5]:
                norm_tile(ti, tidx % 2)
                tidx += 1
            nxt_tile = 5
            if s == 0:
                gate = [b_uT[0], b_uT[1]]
                dma_pool(wo_b.ap(), wo_l.ap(), gate, [b_wob])
                dma_pool(pw[:], pw_l.ap(), gate, [b_pw])
            for pc in range(5):
                groups = inproj_groups(pc)
                ng = len(groups)
                for gi, g_ in enumerate(groups):
                    g_()
                    if pc < 3 and gi % 4 == 3 and nxt_tile < len(order) and nxt_tile < 5 + 4 * (pc + 1):
                        norm_tile(order[nxt_tile], tidx % 2)
                        tidx += 1
                        nxt_tile += 1
                while pc < 3 and nxt_tile < 5 + 4 * (pc + 1):
                    norm_tile(order[nxt_tile], tidx % 2)
                    tidx += 1
                    nxt_tile += 1
            T.barrier()

            from collections import deque
            wo = bfv(O_X, 8 * 1024).rearrange("p (c n) -> p c n", c=8)
            Tst = f32v(O_X + 16384, 4 * TW).rearrange("p (h n) -> p h n", h=4)
            o2 = O_X + 16384 + 12288
            NE = 4
            Et = [bfv(o2 + i * 1024, 512) for i in range(NE)]
            o2 += NE * 1024
            Tb = bfv(o2, 4 * TW).rearrange("p (h n) -> p h n", h=4)
            o2 += 4 * TW * 2
            yT_off = o2
            yT = [bfv(o2 + i * 4096, 8 * QC).rearrange("p (c n) -> p c n", c=8) for i in range(2)]
            o2 += 8192
            ot = [f32v(o2 + i * 1024, 256).rearrange("p (q n) -> p q n", q=2) for i in range(2)]
            o2 += 2048
            yt = [f32v(o2 + i * 1024, 256).rearrange("p (q n) -> p q n", q=2) for i in range(2)]
            ytb_all = [bfv(o2 + i * 1024, 256).rearrange("p (q n) -> p q n", q=2) for i in range(2)]
            o2 += 2048
            tA = f32v(o2, 272)
            tB = f32v(o2 + 1088, 272)
            o2 += 2176
            dT = [bfv(o2 + i * 2048, 4 * QC).rearrange("p (g n) -> p g n", g=4) for i in range(2)]
            o2 += 4096
            junkB = bfv(o2, 1024)
            o2 += 2048
            Qb = [bfv(o2 + i * 1024, 512) for i in range(2)]
            o2 += 2048
            assert o2 <= ARENA, o2
            b_wo = B("wo")
            b_E = [B(f"E{i}") for i in range(NE)]
            b_sb = [B(f"sb{i}") for i in range(2)]
            b_yT = [B(f"yT{i}") for i in range(2)]
            b_ot = [B(f"ot{i}") for i in range(2)]
            b_yt = [B(f"yt{i}") for i in range(2)]
            b_tA, b_tB = B("tA"), B("tB")
            b_dT = [B(f"dT{i}") for i in range(2)]
            b_junkB = B("junkB")
            b_rz = [B("rz0"), B("rz1")]
            b_ssb = [B("ssb0"), B("ssb1")]
            b_Qb = [B("Qb0"), B("Qb1")]

            dma_sp(wo.rearrange("p c n -> p (c n)"), wo_b.ap().rearrange("(p a) n -> p (a n)", p=128),
                   [b_wob], [b_wo])
            for i in range(2):
                pool(lambda e, i=i: e.memset(Qb[i][:, :], 0.0), [], [b_Qb[i]])
            b_Tst = [B(f"Tst{h}") for h in range(4)]
            b_Tb = B("Tb")
            Hb = bfv(yT_off, 4 * TW)
            for h in range(4):
                src = bass.AP(lut_d, h * NLUT, [[1, 128], [1, TW]])
                dma_sp(Tst[:, h, :], src, [b_lutd], [b_Tst[h]])
                act(Hb[:, h * TW:(h + 1) * TW], Tst[:, h, :], AF.Exp, [b_Tst[h], b_cols], b_yT, bias=negbb[:, h:h + 1])
            tb_banks = [7, 3, 4, 5]
            k = 0
            for h in range(4):
                for half in range(2):
                    bk = tb_banks[k % 4]
                    k += 1
                    mm(bank(bk)[:, 0:384], jb[:, :], Hb[:, h * TW + half * 384:h * TW + (half + 1) * 384], True, True,
                       b_yT + [b_jb], [b_banks[bk]])
                    dve(lambda e, o=Tb[:, h, half * 384:(half + 1) * 384], i=bank(bk)[:, 0:384]:
                        e.tensor_copy(out=o, in_=i), [b_banks[bk]], [b_Tb])

            import heapq
            deferred = []
            dseq = [0]

            def defer(at, fn):
                dseq[0] += 1
                heapq.heappush(deferred, (at, dseq[0], fn))

            def run_deferred(cur, nmax):
                k = 0
                while deferred and k < nmax and deferred[0][0] <= cur:
                    heapq.heappop(deferred)[2]()
                    k += 1

            def pool_stage(cq, g, w):
                q0 = NMETA + cq * QC
                ysl = cq % 2

                def zz(lo, n):
                    return UU[:, g, q0 + lo:q0 + lo + n]
                rdU = [b_U[min(4, (q0 - 8) // 512)], b_U[min(4, (q0 + 263) // 512)], b_Upad]

                def padd(o, a, b, reads, writes):
                    pool(lambda e, o=o, a=a, b=b: e.tensor_tensor(out=o, in0=a, in1=b, op=ALU.add), reads, writes)
                if w == 2:
                    padd(tA[:, 0:256], zz(0, 256), zz(-1, 256), rdU, [b_tA])
                    ws, wb, wsl = tA, b_tA, 0
                else:
                    padd(tA[:, 1:272], zz(-7, 271), zz(-8, 271), rdU, [b_tA])
                    padd(tB[:, 2:271], tA[:, 3:272], tA[:, 1:270], [b_tA], [b_tB])
                    ws, wb, wsl = tB, b_tB, 8
                    if w >= 8:
                        padd(tA[:, 4:269], tB[:, 6:271], tB[:, 2:267], [b_tB], [b_tA])
                        ws, wb = tA, b_tA
                    if w == 16:
                        padd(tB[:, 8:265], tA[:, 12:269], tA[:, 4:261], [b_tA], [b_tB])
                        ws, wb = tB, b_tB
                def dpart():
                    dve(lambda e, o=dT[ysl][:, g, :], a=ws[:, wsl:wsl + 256], sc_=1.0 / w, b=zz(0, 256):
                        e.scalar_tensor_tensor(out=o, in0=a, scalar=sc_, in1=b, op0=ALU.mult, op1=ALU.subtract),
                        [wb] + rdU, [b_dT[ysl]])
                    if cq == NQC - 1:
                        right = w - 1 - w // 2
                        for r in range(right):
                            cnt = w - (right - r)
                            col = 255 - r
                            dve(lambda e, o=dT[ysl][:, g, col:col + 1], a=ws[:, wsl + col:wsl + col + 1], sc_=1.0 / cnt,
                                b=zz(col, 1):
                                e.scalar_tensor_tensor(out=o, in0=a, scalar=sc_, in1=b, op0=ALU.mult, op1=ALU.subtract),
                                [wb] + rdU, [b_dT[ysl]])
                return dpart

            def poolmm_stage(cq, g):
                ysl = cq % 2
                mb = 7
                mm(bank(mb)[:, 0:QC], pw[:, g * 128:(g + 1) * 128], dT[ysl][:, g, :], True, True,
                   [b_pw, b_dT[ysl]], [b_banks[mb]])
                dve(lambda e, o=yT[ysl][:, g, :], i=bank(mb)[:, 0:QC], sc_=pscol[:, g:g + 1]:
                    e.tensor_scalar(out=o, in0=i, scalar1=sc_, scalar2=None, op0=ALU.mult),
                    [b_banks[mb], b_cols], [b_yT[ysl]])

            def epilogue_stages(cq, h, osl):
                ysl = cq % 2
                ba, bb = 3 + 2 * osl, 4 + 2 * osl
                rzs = rz[:, osl * 4:(osl + 1) * 4]
                r1n = rz1n[:, osl * 2:(osl + 1) * 2]
                sss = ssb[:, osl * 2:(osl + 1) * 2]
                lns = lnb[:, osl * 2:(osl + 1) * 2]
                rss = rsb[:, osl * 2:(osl + 1) * 2]
                mb = 7

                def st1():
                    for c, bk in ((0, ba), (1, bb)):
                        zsrc = bank(bk)[:, 0:258].rearrange("p (q n) -> p q n", n=129)[:, :, 128:129]
                        zdst = rzs[:, c * 2:(c + 1) * 2].rearrange("p (q o) -> p q o", o=1)
                        dve(lambda e, o=zdst, i=zsrc: e.reciprocal(out=o, in_=i), [b_banks[bk]], [b_rz[osl]])
                    dve(lambda e, o=r1n, i=rzs[:, 2:4]: e.tensor_scalar(out=o, in0=i, scalar1=neglam, scalar2=None,
                                                                         op0=ALU.mult), [b_rz[osl], b_cols], [b_rz[osl]])

                def st2():
                    for qs in range(2):
                        dve(lambda e, o=ot[osl][:, qs, :], i=bank(ba)[:, qs * 129:qs * 129 + 128], sc_=rzs[:, qs:qs + 1]:
                            e.tensor_scalar(out=o, in0=i, scalar1=sc_, scalar2=None, op0=ALU.mult),
                            [b_banks[ba], b_rz[osl]], [b_ot[osl]])

                def st3():
                    for qs in range(2):
                        dve(lambda e, o=ot[osl][:, qs, :], i=bank(bb)[:, qs * 129:qs * 129 + 128], sc_=r1n[:, qs:qs + 1]:
                            e.scalar_tensor_tensor(out=o, in0=i, scalar=sc_, in1=o, op0=ALU.mult, op1=ALU.add),
                            [b_banks[bb], b_rz[osl], b_ot[osl]], [b_ot[osl]])

                def st4():
                    for qs in range(2):
                        dve(lambda e, o=junkB[:, 0:128], i=ot[osl][:, qs, :], a=sss[:, qs:qs + 1]:
                            e.scalar_tensor_tensor(out=o, in0=i, scalar=1.0, in1=i, op0=ALU.mult, op1=ALU.mult,
                                                   accum_out=a),
                            [b_ot[osl]], [b_junkB, b_ssb[osl]])

                def st5():
                    act(lns, sss, AF.Ln, [b_ssb[osl], b_cols], [b_ssb[osl]], bias=eps5, scale=1.0 / 128)
                    act(rss, lns, AF.Exp, [b_ssb[osl]], [b_ssb[osl]], scale=-0.5)

                ytb = ytb_all[osl]

                def st6():
                    for qs in range(2):
                        dve(lambda e, o=ytb[:, qs, :], i=ot[osl][:, qs, :], sc_=rss[:, qs:qs + 1]:
                            e.tensor_scalar(out=o, in0=i, scalar1=sc_, scalar2=None, op0=ALU.mult),
                            [b_ot[osl], b_ssb[osl]], [b_yt[osl]])

                def st7():
                    for qs in range(2):
                        tr(bankb(mb)[:, qs * 128:(qs + 1) * 128], ytb[:, qs, :], 128, [b_yt[osl]], [b_banks[mb]], bf=True)

                def st8():
                    dve(lambda e, o=yT[ysl][:, 4 + h, :], i=bankb(mb)[:, 0:QC]:
                        e.tensor_scalar(out=o, in0=i, scalar1=sg08, scalar2=None, op0=ALU.mult),
                        [b_banks[mb], b_cols], [b_yT[ysl]])
                def st78():
                    st7()
                    st8()
                return [(0, st1), (2, st2), (4, st3), (6, st4), (11, st5), (15, st6), (19, st78)]

            def wo_stages(cq):
                ysl = cq % 2
                sts = []
                k = 0
                for t2 in range(2):
                    tile_i = cq * 2 + t2
                    for half in range(4):
                        def st(t2=t2, half=half, tile_i=tile_i):
                            mb = 7
                            for c in range(8):
                                mm(bank(mb)[:, 0:256], yT[ysl][:, c, t2 * 128:(t2 + 1) * 128],
                                   wo[:, c, half * 256:(half + 1) * 256],
                                   c == 0, c == 7, [b_yT[ysl], b_wo], [b_banks[mb]])
                            hsl = h1[:, tile_i * D + half * 256:tile_i * D + (half + 1) * 256]
                            dve(lambda e, hsl=hsl, mb=mb: e.tensor_tensor(out=hsl, in0=bank(mb)[:, 0:256], in1=hsl, op=ALU.add),
                                [b_banks[mb], b_h1[tile_i]], [b_h1[tile_i]])
                        sts.append((24 + 2 * k, st))
                        k += 1

                    def stq(tile_i=tile_i):
                        dve(lambda e, o=junkB[:, :], i=h1[:, tile_i * D:(tile_i + 1) * D], a=ss2[:, tile_i:tile_i + 1]:
                            e.scalar_tensor_tensor(out=o, in0=i, scalar=1.0, in1=i, op0=ALU.mult, op1=ALU.mult,
                                                   accum_out=a),
                            [b_h1[tile_i]], [b_junkB, b_ss2[tile_i]])
                    sts.append((25 + 2 * (k - 1), stq))
                return sts

            unit = 0

            def av(cqh, j, esl, kn, osl, h):
                ba, bb = 3 + 2 * osl, 4 + 2 * osl
                for c, bk in ((0, ba), (1, bb)):
                    for qs in range(2):
                        mm(bank(bk)[:, qs * 129:(qs + 1) * 129],
                           Et[esl][0:kn, c * QC + qs * 128:c * QC + (qs + 1) * 128],
                           VX[0:kn, j, h, 0:129], (j == 0 and qs == 0), j == NKT - 1,
                           [b_E[esl], b_VX[j]], [b_banks[bk]], skip=True)
                if FILLER:
                    mm(bank(bb)[:, 258:258 + FILLER], Et[esl][0:kn, QC + 128:QC + 256], Qb[osl][0:kn, 0:FILLER],
                       False, False, [b_E[esl], b_Qb[osl]], [b_banks[bb]], skip=True)
                if j == NKT - 1:
                    for off, st in epilogue_stages(cqh, h, osl):
                        defer(unit + off, st)
                    if h == 3:
                        for off, st in wo_stages(cqh):
                            defer(unit + off, st)

            def emit_qb(cq_, h_, slot):
                q0_ = NMETA + cq_ * QC
                for c in range(2):
                    pool(lambda e, o=Qb[slot][c * 64:(c + 1) * 64, c * QC:(c + 1) * QC],
                         i=QT[c * 64:(c + 1) * 64, h_, q0_:q0_ + QC]: e.tensor_copy(out=o, in_=i),
                         [b_QT[min(4, q0_ // 512)], b_QT[min(4, (q0_ + QC - 1) // 512)]], [b_Qb[slot]])

            units = []
            hq_ = 0
            for cq in range(NQC):
                for h in range(4):
                    for j in range(NKT):
                        units.append(dict(cq=cq, h=h, j=j, sl=hq_ % 2, kn=128 if j < NKT - 1 else LPOS - 128 * (NKT - 1)))
                    hq_ += 1
            NU = len(units)

            def emit_qk(u):
                U_ = units[u]
                cq_, h_, j_, kn_, sl_ = U_["cq"], U_["h"], U_["j"], U_["kn"], U_["sl"]
                ssl_ = u % 3
                k0_ = j_ * 128
                mm(bank(ssl_)[0:kn_, :], KT[:, h_, k0_:k0_ + kn_], Qb[sl_][:, :], True, True,
                   [b_KT[j_], b_Qb[sl_]], [b_banks[ssl_]])

            emit_qb(0, 0, 0)
            emit_qk(0)
            emit_qk(1)
            cast_pieces = []
            if s == 0:
                for r in range(11):
                    cast_pieces.append((wg_b.ap()[r * 128:(r + 1) * 128, :], wg_l.ap()[r * 128:(r + 1) * 128, :], b_wgb))
                    cast_pieces.append((wu_b.ap()[r * 128:(r + 1) * 128, :], wu_l.ap()[r * 128:(r + 1) * 128, :], b_wub))
                for r in range(11):
                    cast_pieces.append((wd_b.ap()[r * 256:(r + 1) * 256, :], wd_l.ap()[r * 256:(r + 1) * 256, :], b_wdb))
            for u in range(NU):
                U_ = units[u]
                cq, h, j, kn, sl = U_["cq"], U_["h"], U_["j"], U_["kn"], U_["sl"]
                q0 = NMETA + cq * QC
                unit = u + 1
                if h == 0 and j == 0:
                    for t2 in range(2):
                        tile_i = cq * 2 + t2
                        dma_sp(h1[:, tile_i * D:(tile_i + 1) * D], x_d.ap()[s, tile_i * 128:(tile_i + 1) * 128, :],
                               [], [b_h1[tile_i]])
                    for g, w in enumerate((2, 4, 8, 16)):
                        def pst(cq=cq, g=g, w=w, at=unit + 1 + 8 * g + 6):
                            dpart = pool_stage(cq, g, w)
                            defer(at, dpart)
                        defer(unit + 1 + 8 * g, pst)
                if h == 2 and j == 0:
                    for g in range(4):
                        defer(unit + 3 * g, lambda cq=cq, g=g: poolmm_stage(cq, g))
                ssl = u % 3
                esl = u % NE
                k0 = j * 128
                Dd = k0 - q0
                s_ps = bank(ssl)[0:kn, :]
                e_out = Et[esl][0:kn, :]
                maxrel = Dd + kn - 1
                minrel = Dd - (QC - 1)
                if minrel >= 128:
                    act(e_out, s_ps, AF.Exp, [b_banks[ssl], b_cols], [b_E[esl]], bias=ea[0:kn, h:h + 1])
                else:
                    act(e_out, s_ps, AF.Exp, [b_banks[ssl]], [b_E[esl]])
                if maxrel <= -128 or minrel >= 128:
                    pass
                else:
                    i0 = 368 - Dd
                    assert 0 <= i0 <= TW - QC, (i0, Dd)
                    for c in range(2):
                        dve(lambda e, o=Et[esl][0:kn, c * QC:(c + 1) * QC], b=Tb[0:kn, h, i0:i0 + QC]:
                            e.tensor_tensor(out=o, in0=o, in1=b, op=ALU.mult),
                            [b_Tb, b_E[esl]], [b_E[esl]])
                if j == 4 and cast_pieces:
                    o_, i_, bb_ = cast_pieces.pop(0)
                    dma_pool(o_, i_, [], [bb_])
                if j == 8 and u + 9 < NU:
                    nU = units[u + 9]
                    emit_qb(nU["cq"], nU["h"], nU["sl"])
                if u + 2 < NU:
                    emit_qk(u + 2)
                if u >= 1:
                    pU = units[u - 1]
                    av(pU["cq"], pU["j"], (u - 1) % NE, pU["kn"], pU["sl"], pU["h"])
                run_deferred(unit, 1)
            pU = units[NU - 1]
            unit = NU + 1
            av(pU["cq"], pU["j"], (NU - 1) % NE, pU["kn"], pU["sl"], pU["h"])
            while deferred:
                run_deferred(10 ** 9, 10 ** 6)
            while cast_pieces:
                o_, i_, bb_ = cast_pieces.pop(0)
                dma_pool(o_, i_, [], [bb_])
            T.barrier()

            fT = bfv(0, 8 * 1024).rearrange("p (c n) -> p c n", c=8)
            aT = bfv(16384, 22 * 1024).rearrange("p (k n) -> p k n", k=22)
            Wdh = [bfv(61440 + i * 22528, 22 * 512).rearrange("p (k n) -> p k n", k=22) for i in range(2)]
            wgs = [bfv(106496 + i * 8192, 2048).rearrange("p (c n) -> p c n", c=8) for i in range(2)]
            wus = [bfv(106496 + i * 8192 + 4096, 2048).rearrange("p (c n) -> p c n", c=8) for i in range(2)]
            hn2 = [bfv(122880 + i * 2048, 1024) for i in range(2)]
            sgt = [bfv(126976 + i * 1024, 512) for i in range(2)]
            junkC = bfv(129024, 1024)
            b_fT = [B(f"fT{i}") for i in range(8)]
            b_aT = [B(f"aT{i}") for i in range(4)]
            b_Wdh = [B("Wdh0"), B("Wdh1")]
            b_wgs = [B("wgs0"), B("wgs1")]
            b_hn2 = [B("hn2_0"), B("hn2_1")]
            b_sgt = [B("sgt0"), B("sgt1")]
            b_junkC = B("junkC")
            b_st = B("stats")
            dve(lambda e: e.memset(junkC[:, 0:128], 1.0), [], [b_junkC])
            gain_tile(g2col, junkC[:, 0:128], b_junkC)
            b_rs2 = [B("rs2_0"), B("rs2_1")]
            b_st3t = [B(f"st3_{i}") for i in range(8)]

            def c0_stats(sc):
                rd_ss = [b_ss2[t] for t in range(sc * 8, sc * 8 + 8)]
                act(ln2[:, :], ss2[:, sc * 8:(sc + 1) * 8], AF.Ln, rd_ss + [b_cols], [b_rs2[sc]], bias=eps6, scale=1.0 / D)
                act(rs2x[:, sc * 8:(sc + 1) * 8], ln2[:, :], AF.Exp, [b_rs2[sc]], [b_rs2[sc]], scale=-0.5)

            def c0_copy(sc, i):
                t = sc * 8 + i
                hs = i % 2
                act(hn2[hs][:, :], h1[:, t * D:(t + 1) * D], AF.Copy, [b_h1[t], b_rs2[sc]], [b_hn2[hs]],
                    scale=rs2x[:, sc * 8 + i:sc * 8 + i + 1])

            def c0_tr(sc, i, pbase):
                hs = i % 2
                pb = pbase + hs
                for c in range(8):
                    tr(bankb(pb)[:, c * 128:(c + 1) * 128], hn2[hs][:, c * 128:(c + 1) * 128], 128,
                       [b_hn2[hs]], [b_banks[pb]], bf=True)
                src_ps = bankb(pb).rearrange("p (c n) -> p c n", c=8)
                dst = fT[:, :, i * 128:(i + 1) * 128]
                g = gbc[:, :].rearrange("p (c n) -> p c n", c=8)
                dve(lambda e, o=dst, i_=src_ps, g=g: e.tensor_tensor(out=o, in0=i_, in1=g, op=ALU.mult),
                    [b_banks[pb], b_gbc], [b_fT[i]])

            rs2x = sm[:, 88:104]
            c0_stats(0)
            c0_copy(0, 0)
            for i in range(8):
                if i + 1 < 8:
                    c0_copy(0, i + 1)
                c0_tr(0, i, 0)
            for sc in range(2):
                tiles = list(range(sc * 8, sc * 8 + 8))
                dma_sp(Wdh[0], wd_b.ap().rearrange("(k p) n -> p k n", p=128)[:, :, 0:512], [b_wdb], [b_Wdh[0]])
                gu = 0
                for j in range(11):
                    wsl = j % 2
                    dma_sp(wgs[wsl].rearrange("p c n -> p (c n)"), wg_b.ap()[j * 128:(j + 1) * 128, :],
                           [b_wgb], [b_wgs[wsl]])
                    dma_sp(wus[wsl].rearrange("p c n -> p (c n)"), wu_b.ap()[j * 128:(j + 1) * 128, :],
                           [b_wub], [b_wgs[wsl]])
                    for tc in range(2):
                        rdf = [b_fT[tc * 4 + k] for k in range(4)]
                        for sub in range(2):
                            gsl = gu % 2
                            gu += 1
                            bg, bu_ = 4 + 2 * gsl, 5 + 2 * gsl
                            for c in range(8):
                                mm(bank(bg), wgs[wsl][:, c, sub * 128:(sub + 1) * 128], fT[:, c, tc * 512:(tc + 1) * 512],
                                   c == 0, c == 7, [b_wgs[wsl]] + rdf, [b_banks[bg]])
                            for c in range(8):
                                mm(bank(bu_), wus[wsl][:, c, sub * 128:(sub + 1) * 128], fT[:, c, tc * 512:(tc + 1) * 512],
                                   c == 0, c == 7, [b_wgs[wsl]] + rdf, [b_banks[bu_]])
                            act(sgt[gsl][:, :], bank(bg), AF.Silu, [b_banks[bg]], [b_sgt[gsl]])
                            kf = 2 * j + sub
                            dve(lambda e, gsl=gsl, bu_=bu_, kf=kf, tc=tc:
                                e.tensor_tensor(out=aT[:, kf, tc * 512:(tc + 1) * 512], in0=bank(bu_), in1=sgt[gsl][:, :],
                                                op=ALU.mult), [b_banks[bu_], b_sgt[gsl]], [b_aT[tc * 2 + (kf % 2)]])
                dma_sp(Wdh[1], wd_b.ap().rearrange("(k p) n -> p k n", p=128)[:, :, 512:1024], [b_wdb], [b_Wdh[1]])
                dn = 0
                nxt = sc + 1 if sc + 1 < 2 else None
                if nxt is not None:
                    c0_stats(nxt)
                    c0_copy(nxt, 0)
                for half in range(2):
                    for i, t in enumerate(tiles):
                        bi = dn % 4
                        dn += 1
                        tc = i // 4
                        for k in range(22):
                            mm(bank(bi), aT[:, k, i * 128:(i + 1) * 128], Wdh[half][:, k, :], k == 0, k == 21,
                               [b_aT[tc * 2], b_aT[tc * 2 + 1], b_Wdh[half]], [b_banks[bi]])
                        hsl = h1[:, t * D + half * 512:t * D + (half + 1) * 512]
                        dve(lambda e, hsl=hsl, bi=bi: e.tensor_tensor(out=hsl, in0=bank(bi), in1=hsl, op=ALU.add),
                            [b_banks[bi], b_h1[t]], [b_h1[t]])
                        if nxt is not None and half == 0:
                            if i + 1 < 8:
                                c0_copy(nxt, i + 1)
                            c0_tr(nxt, i, 4)
                        if half == 1:
                            hfull = h1[:, t * D:(t + 1) * D]
                            act(junkC[:, :], hfull, AF.Square, [b_h1[t]], [b_junkC, b_st3t[i]], accum=ss3[:, i:i + 1])
                            act(ln3[:, i:i + 1], ss3[:, i:i + 1], AF.Ln, [b_st3t[i], b_cols], [b_st3t[i]], bias=eps6,
                                scale=1.0 / D)
                            act(rs3[:, i:i + 1], ln3[:, i:i + 1], AF.Exp, [b_st3t[i]], [b_st3t[i]], scale=-0.5)
                            dve(lambda e, hsl=hfull, i=i: e.scalar_tensor_tensor(out=hsl, in0=hsl, scalar=rs3[:, i:i + 1],
                                                                                in1=fgb[:, :], op0=ALU.mult, op1=ALU.mult),
                                [b_h1[t], b_st3t[i], b_fgb], [b_h1[t]])
                            dma_pool(out_d.ap()[s, t * 128:(t + 1) * 128, :], hfull, [b_h1[t]], [])
            T.barrier()

        T.finalize()
        with nc.Block() as block:
            @block.tensor
            def _(e):
                T.play("pe", e, sems, dsems)

            @block.scalar
            def _(e):
                T.play("act", e, sems, dsems)

            @block.vector
            def _(e):
                T.play("dve", e, sems, dsems)

            @block.gpsimd
            def _(e):
                T.play("pool", e, sems, dsems)

            @block.sync
            def _(e):
                T.play("sp", e, sems, dsems, final_wait=True)
    return nc


_CACHE = {}


def _host_layouts(inp):
    f = lambda a: np.ascontiguousarray(np.asarray(a, dtype=np.float32))
    w_in = f(inp["w_in"])[0]
    w_o = f(inp["w_o"])[0]
    w_gate = f(inp["w_gate"])[0]
    w_up = f(inp["w_up"])[0]
    w_down = f(inp["w_down"])[0]
    rel_bias = f(inp["rel_bias"])
    n = np.arange(NLUT)
    bucket = _t5_bucket_np(495 - n)
    onehot = np.zeros((32, NLUT), np.float32)
    onehot[bucket, n] = 1.0
    shared = {
        "meta": f(inp["meta_tokens"]),
        "w_in_l": np.ascontiguousarray(w_in.reshape(8, 128, 2048).transpose(1, 0, 2)).reshape(1024, 2048),
        "w_o_l": np.ascontiguousarray(w_o.reshape(8, 128, 1024).transpose(1, 0, 2)).reshape(1024, 1024),
        "w_gate_l": np.ascontiguousarray(w_gate.reshape(8, 128, 11, 256).transpose(2, 1, 0, 3)).reshape(1408, 2048),
        "w_up_l": np.ascontiguousarray(w_up.reshape(8, 128, 11, 256).transpose(2, 1, 0, 3)).reshape(1408, 2048),
        "w_down_l": w_down,
        "pool_w_l": np.ascontiguousarray(f(inp["pool_w"])[0].transpose(1, 0, 2)).reshape(128, 512),
        "colpack": np.ascontiguousarray(np.concatenate([
            f(inp["norm1_g"])[0].reshape(8, 128).T,
            f(inp["norm2_g"])[0].reshape(8, 128).T,
            f(inp["pool_scale"])[0].reshape(4, 128).T,
            f(inp["subln_g"])[0].reshape(128, 1),
            np.zeros((128, 3), np.float32),
            np.broadcast_to(np.concatenate([rel_bias[15], rel_bias[31]])[None, :], (128, 8)),
        ], axis=1)),
        "fgb": np.ascontiguousarray(np.broadcast_to(f(inp["final_g"])[None, :], (128, D))),
        "lamv": np.ascontiguousarray(np.broadcast_to(np.concatenate(
            [f(inp["lambda_q1"])[0], f(inp["lambda_k1"])[0], f(inp["lambda_q2"])[0], f(inp["lambda_k2"])[0]])[None, :],
            (128, 256))),
        "relb": rel_bias,
        "onehot": onehot,
        "ident": np.eye(128, dtype=np.float32),
        "aident": np.ascontiguousarray(np.eye(128, dtype=np.float32)[::-1]),
    }
    return shared


def kernel(**inputs):
    x = np.ascontiguousarray(np.asarray(inputs["x"], dtype=np.float32))
    shared = _host_layouts(inputs)
    if "nc" not in _CACHE:
        _CACHE["nc"] = build_program()
    nc = _CACHE["nc"]
    in_maps = []
    for c in range(NCORES):
        m = dict(shared)
        m["x"] = x[2 * c:2 * c + 2]
        in_maps.append(m)
    res = run_bass_kernel_spmd(nc, in_maps, core_ids=list(range(NCORES)))
    out = np.concatenate([np.asarray(r["out"], dtype=np.float32) for r in res.results], axis=0)
    return out
```

```python
import math
import numpy as np
import ml_dtypes
import concourse.bass as bass
import concourse.mybir as mybir
from concourse.bass_utils import run_bass_kernel_spmd

F32 = mybir.dt.float32
BF16 = mybir.dt.bfloat16
AF = mybir.ActivationFunctionType
ALU = mybir.AluOpType

NCORES = 8
SEQ = 2048
D = 1024
NMETA = 16
LPOS = SEQ + NMETA
DFF = 2816
NKT = 17
QC = 256
NQC = SEQ // QC
TW = 768
NLUT = 896
LAMBDA_INIT = 0.8 - 0.6 * math.exp(-0.3 * 0)
NS_DMA = 8
FILLER = 0


class Buf:
    __slots__ = ("name", "psum", "last_w", "readers")

    def __init__(self, name, psum=False):
        self.name = name
        self.psum = psum
        self.last_w = None
        self.readers = {}


class Op:
    __slots__ = ("eng", "fn", "deps", "inc", "seq", "dma", "semkey", "val", "prev")

    def __init__(self, eng, fn, deps, dma):
        self.eng = eng
        self.fn = fn
        self.deps = deps
        self.inc = False
        self.seq = 0
        self.dma = dma
        self.semkey = None
        self.val = 0
        self.prev = None


class Tracker:
    ENGS = ("pe", "act", "dve", "pool", "sp")

    def __init__(self):
        self.ops = {e: [] for e in self.ENGS}
        self.dma_cnt = {"sp": 0, "pool": 0}
        self.dma_uses = {}
        self.dma_last = {}
        self.pending = {e: set() for e in self.ENGS}
        self.recent_dma = []
        self.all_dma = []

    def emit(self, eng, fn, reads=(), writes=(), dma=False):
        deps = set()
        for b in reads:
            if b.psum:
                if b.last_w is not None:
                    deps.add(b.last_w)
                deps.update(b.readers.values())
            elif b.last_w is not None:
                deps.add(b.last_w)
        for b in writes:
            if b.last_w is not None:
                deps.add(b.last_w)
            deps.update(b.readers.values())
        if self.pending[eng]:
            deps.update(self.pending[eng])
            self.pending[eng] = set()
        if eng == "pe":
            deps = {d for d in deps if not (d.eng == "pe" and not d.dma)}
        op = Op(eng, fn, deps, dma)
        if dma:
            k = self.dma_cnt[eng] % NS_DMA
            self.dma_cnt[eng] += 1
            key = (eng, k)
            n = self.dma_uses.get(key, 0) + 1
            self.dma_uses[key] = n
            op.semkey = key
            op.val = 16 * n
            op.prev = self.dma_last.get(key)
            self.dma_last[key] = op
            self.recent_dma.append(op)
            self.all_dma.append(op)
        self.ops[eng].append(op)
        for b in reads:
            if b.psum:
                b.last_w = op
                b.readers = {}
            else:
                b.readers[(eng, id(op)) if dma else eng] = op
        for b in writes:
            b.last_w = op
            b.readers = {}
        return op

    def barrier(self, exclude=()):
        deps = set(self.recent_dma) - set(exclude)
        self.recent_dma = []
        for e in self.ENGS:
            if self.ops[e]:
                last = self.ops[e][-1]
                if not last.dma:
                    deps.add(last)
                else:
                    for o in reversed(self.ops[e]):
                        if not o.dma:
                            deps.add(o)
                            break
        for e in self.ENGS:
            self.pending[e] = set(deps) | self.pending[e]

    def finalize(self):
        for e in self.ENGS:
            for op in self.ops[e]:
                for d in op.deps:
                    if not d.dma:
                        d.inc = True
        for e in self.ENGS:
            c = 0
            for op in self.ops[e]:
                if op.inc and not op.dma:
                    c += 1
                    op.seq = c

    def play(self, eng, handle, sems, dsems, final_wait=False):
        waited = {}

        def wait(key, semobj, val):
            if waited.get(key, 0) >= val:
                return
            handle.wait_ge(semobj, val)
            waited[key] = val

        for op in self.ops[eng]:
            for d in op.deps:
                if d.dma:
                    wait(d.semkey, dsems[d.semkey], d.val)
                else:
                    wait(d.eng, sems[d.eng], d.seq)
            if op.dma and op.prev is not None:
                wait(op.prev.semkey, dsems[op.prev.semkey], op.prev.val)
            inst = op.fn(handle)
            if op.dma:
                inst.then_inc(dsems[op.semkey], 16)
            elif op.inc:
                inst.then_inc(sems[eng], 1)
        if final_wait:
            for key, op in self.dma_last.items():
                wait(key, dsems[key], op.val)


def _t5_bucket_np(rel):
    rel = np.asarray(rel, dtype=np.int64)
    nb = 16
    ret = np.where(rel > 0, nb, 0)
    n = np.abs(rel)
    max_exact = 8
    nf = np.maximum(n, 1).astype(np.float32)
    large = max_exact + (np.log(nf / np.float32(max_exact)) / np.float32(math.log(128 / max_exact))
                         * np.float32(nb - max_exact)).astype(np.int32)
    large = np.minimum(large, nb - 1)
    return ret + np.where(n < max_exact, n, large)


def build_program():
    nc = bass.Bass("TRN2", target_bir_lowering=False)
    T = Tracker()

    def din(name, shape, dt=F32):
        return nc.dram_tensor(name, list(shape), dt, kind="ExternalInput")

    x_d = din("x", [2, SEQ, D])
    meta_d = din("meta", [NMETA, D])
    win_l = din("w_in_l", [1024, 2048])
    wo_l = din("w_o_l", [1024, 1024])
    wg_l = din("w_gate_l", [1408, 2048])
    wu_l = din("w_up_l", [1408, 2048])
    wd_l = din("w_down_l", [DFF, 1024])
    pw_l = din("pool_w_l", [128, 512])
    colpack_d = din("colpack", [128, 32])
    fgb_d = din("fgb", [128, D])
    lam_d = din("lamv", [128, 256])
    relb_d = din("relb", [32, 4])
    oh_d = din("onehot", [32, NLUT])
    ident_d = din("ident", [128, 128])
    aident_d = din("aident", [128, 128])
    out_d = nc.dram_tensor("out", [2, SEQ, D], F32, kind="ExternalOutput")

    win_b = nc.dram_tensor("win_b", [1024, 2048], BF16, kind="Internal")
    wo_b = nc.dram_tensor("wo_b", [1024, 1024], BF16, kind="Internal")
    wg_b = nc.dram_tensor("wg_b", [1408, 2048], BF16, kind="Internal")
    wu_b = nc.dram_tensor("wu_b", [1408, 2048], BF16, kind="Internal")
    wd_b = nc.dram_tensor("wd_b", [DFF, 1024], BF16, kind="Internal")
    lut_d = nc.dram_tensor("lut_d", [4, NLUT], F32, kind="Internal")

    ARENA = 131072
    ctxs = dict(
        arena=nc.sbuf_tensor("arena", [128, ARENA // 2], BF16),
        h1=nc.sbuf_tensor("h1", [128, 16 * D], F32),
        ident=nc.sbuf_tensor("ident_sb", [128, 128], F32),
        gbc=nc.sbuf_tensor("gbc", [128, D], F32),
        fgb=nc.sbuf_tensor("fgb_sb", [128, D], F32),
        pw=nc.sbuf_tensor("pw_sb", [128, 512], BF16),
        identb=nc.sbuf_tensor("identb_sb", [128, 128], BF16),
        jb=nc.sbuf_tensor("jb_sb", [128, 128], BF16),
        cols=nc.sbuf_tensor("cols", [128, 64], F32),
        lamv=nc.sbuf_tensor("lamv_sb", [128, 256], F32),
        sm=nc.sbuf_tensor("sm", [128, 128], F32),
        relb=nc.sbuf_tensor("relb_sb", [32, 4], F32),
        oh=nc.sbuf_tensor("oh_sb", [32, NLUT], F32),
        ps=nc.psum_tensor("ps", [128, 8 * 512], F32),
    )
    sem_names = ["pe", "act", "dve", "pool", "sp"]
    from contextlib import ExitStack
    with ExitStack() as es:
        tens = {k: es.enter_context(v) for k, v in ctxs.items()}
        sems = {e: es.enter_context(nc.semaphore("s_" + e)) for e in sem_names}
        dsems = {}
        for e in ("sp", "pool"):
            for k in range(NS_DMA):
                dsems[(e, k)] = es.enter_context(nc.semaphore(f"d_{e}{k}"))

        arena_bf = tens["arena"]
        arena_f = arena_bf.bitcast(F32)
        h1 = tens["h1"]
        ident = tens["ident"]
        gbc = tens["gbc"]
        fgb = tens["fgb"]
        pw = tens["pw"]
        cols = tens["cols"]
        lamv = tens["lamv"]
        sm = tens["sm"]
        relb = tens["relb"]
        ohsb = tens["oh"]
        ps = tens["ps"]
        psb = ps.bitcast(BF16)
        identb = tens["identb"]
        jb = tens["jb"]

        def bfv(off, n):
            return arena_bf[:, off // 2: off // 2 + n]

        def f32v(off, n):
            return arena_f[:, off // 4: off // 4 + n]

        def bank(i):
            return ps[:, i * 512:(i + 1) * 512]

        def bankb(i):
            return psb[:, i * 1024:(i + 1) * 1024]

        g1col = cols[:, 0:8]
        g2col = cols[:, 8:16]
        pscol = cols[:, 16:20]
        sgcol = cols[:, 20:21]
        sg08 = cols[:, 21:22]
        farb = cols[:, 24:32]
        neglam = cols[:, 32:33]
        eps6 = cols[:, 33:34]
        eps5 = cols[:, 34:35]
        lam_s = cols[:, 36:38]
        lam_e = cols[:, 38:40]
        ssA = sm[:, 0:2]
        lnA = sm[:, 2:4]
        rsA = sm[:, 4:6]
        ss2 = sm[:, 8:24]
        ln2 = sm[:, 24:32]
        rs2 = sm[:, 32:40]
        ss3 = sm[:, 40:48]
        ln3 = sm[:, 48:56]
        rs3 = sm[:, 56:64]
        rz = sm[:, 64:72]
        rz1n = sm[:, 72:76]
        ssb = sm[:, 76:80]
        lnb = sm[:, 80:84]
        rsb = sm[:, 84:88]

        B = lambda name, psum=False: Buf(name, psum)
        b_banks = [B(f"bank{i}", True) for i in range(8)]
        b_h1 = [B(f"h1_{i}") for i in range(16)]
        b_ident = B("ident")
        b_gbc = B("gbc")
        b_fgb = B("fgb")
        b_pw = B("pw")
        b_cols = B("cols")
        b_lamv = B("lamv")
        b_lams = B("lams")
        b_relb = B("relb")
        b_oh = B("oh")
        b_winb, b_wob, b_wgb, b_wub, b_wdb = B("winb"), B("wob"), B("wgb"), B("wub"), B("wdb")
        b_lutd = B("lutd")
        b_ss2 = [B(f"ss2_{i}") for i in range(16)]

        emit = T.emit

        def dma_sp(out, in_, reads, writes):
            return emit("sp", lambda e, o=out, i=in_: e.dma_start(out=o, in_=i), reads, writes, dma=True)

        def dma_pool(out, in_, reads, writes):
            return emit("pool", lambda e, o=out, i=in_: e.dma_start(out=o, in_=i), reads, writes, dma=True)

        def mm(out, lhsT, rhs, start, stop, reads, writes, skip=False):
            return emit("pe", lambda e, o=out, l=lhsT, r=rhs, s=start, t=stop, k=skip:
                        e.matmul(o, lhsT=l, rhs=r, start=s, stop=t, skip_group_check=k), reads, writes)

        def tr(out, in_, npart, reads, writes, bf=False):
            idn = (identb if bf else ident)[0:npart, 0:npart]
            return emit("pe", lambda e, o=out, i=in_, d=idn: e.transpose(o, i, d), list(reads) + [b_ident], writes)

        def act(out, in_, func, reads, writes, bias=None, scale=None, accum=None):
            def fn(e, o=out, i=in_, f=func, b=bias, s=scale, a=accum):
                kw = {}
                if b is not None:
                    kw["bias"] = b
                if s is not None:
                    kw["scale"] = s
                if a is not None:
                    kw["accum_out"] = a
                return e.activation(out=o, in_=i, func=f, **kw)
            return emit("act", fn, reads, writes)

        def dve(fn, reads, writes):
            return emit("dve", fn, reads, writes)

        def pool(fn, reads, writes):
            return emit("pool", fn, reads, writes)

        dma_sp(cols[:, 0:32], colpack_d.ap(), [], [b_cols])
        dma_sp(ident[:], ident_d.ap(), [], [b_ident])
        dma_sp(lamv[:], lam_d.ap(), [], [b_lamv])
        dma_sp(relb[:], relb_d.ap(), [], [b_relb])
        dma_sp(ohsb[:], oh_d.ap(), [], [b_oh])
        win0 = bfv(67600, 8 * 2048).rearrange("p (c n) -> p c n", c=8)
        b_win0 = B("win0")
        b_stg = [B(f"wstg{c}") for c in range(8)]
        for c in range(8):
            dma_sp(h1[:, c * 2048:(c + 1) * 2048],
                   win_l.ap().rearrange("(p a) n -> p a n", p=128)[:, c, :], [], [b_stg[c]])
        dma_sp(fgb[:], fgb_d.ap(), [], [b_fgb])
        b_win0c = [B(f"win0_{c}") for c in range(8)]
        for c in range(8):
            dve(lambda e, c=c: e.tensor_copy(out=win0[:, c, :], in_=h1[:, c * 2048:(c + 1) * 2048]),
                [b_stg[c]], [b_win0c[c]])
        early_ex = [dma_sp(win_b.ap().rearrange("(p a) n -> p (a n)", p=128), win0.rearrange("p c n -> p (c n)"),
                           b_win0c, [b_winb])]

        dve(lambda e: e.tensor_copy(out=identb[:], in_=ident[:]), [b_ident], [b_ident])
        b_jb = B("jb")
        jstage = f32v(8192, 128)
        b_jst = B("jstage")
        dma_sp(jstage, aident_d.ap(), [], [b_jst])
        dve(lambda e: e.tensor_copy(out=jb[:], in_=jstage), [b_jst], [b_jb])
        dve(lambda e: e.memset(cols[:, 33:34], 1e-6), [], [b_cols])
        dve(lambda e: e.memset(cols[:, 34:35], 1e-5), [], [b_cols])
        dve(lambda e: e.tensor_scalar(out=sg08, in0=sgcol, scalar1=float(1.0 - LAMBDA_INIT), scalar2=None,
                                      op0=ALU.mult), [b_cols], [b_cols])
        negbb = cols[:, 40:44]
        ea = cols[:, 44:48]
        dve(lambda e: e.tensor_scalar(out=negbb, in0=farb[:, 0:4], scalar1=-1.0, scalar2=None, op0=ALU.mult),
            [b_cols], [b_cols])
        dve(lambda e: e.tensor_tensor(out=ea, in0=farb[:, 4:8], in1=farb[:, 0:4], op=ALU.subtract),
            [b_cols], [b_cols])
        b_lamp = B("lamp")
        lamp = f32v(0, 128)
        lamj = bfv(1024, 128)
        dve(lambda e: e.tensor_tensor(out=lamp[:, 0:64], in0=lamv[:, 0:64], in1=lamv[:, 64:128], op=ALU.mult),
            [b_lamv], [b_lamp])
        dve(lambda e: e.tensor_tensor(out=lamp[:, 64:128], in0=lamv[:, 128:192], in1=lamv[:, 192:256], op=ALU.mult),
            [b_lamv], [b_lamp])
        b_lamj = B("lamj")
        act(lamj[:, 0:64], lamp[:, 0:64], AF.Copy, [b_lamp], [b_lamj, b_lams], accum=lam_s[:, 0:1])
        act(lamj[:, 0:64], lamp[:, 64:128], AF.Copy, [b_lamp], [b_lamj, b_lams], accum=lam_s[:, 1:2])
        act(lam_e, lam_s, AF.Exp, [b_lams], [b_lams])
        dve(lambda e: e.tensor_tensor(out=neglam, in0=lam_e[:, 1:2], in1=lam_e[:, 0:1], op=ALU.subtract),
            [b_lams], [b_cols])
        dve(lambda e: e.tensor_scalar(out=neglam, in0=neglam, scalar1=float(-LAMBDA_INIT), scalar2=None,
                                      op0=ALU.add), [b_cols], [b_cols])
        lutsb = f32v(4096, NLUT)
        b_lutsb = B("lutsb")
        for half in range(2):
            mm(bank(0)[0:4, 0:448], relb[:, :], ohsb[:, half * 448:(half + 1) * 448], True, True,
               [b_relb, b_oh], [b_banks[0]])
            dve(lambda e, hf=half: e.tensor_copy(out=lutsb[0:4, hf * 448:(hf + 1) * 448], in_=bank(0)[0:4, 0:448]),
                [b_banks[0]], [b_lutsb])
        dma_sp(lut_d.ap(), lutsb[0:4, :], [b_lutsb], [b_lutd])
        T.barrier(exclude=early_ex)

        O_QT, O_KT, O_U, O_VX, O_X = 0, 16640, 33280, 49920, 67600
        PW_ = 2080
        QT = bfv(O_QT, 4 * PW_).rearrange("p (h n) -> p h n", h=4)
        KT = bfv(O_KT, 4 * PW_).rearrange("p (h n) -> p h n", h=4)
        UU = bfv(O_U, 4 * PW_).rearrange("p (h n) -> p h n", h=4)
        VX = bfv(O_VX, NKT * 4 * 130).rearrange("p (k h n) -> p k h n", k=NKT, h=4)

        def gain_tile(gcol, ones_ap, b_ones):
            for c in range(8):
                dve(lambda e, c=c: e.tensor_scalar(out=gbc[:, c * 128:(c + 1) * 128], in0=ones_ap,
                                                   scalar1=gcol[:, c:c + 1], scalar2=None, op0=ALU.mult),
                    [b_cols, b_ones], [b_gbc])

        for s in range(2):
            b_QT = [B(f"QT{i}") for i in range(5)]
            b_KT = [B(f"KT{i}") for i in range(NKT)]
            b_U = [B(f"U{i}") for i in range(5)]
            b_Upad = B("Upad")
            b_VX = [B(f"VX{i}") for i in range(NKT)]
            win = bfv(O_X, 8 * 2048).rearrange("p (c n) -> p c n", c=8)
            uT = [bfv(O_X + 32768 + i * 8192, 8 * 512).rearrange("p (c n) -> p c n", c=8) for i in range(2)]
            xs = [f32v(O_X + 49152 + i * 4096, 1024) for i in range(2)]
            xb = [bfv(O_X + 57344 + i * 2048, 1024) for i in range(2)]
            b_win = B("win")
            b_uT = [B("uT0"), B("uT1")]
            b_xs = [B("xs0"), B("xs1")]
            b_xb = [B("xb0"), B("xb1")]
            b_ssA = [B("ssA0"), B("ssA1")]

            rd_win = b_win0c if s == 0 else [b_win]
            if s == 0:
                pass
            else:
                dma_sp(win.rearrange("p c n -> p (c n)"), win_b.ap().rearrange("(p a) n -> p (a n)", p=128),
                       [b_winb], [b_win])
            dve(lambda e: e.memset(xs[1][:, 0:128], 1.0), [], [b_xs[1]])
            gain_tile(g1col, xs[1][:, 0:128], b_xs[1])
            pool(lambda e: e.memset(UU[:, :, 2064:2080], 0.0), [], [b_Upad])
            pool(lambda e: e.memset(VX[:, :, :, 128:130], 1.0), [], b_VX)

            def load_tile(ti, slot):
                np_ = NMETA if ti < 0 else 128
                src = meta_d.ap() if ti < 0 else x_d.ap()[s, ti * 128:(ti + 1) * 128, :]
                dma_sp(xs[slot][0:np_, :], src, [], [b_xs[slot]])

            def norm_tile(ti, slot):
                np_ = NMETA if ti < 0 else 128
                xt = xs[slot]
                xbt = xb[slot]
                act(xbt[0:np_, :], xt[0:np_, :], AF.Square, [b_xs[slot]], [b_xb[slot], b_ssA[slot]],
                    accum=ssA[0:np_, slot:slot + 1])
                act(lnA[0:np_, slot:slot + 1], ssA[0:np_, slot:slot + 1], AF.Ln, [b_ssA[slot], b_cols], [b_ssA[slot]],
                    bias=eps6[0:np_, :], scale=1.0 / D)
                act(rsA[0:np_, slot:slot + 1], lnA[0:np_, slot:slot + 1], AF.Exp, [b_ssA[slot]], [b_ssA[slot]],
                    scale=-0.5)
                act(xbt[0:np_, :], xt[0:np_, :], AF.Copy, [b_ssA[slot], b_xs[slot]], [b_xb[slot]],
                    scale=rsA[0:np_, slot:slot + 1])
                pb = slot
                for c in range(8):
                    tr(bankb(pb)[:, c * 128:c * 128 + np_], xbt[0:np_, c * 128:(c + 1) * 128], np_,
                       [b_xb[slot]], [b_banks[pb]], bf=True)
                p0 = 0 if ti < 0 else NMETA + ti * 128
                a = p0
                while a < p0 + np_:
                    pc = a // 512
                    bnd = min(p0 + np_, (pc + 1) * 512)
                    n = bnd - a
                    off = a - p0
                    src_ps = bankb(pb).rearrange("p (c n) -> p c n", c=8)[:, :, off:off + n]
                    dst = uT[pc % 2][:, :, a - pc * 512:a - pc * 512 + n]
                    g = gbc[:, :].rearrange("p (c n) -> p c n", c=8)[:, :, 0:n]
                    dve(lambda e, o=dst, i=src_ps, g=g: e.tensor_tensor(out=o, in0=i, in1=g, op=ALU.mult),
                        [b_banks[pb], b_gbc], [b_uT[pc % 2]])
                    a = bnd

            kindc = [0]

            def inproj_groups(pc):
                n = 512 if pc < 4 else 16
                u = uT[pc % 2]
                bu = b_uT[pc % 2]
                pos0 = pc * 512
                groups = []
                for grp in range(3):
                    for h in range(4):
                        def g_(grp=grp, h=h):
                            bi = 2 + (kindc[0] % 6)
                            kindc[0] += 1
                            colbase = {0: 512, 1: 1024, 2: 0}[grp] + h * 128
                            for c in range(8):
                                mm(bank(bi)[:, 0:n], win[:, c, colbase:colbase + 128], u[:, c, 0:n], c == 0, c == 7,
                                   rd_win + [bu], [b_banks[bi]])
                            if grp == 0:
                                dve(lambda e, o=QT[:, h, pos0:pos0 + n], i=bank(bi)[:, 0:n]:
                                    e.tensor_scalar(out=o, in0=i, scalar1=0.125, scalar2=None, op0=ALU.mult),
                                    [b_banks[bi]], [b_QT[pc]])
                            elif grp == 1:
                                wr = [b_KT[k] for k in range(pos0 // 128, (pos0 + n - 1) // 128 + 1)]
                                dve(lambda e, o=KT[:, h, pos0:pos0 + n], i=bank(bi)[:, 0:n]:
                                    e.tensor_copy(out=o, in_=i), [b_banks[bi]], wr)
                            else:
                                act(UU[:, h, pos0:pos0 + n], bank(bi)[:, 0:n], AF.Copy, [b_banks[bi]], [b_U[pc]])
                        groups.append(g_)
                nt = (n + 127) // 128
                for kk in range(nt):
                    def gv(kk=kk):
                        m = min(128, n - kk * 128)
                        kt = pos0 // 128 + kk
                        bi = 2 + (kindc[0] % 6)
                        kindc[0] += 1
                        for c in range(8):
                            mm(bank(bi)[0:m, :], u[:, c, kk * 128:kk * 128 + m], win[:, c, 1536:2048], c == 0, c == 7,
                               rd_win + [bu], [b_banks[bi]])
                        src_ps = bank(bi)[0:m, :].rearrange("p (h n) -> p h n", h=4)
                        if kk % 2 == 0:
                            dve(lambda e, o=VX[0:m, kt, :, 0:128], sp_=src_ps: e.tensor_copy(out=o, in_=sp_),
                                [b_banks[bi]], [b_VX[kt]])
                        else:
                            act(VX[0:m, kt, :, 0:128], src_ps, AF.Copy, [b_banks[bi]], [b_VX[kt]])
                    groups.append(gv)
                return groups

            tidx = 0
            order = [-1] + list(range(16))
            load_tile(order[0], 0)
            load_tile(order[1], 1)
            _nt = norm_tile

            def norm_tile(ti, slot):
                _nt(ti, slot)
                k = order.index(ti)
                if k + 2 < len(order):
                    load_tile(order[k + 2], k % 2)
            for ti in order[0:5]:
                norm_tile(ti, tidx % 2)
                tidx += 1
            nxt_tile = 5
            if s == 0:
                gate = [b_uT[0], b_uT[1]]
                dma_pool(wo_b.ap(), wo_l.ap(), gate, [b_wob])
                dma_pool(pw[:], pw_l.ap(), gate, [b_pw])
            for pc in range(5):
                groups = inproj_groups(pc)
                ng = len(groups)
                for gi, g_ in enumerate(groups):
                    g_()
                    if pc < 3 and gi % 4 == 3 and nxt_tile < len(order) and nxt_tile < 5 + 4 * (pc + 1):
                        norm_tile(order[nxt_tile], tidx % 2)
                        tidx += 1
                        nxt_tile += 1
                while pc < 3 and nxt_tile < 5 + 4 * (pc + 1):
                    norm_tile(order[nxt_tile], tidx % 2)
                    tidx += 1
                    nxt_tile += 1
            T.barrier()

            from collections import deque
            wo = bfv(O_X, 8 * 1024).rearrange("p (c n) -> p c n", c=8)
            Tst = f32v(O_X + 16384, 4 * TW).rearrange("p (h n) -> p h n", h=4)
            o2 = O_X + 16384 + 12288
            NE = 4
            Et = [bfv(o2 + i * 1024, 512) for i in range(NE)]
            o2 += NE * 1024
            Tb = bfv(o2, 4 * TW).rearrange("p (h n) -> p h n", h=4)
            o2 += 4 * TW * 2
            yT_off = o2
            yT = [bfv(o2 + i * 4096, 8 * QC).rearrange("p (c n) -> p c n", c=8) for i in range(2)]
            o2 += 8192
            ot = [f32v(o2 + i * 1024, 256).rearrange("p (q n) -> p q n", q=2) for i in range(2)]
            o2 += 2048
            yt = [f32v(o2 + i * 1024, 256).rearrange("p (q n) -> p q n", q=2) for i in range(2)]
            ytb_all = [bfv(o2 + i * 1024, 256).rearrange("p (q n) -> p q n", q=2) for i in range(2)]
            o2 += 2048
            tA = f32v(o2, 272)
            tB = f32v(o2 + 1088, 272)
            o2 += 2176
            dT = [bfv(o2 + i * 2048, 4 * QC).rearrange("p (g n) -> p g n", g=4) for i in range(2)]
            o2 += 4096
            junkB = bfv(o2, 1024)
            o2 += 2048
            Qb = [bfv(o2 + i * 1024, 512) for i in range(2)]
            o2 += 2048
            assert o2 <= ARENA, o2
            b_wo = B("wo")
            b_E = [B(f"E{i}") for i in range(NE)]
            b_sb = [B(f"sb{i}") for i in range(2)]
            b_yT = [B(f"yT{i}") for i in range(2)]
            b_ot = [B(f"ot{i}") for i in range(2)]
            b_yt = [B(f"yt{i}") for i in range(2)]
            b_tA, b_tB = B("tA"), B("tB")
            b_dT = [B(f"dT{i}") for i in range(2)]
            b_junkB = B("junkB")
            b_rz = [B("rz0"), B("rz1")]
            b_ssb = [B("ssb0"), B("ssb1")]
            b_Qb = [B("Qb0"), B("Qb1")]

            dma_sp(wo.rearrange("p c n -> p (c n)"), wo_b.ap().rearrange("(p a) n -> p (a n)", p=128),
                   [b_wob], [b_wo])
            for i in range(2):
                pool(lambda e, i=i: e.memset(Qb[i][:, :], 0.0), [], [b_Qb[i]])
            b_Tst = [B(f"Tst{h}") for h in range(4)]
            b_Tb = B("Tb")
            Hb = bfv(yT_off, 4 * TW)
            for h in range(4):
                src = bass.AP(lut_d, h * NLUT, [[1, 128], [1, TW]])
                dma_sp(Tst[:, h, :], src, [b_lutd], [b_Tst[h]])
                act(Hb[:, h * TW:(h + 1) * TW], Tst[:, h, :], AF.Exp, [b_Tst[h], b_cols], b_yT, bias=negbb[:, h:h + 1])
            tb_banks = [7, 3, 4, 5]
            k = 0
            for h in range(4):
                for half in range(2):
                    bk = tb_banks[k % 4]
                    k += 1
                    mm(bank(bk)[:, 0:384], jb[:, :], Hb[:, h * TW + half * 384:h * TW + (half + 1) * 384], True, True,
                       b_yT + [b_jb], [b_banks[bk]])
                    dve(lambda e, o=Tb[:, h, half * 384:(half + 1) * 384], i=bank(bk)[:, 0:384]:
                        e.tensor_copy(out=o, in_=i), [b_banks[bk]], [b_Tb])

            import heapq
            deferred = []
            dseq = [0]

            def defer(at, fn):
                dseq[0] += 1
                heapq.heappush(deferred, (at, dseq[0], fn))

            def run_deferred(cur, nmax):
                k = 0
                while deferred and k < nmax and deferred[0][0] <= cur:
                    heapq.heappop(deferred)[2]()
                    k += 1

            def pool_stage(cq, g, w):
                q0 = NMETA + cq * QC
                ysl = cq % 2

                def zz(lo, n):
                    return UU[:, g, q0 + lo:q0 + lo + n]
                rdU = [b_U[min(4, (q0 - 8) // 512)], b_U[min(4, (q0 + 263) // 512)], b_Upad]

                def padd(o, a, b, reads, writes):
                    pool(lambda e, o=o, a=a, b=b: e.tensor_tensor(out=o, in0=a, in1=b, op=ALU.add), reads, writes)
                if w == 2:
                    padd(tA[:, 0:256], zz(0, 256), zz(-1, 256), rdU, [b_tA])
                    ws, wb, wsl = tA, b_tA, 0
                else:
                    padd(tA[:, 1:272], zz(-7, 271), zz(-8, 271), rdU, [b_tA])
                    padd(tB[:, 2:271], tA[:, 3:272], tA[:, 1:270], [b_tA], [b_tB])
                    ws, wb, wsl = tB, b_tB, 8
                    if w >= 8:
                        padd(tA[:, 4:269], tB[:, 6:271], tB[:, 2:267], [b_tB], [b_tA])
                        ws, wb = tA, b_tA
                    if w == 16:
                        padd(tB[:, 8:265], tA[:, 12:269], tA[:, 4:261], [b_tA], [b_tB])
                        ws, wb = tB, b_tB
                def dpart():
                    dve(lambda e, o=dT[ysl][:, g, :], a=ws[:, wsl:wsl + 256], sc_=1.0 / w, b=zz(0, 256):
                        e.scalar_tensor_tensor(out=o, in0=a, scalar=sc_, in1=b, op0=ALU.mult, op1=ALU.subtract),
                        [wb] + rdU, [b_dT[ysl]])
                    if cq == NQC - 1:
                        right = w - 1 - w // 2
                        for r in range(right):
                            cnt = w - (right - r)
                            col = 255 - r
                            dve(lambda e, o=dT[ysl][:, g, col:col + 1], a=ws[:, wsl + col:wsl + col + 1], sc_=1.0 / cnt,
                                b=zz(col, 1):
                                e.scalar_tensor_tensor(out=o, in0=a, scalar=sc_, in1=b, op0=ALU.mult, op1=ALU.subtract),
                                [wb] + rdU, [b_dT[ysl]])
                return dpart

            def poolmm_stage(cq, g):
                ysl = cq % 2
                mb = 7
                mm(bank(mb)[:, 0:QC], pw[:, g * 128:(g + 1) * 128], dT[ysl][:, g, :], True, True,
                   [b_pw, b_dT[ysl]], [b_banks[mb]])
                dve(lambda e, o=yT[ysl][:, g, :], i=bank(mb)[:, 0:QC], sc_=pscol[:, g:g + 1]:
                    e.tensor_scalar(out=o, in0=i, scalar1=sc_, scalar2=None, op0=ALU.mult),
                    [b_banks[mb], b_cols], [b_yT[ysl]])

            def epilogue_stages(cq, h, osl):
                ysl = cq % 2
                ba, bb = 3 + 2 * osl, 4 + 2 * osl
                rzs = rz[:, osl * 4:(osl + 1) * 4]
                r1n = rz1n[:, osl * 2:(osl + 1) * 2]
                sss = ssb[:, osl * 2:(osl + 1) * 2]
                lns = lnb[:, osl * 2:(osl + 1) * 2]
                rss = rsb[:, osl * 2:(osl + 1) * 2]
                mb = 7

                def st1():
                    for c, bk in ((0, ba), (1, bb)):
                        zsrc = bank(bk)[:, 0:258].rearrange("p (q n) -> p q n", n=129)[:, :, 128:129]
                        zdst = rzs[:, c * 2:(c + 1) * 2].rearrange("p (q o) -> p q o", o=1)
                        dve(lambda e, o=zdst, i=zsrc: e.reciprocal(out=o, in_=i), [b_banks[bk]], [b_rz[osl]])
                    dve(lambda e, o=r1n, i=rzs[:, 2:4]: e.tensor_scalar(out=o, in0=i, scalar1=neglam, scalar2=None,
                                                                         op0=ALU.mult), [b_rz[osl], b_cols], [b_rz[osl]])

                def st2():
                    for qs in range(2):
                        dve(lambda e, o=ot[osl][:, qs, :], i=bank(ba)[:, qs * 129:qs * 129 + 128], sc_=rzs[:, qs:qs + 1]:
                            e.tensor_scalar(out=o, in0=i, scalar1=sc_, scalar2=None, op0=ALU.mult),
                            [b_banks[ba], b_rz[osl]], [b_ot[osl]])

                def st3():
                    for qs in range(2):
                        dve(lambda e, o=ot[osl][:, qs, :], i=bank(bb)[:, qs * 129:qs * 129 + 128], sc_=r1n[:, qs:qs + 1]:
                            e.scalar_tensor_tensor(out=o, in0=i, scalar=sc_, in1=o, op0=ALU.mult, op1=ALU.add),
                            [b_banks[bb], b_rz[osl], b_ot[osl]], [b_ot[osl]])

                def st4():
                    for qs in range(2):
                        dve(lambda e, o=junkB[:, 0:128], i=ot[osl][:, qs, :], a=sss[:, qs:qs + 1]:
                            e.scalar_tensor_tensor(out=o, in0=i, scalar=1.0, in1=i, op0=ALU.mult, op1=ALU.mult,
                                                   accum_out=a),
                            [b_ot[osl]], [b_junkB, b_ssb[osl]])

                def st5():
                    act(lns, sss, AF.Ln, [b_ssb[osl], b_cols], [b_ssb[osl]], bias=eps5, scale=1.0 / 128)
                    act(rss, lns, AF.Exp, [b_ssb[osl]], [b_ssb[osl]], scale=-0.5)

                ytb = ytb_all[osl]

                def st6():
                    for qs in range(2):
                        dve(lambda e, o=ytb[:, qs, :], i=ot[osl][:, qs, :], sc_=rss[:, qs:qs + 1]:
                            e.tensor_scalar(out=o, in0=i, scalar1=sc_, scalar2=None, op0=ALU.mult),
                            [b_ot[osl], b_ssb[osl]], [b_yt[osl]])

                def st7():
                    for qs in range(2):
                        tr(bankb(mb)[:, qs * 128:(qs + 1) * 128], ytb[:, qs, :], 128, [b_yt[osl]], [b_banks[mb]], bf=True)

                def st8():
                    dve(lambda e, o=yT[ysl][:, 4 + h, :], i=bankb(mb)[:, 0:QC]:
                        e.tensor_scalar(out=o, in0=i, scalar1=sg08, scalar2=None, op0=ALU.mult),
                        [b_banks[mb], b_cols], [b_yT[ysl]])
                def st78():
                    st7()
                    st8()
                return [(0, st1), (2, st2), (4, st3), (6, st4), (11, st5), (15, st6), (19, st78)]

            def wo_stages(cq):
                ysl = cq % 2
                sts = []
                k = 0
                for t2 in range(2):
                    tile_i = cq * 2 + t2
                    for half in range(4):
                        def st(t2=t2, half=half, tile_i=tile_i):
                            mb = 7
                            for c in range(8):
                                mm(bank(mb)[:, 0:256], yT[ysl][:, c, t2 * 128:(t2 + 1) * 128],
                                   wo[:, c, half * 256:(half + 1) * 256],
                                   c == 0, c == 7, [b_yT[ysl], b_wo], [b_banks[mb]])
                            hsl = h1[:, tile_i * D + half * 256:tile_i * D + (half + 1) * 256]
                            dve(lambda e, hsl=hsl, mb=mb: e.tensor_tensor(out=hsl, in0=bank(mb)[:, 0:256], in1=hsl, op=ALU.add),
                                [b_banks[mb], b_h1[tile_i]], [b_h1[tile_i]])
                        sts.append((24 + 2 * k, st))
                        k += 1

                    def stq(tile_i=tile_i):
                        dve(lambda e, o=junkB[:, :], i=h1[:, tile_i * D:(tile_i + 1) * D], a=ss2[:, tile_i:tile_i + 1]:
                            e.scalar_tensor_tensor(out=o, in0=i, scalar=1.0, in1=i, op0=ALU.mult, op1=ALU.mult,
                                                   accum_out=a),
                            [b_h1[tile_i]], [b_junkB, b_ss2[tile_i]])
                    sts.append((25 + 2 * (k - 1), stq))
                return sts

            unit = 0

            def av(cqh, j, esl, kn, osl, h):
                ba, bb = 3 + 2 * osl, 4 + 2 * osl
                for c, bk in ((0, ba), (1, bb)):
                    for qs in range(2):
                        mm(bank(bk)[:, qs * 129:(qs + 1) * 129],
                           Et[esl][0:kn, c * QC + qs * 128:c * QC + (qs + 1) * 128],
                           VX[0:kn, j, h, 0:129], (j == 0 and qs == 0), j == NKT - 1,
                           [b_E[esl], b_VX[j]], [b_banks[bk]], skip=True)
                if FILLER:
                    mm(bank(bb)[:, 258:258 + FILLER], Et[esl][0:kn, QC + 128:QC + 256], Qb[osl][0:kn, 0:FILLER],
                       False, False, [b_E[esl], b_Qb[osl]], [b_banks[bb]], skip=True)
                if j == NKT - 1:
                    for off, st in epilogue_stages(cqh, h, osl):
                        defer(unit + off, st)
                    if h == 3:
                        for off, st in wo_stages(cqh):
                            defer(unit + off, st)

            def emit_qb(cq_, h_, slot):
                q0_ = NMETA + cq_ * QC
                for c in range(2):
                    pool(lambda e, o=Qb[slot][c * 64:(c + 1) * 64, c * QC:(c + 1) * QC],
                         i=QT[c * 64:(c + 1) * 64, h_, q0_:q0_ + QC]: e.tensor_copy(out=o, in_=i),
                         [b_QT[min(4, q0_ // 512)], b_QT[min(4, (q0_ + QC - 1) // 512)]], [b_Qb[slot]])

            units = []
            hq_ = 0
            for cq in range(NQC):
                for h in range(4):
                    for j in range(NKT):
                        units.append(dict(cq=cq, h=h, j=j, sl=hq_ % 2, kn=128 if j < NKT - 1 else LPOS - 128 * (NKT - 1)))
                    hq_ += 1
            NU = len(units)

            def emit_qk(u):
                U_ = units[u]
                cq_, h_, j_, kn_, sl_ = U_["cq"], U_["h"], U_["j"], U_["kn"], U_["sl"]
                ssl_ = u % 3
                k0_ = j_ * 128
                mm(bank(ssl_)[0:kn_, :], KT[:, h_, k0_:k0_ + kn_], Qb[sl_][:, :], True, True,
                   [b_KT[j_], b_Qb[sl_]], [b_banks[ssl_]])

            emit_qb(0, 0, 0)
            emit_qk(0)
            emit_qk(1)
            cast_pieces = []
            if s == 0:
                for r in range(11):
                    cast_pieces.append((wg_b.ap()[r * 128:(r + 1) * 128, :], wg_l.ap()[r * 128:(r + 1) * 128, :], b_wgb))
                    cast_pieces.append((wu_b.ap()[r * 128:(r + 1) * 128, :], wu_l.ap()[r * 128:(r + 1) * 128, :], b_wub))
                for r in range(11):
                    cast_pieces.append((wd_b.ap()[r * 256:(r + 1) * 256, :], wd_l.ap()[r * 256:(r + 1) * 256, :], b_wdb))
            for u in range(NU):
                U_ = units[u]
                cq, h, j, kn, sl = U_["cq"], U_["h"], U_["j"], U_["kn"], U_["sl"]
                q0 = NMETA + cq * QC
                unit = u + 1
                if h == 0 and j == 0:
                    for t2 in range(2):
                        tile_i = cq * 2 + t2
                        dma_sp(h1[:, tile_i * D:(tile_i + 1) * D], x_d.ap()[s, tile_i * 128:(tile_i + 1) * 128, :],
                               [], [b_h1[tile_i]])
                    for g, w in enumerate((2, 4, 8, 16)):
                        def pst(cq=cq, g=g, w=w, at=unit + 1 + 8 * g + 6):
                            dpart = pool_stage(cq, g, w)
                            defer(at, dpart)
                        defer(unit + 1 + 8 * g, pst)
                if h == 2 and j == 0:
                    for g in range(4):
                        defer(unit + 3 * g, lambda cq=cq, g=g: poolmm_stage(cq, g))
                ssl = u % 3
                esl = u % NE
                k0 = j * 128
                Dd = k0 - q0
                s_ps = bank(ssl)[0:kn, :]
                e_out = Et[esl][0:kn, :]
                maxrel = Dd + kn - 1
                minrel = Dd - (QC - 1)
                if minrel >= 128:
                    act(e_out, s_ps, AF.Exp, [b_banks[ssl], b_cols], [b_E[esl]], bias=ea[0:kn, h:h + 1])
                else:
                    act(e_out, s_ps, AF.Exp, [b_banks[ssl]], [b_E[esl]])
                if maxrel <= -128 or minrel >= 128:
                    pass
                else:
                    i0 = 368 - Dd
                    assert 0 <= i0 <= TW - QC, (i0, Dd)
                    tb0 = Tb[0:kn, h, i0:i0 + QC]
                    tbb = bass.AP(tb0.tensor, tb0.offset, [list(tb0.ap[0]), [0, 2], list(tb0.ap[1])])
                    e3 = Et[esl][0:kn, :].rearrange("p (c n) -> p c n", c=2)
                    dve(lambda e, o=e3, b=tbb: e.tensor_tensor(out=o, in0=o, in1=b, op=ALU.mult),
                        [b_Tb, b_E[esl]], [b_E[esl]])
                if j == 4 and cast_pieces:
                    o_, i_, bb_ = cast_pieces.pop(0)
                    dma_pool(o_, i_, [], [bb_])
                if j == 8 and u + 9 < NU:
                    nU = units[u + 9]
                    emit_qb(nU["cq"], nU["h"], nU["sl"])
                if u + 2 < NU:
                    emit_qk(u + 2)
                if u >= 1:
                    pU = units[u - 1]
                    av(pU["cq"], pU["j"], (u - 1) % NE, pU["kn"], pU["sl"], pU["h"])
                run_deferred(unit, 1)
            pU = units[NU - 1]
            unit = NU + 1
            av(pU["cq"], pU["j"], (NU - 1) % NE, pU["kn"], pU["sl"], pU["h"])
            while deferred:
                run_deferred(10 ** 9, 10 ** 6)
            while cast_pieces:
                o_, i_, bb_ = cast_pieces.pop(0)
                dma_pool(o_, i_, [], [bb_])
            T.barrier()

            fT = bfv(0, 8 * 1024).rearrange("p (c n) -> p c n", c=8)
            aT = bfv(16384, 22 * 1024).rearrange("p (k n) -> p k n", k=22)
            Wdh = [bfv(61440 + i * 22528, 22 * 512).rearrange("p (k n) -> p k n", k=22) for i in range(2)]
            wgs = [bfv(106496 + i * 8192, 2048).rearrange("p (c n) -> p c n", c=8) for i in range(2)]
            wus = [bfv(106496 + i * 8192 + 4096, 2048).rearrange("p (c n) -> p c n", c=8) for i in range(2)]
            hn2 = [bfv(122880 + i * 2048, 1024) for i in range(2)]
            sgt = [bfv(126976 + i * 1024, 512) for i in range(2)]
            junkC = bfv(129024, 1024)
            b_fT = [B(f"fT{i}") for i in range(8)]
            b_aT = [B(f"aT{i}") for i in range(4)]
            b_Wdh = [B("Wdh0"), B("Wdh1")]
            b_wgs = [B("wgs0"), B("wgs1")]
            b_hn2 = [B("hn2_0"), B("hn2_1")]
            b_sgt = [B("sgt0"), B("sgt1")]
            b_junkC = B("junkC")
            b_st = B("stats")
            dve(lambda e: e.memset(junkC[:, 0:128], 1.0), [], [b_junkC])
            gain_tile(g2col, junkC[:, 0:128], b_junkC)
            b_rs2 = [B("rs2_0"), B("rs2_1")]
            b_st3t = [B(f"st3_{i}") for i in range(8)]

            def c0_stats(sc):
                rd_ss = [b_ss2[t] for t in range(sc * 8, sc * 8 + 8)]
                act(ln2[:, :], ss2[:, sc * 8:(sc + 1) * 8], AF.Ln, rd_ss + [b_cols], [b_rs2[sc]], bias=eps6, scale=1.0 / D)
                act(rs2x[:, sc * 8:(sc + 1) * 8], ln2[:, :], AF.Exp, [b_rs2[sc]], [b_rs2[sc]], scale=-0.5)

            def c0_copy(sc, i):
                t = sc * 8 + i
                hs = i % 2
                act(hn2[hs][:, :], h1[:, t * D:(t + 1) * D], AF.Copy, [b_h1[t], b_rs2[sc]], [b_hn2[hs]],
                    scale=rs2x[:, sc * 8 + i:sc * 8 + i + 1])

            def c0_tr(sc, i, pbase):
                hs = i % 2
                pb = pbase + hs
                for c in range(8):
                    tr(bankb(pb)[:, c * 128:(c + 1) * 128], hn2[hs][:, c * 128:(c + 1) * 128], 128,
                       [b_hn2[hs]], [b_banks[pb]], bf=True)
                src_ps = bankb(pb).rearrange("p (c n) -> p c n", c=8)
                dst = fT[:, :, i * 128:(i + 1) * 128]
                g = gbc[:, :].rearrange("p (c n) -> p c n", c=8)
                dve(lambda e, o=dst, i_=src_ps, g=g: e.tensor_tensor(out=o, in0=i_, in1=g, op=ALU.mult),
                    [b_banks[pb], b_gbc], [b_fT[i]])

            rs2x = sm[:, 88:104]
            c0_stats(0)
            c0_copy(0, 0)
            for i in range(8):
                if i + 1 < 8:
                    c0_copy(0, i + 1)
                c0_tr(0, i, 0)
            for sc in range(2):
                tiles = list(range(sc * 8, sc * 8 + 8))
                dma_sp(Wdh[0], wd_b.ap().rearrange("(k p) n -> p k n", p=128)[:, :, 0:512], [b_wdb], [b_Wdh[0]])
                gu = 0
                for j in range(11):
                    wsl = j % 2
                    dma_sp(wgs[wsl].rearrange("p c n -> p (c n)"), wg_b.ap()[j * 128:(j + 1) * 128, :],
                           [b_wgb], [b_wgs[wsl]])
                    dma_sp(wus[wsl].rearrange("p c n -> p (c n)"), wu_b.ap()[j * 128:(j + 1) * 128, :],
                           [b_wub], [b_wgs[wsl]])
                    for tc in range(2):
                        rdf = [b_fT[tc * 4 + k] for k in range(4)]
                        for sub in range(2):
                            gsl = gu % 2
                            gu += 1
                            bg, bu_ = 4 + 2 * gsl, 5 + 2 * gsl
                            for c in range(8):
                                mm(bank(bg), wgs[wsl][:, c, sub * 128:(sub + 1) * 128], fT[:, c, tc * 512:(tc + 1) * 512],
                                   c == 0, c == 7, [b_wgs[wsl]] + rdf, [b_banks[bg]])
                            for c in range(8):
                                mm(bank(bu_), wus[wsl][:, c, sub * 128:(sub + 1) * 128], fT[:, c, tc * 512:(tc + 1) * 512],
                                   c == 0, c == 7, [b_wgs[wsl]] + rdf, [b_banks[bu_]])
                            act(sgt[gsl][:, :], bank(bg), AF.Silu, [b_banks[bg]], [b_sgt[gsl]])
                            kf = 2 * j + sub
                            dve(lambda e, gsl=gsl, bu_=bu_, kf=kf, tc=tc:
                                e.tensor_tensor(out=aT[:, kf, tc * 512:(tc + 1) * 512], in0=bank(bu_), in1=sgt[gsl][:, :],
                                                op=ALU.mult), [b_banks[bu_], b_sgt[gsl]], [b_aT[tc * 2 + (kf % 2)]])
                dma_sp(Wdh[1], wd_b.ap().rearrange("(k p) n -> p k n", p=128)[:, :, 512:1024], [b_wdb], [b_Wdh[1]])
                dn = 0
                nxt = sc + 1 if sc + 1 < 2 else None
                if nxt is not None:
                    c0_stats(nxt)
                    c0_copy(nxt, 0)
                for half in range(2):
                    for i, t in enumerate(tiles):
                        bi = dn % 4
                        dn += 1
                        tc = i // 4
                        for k in range(22):
                            mm(bank(bi), aT[:, k, i * 128:(i + 1) * 128], Wdh[half][:, k, :], k == 0, k == 21,
                               [b_aT[tc * 2], b_aT[tc * 2 + 1], b_Wdh[half]], [b_banks[bi]])
                        hsl = h1[:, t * D + half * 512:t * D + (half + 1) * 512]
                        dve(lambda e, hsl=hsl, bi=bi: e.tensor_tensor(out=hsl, in0=bank(bi), in1=hsl, op=ALU.add),
                            [b_banks[bi], b_h1[t]], [b_h1[t]])
                        if nxt is not None and half == 0:
                            if i + 1 < 8:
                                c0_copy(nxt, i + 1)
                            c0_tr(nxt, i, 4)
                        if half == 1:
                            hfull = h1[:, t * D:(t + 1) * D]
                            act(junkC[:, :], hfull, AF.Square, [b_h1[t]], [b_junkC, b_st3t[i]], accum=ss3[:, i:i + 1])
                            act(ln3[:, i:i + 1], ss3[:, i:i + 1], AF.Ln, [b_st3t[i], b_cols], [b_st3t[i]], bias=eps6,
                                scale=1.0 / D)
                            act(rs3[:, i:i + 1], ln3[:, i:i + 1], AF.Exp, [b_st3t[i]], [b_st3t[i]], scale=-0.5)
                            dve(lambda e, hsl=hfull, i=i: e.scalar_tensor_tensor(out=hsl, in0=hsl, scalar=rs3[:, i:i + 1],
                                                                                in1=fgb[:, :], op0=ALU.mult, op1=ALU.mult),
                                [b_h1[t], b_st3t[i], b_fgb], [b_h1[t]])
                            dma_pool(out_d.ap()[s, t * 128:(t + 1) * 128, :], hfull, [b_h1[t]], [])
            T.barrier()

        T.finalize()
        with nc.Block() as block:
            @block.tensor
            def _(e):
                T.play("pe", e, sems, dsems)

            @block.scalar
            def _(e):
                T.play("act", e, sems, dsems)

            @block.vector
            def _(e):
                T.play("dve", e, sems, dsems)

            @block.gpsimd
            def _(e):
                T.play("pool", e, sems, dsems)

            @block.sync
            def _(e):
                T.play("sp", e, sems, dsems, final_wait=True)
    return nc


_CACHE = {}


def _host_layouts(inp):
    f = lambda a: np.ascontiguousarray(np.asarray(a, dtype=np.float32))
    w_in = f(inp["w_in"])[0]
    w_o = f(inp["w_o"])[0]
    w_gate = f(inp["w_gate"])[0]
    w_up = f(inp["w_up"])[0]
    w_down = f(inp["w_down"])[0]
    rel_bias = f(inp["rel_bias"])
    n = np.arange(NLUT)
    bucket = _t5_bucket_np(495 - n)
    onehot = np.zeros((32, NLUT), np.float32)
    onehot[bucket, n] = 1.0
    shared = {
        "meta": f(inp["meta_tokens"]),
        "w_in_l": np.ascontiguousarray(w_in.reshape(8, 128, 2048).transpose(1, 0, 2)).reshape(1024, 2048),
        "w_o_l": np.ascontiguousarray(w_o.reshape(8, 128, 1024).transpose(1, 0, 2)).reshape(1024, 1024),
        "w_gate_l": np.ascontiguousarray(w_gate.reshape(8, 128, 11, 256).transpose(2, 1, 0, 3)).reshape(1408, 2048),
        "w_up_l": np.ascontiguousarray(w_up.reshape(8, 128, 11, 256).transpose(2, 1, 0, 3)).reshape(1408, 2048),
        "w_down_l": w_down,
        "pool_w_l": np.ascontiguousarray(f(inp["pool_w"])[0].transpose(1, 0, 2)).reshape(128, 512),
        "colpack": np.ascontiguousarray(np.concatenate([
            f(inp["norm1_g"])[0].reshape(8, 128).T,
            f(inp["norm2_g"])[0].reshape(8, 128).T,
            f(inp["pool_scale"])[0].reshape(4, 128).T,
            f(inp["subln_g"])[0].reshape(128, 1),
            np.zeros((128, 3), np.float32),
            np.broadcast_to(np.concatenate([rel_bias[15], rel_bias[31]])[None, :], (128, 8)),
        ], axis=1)),
        "fgb": np.ascontiguousarray(np.broadcast_to(f(inp["final_g"])[None, :], (128, D))),
        "lamv": np.ascontiguousarray(np.broadcast_to(np.concatenate(
            [f(inp["lambda_q1"])[0], f(inp["lambda_k1"])[0], f(inp["lambda_q2"])[0], f(inp["lambda_k2"])[0]])[None, :],
            (128, 256))),
        "relb": rel_bias,
        "onehot": onehot,
        "ident": np.eye(128, dtype=np.float32),
        "aident": np.ascontiguousarray(np.eye(128, dtype=np.float32)[::-1]),
    }
    return shared


def kernel(**inputs):
    x = np.ascontiguousarray(np.asarray(inputs["x"], dtype=np.float32))
    shared = _host_layouts(inputs)
    if "nc" not in _CACHE:
        _CACHE["nc"] = build_program()
    nc = _CACHE["nc"]
    in_maps = []
    for c in range(NCORES):
        m = dict(shared)
        m["x"] = x[2 * c:2 * c + 2]
        in_maps.append(m)
    res = run_bass_kernel_spmd(nc, in_maps, core_ids=list(range(NCORES)))
    out = np.concatenate([np.asarray(r["out"], dtype=np.float32) for r in res.results], axis=0)
    return out
```

```python
import math
import numpy as np
import ml_dtypes
import concourse.bass as bass
import concourse.mybir as mybir
from concourse.bass_utils import run_bass_kernel_spmd

F32 = mybir.dt.float32
BF16 = mybir.dt.bfloat16
AF = mybir.ActivationFunctionType
ALU = mybir.AluOpType

NCORES = 8
SEQ = 2048
D = 1024
NMETA = 16
LPOS = SEQ + NMETA
DFF = 2816
NKT = 17
QC = 256
NQC = SEQ // QC
TW = 768
NLUT = 896
LAMBDA_INIT = 0.8 - 0.6 * math.exp(-0.3 * 0)
NS_DMA = 8
FILLER = 0


class Buf:
    __slots__ = ("name", "psum", "last_w", "readers")

    def __init__(self, name, psum=False):
        self.name = name
        self.psum = psum
        self.last_w = None
        self.readers = {}


class Op:
    __slots__ = ("eng", "fn", "deps", "inc", "seq", "dma", "semkey", "val", "prev")

    def __init__(self, eng, fn, deps, dma):
        self.eng = eng
        self.fn = fn
        self.deps = deps
        self.inc = False
        self.seq = 0
        self.dma = dma
        self.semkey = None
        self.val = 0
        self.prev = None


class Tracker:
    ENGS = ("pe", "act", "dve", "pool", "sp")

    def __init__(self):
        self.ops = {e: [] for e in self.ENGS}
        self.dma_cnt = {"sp": 0, "pool": 0}
        self.dma_uses = {}
        self.dma_last = {}
        self.pending = {e: set() for e in self.ENGS}
        self.recent_dma = []
        self.all_dma = []

    def emit(self, eng, fn, reads=(), writes=(), dma=False):
        deps = set()
        for b in reads:
            if b.psum:
                if b.last_w is not None:
                    deps.add(b.last_w)
                deps.update(b.readers.values())
            elif b.last_w is not None:
                deps.add(b.last_w)
        for b in writes:
            if b.last_w is not None:
                deps.add(b.last_w)
            deps.update(b.readers.values())
        if self.pending[eng]:
            deps.update(self.pending[eng])
            self.pending[eng] = set()
        if eng == "pe":
            deps = {d for d in deps if not (d.eng == "pe" and not d.dma)}
        op = Op(eng, fn, deps, dma)
        if dma:
            k = self.dma_cnt[eng] % NS_DMA
            self.dma_cnt[eng] += 1
            key = (eng, k)
            n = self.dma_uses.get(key, 0) + 1
            self.dma_uses[key] = n
            op.semkey = key
            op.val = 16 * n
            op.prev = self.dma_last.get(key)
            self.dma_last[key] = op
            self.recent_dma.append(op)
            self.all_dma.append(op)
        self.ops[eng].append(op)
        for b in reads:
            if b.psum:
                b.last_w = op
                b.readers = {}
            else:
                b.readers[(eng, id(op)) if dma else eng] = op
        for b in writes:
            b.last_w = op
            b.readers = {}
        return op

    def barrier(self, exclude=()):
        deps = set(self.recent_dma) - set(exclude)
        self.recent_dma = []
        for e in self.ENGS:
            if self.ops[e]:
                last = self.ops[e][-1]
                if not last.dma:
                    deps.add(last)
                else:
                    for o in reversed(self.ops[e]):
                        if not o.dma:
                            deps.add(o)
                            break
        for e in self.ENGS:
            self.pending[e] = set(deps) | self.pending[e]

    def finalize(self):
        for e in self.ENGS:
            for op in self.ops[e]:
                for d in op.deps:
                    if not d.dma:
                        d.inc = True
        for e in self.ENGS:
            c = 0
            for op in self.ops[e]:
                if op.inc and not op.dma:
                    c += 1
                    op.seq = c

    def play(self, eng, handle, sems, dsems, final_wait=False):
        waited = {}

        def wait(key, semobj, val):
            if waited.get(key, 0) >= val:
                return
            handle.wait_ge(semobj, val)
            waited[key] = val

        for op in self.ops[eng]:
            for d in op.deps:
                if d.dma:
                    wait(d.semkey, dsems[d.semkey], d.val)
                else:
                    wait(d.eng, sems[d.eng], d.seq)
            if op.dma and op.prev is not None:
                wait(op.prev.semkey, dsems[op.prev.semkey], op.prev.val)
            inst = op.fn(handle)
            if op.dma:
                inst.then_inc(dsems[op.semkey], 16)
            elif op.inc:
                inst.then_inc(sems[eng], 1)
        if final_wait:
            for key, op in self.dma_last.items():
                wait(key, dsems[key], op.val)


def _t5_bucket_np(rel):
    rel = np.asarray(rel, dtype=np.int64)
    nb = 16
    ret = np.where(rel > 0, nb, 0)
    n = np.abs(rel)
    max_exact = 8
    nf = np.maximum(n, 1).astype(np.float32)
    large = max_exact + (np.log(nf / np.float32(max_exact)) / np.float32(math.log(128 / max_exact))
                         * np.float32(nb - max_exact)).astype(np.int32)
    large = np.minimum(large, nb - 1)
    return ret + np.where(n < max_exact, n, large)


def build_program():
    nc = bass.Bass("TRN2", target_bir_lowering=False)
    T = Tracker()

    def din(name, shape, dt=F32):
        return nc.dram_tensor(name, list(shape), dt, kind="ExternalInput")

    x_d = din("x", [2, SEQ, D])
    meta_d = din("meta", [NMETA, D])
    win_l = din("w_in_l", [1024, 2048])
    wo_l = din("w_o_l", [1024, 1024])
    wg_l = din("w_gate_l", [1408, 2048])
    wu_l = din("w_up_l", [1408, 2048])
    wd_l = din("w_down_l", [DFF, 1024])
    pw_l = din("pool_w_l", [128, 512])
    colpack_d = din("colpack", [128, 32])
    fgb_d = din("fgb", [128, D])
    lam_d = din("lamv", [128, 256])
    relb_d = din("relb", [32, 4])
    oh_d = din("onehot", [32, NLUT])
    ident_d = din("ident", [128, 128])
    aident_d = din("aident", [128, 128])
    out_d = nc.dram_tensor("out", [2, SEQ, D], F32, kind="ExternalOutput")

    win_b = nc.dram_tensor("win_b", [1024, 2048], BF16, kind="Internal")
    wo_b = nc.dram_tensor("wo_b", [1024, 1024], BF16, kind="Internal")
    wg_b = nc.dram_tensor("wg_b", [1408, 2048], BF16, kind="Internal")
    wu_b = nc.dram_tensor("wu_b", [1408, 2048], BF16, kind="Internal")
    wd_b = nc.dram_tensor("wd_b", [DFF, 1024], BF16, kind="Internal")
    lut_d = nc.dram_tensor("lut_d", [4, NLUT], F32, kind="Internal")

    ARENA = 131072
    ctxs = dict(
        arena=nc.sbuf_tensor("arena", [128, ARENA // 2], BF16),
        h1=nc.sbuf_tensor("h1", [128, 16 * D], F32),
        ident=nc.sbuf_tensor("ident_sb", [128, 128], F32),
        gbc=nc.sbuf_tensor("gbc", [128, D], F32),
        fgb=nc.sbuf_tensor("fgb_sb", [128, D], F32),
        pw=nc.sbuf_tensor("pw_sb", [128, 512], BF16),
        identb=nc.sbuf_tensor("identb_sb", [128, 128], BF16),
        jb=nc.sbuf_tensor("jb_sb", [128, 128], BF16),
        cols=nc.sbuf_tensor("cols", [128, 64], F32),
        lamv=nc.sbuf_tensor("lamv_sb", [128, 256], F32),
        sm=nc.sbuf_tensor("sm", [128, 128], F32),
        relb=nc.sbuf_tensor("relb_sb", [32, 4], F32),
        oh=nc.sbuf_tensor("oh_sb", [32, NLUT], F32),
        ps=nc.psum_tensor("ps", [128, 8 * 512], F32),
    )
    sem_names = ["pe", "act", "dve", "pool", "sp"]
    from contextlib import ExitStack
    with ExitStack() as es:
        tens = {k: es.enter_context(v) for k, v in ctxs.items()}
        sems = {e: es.enter_context(nc.semaphore("s_" + e)) for e in sem_names}
        dsems = {}
        for e in ("sp", "pool"):
            for k in range(NS_DMA):
                dsems[(e, k)] = es.enter_context(nc.semaphore(f"d_{e}{k}"))

        arena_bf = tens["arena"]
        arena_f = arena_bf.bitcast(F32)
        h1 = tens["h1"]
        ident = tens["ident"]
        gbc = tens["gbc"]
        fgb = tens["fgb"]
        pw = tens["pw"]
        cols = tens["cols"]
        lamv = tens["lamv"]
        sm = tens["sm"]
        relb = tens["relb"]
        ohsb = tens["oh"]
        ps = tens["ps"]
        psb = ps.bitcast(BF16)
        identb = tens["identb"]
        jb = tens["jb"]

        def bfv(off, n):
            return arena_bf[:, off // 2: off // 2 + n]

        def f32v(off, n):
            return arena_f[:, off // 4: off // 4 + n]

        def bank(i):
            return ps[:, i * 512:(i + 1) * 512]

        def bankb(i):
            return psb[:, i * 1024:(i + 1) * 1024]

        g1col = cols[:, 0:8]
        g2col = cols[:, 8:16]
        pscol = cols[:, 16:20]
        sgcol = cols[:, 20:21]
        sg08 = cols[:, 21:22]
        farb = cols[:, 24:32]
        neglam = cols[:, 32:33]
        eps6 = cols[:, 33:34]
        eps5 = cols[:, 34:35]
        lam_s = cols[:, 36:38]
        lam_e = cols[:, 38:40]
        ssA = sm[:, 0:2]
        lnA = sm[:, 2:4]
        rsA = sm[:, 4:6]
        ss2 = sm[:, 8:24]
        ln2 = sm[:, 24:32]
        rs2 = sm[:, 32:40]
        ss3 = sm[:, 40:48]
        ln3 = sm[:, 48:56]
        rs3 = sm[:, 56:64]
        rz = sm[:, 64:72]
        rz1n = sm[:, 72:76]
        ssb = sm[:, 76:80]
        lnb = sm[:, 80:84]
        rsb = sm[:, 84:88]

        B = lambda name, psum=False: Buf(name, psum)
        b_banks = [B(f"bank{i}", True) for i in range(8)]
        b_h1 = [B(f"h1_{i}") for i in range(16)]
        b_ident = B("ident")
        b_gbc = B("gbc")
        b_fgb = B("fgb")
        b_pw = B("pw")
        b_cols = B("cols")
        b_lamv = B("lamv")
        b_lams = B("lams")
        b_relb = B("relb")
        b_oh = B("oh")
        b_winb, b_wob, b_wgb, b_wub, b_wdb = B("winb"), B("wob"), B("wgb"), B("wub"), B("wdb")
        b_lutd = B("lutd")
        b_ss2 = [B(f"ss2_{i}") for i in range(16)]

        emit = T.emit

        def dma_sp(out, in_, reads, writes):
            return emit("sp", lambda e, o=out, i=in_: e.dma_start(out=o, in_=i), reads, writes, dma=True)

        def dma_pool(out, in_, reads, writes):
            return emit("pool", lambda e, o=out, i=in_: e.dma_start(out=o, in_=i), reads, writes, dma=True)

        def mm(out, lhsT, rhs, start, stop, reads, writes, skip=False):
            return emit("pe", lambda e, o=out, l=lhsT, r=rhs, s=start, t=stop, k=skip:
                        e.matmul(o, lhsT=l, rhs=r, start=s, stop=t, skip_group_check=k), reads, writes)

        def tr(out, in_, npart, reads, writes, bf=False):
            idn = (identb if bf else ident)[0:npart, 0:npart]
            return emit("pe", lambda e, o=out, i=in_, d=idn: e.transpose(o, i, d), list(reads) + [b_ident], writes)

        def act(out, in_, func, reads, writes, bias=None, scale=None, accum=None):
            def fn(e, o=out, i=in_, f=func, b=bias, s=scale, a=accum):
                kw = {}
                if b is not None:
                    kw["bias"] = b
                if s is not None:
                    kw["scale"] = s
                if a is not None:
                    kw["accum_out"] = a
                return e.activation(out=o, in_=i, func=f, **kw)
            return emit("act", fn, reads, writes)

        def dve(fn, reads, writes):
            return emit("dve", fn, reads, writes)

        def pool(fn, reads, writes):
            return emit("pool", fn, reads, writes)

        dma_sp(cols[:, 0:32], colpack_d.ap(), [], [b_cols])
        dma_sp(ident[:], ident_d.ap(), [], [b_ident])
        dma_sp(lamv[:], lam_d.ap(), [], [b_lamv])
        dma_sp(relb[:], relb_d.ap(), [], [b_relb])
        dma_sp(ohsb[:], oh_d.ap(), [], [b_oh])
        win0 = bfv(67600, 8 * 2048).rearrange("p (c n) -> p c n", c=8)
        b_win0 = B("win0")
        b_stg = [B(f"wstg{c}") for c in range(8)]
        for c in range(8):
            dma_sp(h1[:, c * 2048:(c + 1) * 2048],
                   win_l.ap().rearrange("(p a) n -> p a n", p=128)[:, c, :], [], [b_stg[c]])
        dma_sp(fgb[:], fgb_d.ap(), [], [b_fgb])
        b_win0c = [B(f"win0_{c}") for c in range(8)]
        for c in range(8):
            dve(lambda e, c=c: e.tensor_copy(out=win0[:, c, :], in_=h1[:, c * 2048:(c + 1) * 2048]),
                [b_stg[c]], [b_win0c[c]])
        early_ex = [dma_sp(win_b.ap().rearrange("(p a) n -> p (a n)", p=128), win0.rearrange("p c n -> p (c n)"),
                           b_win0c, [b_winb])]

        dve(lambda e: e.tensor_copy(out=identb[:], in_=ident[:]), [b_ident], [b_ident])
        b_jb = B("jb")
        jstage = f32v(8192, 128)
        b_jst = B("jstage")
        dma_sp(jstage, aident_d.ap(), [], [b_jst])
        dve(lambda e: e.tensor_copy(out=jb[:], in_=jstage), [b_jst], [b_jb])
        dve(lambda e: e.memset(cols[:, 33:34], 1e-6), [], [b_cols])
        dve(lambda e: e.memset(cols[:, 34:35], 1e-5), [], [b_cols])
        dve(lambda e: e.tensor_scalar(out=sg08, in0=sgcol, scalar1=float(1.0 - LAMBDA_INIT), scalar2=None,
                                      op0=ALU.mult), [b_cols], [b_cols])
        negbb = cols[:, 40:44]
        ea = cols[:, 44:48]
        dve(lambda e: e.tensor_scalar(out=negbb, in0=farb[:, 0:4], scalar1=-1.0, scalar2=None, op0=ALU.mult),
            [b_cols], [b_cols])
        dve(lambda e: e.tensor_tensor(out=ea, in0=farb[:, 4:8], in1=farb[:, 0:4], op=ALU.subtract),
            [b_cols], [b_cols])
        b_lamp = B("lamp")
        lamp = f32v(0, 128)
        lamj = bfv(1024, 128)
        dve(lambda e: e.tensor_tensor(out=lamp[:, 0:64], in0=lamv[:, 0:64], in1=lamv[:, 64:128], op=ALU.mult),
            [b_lamv], [b_lamp])
        dve(lambda e: e.tensor_tensor(out=lamp[:, 64:128], in0=lamv[:, 128:192], in1=lamv[:, 192:256], op=ALU.mult),
            [b_lamv], [b_lamp])
        b_lamj = B("lamj")
        act(lamj[:, 0:64], lamp[:, 0:64], AF.Copy, [b_lamp], [b_lamj, b_lams], accum=lam_s[:, 0:1])
        act(lamj[:, 0:64], lamp[:, 64:128], AF.Copy, [b_lamp], [b_lamj, b_lams], accum=lam_s[:, 1:2])
        act(lam_e, lam_s, AF.Exp, [b_lams], [b_lams])
        dve(lambda e: e.tensor_tensor(out=neglam, in0=lam_e[:, 1:2], in1=lam_e[:, 0:1], op=ALU.subtract),
            [b_lams], [b_cols])
        dve(lambda e: e.tensor_scalar(out=neglam, in0=neglam, scalar1=float(-LAMBDA_INIT), scalar2=None,
                                      op0=ALU.add), [b_cols], [b_cols])
        lutsb = f32v(4096, NLUT)
        b_lutsb = B("lutsb")
        for half in range(2):
            mm(bank(0)[0:4, 0:448], relb[:, :], ohsb[:, half * 448:(half + 1) * 448], True, True,
               [b_relb, b_oh], [b_banks[0]])
            dve(lambda e, hf=half: e.tensor_copy(out=lutsb[0:4, hf * 448:(hf + 1) * 448], in_=bank(0)[0:4, 0:448]),
                [b_banks[0]], [b_lutsb])
        dma_sp(lut_d.ap(), lutsb[0:4, :], [b_lutsb], [b_lutd])
        T.barrier(exclude=early_ex)

        O_QT, O_KT, O_U, O_VX, O_X = 0, 16640, 33280, 49920, 67600
        PW_ = 2080
        QT = bfv(O_QT, 4 * PW_).rearrange("p (h n) -> p h n", h=4)
        KT = bfv(O_KT, 4 * PW_).rearrange("p (h n) -> p h n", h=4)
        UU = bfv(O_U, 4 * PW_).rearrange("p (h n) -> p h n", h=4)
        VX = bfv(O_VX, NKT * 4 * 130).rearrange("p (k h n) -> p k h n", k=NKT, h=4)

        def gain_tile(gcol, ones_ap, b_ones):
            for c in range(8):
                dve(lambda e, c=c: e.tensor_scalar(out=gbc[:, c * 128:(c + 1) * 128], in0=ones_ap,
                                                   scalar1=gcol[:, c:c + 1], scalar2=None, op0=ALU.mult),
                    [b_cols, b_ones], [b_gbc])

        for s in range(2):
            b_QT = [B(f"QT{i}") for i in range(5)]
            b_KT = [B(f"KT{i}") for i in range(NKT)]
            b_U = [B(f"U{i}") for i in range(5)]
            b_Upad = B("Upad")
            b_VX = [B(f"VX{i}") for i in range(NKT)]
            win = bfv(O_X, 8 * 2048).rearrange("p (c n) -> p c n", c=8)
            uT = [bfv(O_X + 32768 + i * 8192, 8 * 512).rearrange("p (c n) -> p c n", c=8) for i in range(2)]
            xs = [f32v(O_X + 49152 + i * 4096, 1024) for i in range(2)]
            xb = [bfv(O_X + 57344 + i * 2048, 1024) for i in range(2)]
            b_win = B("win")
            b_uT = [B("uT0"), B("uT1")]
            b_xs = [B("xs0"), B("xs1")]
            b_xb = [B("xb0"), B("xb1")]
            b_ssA = [B("ssA0"), B("ssA1")]

            rd_win = b_win0c if s == 0 else [b_win]
            if s == 0:
                pass
            else:
                dma_sp(win.rearrange("p c n -> p (c n)"), win_b.ap().rearrange("(p a) n -> p (a n)", p=128),
                       [b_winb], [b_win])
            dve(lambda e: e.memset(xs[1][:, 0:128], 1.0), [], [b_xs[1]])
            gain_tile(g1col, xs[1][:, 0:128], b_xs[1])
            pool(lambda e: e.memset(UU[:, :, 2064:2080], 0.0), [], [b_Upad])
            pool(lambda e: e.memset(VX[:, :, :, 128:130], 1.0), [], b_VX)

            def load_tile(ti, slot):
                np_ = NMETA if ti < 0 else 128
                src = meta_d.ap() if ti < 0 else x_d.ap()[s, ti * 128:(ti + 1) * 128, :]
                dma_sp(xs[slot][0:np_, :], src, [], [b_xs[slot]])

            def norm_compute(ti, slot):
                np_ = NMETA if ti < 0 else 128
                xt = xs[slot]
                xbt = xb[slot]
                act(xbt[0:np_, :], xt[0:np_, :], AF.Square, [b_xs[slot]], [b_xb[slot], b_ssA[slot]],
                    accum=ssA[0:np_, slot:slot + 1])
                act(lnA[0:np_, slot:slot + 1], ssA[0:np_, slot:slot + 1], AF.Ln, [b_ssA[slot], b_cols], [b_ssA[slot]],
                    bias=eps6[0:np_, :], scale=1.0 / D)
                act(rsA[0:np_, slot:slot + 1], lnA[0:np_, slot:slot + 1], AF.Exp, [b_ssA[slot]], [b_ssA[slot]],
                    scale=-0.5)
                act(xbt[0:np_, :], xt[0:np_, :], AF.Copy, [b_ssA[slot], b_xs[slot]], [b_xb[slot]],
                    scale=rsA[0:np_, slot:slot + 1])

            def norm_transpose(ti, slot):
                np_ = NMETA if ti < 0 else 128
                xbt = xb[slot]
                pb = slot
                for c in range(8):
                    tr(bankb(pb)[:, c * 128:c * 128 + np_], xbt[0:np_, c * 128:(c + 1) * 128], np_,
                       [b_xb[slot]], [b_banks[pb]], bf=True)
                p0 = 0 if ti < 0 else NMETA + ti * 128
                a = p0
                while a < p0 + np_:
                    pc = a // 512
                    bnd = min(p0 + np_, (pc + 1) * 512)
                    n = bnd - a
                    off = a - p0
                    src_ps = bankb(pb).rearrange("p (c n) -> p c n", c=8)[:, :, off:off + n]
                    dst = uT[pc % 2][:, :, a - pc * 512:a - pc * 512 + n]
                    g = gbc[:, :].rearrange("p (c n) -> p c n", c=8)[:, :, 0:n]
                    dve(lambda e, o=dst, i=src_ps, g=g: e.tensor_tensor(out=o, in0=i, in1=g, op=ALU.mult),
                        [b_banks[pb], b_gbc], [b_uT[pc % 2]])
                    a = bnd

            kindc = [0]

            def inproj_groups(pc):
                n = 512 if pc < 4 else 16
                u = uT[pc % 2]
                bu = b_uT[pc % 2]
                pos0 = pc * 512
                groups = []
                for grp in range(3):
                    for h in range(4):
                        def g_(grp=grp, h=h):
                            bi = 2 + (kindc[0] % 6)
                            kindc[0] += 1
                            colbase = {0: 512, 1: 1024, 2: 0}[grp] + h * 128
                            for c in range(8):
                                mm(bank(bi)[:, 0:n], win[:, c, colbase:colbase + 128], u[:, c, 0:n], c == 0, c == 7,
                                   rd_win + [bu], [b_banks[bi]])
                            if grp == 0:
                                dve(lambda e, o=QT[:, h, pos0:pos0 + n], i=bank(bi)[:, 0:n]:
                                    e.tensor_scalar(out=o, in0=i, scalar1=0.125, scalar2=None, op0=ALU.mult),
                                    [b_banks[bi]], [b_QT[pc]])
                            elif grp == 1:
                                wr = [b_KT[k] for k in range(pos0 // 128, (pos0 + n - 1) // 128 + 1)]
                                dve(lambda e, o=KT[:, h, pos0:pos0 + n], i=bank(bi)[:, 0:n]:
                                    e.tensor_copy(out=o, in_=i), [b_banks[bi]], wr)
                            else:
                                act(UU[:, h, pos0:pos0 + n], bank(bi)[:, 0:n], AF.Copy, [b_banks[bi]], [b_U[pc]])
                        groups.append(g_)
                nt = (n + 127) // 128
                for kk in range(nt):
                    def gv(kk=kk):
                        m = min(128, n - kk * 128)
                        kt = pos0 // 128 + kk
                        bi = 2 + (kindc[0] % 6)
                        kindc[0] += 1
                        for c in range(8):
                            mm(bank(bi)[0:m, :], u[:, c, kk * 128:kk * 128 + m], win[:, c, 1536:2048], c == 0, c == 7,
                               rd_win + [bu], [b_banks[bi]])
                        src_ps = bank(bi)[0:m, :].rearrange("p (h n) -> p h n", h=4)
                        if kk % 2 == 0:
                            dve(lambda e, o=VX[0:m, kt, :, 0:128], sp_=src_ps: e.tensor_copy(out=o, in_=sp_),
                                [b_banks[bi]], [b_VX[kt]])
                        else:
                            act(VX[0:m, kt, :, 0:128], src_ps, AF.Copy, [b_banks[bi]], [b_VX[kt]])
                    groups.append(gv)
                return groups

            tidx = 0
            order = [-1] + list(range(16))
            load_tile(order[0], 0)
            load_tile(order[1], 1)
            ptr = {"c": 0, "t": 0}

            def do_compute():
                k = ptr["c"]
                norm_compute(order[k], k % 2)
                if k + 2 < len(order):
                    load_tile(order[k + 2], k % 2)
                ptr["c"] += 1

            def norm_tile(ti, slot):
                if ptr["c"] == ptr["t"]:
                    do_compute()
                if ptr["c"] < len(order) and ptr["c"] == ptr["t"] + 1:
                    do_compute()
                k = ptr["t"]
                assert order[k] == ti
                norm_transpose(ti, k % 2)
                ptr["t"] += 1
            for ti in order[0:5]:
                norm_tile(ti, tidx % 2)
                tidx += 1
            nxt_tile = 5
            if s == 0:
                gate = [b_uT[0], b_uT[1]]
                dma_pool(wo_b.ap(), wo_l.ap(), gate, [b_wob])
                dma_pool(pw[:], pw_l.ap(), gate, [b_pw])
            for pc in range(5):
                groups = inproj_groups(pc)
                ng = len(groups)
                for gi, g_ in enumerate(groups):
                    g_()
                    if pc < 3 and gi % 4 == 3 and nxt_tile < len(order) and nxt_tile < 5 + 4 * (pc + 1):
                        norm_tile(order[nxt_tile], tidx % 2)
                        tidx += 1
                        nxt_tile += 1
                while pc < 3 and nxt_tile < 5 + 4 * (pc + 1):
                    norm_tile(order[nxt_tile], tidx % 2)
                    tidx += 1
                    nxt_tile += 1
            T.barrier()

            from collections import deque
            wo = bfv(O_X, 8 * 1024).rearrange("p (c n) -> p c n", c=8)
            Tst = f32v(O_X + 16384, 4 * TW).rearrange("p (h n) -> p h n", h=4)
            o2 = O_X + 16384 + 12288
            NE = 4
            Et = [bfv(o2 + i * 1024, 512) for i in range(NE)]
            o2 += NE * 1024
            Tb = bfv(o2, 4 * TW).rearrange("p (h n) -> p h n", h=4)
            o2 += 4 * TW * 2
            yT_off = o2
            yT = [bfv(o2 + i * 4096, 8 * QC).rearrange("p (c n) -> p c n", c=8) for i in range(2)]
            o2 += 8192
            ot = [f32v(o2 + i * 1024, 256).rearrange("p (q n) -> p q n", q=2) for i in range(2)]
            o2 += 2048
            yt = [f32v(o2 + i * 1024, 256).rearrange("p (q n) -> p q n", q=2) for i in range(2)]
            ytb_all = [bfv(o2 + i * 1024, 256).rearrange("p (q n) -> p q n", q=2) for i in range(2)]
            o2 += 2048
            tA = f32v(o2, 272)
            tB = f32v(o2 + 1088, 272)
            o2 += 2176
            dT = [bfv(o2 + i * 2048, 4 * QC).rearrange("p (g n) -> p g n", g=4) for i in range(2)]
            o2 += 4096
            junkB = bfv(o2, 1024)
            o2 += 2048
            Qb = [bfv(o2 + i * 1024, 512) for i in range(2)]
            o2 += 2048
            assert o2 <= ARENA, o2
            b_wo = B("wo")
            b_E = [B(f"E{i}") for i in range(NE)]
            b_sb = [B(f"sb{i}") for i in range(2)]
            b_yT = [B(f"yT{i}") for i in range(2)]
            b_ot = [B(f"ot{i}") for i in range(2)]
            b_yt = [B(f"yt{i}") for i in range(2)]
            b_tA, b_tB = B("tA"), B("tB")
            b_dT = [B(f"dT{i}") for i in range(2)]
            b_junkB = B("junkB")
            b_rz = [B("rz0"), B("rz1")]
            b_ssb = [B("ssb0"), B("ssb1")]
            b_Qb = [B("Qb0"), B("Qb1")]

            dma_sp(wo.rearrange("p c n -> p (c n)"), wo_b.ap().rearrange("(p a) n -> p (a n)", p=128),
                   [b_wob], [b_wo])
            for i in range(2):
                pool(lambda e, i=i: e.memset(Qb[i][:, :], 0.0), [], [b_Qb[i]])
            b_Tst = [B(f"Tst{h}") for h in range(4)]
            b_Tb = B("Tb")
            Hb = bfv(yT_off, 4 * TW)
            for h in range(4):
                src = bass.AP(lut_d, h * NLUT, [[1, 128], [1, TW]])
                dma_sp(Tst[:, h, :], src, [b_lutd], [b_Tst[h]])
                act(Hb[:, h * TW:(h + 1) * TW], Tst[:, h, :], AF.Exp, [b_Tst[h], b_cols], b_yT, bias=negbb[:, h:h + 1])
            tb_banks = [7, 3, 4, 5]
            k = 0
            for h in range(4):
                for half in range(2):
                    bk = tb_banks[k % 4]
                    k += 1
                    mm(bank(bk)[:, 0:384], jb[:, :], Hb[:, h * TW + half * 384:h * TW + (half + 1) * 384], True, True,
                       b_yT + [b_jb], [b_banks[bk]])
                    dve(lambda e, o=Tb[:, h, half * 384:(half + 1) * 384], i=bank(bk)[:, 0:384]:
                        e.tensor_copy(out=o, in_=i), [b_banks[bk]], [b_Tb])

            import heapq
            deferred = []
            dseq = [0]

            def defer(at, fn):
                dseq[0] += 1
                heapq.heappush(deferred, (at, dseq[0], fn))

            def run_deferred(cur, nmax):
                k = 0
                while deferred and k < nmax and deferred[0][0] <= cur:
                    heapq.heappop(deferred)[2]()
                    k += 1

            def pool_stage(cq, g, w):
                q0 = NMETA + cq * QC
                ysl = cq % 2

                def zz(lo, n):
                    return UU[:, g, q0 + lo:q0 + lo + n]
                rdU = [b_U[min(4, (q0 - 8) // 512)], b_U[min(4, (q0 + 263) // 512)], b_Upad]

                def padd(o, a, b, reads, writes):
                    pool(lambda e, o=o, a=a, b=b: e.tensor_tensor(out=o, in0=a, in1=b, op=ALU.add), reads, writes)
                if w == 2:
                    padd(tA[:, 0:256], zz(0, 256), zz(-1, 256), rdU, [b_tA])
                    ws, wb, wsl = tA, b_tA, 0
                else:
                    padd(tA[:, 1:272], zz(-7, 271), zz(-8, 271), rdU, [b_tA])
                    padd(tB[:, 2:271], tA[:, 3:272], tA[:, 1:270], [b_tA], [b_tB])
                    ws, wb, wsl = tB, b_tB, 8
                    if w >= 8:
                        padd(tA[:, 4:269], tB[:, 6:271], tB[:, 2:267], [b_tB], [b_tA])
                        ws, wb = tA, b_tA
                    if w == 16:
                        padd(tB[:, 8:265], tA[:, 12:269], tA[:, 4:261], [b_tA], [b_tB])
                        ws, wb = tB, b_tB
                def dpart():
                    dve(lambda e, o=dT[ysl][:, g, :], a=ws[:, wsl:wsl + 256], sc_=1.0 / w, b=zz(0, 256):
                        e.scalar_tensor_tensor(out=o, in0=a, scalar=sc_, in1=b, op0=ALU.mult, op1=ALU.subtract),
                        [wb] + rdU, [b_dT[ysl]])
                    if cq == NQC - 1:
                        right = w - 1 - w // 2
                        for r in range(right):
                            cnt = w - (right - r)
                            col = 255 - r
                            dve(lambda e, o=dT[ysl][:, g, col:col + 1], a=ws[:, wsl + col:wsl + col + 1], sc_=1.0 / cnt,
                                b=zz(col, 1):
                                e.scalar_tensor_tensor(out=o, in0=a, scalar=sc_, in1=b, op0=ALU.mult, op1=ALU.subtract),
                                [wb] + rdU, [b_dT[ysl]])
                return dpart

            def poolmm_stage(cq, g):
                ysl = cq % 2
                mb = 7
                mm(bank(mb)[:, 0:QC], pw[:, g * 128:(g + 1) * 128], dT[ysl][:, g, :], True, True,
                   [b_pw, b_dT[ysl]], [b_banks[mb]])
                dve(lambda e, o=yT[ysl][:, g, :], i=bank(mb)[:, 0:QC], sc_=pscol[:, g:g + 1]:
                    e.tensor_scalar(out=o, in0=i, scalar1=sc_, scalar2=None, op0=ALU.mult),
                    [b_banks[mb], b_cols], [b_yT[ysl]])

            def epilogue_stages(cq, h, osl):
                ysl = cq % 2
                ba, bb = 3 + 2 * osl, 4 + 2 * osl
                rzs = rz[:, osl * 4:(osl + 1) * 4]
                r1n = rz1n[:, osl * 2:(osl + 1) * 2]
                sss = ssb[:, osl * 2:(osl + 1) * 2]
                lns = lnb[:, osl * 2:(osl + 1) * 2]
                rss = rsb[:, osl * 2:(osl + 1) * 2]
                mb = 7

                def st1():
                    for c, bk in ((0, ba), (1, bb)):
                        zsrc = bank(bk)[:, 0:258].rearrange("p (q n) -> p q n", n=129)[:, :, 128:129]
                        zdst = rzs[:, c * 2:(c + 1) * 2].rearrange("p (q o) -> p q o", o=1)
                        dve(lambda e, o=zdst, i=zsrc: e.reciprocal(out=o, in_=i), [b_banks[bk]], [b_rz[osl]])
                    dve(lambda e, o=r1n, i=rzs[:, 2:4]: e.tensor_scalar(out=o, in0=i, scalar1=neglam, scalar2=None,
                                                                         op0=ALU.mult), [b_rz[osl], b_cols], [b_rz[osl]])

                def st2():
                    for qs in range(2):
                        dve(lambda e, o=ot[osl][:, qs, :], i=bank(ba)[:, qs * 129:qs * 129 + 128], sc_=rzs[:, qs:qs + 1]:
                            e.tensor_scalar(out=o, in0=i, scalar1=sc_, scalar2=None, op0=ALU.mult),
                            [b_banks[ba], b_rz[osl]], [b_ot[osl]])

                def st3():
                    for qs in range(2):
                        dve(lambda e, o=ot[osl][:, qs, :], i=bank(bb)[:, qs * 129:qs * 129 + 128], sc_=r1n[:, qs:qs + 1]:
                            e.scalar_tensor_tensor(out=o, in0=i, scalar=sc_, in1=o, op0=ALU.mult, op1=ALU.add),
                            [b_banks[bb], b_rz[osl], b_ot[osl]], [b_ot[osl]])

                def st4():
                    for qs in range(2):
                        dve(lambda e, o=junkB[:, 0:128], i=ot[osl][:, qs, :], a=sss[:, qs:qs + 1]:
                            e.scalar_tensor_tensor(out=o, in0=i, scalar=1.0, in1=i, op0=ALU.mult, op1=ALU.mult,
                                                   accum_out=a),
                            [b_ot[osl]], [b_junkB, b_ssb[osl]])

                def st5():
                    act(lns, sss, AF.Ln, [b_ssb[osl], b_cols], [b_ssb[osl]], bias=eps5, scale=1.0 / 128)
                    act(rss, lns, AF.Exp, [b_ssb[osl]], [b_ssb[osl]], scale=-0.5)

                ytb = ytb_all[osl]

                def st6():
                    for qs in range(2):
                        dve(lambda e, o=ytb[:, qs, :], i=ot[osl][:, qs, :], sc_=rss[:, qs:qs + 1]:
                            e.tensor_scalar(out=o, in0=i, scalar1=sc_, scalar2=None, op0=ALU.mult),
                            [b_ot[osl], b_ssb[osl]], [b_yt[osl]])

                def st7():
                    for qs in range(2):
                        tr(bankb(mb)[:, qs * 128:(qs + 1) * 128], ytb[:, qs, :], 128, [b_yt[osl]], [b_banks[mb]], bf=True)

                def st8():
                    dve(lambda e, o=yT[ysl][:, 4 + h, :], i=bankb(mb)[:, 0:QC]:
                        e.tensor_scalar(out=o, in0=i, scalar1=sg08, scalar2=None, op0=ALU.mult),
                        [b_banks[mb], b_cols], [b_yT[ysl]])
                def st78():
                    st7()
                    st8()
                return [(0, st1), (2, st2), (4, st3), (6, st4), (11, st5), (15, st6), (19, st78)]

            def wo_stages(cq):
                ysl = cq % 2
                sts = []
                k = 0
                for t2 in range(2):
                    tile_i = cq * 2 + t2
                    for half in range(4):
                        def st(t2=t2, half=half, tile_i=tile_i):
                            mb = 7
                            for c in range(8):
                                mm(bank(mb)[:, 0:256], yT[ysl][:, c, t2 * 128:(t2 + 1) * 128],
                                   wo[:, c, half * 256:(half + 1) * 256],
                                   c == 0, c == 7, [b_yT[ysl], b_wo], [b_banks[mb]])
                            hsl = h1[:, tile_i * D + half * 256:tile_i * D + (half + 1) * 256]
                            dve(lambda e, hsl=hsl, mb=mb: e.tensor_tensor(out=hsl, in0=bank(mb)[:, 0:256], in1=hsl, op=ALU.add),
                                [b_banks[mb], b_h1[tile_i]], [b_h1[tile_i]])
                        sts.append((24 + 2 * k, st))
                        k += 1

                    def stq(tile_i=tile_i):
                        dve(lambda e, o=junkB[:, :], i=h1[:, tile_i * D:(tile_i + 1) * D], a=ss2[:, tile_i:tile_i + 1]:
                            e.scalar_tensor_tensor(out=o, in0=i, scalar=1.0, in1=i, op0=ALU.mult, op1=ALU.mult,
                                                   accum_out=a),
                            [b_h1[tile_i]], [b_junkB, b_ss2[tile_i]])
                    sts.append((25 + 2 * (k - 1), stq))
                return sts

            unit = 0

            def av(cqh, j, esl, kn, osl, h):
                ba, bb = 3 + 2 * osl, 4 + 2 * osl
                for c, bk in ((0, ba), (1, bb)):
                    for qs in range(2):
                        mm(bank(bk)[:, qs * 129:(qs + 1) * 129],
                           Et[esl][0:kn, c * QC + qs * 128:c * QC + (qs + 1) * 128],
                           VX[0:kn, j, h, 0:129], (j == 0 and qs == 0), j == NKT - 1,
                           [b_E[esl], b_VX[j]], [b_banks[bk]], skip=True)
                if FILLER:
                    mm(bank(bb)[:, 258:258 + FILLER], Et[esl][0:kn, QC + 128:QC + 256], Qb[osl][0:kn, 0:FILLER],
                       False, False, [b_E[esl], b_Qb[osl]], [b_banks[bb]], skip=True)
                if j == NKT - 1:
                    for off, st in epilogue_stages(cqh, h, osl):
                        defer(unit + off, st)
                    if h == 3:
                        for off, st in wo_stages(cqh):
                            defer(unit + off, st)

            def emit_qb(cq_, h_, slot):
                q0_ = NMETA + cq_ * QC
                for c in range(2):
                    pool(lambda e, o=Qb[slot][c * 64:(c + 1) * 64, c * QC:(c + 1) * QC],
                         i=QT[c * 64:(c + 1) * 64, h_, q0_:q0_ + QC]: e.tensor_copy(out=o, in_=i),
                         [b_QT[min(4, q0_ // 512)], b_QT[min(4, (q0_ + QC - 1) // 512)]], [b_Qb[slot]])

            units = []
            hq_ = 0
            for cq in range(NQC):
                for h in range(4):
                    for j in range(NKT):
                        units.append(dict(cq=cq, h=h, j=j, sl=hq_ % 2, kn=128 if j < NKT - 1 else LPOS - 128 * (NKT - 1)))
                    hq_ += 1
            NU = len(units)

            def emit_qk(u):
                U_ = units[u]
                cq_, h_, j_, kn_, sl_ = U_["cq"], U_["h"], U_["j"], U_["kn"], U_["sl"]
                ssl_ = u % 3
                k0_ = j_ * 128
                mm(bank(ssl_)[0:kn_, :], KT[:, h_, k0_:k0_ + kn_], Qb[sl_][:, :], True, True,
                   [b_KT[j_], b_Qb[sl_]], [b_banks[ssl_]])

            emit_qb(0, 0, 0)
            emit_qk(0)
            emit_qk(1)
            cast_pieces = []
            if s == 0:
                for r in range(11):
                    cast_pieces.append((wg_b.ap()[r * 128:(r + 1) * 128, :], wg_l.ap()[r * 128:(r + 1) * 128, :], b_wgb))
                    cast_pieces.append((wu_b.ap()[r * 128:(r + 1) * 128, :], wu_l.ap()[r * 128:(r + 1) * 128, :], b_wub))
                for r in range(11):
                    cast_pieces.append((wd_b.ap()[r * 256:(r + 1) * 256, :], wd_l.ap()[r * 256:(r + 1) * 256, :], b_wdb))
            for u in range(NU):
                U_ = units[u]
                cq, h, j, kn, sl = U_["cq"], U_["h"], U_["j"], U_["kn"], U_["sl"]
                q0 = NMETA + cq * QC
                unit = u + 1
                if h == 0 and j == 0:
                    for t2 in range(2):
                        tile_i = cq * 2 + t2
                        dma_sp(h1[:, tile_i * D:(tile_i + 1) * D], x_d.ap()[s, tile_i * 128:(tile_i + 1) * 128, :],
                               [], [b_h1[tile_i]])
                    for g, w in enumerate((2, 4, 8, 16)):
                        def pst(cq=cq, g=g, w=w, at=unit + 1 + 8 * g + 6):
                            dpart = pool_stage(cq, g, w)
                            defer(at, dpart)
                        defer(unit + 1 + 8 * g, pst)
                if h == 2 and j == 0:
                    for g in range(4):
                        defer(unit + 3 * g, lambda cq=cq, g=g: poolmm_stage(cq, g))
                ssl = u % 3
                esl = u % NE
                k0 = j * 128
                Dd = k0 - q0
                s_ps = bank(ssl)[0:kn, :]
                e_out = Et[esl][0:kn, :]
                maxrel = Dd + kn - 1
                minrel = Dd - (QC - 1)
                if minrel >= 128:
                    act(e_out, s_ps, AF.Exp, [b_banks[ssl], b_cols], [b_E[esl]], bias=ea[0:kn, h:h + 1])
                else:
                    act(e_out, s_ps, AF.Exp, [b_banks[ssl]], [b_E[esl]])
                if maxrel <= -128 or minrel >= 128:
                    pass
                else:
                    i0 = 368 - Dd
                    assert 0 <= i0 <= TW - QC, (i0, Dd)
                    tb0 = Tb[0:kn, h, i0:i0 + QC]
                    tbb = bass.AP(tb0.tensor, tb0.offset, [list(tb0.ap[0]), [0, 2], list(tb0.ap[1])])
                    e3 = Et[esl][0:kn, :].rearrange("p (c n) -> p c n", c=2)
                    dve(lambda e, o=e3, b=tbb: e.tensor_tensor(out=o, in0=o, in1=b, op=ALU.mult),
                        [b_Tb, b_E[esl]], [b_E[esl]])
                if j == 4 and cast_pieces:
                    o_, i_, bb_ = cast_pieces.pop(0)
                    dma_pool(o_, i_, [], [bb_])
                if j == 8 and u + 9 < NU:
                    nU = units[u + 9]
                    emit_qb(nU["cq"], nU["h"], nU["sl"])
                if u + 2 < NU:
                    emit_qk(u + 2)
                if u >= 1:
                    pU = units[u - 1]
                    av(pU["cq"], pU["j"], (u - 1) % NE, pU["kn"], pU["sl"], pU["h"])
                run_deferred(unit, 1)
            pU = units[NU - 1]
            unit = NU + 1
            av(pU["cq"], pU["j"], (NU - 1) % NE, pU["kn"], pU["sl"], pU["h"])
            while deferred:
                run_deferred(10 ** 9, 10 ** 6)
            while cast_pieces:
                o_, i_, bb_ = cast_pieces.pop(0)
                dma_pool(o_, i_, [], [bb_])
            T.barrier()

            fT = bfv(0, 8 * 1024).rearrange("p (c n) -> p c n", c=8)
            aT = bfv(16384, 22 * 1024).rearrange("p (k n) -> p k n", k=22)
            Wdh = [bfv(61440 + i * 22528, 22 * 512).rearrange("p (k n) -> p k n", k=22) for i in range(2)]
            wgs = [bfv(106496 + i * 8192, 2048).rearrange("p (c n) -> p c n", c=8) for i in range(2)]
            wus = [bfv(106496 + i * 8192 + 4096, 2048).rearrange("p (c n) -> p c n", c=8) for i in range(2)]
            hn2 = [bfv(122880 + i * 2048, 1024) for i in range(2)]
            sgt = [bfv(126976 + i * 1024, 512) for i in range(2)]
            junkC = bfv(129024, 1024)
            b_fT = [B(f"fT{i}") for i in range(8)]
            b_aT = [B(f"aT{i}") for i in range(4)]
            b_Wdh = [B("Wdh0"), B("Wdh1")]
            b_wgs = [B("wgs0"), B("wgs1")]
            b_hn2 = [B("hn2_0"), B("hn2_1")]
            b_sgt = [B("sgt0"), B("sgt1")]
            b_junkC = B("junkC")
            b_st = B("stats")
            dve(lambda e: e.memset(junkC[:, 0:128], 1.0), [], [b_junkC])
            gain_tile(g2col, junkC[:, 0:128], b_junkC)
            b_rs2 = [B("rs2_0"), B("rs2_1")]
            b_st3t = [B(f"st3_{i}") for i in range(8)]

            def c0_stats(sc):
                rd_ss = [b_ss2[t] for t in range(sc * 8, sc * 8 + 8)]
                act(ln2[:, :], ss2[:, sc * 8:(sc + 1) * 8], AF.Ln, rd_ss + [b_cols], [b_rs2[sc]], bias=eps6, scale=1.0 / D)
                act(rs2x[:, sc * 8:(sc + 1) * 8], ln2[:, :], AF.Exp, [b_rs2[sc]], [b_rs2[sc]], scale=-0.5)

            def c0_copy(sc, i):
                t = sc * 8 + i
                hs = i % 2
                act(hn2[hs][:, :], h1[:, t * D:(t + 1) * D], AF.Copy, [b_h1[t], b_rs2[sc]], [b_hn2[hs]],
                    scale=rs2x[:, sc * 8 + i:sc * 8 + i + 1])

            def c0_tr(sc, i, pbase):
                hs = i % 2
                pb = pbase + hs
                for c in range(8):
                    tr(bankb(pb)[:, c * 128:(c + 1) * 128], hn2[hs][:, c * 128:(c + 1) * 128], 128,
                       [b_hn2[hs]], [b_banks[pb]], bf=True)
                src_ps = bankb(pb).rearrange("p (c n) -> p c n", c=8)
                dst = fT[:, :, i * 128:(i + 1) * 128]
                g = gbc[:, :].rearrange("p (c n) -> p c n", c=8)
                dve(lambda e, o=dst, i_=src_ps, g=g: e.tensor_tensor(out=o, in0=i_, in1=g, op=ALU.mult),
                    [b_banks[pb], b_gbc], [b_fT[i]])

            rs2x = sm[:, 88:104]
            c0_stats(0)
            c0_copy(0, 0)
            for i in range(8):
                if i + 1 < 8:
                    c0_copy(0, i + 1)
                c0_tr(0, i, 0)
            for sc in range(2):
                tiles = list(range(sc * 8, sc * 8 + 8))
                dma_sp(Wdh[0], wd_b.ap().rearrange("(k p) n -> p k n", p=128)[:, :, 0:512], [b_wdb], [b_Wdh[0]])
                gu = 0
                for j in range(11):
                    wsl = j % 2
                    dma_sp(wgs[wsl].rearrange("p c n -> p (c n)"), wg_b.ap()[j * 128:(j + 1) * 128, :],
                           [b_wgb], [b_wgs[wsl]])
                    dma_sp(wus[wsl].rearrange("p c n -> p (c n)"), wu_b.ap()[j * 128:(j + 1) * 128, :],
                           [b_wub], [b_wgs[wsl]])
                    for tc in range(2):
                        rdf = [b_fT[tc * 4 + k] for k in range(4)]
                        for sub in range(2):
                            gsl = gu % 2
                            gu += 1
                            bg, bu_ = 4 + 2 * gsl, 5 + 2 * gsl
                            for c in range(8):
                                mm(bank(bg), wgs[wsl][:, c, sub * 128:(sub + 1) * 128], fT[:, c, tc * 512:(tc + 1) * 512],
                                   c == 0, c == 7, [b_wgs[wsl]] + rdf, [b_banks[bg]])
                            for c in range(8):
                                mm(bank(bu_), wus[wsl][:, c, sub * 128:(sub + 1) * 128], fT[:, c, tc * 512:(tc + 1) * 512],
                                   c == 0, c == 7, [b_wgs[wsl]] + rdf, [b_banks[bu_]])
                            act(sgt[gsl][:, :], bank(bg), AF.Silu, [b_banks[bg]], [b_sgt[gsl]])
                            kf = 2 * j + sub
                            dve(lambda e, gsl=gsl, bu_=bu_, kf=kf, tc=tc:
                                e.tensor_tensor(out=aT[:, kf, tc * 512:(tc + 1) * 512], in0=bank(bu_), in1=sgt[gsl][:, :],
                                                op=ALU.mult), [b_banks[bu_], b_sgt[gsl]], [b_aT[tc * 2 + (kf % 2)]])
                dma_sp(Wdh[1], wd_b.ap().rearrange("(k p) n -> p k n", p=128)[:, :, 512:1024], [b_wdb], [b_Wdh[1]])
                dn = 0
                nxt = sc + 1 if sc + 1 < 2 else None
                if nxt is not None:
                    c0_stats(nxt)
                    c0_copy(nxt, 0)
                for half in range(2):
                    for i, t in enumerate(tiles):
                        bi = dn % 4
                        dn += 1
                        tc = i // 4
                        for k in range(22):
                            mm(bank(bi), aT[:, k, i * 128:(i + 1) * 128], Wdh[half][:, k, :], k == 0, k == 21,
                               [b_aT[tc * 2], b_aT[tc * 2 + 1], b_Wdh[half]], [b_banks[bi]])
                        hsl = h1[:, t * D + half * 512:t * D + (half + 1) * 512]
                        dve(lambda e, hsl=hsl, bi=bi: e.tensor_tensor(out=hsl, in0=bank(bi), in1=hsl, op=ALU.add),
                            [b_banks[bi], b_h1[t]], [b_h1[t]])
                        if nxt is not None and half == 0:
                            if i + 1 < 8:
                                c0_copy(nxt, i + 1)
                            c0_tr(nxt, i, 4)
                        if half == 1:
                            hfull = h1[:, t * D:(t + 1) * D]
                            act(junkC[:, :], hfull, AF.Square, [b_h1[t]], [b_junkC, b_st3t[i]], accum=ss3[:, i:i + 1])
                            act(ln3[:, i:i + 1], ss3[:, i:i + 1], AF.Ln, [b_st3t[i], b_cols], [b_st3t[i]], bias=eps6,
                                scale=1.0 / D)
                            act(rs3[:, i:i + 1], ln3[:, i:i + 1], AF.Exp, [b_st3t[i]], [b_st3t[i]], scale=-0.5)
                            dve(lambda e, hsl=hfull, i=i: e.scalar_tensor_tensor(out=hsl, in0=hsl, scalar=rs3[:, i:i + 1],
                                                                                in1=fgb[:, :], op0=ALU.mult, op1=ALU.mult),
                                [b_h1[t], b_st3t[i], b_fgb], [b_h1[t]])
                            dma_pool(out_d.ap()[s, t * 128:(t + 1) * 128, :], hfull, [b_h1[t]], [])
            T.barrier()

        T.finalize()
        with nc.Block() as block:
            @block.tensor
            def _(e):
                T.play("pe", e, sems, dsems)

            @block.scalar
            def _(e):
                T.play("act", e, sems, dsems)

            @block.vector
            def _(e):
                T.play("dve", e, sems, dsems)

            @block.gpsimd
            def _(e):
                T.play("pool", e, sems, dsems)

            @block.sync
            def _(e):
                T.play("sp", e, sems, dsems, final_wait=True)
    return nc


_CACHE = {}


def _host_layouts(inp):
    f = lambda a: np.ascontiguousarray(np.asarray(a, dtype=np.float32))
    w_in = f(inp["w_in"])[0]
    w_o = f(inp["w_o"])[0]
    w_gate = f(inp["w_gate"])[0]
    w_up = f(inp["w_up"])[0]
    w_down = f(inp["w_down"])[0]
    rel_bias = f(inp["rel_bias"])
    n = np.arange(NLUT)
    bucket = _t5_bucket_np(495 - n)
    onehot = np.zeros((32, NLUT), np.float32)
    onehot[bucket, n] = 1.0
    shared = {
        "meta": f(inp["meta_tokens"]),
        "w_in_l": np.ascontiguousarray(w_in.reshape(8, 128, 2048).transpose(1, 0, 2)).reshape(1024, 2048),
        "w_o_l": np.ascontiguousarray(w_o.reshape(8, 128, 1024).transpose(1, 0, 2)).reshape(1024, 1024),
        "w_gate_l": np.ascontiguousarray(w_gate.reshape(8, 128, 11, 256).transpose(2, 1, 0, 3)).reshape(1408, 2048),
        "w_up_l": np.ascontiguousarray(w_up.reshape(8, 128, 11, 256).transpose(2, 1, 0, 3)).reshape(1408, 2048),
        "w_down_l": w_down,
        "pool_w_l": np.ascontiguousarray(f(inp["pool_w"])[0].transpose(1, 0, 2)).reshape(128, 512),
        "colpack": np.ascontiguousarray(np.concatenate([
            f(inp["norm1_g"])[0].reshape(8, 128).T,
            f(inp["norm2_g"])[0].reshape(8, 128).T,
            f(inp["pool_scale"])[0].reshape(4, 128).T,
            f(inp["subln_g"])[0].reshape(128, 1),
            np.zeros((128, 3), np.float32),
            np.broadcast_to(np.concatenate([rel_bias[15], rel_bias[31]])[None, :], (128, 8)),
        ], axis=1)),
        "fgb": np.ascontiguousarray(np.broadcast_to(f(inp["final_g"])[None, :], (128, D))),
        "lamv": np.ascontiguousarray(np.broadcast_to(np.concatenate(
            [f(inp["lambda_q1"])[0], f(inp["lambda_k1"])[0], f(inp["lambda_q2"])[0], f(inp["lambda_k2"])[0]])[None, :],
            (128, 256))),
        "relb": rel_bias,
        "onehot": onehot,
        "ident": np.eye(128, dtype=np.float32),
        "aident": np.ascontiguousarray(np.eye(128, dtype=np.float32)[::-1]),
    }
    return shared


def kernel(**inputs):
    x = np.ascontiguousarray(np.asarray(inputs["x"], dtype=np.float32))
    shared = _host_layouts(inputs)
    if "nc" not in _CACHE:
        _CACHE["nc"] = build_program()
    nc = _CACHE["nc"]
    in_maps = []
    for c in range(NCORES):
        m = dict(shared)
        m["x"] = x[2 * c:2 * c + 2]
        in_maps.append(m)
    res = run_bass_kernel_spmd(nc, in_maps, core_ids=list(range(NCORES)))
    out = np.concatenate([np.asarray(r["out"], dtype=np.float32) for r in res.results], axis=0)
    return out
```

```python
import math
import numpy as np
import ml_dtypes
import concourse.bass as bass
import concourse.mybir as mybir
from concourse.bass_utils import run_bass_kernel_spmd

F32 = mybir.dt.float32
BF16 = mybir.dt.bfloat16
AF = mybir.ActivationFunctionType
ALU = mybir.AluOpType

NCORES = 8
SEQ = 2048
D = 1024
NMETA = 16
LPOS = SEQ + NMETA
DFF = 2816
NKT = 17
QC = 256
NQC = SEQ // QC
TW = 768
NLUT = 896
LAMBDA_INIT = 0.8 - 0.6 * math.exp(-0.3 * 0)
NS_DMA = 8
FILLER = 0


class Buf:
    __slots__ = ("name", "psum", "last_w", "readers")

    def __init__(self, name, psum=False):
        self.name = name
        self.psum = psum
        self.last_w = None
        self.readers = {}


class Op:
    __slots__ = ("eng", "fn", "deps", "inc", "seq", "dma", "semkey", "val", "prev")

    def __init__(self, eng, fn, deps, dma):
        self.eng = eng
        self.fn = fn
        self.deps = deps
        self.inc = False
        self.seq = 0
        self.dma = dma
        self.semkey = None
        self.val = 0
        self.prev = None


class Tracker:
    ENGS = ("pe", "act", "dve", "pool", "sp")

    def __init__(self):
        self.ops = {e: [] for e in self.ENGS}
        self.dma_cnt = {"sp": 0, "pool": 0}
        self.dma_uses = {}
        self.dma_last = {}
        self.pending = {e: set() for e in self.ENGS}
        self.recent_dma = []
        self.all_dma = []

    def emit(self, eng, fn, reads=(), writes=(), dma=False):
        deps = set()
        for b in reads:
            if b.psum:
                if b.last_w is not None:
                    deps.add(b.last_w)
                deps.update(b.readers.values())
            elif b.last_w is not None:
                deps.add(b.last_w)
        for b in writes:
            if b.last_w is not None:
                deps.add(b.last_w)
            deps.update(b.readers.values())
        if self.pending[eng]:
            deps.update(self.pending[eng])
            self.pending[eng] = set()
        if eng == "pe":
            deps = {d for d in deps if not (d.eng == "pe" and not d.dma)}
        op = Op(eng, fn, deps, dma)
        if dma:
            k = self.dma_cnt[eng] % NS_DMA
            self.dma_cnt[eng] += 1
            key = (eng, k)
            n = self.dma_uses.get(key, 0) + 1
            self.dma_uses[key] = n
            op.semkey = key
            op.val = 16 * n
            op.prev = self.dma_last.get(key)
            self.dma_last[key] = op
            self.recent_dma.append(op)
            self.all_dma.append(op)
        self.ops[eng].append(op)
        for b in reads:
            if b.psum:
                b.last_w = op
                b.readers = {}
            else:
                b.readers[(eng, id(op)) if dma else eng] = op
        for b in writes:
            b.last_w = op
            b.readers = {}
        return op

    def barrier(self, exclude=()):
        deps = set(self.recent_dma) - set(exclude)
        self.recent_dma = [d for d in self.recent_dma if d in set(exclude)]
        for e in self.ENGS:
            if self.ops[e]:
                last = self.ops[e][-1]
                if not last.dma:
                    deps.add(last)
                else:
                    for o in reversed(self.ops[e]):
                        if not o.dma:
                            deps.add(o)
                            break
        for e in self.ENGS:
            self.pending[e] = set(deps) | self.pending[e]

    def finalize(self):
        for e in self.ENGS:
            for op in self.ops[e]:
                for d in op.deps:
                    if not d.dma:
                        d.inc = True
        for e in self.ENGS:
            c = 0
            for op in self.ops[e]:
                if op.inc and not op.dma:
                    c += 1
                    op.seq = c

    def play(self, eng, handle, sems, dsems, final_wait=False):
        waited = {}

        def wait(key, semobj, val):
            if waited.get(key, 0) >= val:
                return
            handle.wait_ge(semobj, val)
            waited[key] = val

        for op in self.ops[eng]:
            for d in op.deps:
                if d.dma:
                    wait(d.semkey, dsems[d.semkey], d.val)
                else:
                    wait(d.eng, sems[d.eng], d.seq)
            if op.dma and op.prev is not None:
                wait(op.prev.semkey, dsems[op.prev.semkey], op.prev.val)
            inst = op.fn(handle)
            if op.dma:
                inst.then_inc(dsems[op.semkey], 16)
            elif op.inc:
                inst.then_inc(sems[eng], 1)
        if final_wait:
            for key, op in self.dma_last.items():
                wait(key, dsems[key], op.val)


def _t5_bucket_np(rel):
    rel = np.asarray(rel, dtype=np.int64)
    nb = 16
    ret = np.where(rel > 0, nb, 0)
    n = np.abs(rel)
    max_exact = 8
    nf = np.maximum(n, 1).astype(np.float32)
    large = max_exact + (np.log(nf / np.float32(max_exact)) / np.float32(math.log(128 / max_exact))
                         * np.float32(nb - max_exact)).astype(np.int32)
    large = np.minimum(large, nb - 1)
    return ret + np.where(n < max_exact, n, large)


def build_program():
    nc = bass.Bass("TRN2", target_bir_lowering=False)
    T = Tracker()

    def din(name, shape, dt=F32):
        return nc.dram_tensor(name, list(shape), dt, kind="ExternalInput")

    x_d = din("x", [2, SEQ, D])
    meta_d = din("meta", [NMETA, D])
    win_l = din("w_in_l", [1024, 2048])
    wo_l = din("w_o_l", [1024, 1024])
    wg_l = din("w_gate_l", [1408, 2048])
    wu_l = din("w_up_l", [1408, 2048])
    wd_l = din("w_down_l", [DFF, 1024])
    pw_l = din("pool_w_l", [128, 512])
    colpack_d = din("colpack", [128, 32])
    fgb_d = din("fgb", [128, D])
    lam_d = din("lamv", [128, 256])
    relb_d = din("relb", [32, 4])
    oh_d = din("onehot", [32, NLUT])
    ident_d = din("ident", [128, 128])
    aident_d = din("aident", [128, 128])
    out_d = nc.dram_tensor("out", [2, SEQ, D], F32, kind="ExternalOutput")

    win_b = nc.dram_tensor("win_b", [1024, 2048], BF16, kind="Internal")
    wo_b = nc.dram_tensor("wo_b", [1024, 1024], BF16, kind="Internal")
    wg_b = nc.dram_tensor("wg_b", [1408, 2048], BF16, kind="Internal")
    wu_b = nc.dram_tensor("wu_b", [1408, 2048], BF16, kind="Internal")
    wd_b = nc.dram_tensor("wd_b", [DFF, 1024], BF16, kind="Internal")
    lut_d = nc.dram_tensor("lut_d", [4, NLUT], F32, kind="Internal")

    ARENA = 131072
    ctxs = dict(
        arena=nc.sbuf_tensor("arena", [128, ARENA // 2], BF16),
        h1=nc.sbuf_tensor("h1", [128, 16 * D], F32),
        ident=nc.sbuf_tensor("ident_sb", [128, 128], F32),
        gbc=nc.sbuf_tensor("gbc", [128, D], F32),
        fgb=nc.sbuf_tensor("fgb_sb", [128, D], F32),
        pw=nc.sbuf_tensor("pw_sb", [128, 512], BF16),
        identb=nc.sbuf_tensor("identb_sb", [128, 128], BF16),
        jb=nc.sbuf_tensor("jb_sb", [128, 128], BF16),
        cols=nc.sbuf_tensor("cols", [128, 64], F32),
        lamv=nc.sbuf_tensor("lamv_sb", [128, 256], F32),
        sm=nc.sbuf_tensor("sm", [128, 128], F32),
        relb=nc.sbuf_tensor("relb_sb", [32, 4], F32),
        oh=nc.sbuf_tensor("oh_sb", [32, NLUT], F32),
        ps=nc.psum_tensor("ps", [128, 8 * 512], F32),
    )
    sem_names = ["pe", "act", "dve", "pool", "sp"]
    from contextlib import ExitStack
    with ExitStack() as es:
        tens = {k: es.enter_context(v) for k, v in ctxs.items()}
        sems = {e: es.enter_context(nc.semaphore("s_" + e)) for e in sem_names}
        dsems = {}
        for e in ("sp", "pool"):
            for k in range(NS_DMA):
                dsems[(e, k)] = es.enter_context(nc.semaphore(f"d_{e}{k}"))

        arena_bf = tens["arena"]
        arena_f = arena_bf.bitcast(F32)
        h1 = tens["h1"]
        ident = tens["ident"]
        gbc = tens["gbc"]
        fgb = tens["fgb"]
        pw = tens["pw"]
        cols = tens["cols"]
        lamv = tens["lamv"]
        sm = tens["sm"]
        relb = tens["relb"]
        ohsb = tens["oh"]
        ps = tens["ps"]
        psb = ps.bitcast(BF16)
        identb = tens["identb"]
        jb = tens["jb"]

        def bfv(off, n):
            return arena_bf[:, off // 2: off // 2 + n]

        def f32v(off, n):
            return arena_f[:, off // 4: off // 4 + n]

        def bank(i):
            return ps[:, i * 512:(i + 1) * 512]

        def bankb(i):
            return psb[:, i * 1024:(i + 1) * 1024]

        g1col = cols[:, 0:8]
        g2col = cols[:, 8:16]
        pscol = cols[:, 16:20]
        sgcol = cols[:, 20:21]
        sg08 = cols[:, 21:22]
        farb = cols[:, 24:32]
        neglam = cols[:, 32:33]
        eps6 = cols[:, 33:34]
        eps5 = cols[:, 34:35]
        lam_s = cols[:, 36:38]
        lam_e = cols[:, 38:40]
        ssA = sm[:, 0:2]
        lnA = sm[:, 2:4]
        rsA = sm[:, 4:6]
        ss2 = sm[:, 8:24]
        ln2 = sm[:, 24:32]
        rs2 = sm[:, 32:40]
        ss3 = sm[:, 40:48]
        ln3 = sm[:, 48:56]
        rs3 = sm[:, 56:64]
        rz = sm[:, 64:72]
        rz1n = sm[:, 72:76]
        ssb = sm[:, 76:80]
        lnb = sm[:, 80:84]
        rsb = sm[:, 84:88]

        B = lambda name, psum=False: Buf(name, psum)
        b_banks = [B(f"bank{i}", True) for i in range(8)]
        b_h1 = [B(f"h1_{i}") for i in range(16)]
        b_ident = B("ident")
        b_gbc = B("gbc")
        b_fgb = B("fgb")
        b_pw = B("pw")
        b_cols = B("cols")
        b_lamv = B("lamv")
        b_lams = B("lams")
        b_relb = B("relb")
        b_oh = B("oh")
        b_winb, b_wob, b_wgb, b_wub, b_wdb = B("winb"), B("wob"), B("wgb"), B("wub"), B("wdb")
        b_lutd = B("lutd")
        b_ss2 = [B(f"ss2_{i}") for i in range(16)]

        emit = T.emit

        def dma_sp(out, in_, reads, writes):
            return emit("sp", lambda e, o=out, i=in_: e.dma_start(out=o, in_=i), reads, writes, dma=True)

        def dma_pool(out, in_, reads, writes):
            return emit("pool", lambda e, o=out, i=in_: e.dma_start(out=o, in_=i), reads, writes, dma=True)

        def mm(out, lhsT, rhs, start, stop, reads, writes, skip=False):
            return emit("pe", lambda e, o=out, l=lhsT, r=rhs, s=start, t=stop, k=skip:
                        e.matmul(o, lhsT=l, rhs=r, start=s, stop=t, skip_group_check=k), reads, writes)

        def tr(out, in_, npart, reads, writes, bf=False):
            idn = (identb if bf else ident)[0:npart, 0:npart]
            return emit("pe", lambda e, o=out, i=in_, d=idn: e.transpose(o, i, d), list(reads) + [b_ident], writes)

        def act(out, in_, func, reads, writes, bias=None, scale=None, accum=None):
            def fn(e, o=out, i=in_, f=func, b=bias, s=scale, a=accum):
                kw = {}
                if b is not None:
                    kw["bias"] = b
                if s is not None:
                    kw["scale"] = s
                if a is not None:
                    kw["accum_out"] = a
                return e.activation(out=o, in_=i, func=f, **kw)
            return emit("act", fn, reads, writes)

        def dve(fn, reads, writes):
            return emit("dve", fn, reads, writes)

        def pool(fn, reads, writes):
            return emit("pool", fn, reads, writes)

        dma_sp(cols[:, 0:32], colpack_d.ap(), [], [b_cols])
        dma_sp(ident[:], ident_d.ap(), [], [b_ident])
        dma_sp(lamv[:], lam_d.ap(), [], [b_lamv])
        dma_sp(relb[:], relb_d.ap(), [], [b_relb])
        dma_sp(ohsb[:], oh_d.ap(), [], [b_oh])
        win0 = bfv(67600, 8 * 2048).rearrange("p (c n) -> p c n", c=8)
        b_win0 = B("win0")
        b_stg = [B(f"wstg{c}") for c in range(8)]
        for c in range(8):
            dma_sp(h1[:, c * 2048:(c + 1) * 2048],
                   win_l.ap().rearrange("(p a) n -> p a n", p=128)[:, c, :], [], [b_stg[c]])
        dma_sp(fgb[:], fgb_d.ap(), [], [b_fgb])
        b_win0c = [B(f"win0_{c}") for c in range(8)]
        for c in range(8):
            dve(lambda e, c=c: e.tensor_copy(out=win0[:, c, :], in_=h1[:, c * 2048:(c + 1) * 2048]),
                [b_stg[c]], [b_win0c[c]])
        early_ex = [dma_sp(win_b.ap().rearrange("(p a) n -> p (a n)", p=128), win0.rearrange("p c n -> p (c n)"),
                           b_win0c, [b_winb])]

        dve(lambda e: e.tensor_copy(out=identb[:], in_=ident[:]), [b_ident], [b_ident])
        b_jb = B("jb")
        jstage = f32v(8192, 128)
        b_jst = B("jstage")
        dma_sp(jstage, aident_d.ap(), [], [b_jst])
        dve(lambda e: e.tensor_copy(out=jb[:], in_=jstage), [b_jst], [b_jb])
        dve(lambda e: e.memset(cols[:, 33:34], 1e-6), [], [b_cols])
        dve(lambda e: e.memset(cols[:, 34:35], 1e-5), [], [b_cols])
        dve(lambda e: e.tensor_scalar(out=sg08, in0=sgcol, scalar1=float(1.0 - LAMBDA_INIT), scalar2=None,
                                      op0=ALU.mult), [b_cols], [b_cols])
        negbb = cols[:, 40:44]
        ea = cols[:, 44:48]
        dve(lambda e: e.tensor_scalar(out=negbb, in0=farb[:, 0:4], scalar1=-1.0, scalar2=None, op0=ALU.mult),
            [b_cols], [b_cols])
        dve(lambda e: e.tensor_tensor(out=ea, in0=farb[:, 4:8], in1=farb[:, 0:4], op=ALU.subtract),
            [b_cols], [b_cols])
        b_lamp = B("lamp")
        lamp = f32v(0, 128)
        lamj = bfv(1024, 128)
        dve(lambda e: e.tensor_tensor(out=lamp[:, 0:64], in0=lamv[:, 0:64], in1=lamv[:, 64:128], op=ALU.mult),
            [b_lamv], [b_lamp])
        dve(lambda e: e.tensor_tensor(out=lamp[:, 64:128], in0=lamv[:, 128:192], in1=lamv[:, 192:256], op=ALU.mult),
            [b_lamv], [b_lamp])
        b_lamj = B("lamj")
        act(lamj[:, 0:64], lamp[:, 0:64], AF.Copy, [b_lamp], [b_lamj, b_lams], accum=lam_s[:, 0:1])
        act(lamj[:, 0:64], lamp[:, 64:128], AF.Copy, [b_lamp], [b_lamj, b_lams], accum=lam_s[:, 1:2])
        act(lam_e, lam_s, AF.Exp, [b_lams], [b_lams])
        dve(lambda e: e.tensor_tensor(out=neglam, in0=lam_e[:, 1:2], in1=lam_e[:, 0:1], op=ALU.subtract),
            [b_lams], [b_cols])
        dve(lambda e: e.tensor_scalar(out=neglam, in0=neglam, scalar1=float(-LAMBDA_INIT), scalar2=None,
                                      op0=ALU.add), [b_cols], [b_cols])
        lutsb = f32v(4096, NLUT)
        b_lutsb = B("lutsb")
        for half in range(2):
            mm(bank(0)[0:4, 0:448], relb[:, :], ohsb[:, half * 448:(half + 1) * 448], True, True,
               [b_relb, b_oh], [b_banks[0]])
            dve(lambda e, hf=half: e.tensor_copy(out=lutsb[0:4, hf * 448:(hf + 1) * 448], in_=bank(0)[0:4, 0:448]),
                [b_banks[0]], [b_lutsb])
        dma_sp(lut_d.ap(), lutsb[0:4, :], [b_lutsb], [b_lutd])
        T.barrier(exclude=early_ex)

        O_QT, O_KT, O_U, O_VX, O_X = 0, 16640, 33280, 49920, 67600
        PW_ = 2080
        QT = bfv(O_QT, 4 * PW_).rearrange("p (h n) -> p h n", h=4)
        KT = bfv(O_KT, 4 * PW_).rearrange("p (h n) -> p h n", h=4)
        UU = bfv(O_U, 4 * PW_).rearrange("p (h n) -> p h n", h=4)
        VX = bfv(O_VX, NKT * 4 * 130).rearrange("p (k h n) -> p k h n", k=NKT, h=4)

        def gain_tile(gcol, ones_ap, b_ones):
            for c in range(8):
                dve(lambda e, c=c: e.tensor_scalar(out=gbc[:, c * 128:(c + 1) * 128], in0=ones_ap,
                                                   scalar1=gcol[:, c:c + 1], scalar2=None, op0=ALU.mult),
                    [b_cols, b_ones], [b_gbc])

        for s in range(2):
            b_QT = [B(f"QT{i}") for i in range(5)]
            b_KT = [B(f"KT{i}") for i in range(NKT)]
            b_U = [B(f"U{i}") for i in range(5)]
            b_Upad = B("Upad")
            b_VX = [B(f"VX{i}") for i in range(NKT)]
            win = bfv(O_X, 8 * 2048).rearrange("p (c n) -> p c n", c=8)
            uT = [bfv(O_X + 32768 + i * 8192, 8 * 512).rearrange("p (c n) -> p c n", c=8) for i in range(2)]
            xs = [f32v(O_X + 49152 + i * 4096, 1024) for i in range(2)]
            xb = [bfv(O_X + 57344 + i * 2048, 1024) for i in range(2)]
            b_win = B("win")
            b_uT = [B("uT0"), B("uT1")]
            b_xs = [B("xs0"), B("xs1")]
            b_xb = [B("xb0"), B("xb1")]
            b_ssA = [B("ssA0"), B("ssA1")]

            rd_win = b_win0c if s == 0 else [b_win]
            if s == 0:
                pass
            else:
                dma_sp(win.rearrange("p c n -> p (c n)"), win_b.ap().rearrange("(p a) n -> p (a n)", p=128),
                       [b_winb], [b_win])
            dve(lambda e: e.memset(xs[1][:, 0:128], 1.0), [], [b_xs[1]])
            gain_tile(g1col, xs[1][:, 0:128], b_xs[1])
            pool(lambda e: e.memset(UU[:, :, 2064:2080], 0.0), [], [b_Upad])
            pool(lambda e: e.memset(VX[:, :, :, 128:130], 1.0), [], b_VX)

            def load_tile(ti, slot):
                np_ = NMETA if ti < 0 else 128
                src = meta_d.ap() if ti < 0 else x_d.ap()[s, ti * 128:(ti + 1) * 128, :]
                dma_sp(xs[slot][0:np_, :], src, [], [b_xs[slot]])

            def norm_compute(ti, slot):
                np_ = NMETA if ti < 0 else 128
                xt = xs[slot]
                xbt = xb[slot]
                act(xbt[0:np_, :], xt[0:np_, :], AF.Square, [b_xs[slot]], [b_xb[slot], b_ssA[slot]],
                    accum=ssA[0:np_, slot:slot + 1])
                act(lnA[0:np_, slot:slot + 1], ssA[0:np_, slot:slot + 1], AF.Ln, [b_ssA[slot], b_cols], [b_ssA[slot]],
                    bias=eps6[0:np_, :], scale=1.0 / D)
                act(rsA[0:np_, slot:slot + 1], lnA[0:np_, slot:slot + 1], AF.Exp, [b_ssA[slot]], [b_ssA[slot]],
                    scale=-0.5)
                act(xbt[0:np_, :], xt[0:np_, :], AF.Copy, [b_ssA[slot], b_xs[slot]], [b_xb[slot]],
                    scale=rsA[0:np_, slot:slot + 1])

            def norm_transpose(ti, slot):
                np_ = NMETA if ti < 0 else 128
                xbt = xb[slot]
                pb = slot
                for c in range(8):
                    tr(bankb(pb)[:, c * 128:c * 128 + np_], xbt[0:np_, c * 128:(c + 1) * 128], np_,
                       [b_xb[slot]], [b_banks[pb]], bf=True)
                p0 = 0 if ti < 0 else NMETA + ti * 128
                a = p0
                while a < p0 + np_:
                    pc = a // 512
                    bnd = min(p0 + np_, (pc + 1) * 512)
                    n = bnd - a
                    off = a - p0
                    src_ps = bankb(pb).rearrange("p (c n) -> p c n", c=8)[:, :, off:off + n]
                    dst = uT[pc % 2][:, :, a - pc * 512:a - pc * 512 + n]
                    g = gbc[:, :].rearrange("p (c n) -> p c n", c=8)[:, :, 0:n]
                    dve(lambda e, o=dst, i=src_ps, g=g: e.tensor_tensor(out=o, in0=i, in1=g, op=ALU.mult),
                        [b_banks[pb], b_gbc], [b_uT[pc % 2]])
                    a = bnd

            kindc = [0]

            def inproj_groups(pc):
                n = 512 if pc < 4 else 16
                u = uT[pc % 2]
                bu = b_uT[pc % 2]
                pos0 = pc * 512
                groups = []
                for grp in range(3):
                    for h in range(4):
                        def g_(grp=grp, h=h):
                            bi = 2 + (kindc[0] % 6)
                            kindc[0] += 1
                            colbase = {0: 512, 1: 1024, 2: 0}[grp] + h * 128
                            for c in range(8):
                                mm(bank(bi)[:, 0:n], win[:, c, colbase:colbase + 128], u[:, c, 0:n], c == 0, c == 7,
                                   rd_win + [bu], [b_banks[bi]])
                            if grp == 0:
                                dve(lambda e, o=QT[:, h, pos0:pos0 + n], i=bank(bi)[:, 0:n]:
                                    e.tensor_scalar(out=o, in0=i, scalar1=0.125, scalar2=None, op0=ALU.mult),
                                    [b_banks[bi]], [b_QT[pc]])
                            elif grp == 1:
                                wr = [b_KT[k] for k in range(pos0 // 128, (pos0 + n - 1) // 128 + 1)]
                                dve(lambda e, o=KT[:, h, pos0:pos0 + n], i=bank(bi)[:, 0:n]:
                                    e.tensor_copy(out=o, in_=i), [b_banks[bi]], wr)
                            else:
                                act(UU[:, h, pos0:pos0 + n], bank(bi)[:, 0:n], AF.Copy, [b_banks[bi]], [b_U[pc]])
                        groups.append(g_)
                nt = (n + 127) // 128
                for kk in range(nt):
                    def gv(kk=kk):
                        m = min(128, n - kk * 128)
                        kt = pos0 // 128 + kk
                        bi = 2 + (kindc[0] % 6)
                        kindc[0] += 1
                        for c in range(8):
                            mm(bank(bi)[0:m, :], u[:, c, kk * 128:kk * 128 + m], win[:, c, 1536:2048], c == 0, c == 7,
                               rd_win + [bu], [b_banks[bi]])
                        src_ps = bank(bi)[0:m, :].rearrange("p (h n) -> p h n", h=4)
                        if kk % 2 == 0:
                            dve(lambda e, o=VX[0:m, kt, :, 0:128], sp_=src_ps: e.tensor_copy(out=o, in_=sp_),
                                [b_banks[bi]], [b_VX[kt]])
                        else:
                            act(VX[0:m, kt, :, 0:128], src_ps, AF.Copy, [b_banks[bi]], [b_VX[kt]])
                    groups.append(gv)
                return groups

            tidx = 0
            order = [-1] + list(range(16))
            load_tile(order[0], 0)
            load_tile(order[1], 1)
            ptr = {"c": 0, "t": 0}

            def do_compute():
                k = ptr["c"]
                norm_compute(order[k], k % 2)
                if k + 2 < len(order):
                    load_tile(order[k + 2], k % 2)
                ptr["c"] += 1

            def norm_tile(ti, slot):
                if ptr["c"] == ptr["t"]:
                    do_compute()
                if ptr["c"] < len(order) and ptr["c"] == ptr["t"] + 1:
                    do_compute()
                k = ptr["t"]
                assert order[k] == ti
                norm_transpose(ti, k % 2)
                ptr["t"] += 1
            for ti in order[0:5]:
                norm_tile(ti, tidx % 2)
                tidx += 1
            nxt_tile = 5
            if s == 0:
                gate = [b_uT[0], b_uT[1]]
                dma_pool(wo_b.ap(), wo_l.ap(), gate, [b_wob])
                dma_pool(pw[:], pw_l.ap(), gate, [b_pw])
            for pc in range(5):
                groups = inproj_groups(pc)
                ng = len(groups)
                for gi, g_ in enumerate(groups):
                    g_()
                    if pc < 3 and gi % 4 == 3 and nxt_tile < len(order) and nxt_tile < 5 + 4 * (pc + 1):
                        norm_tile(order[nxt_tile], tidx % 2)
                        tidx += 1
                        nxt_tile += 1
                while pc < 3 and nxt_tile < 5 + 4 * (pc + 1):
                    norm_tile(order[nxt_tile], tidx % 2)
                    tidx += 1
                    nxt_tile += 1
            T.barrier()

            from collections import deque
            wo = bfv(O_X, 8 * 1024).rearrange("p (c n) -> p c n", c=8)
            Tst = f32v(O_X + 16384, 4 * TW).rearrange("p (h n) -> p h n", h=4)
            o2 = O_X + 16384 + 12288
            NE = 4
            Et = [bfv(o2 + i * 1024, 512) for i in range(NE)]
            o2 += NE * 1024
            Tb = bfv(o2, 4 * TW).rearrange("p (h n) -> p h n", h=4)
            o2 += 4 * TW * 2
            yT_off = o2
            yT = [bfv(o2 + i * 4096, 8 * QC).rearrange("p (c n) -> p c n", c=8) for i in range(2)]
            o2 += 8192
            ot = [f32v(o2 + i * 1024, 256).rearrange("p (q n) -> p q n", q=2) for i in range(2)]
            o2 += 2048
            yt = [f32v(o2 + i * 1024, 256).rearrange("p (q n) -> p q n", q=2) for i in range(2)]
            ytb_all = [bfv(o2 + i * 1024, 256).rearrange("p (q n) -> p q n", q=2) for i in range(2)]
            o2 += 2048
            tA = f32v(o2, 272)
            tB = f32v(o2 + 1088, 272)
            o2 += 2176
            dT = [bfv(o2 + i * 2048, 4 * QC).rearrange("p (g n) -> p g n", g=4) for i in range(2)]
            o2 += 4096
            junkB = bfv(o2, 1024)
            o2 += 2048
            Qb = [bfv(o2 + i * 1024, 512) for i in range(2)]
            o2 += 2048
            assert o2 <= ARENA, o2
            b_wo = B("wo")
            b_E = [B(f"E{i}") for i in range(NE)]
            b_sb = [B(f"sb{i}") for i in range(2)]
            b_yT = [B(f"yT{i}") for i in range(2)]
            b_ot = [B(f"ot{i}") for i in range(2)]
            b_yt = [B(f"yt{i}") for i in range(2)]
            b_tA, b_tB = B("tA"), B("tB")
            b_dT = [B(f"dT{i}") for i in range(2)]
            b_junkB = B("junkB")
            b_rz = [B("rz0"), B("rz1")]
            b_ssb = [B("ssb0"), B("ssb1")]
            b_Qb = [B("Qb0"), B("Qb1")]

            dma_sp(wo.rearrange("p c n -> p (c n)"), wo_b.ap().rearrange("(p a) n -> p (a n)", p=128),
                   [b_wob], [b_wo])
            for i in range(2):
                pool(lambda e, i=i: e.memset(Qb[i][:, :], 0.0), [], [b_Qb[i]])
            b_Tst = [B(f"Tst{h}") for h in range(4)]
            b_Tb = B("Tb")
            Hb = bfv(yT_off, 4 * TW)
            for h in range(4):
                src = bass.AP(lut_d, h * NLUT, [[1, 128], [1, TW]])
                dma_sp(Tst[:, h, :], src, [b_lutd], [b_Tst[h]])
                act(Hb[:, h * TW:(h + 1) * TW], Tst[:, h, :], AF.Exp, [b_Tst[h], b_cols], b_yT, bias=negbb[:, h:h + 1])
            tb_banks = [7, 3, 4, 5]
            k = 0
            for h in range(4):
                for half in range(2):
                    bk = tb_banks[k % 4]
                    k += 1
                    mm(bank(bk)[:, 0:384], jb[:, :], Hb[:, h * TW + half * 384:h * TW + (half + 1) * 384], True, True,
                       b_yT + [b_jb], [b_banks[bk]])
                    dve(lambda e, o=Tb[:, h, half * 384:(half + 1) * 384], i=bank(bk)[:, 0:384]:
                        e.tensor_copy(out=o, in_=i), [b_banks[bk]], [b_Tb])

            import heapq
            deferred = []
            dseq = [0]

            def defer(at, fn):
                dseq[0] += 1
                heapq.heappush(deferred, (at, dseq[0], fn))

            def run_deferred(cur, nmax):
                k = 0
                while deferred and k < nmax and deferred[0][0] <= cur:
                    heapq.heappop(deferred)[2]()
                    k += 1

            def pool_stage(cq, g, w):
                q0 = NMETA + cq * QC
                ysl = cq % 2

                def zz(lo, n):
                    return UU[:, g, q0 + lo:q0 + lo + n]
                rdU = [b_U[min(4, (q0 - 8) // 512)], b_U[min(4, (q0 + 263) // 512)], b_Upad]

                def padd(o, a, b, reads, writes):
                    pool(lambda e, o=o, a=a, b=b: e.tensor_tensor(out=o, in0=a, in1=b, op=ALU.add), reads, writes)
                if w == 2:
                    padd(tA[:, 0:256], zz(0, 256), zz(-1, 256), rdU, [b_tA])
                    ws, wb, wsl = tA, b_tA, 0
                else:
                    padd(tA[:, 1:272], zz(-7, 271), zz(-8, 271), rdU, [b_tA])
                    padd(tB[:, 2:271], tA[:, 3:272], tA[:, 1:270], [b_tA], [b_tB])
                    ws, wb, wsl = tB, b_tB, 8
                    if w >= 8:
                        padd(tA[:, 4:269], tB[:, 6:271], tB[:, 2:267], [b_tB], [b_tA])
                        ws, wb = tA, b_tA
                    if w == 16:
                        padd(tB[:, 8:265], tA[:, 12:269], tA[:, 4:261], [b_tA], [b_tB])
                        ws, wb = tB, b_tB
                def dpart():
                    dve(lambda e, o=dT[ysl][:, g, :], a=ws[:, wsl:wsl + 256], sc_=1.0 / w, b=zz(0, 256):
                        e.scalar_tensor_tensor(out=o, in0=a, scalar=sc_, in1=b, op0=ALU.mult, op1=ALU.subtract),
                        [wb] + rdU, [b_dT[ysl]])
                    if cq == NQC - 1:
                        right = w - 1 - w // 2
                        for r in range(right):
                            cnt = w - (right - r)
                            col = 255 - r
                            dve(lambda e, o=dT[ysl][:, g, col:col + 1], a=ws[:, wsl + col:wsl + col + 1], sc_=1.0 / cnt,
                                b=zz(col, 1):
                                e.scalar_tensor_tensor(out=o, in0=a, scalar=sc_, in1=b, op0=ALU.mult, op1=ALU.subtract),
                                [wb] + rdU, [b_dT[ysl]])
                return dpart

            def poolmm_stage(cq, g):
                ysl = cq % 2
                mb = 7
                mm(bank(mb)[:, 0:QC], pw[:, g * 128:(g + 1) * 128], dT[ysl][:, g, :], True, True,
                   [b_pw, b_dT[ysl]], [b_banks[mb]])
                dve(lambda e, o=yT[ysl][:, g, :], i=bank(mb)[:, 0:QC], sc_=pscol[:, g:g + 1]:
                    e.tensor_scalar(out=o, in0=i, scalar1=sc_, scalar2=None, op0=ALU.mult),
                    [b_banks[mb], b_cols], [b_yT[ysl]])

            def epilogue_stages(cq, h, osl):
                ysl = cq % 2
                ba, bb = 3 + 2 * osl, 4 + 2 * osl
                rzs = rz[:, osl * 4:(osl + 1) * 4]
                r1n = rz1n[:, osl * 2:(osl + 1) * 2]
                sss = ssb[:, osl * 2:(osl + 1) * 2]
                lns = lnb[:, osl * 2:(osl + 1) * 2]
                rss = rsb[:, osl * 2:(osl + 1) * 2]
                mb = 7

                def st1():
                    for c, bk in ((0, ba), (1, bb)):
                        zsrc = bank(bk)[:, 0:258].rearrange("p (q n) -> p q n", n=129)[:, :, 128:129]
                        zdst = rzs[:, c * 2:(c + 1) * 2].rearrange("p (q o) -> p q o", o=1)
                        dve(lambda e, o=zdst, i=zsrc: e.reciprocal(out=o, in_=i), [b_banks[bk]], [b_rz[osl]])
                    dve(lambda e, o=r1n, i=rzs[:, 2:4]: e.tensor_scalar(out=o, in0=i, scalar1=neglam, scalar2=None,
                                                                         op0=ALU.mult), [b_rz[osl], b_cols], [b_rz[osl]])

                def st2():
                    for qs in range(2):
                        dve(lambda e, o=ot[osl][:, qs, :], i=bank(ba)[:, qs * 129:qs * 129 + 128], sc_=rzs[:, qs:qs + 1]:
                            e.tensor_scalar(out=o, in0=i, scalar1=sc_, scalar2=None, op0=ALU.mult),
                            [b_banks[ba], b_rz[osl]], [b_ot[osl]])

                def st3():
                    for qs in range(2):
                        dve(lambda e, o=ot[osl][:, qs, :], i=bank(bb)[:, qs * 129:qs * 129 + 128], sc_=r1n[:, qs:qs + 1]:
                            e.scalar_tensor_tensor(out=o, in0=i, scalar=sc_, in1=o, op0=ALU.mult, op1=ALU.add),
                            [b_banks[bb], b_rz[osl], b_ot[osl]], [b_ot[osl]])

                def st4():
                    for qs in range(2):
                        dve(lambda e, o=junkB[:, 0:128], i=ot[osl][:, qs, :], a=sss[:, qs:qs + 1]:
                            e.scalar_tensor_tensor(out=o, in0=i, scalar=1.0, in1=i, op0=ALU.mult, op1=ALU.mult,
                                                   accum_out=a),
                            [b_ot[osl]], [b_junkB, b_ssb[osl]])

                def st5():
                    act(lns, sss, AF.Ln, [b_ssb[osl], b_cols], [b_ssb[osl]], bias=eps5, scale=1.0 / 128)
                    act(rss, lns, AF.Exp, [b_ssb[osl]], [b_ssb[osl]], scale=-0.5)

                ytb = ytb_all[osl]

                def st6():
                    for qs in range(2):
                        dve(lambda e, o=ytb[:, qs, :], i=ot[osl][:, qs, :], sc_=rss[:, qs:qs + 1]:
                            e.tensor_scalar(out=o, in0=i, scalar1=sc_, scalar2=None, op0=ALU.mult),
                            [b_ot[osl], b_ssb[osl]], [b_yt[osl]])

                def st7():
                    for qs in range(2):
                        tr(bankb(mb)[:, qs * 128:(qs + 1) * 128], ytb[:, qs, :], 128, [b_yt[osl]], [b_banks[mb]], bf=True)

                def st8():
                    dve(lambda e, o=yT[ysl][:, 4 + h, :], i=bankb(mb)[:, 0:QC]:
                        e.tensor_scalar(out=o, in0=i, scalar1=sg08, scalar2=None, op0=ALU.mult),
                        [b_banks[mb], b_cols], [b_yT[ysl]])
                def st78():
                    st7()
                    st8()
                return [(0, st1), (2, st2), (4, st3), (6, st4), (11, st5), (15, st6), (19, st78)]

            def wo_stages(cq):
                ysl = cq % 2
                sts = []
                k = 0
                for t2 in range(2):
                    tile_i = cq * 2 + t2
                    for half in range(4):
                        def st(t2=t2, half=half, tile_i=tile_i):
                            mb = 7
                            for c in range(8):
                                mm(bank(mb)[:, 0:256], yT[ysl][:, c, t2 * 128:(t2 + 1) * 128],
                                   wo[:, c, half * 256:(half + 1) * 256],
                                   c == 0, c == 7, [b_yT[ysl], b_wo], [b_banks[mb]])
                            hsl = h1[:, tile_i * D + half * 256:tile_i * D + (half + 1) * 256]
                            dve(lambda e, hsl=hsl, mb=mb: e.tensor_tensor(out=hsl, in0=bank(mb)[:, 0:256], in1=hsl, op=ALU.add),
                                [b_banks[mb], b_h1[tile_i]], [b_h1[tile_i]])
                        sts.append((24 + 2 * k, st))
                        k += 1

                    def stq(tile_i=tile_i):
                        dve(lambda e, o=junkB[:, :], i=h1[:, tile_i * D:(tile_i + 1) * D], a=ss2[:, tile_i:tile_i + 1]:
                            e.scalar_tensor_tensor(out=o, in0=i, scalar=1.0, in1=i, op0=ALU.mult, op1=ALU.mult,
                                                   accum_out=a),
                            [b_h1[tile_i]], [b_junkB, b_ss2[tile_i]])
                    sts.append((25 + 2 * (k - 1), stq))
                return sts

            unit = 0

            def av(cqh, j, esl, kn, osl, h, pos):
                ba, bb = 3 + 2 * osl, 4 + 2 * osl
                for c, bk in ((0, ba), (1, bb)):
                    for qs in range(2):
                        mm(bank(bk)[:, qs * 129:(qs + 1) * 129],
                           Et[esl][0:kn, c * QC + qs * 128:c * QC + (qs + 1) * 128],
                           VX[0:kn, j, h, 0:129], (pos == 0 and qs == 0), pos == NKT - 1,
                           [b_E[esl], b_VX[j]], [b_banks[bk]], skip=True)
                if FILLER:
                    mm(bank(bb)[:, 258:258 + FILLER], Et[esl][0:kn, QC + 128:QC + 256], Qb[osl][0:kn, 0:FILLER],
                       False, False, [b_E[esl], b_Qb[osl]], [b_banks[bb]], skip=True)
                if pos == NKT - 1:
                    for off, st in epilogue_stages(cqh, h, osl):
                        defer(unit + off, st)
                    if h == 3:
                        for off, st in wo_stages(cqh):
                            defer(unit + off, st)

            def emit_qb(cq_, h_, slot):
                q0_ = NMETA + cq_ * QC
                for c in range(2):
                    pool(lambda e, o=Qb[slot][c * 64:(c + 1) * 64, c * QC:(c + 1) * QC],
                         i=QT[c * 64:(c + 1) * 64, h_, q0_:q0_ + QC]: e.tensor_copy(out=o, in_=i),
                         [b_QT[min(4, q0_ // 512)], b_QT[min(4, (q0_ + QC - 1) // 512)]], [b_Qb[slot]])

            units = []
            hq_ = 0
            def _is_near(cq, j):
                kn_ = 128 if j < NKT - 1 else LPOS - 128 * (NKT - 1)
                Dd_ = j * 128 - (NMETA + cq * QC)
                return not (Dd_ + kn_ - 1 <= -128 or Dd_ - (QC - 1) >= 128)
            for cq in range(NQC):
                jorder = list(range(NKT))
                for h in range(4):
                    for pos, j in enumerate(jorder):
                        units.append(dict(cq=cq, h=h, j=j, pos=pos, sl=hq_ % 2,
                                          kn=128 if j < NKT - 1 else LPOS - 128 * (NKT - 1)))
                    hq_ += 1
            NU = len(units)

            def emit_qk(u):
                U_ = units[u]
                cq_, h_, j_, kn_, sl_ = U_["cq"], U_["h"], U_["j"], U_["kn"], U_["sl"]
                ssl_ = u % 3
                k0_ = j_ * 128
                mm(bank(ssl_)[0:kn_, :], KT[:, h_, k0_:k0_ + kn_], Qb[sl_][:, :], True, True,
                   [b_KT[j_], b_Qb[sl_]], [b_banks[ssl_]])

            emit_qb(0, 0, 0)
            emit_qk(0)
            emit_qk(1)
            cast_pieces = []
            if s == 0:
                for r in range(11):
                    cast_pieces.append((wg_b.ap()[r * 128:(r + 1) * 128, :], wg_l.ap()[r * 128:(r + 1) * 128, :], b_wgb))
                    cast_pieces.append((wu_b.ap()[r * 128:(r + 1) * 128, :], wu_l.ap()[r * 128:(r + 1) * 128, :], b_wub))
                for r in range(11):
                    cast_pieces.append((wd_b.ap()[r * 256:(r + 1) * 256, :], wd_l.ap()[r * 256:(r + 1) * 256, :], b_wdb))
            for u in range(NU):
                U_ = units[u]
                cq, h, j, kn, sl, pos = U_["cq"], U_["h"], U_["j"], U_["kn"], U_["sl"], U_["pos"]
                q0 = NMETA + cq * QC
                unit = u + 1
                if h == 0 and pos == 0:
                    for t2 in range(2):
                        tile_i = cq * 2 + t2
                        dma_sp(h1[:, tile_i * D:(tile_i + 1) * D], x_d.ap()[s, tile_i * 128:(tile_i + 1) * 128, :],
                               [], [b_h1[tile_i]])
                    for g, w in enumerate((2, 4, 8, 16)):
                        def pst(cq=cq, g=g, w=w, at=unit + 1 + 8 * g + 6):
                            dpart = pool_stage(cq, g, w)
                            defer(at, dpart)
                        defer(unit + 1 + 8 * g, pst)
                if h == 2 and pos == 0:
                    for g in range(4):
                        defer(unit + 3 * g, lambda cq=cq, g=g: poolmm_stage(cq, g))
                ssl = u % 3
                esl = u % NE
                k0 = j * 128
                Dd = k0 - q0
                s_ps = bank(ssl)[0:kn, :]
                e_out = Et[esl][0:kn, :]
                maxrel = Dd + kn - 1
                minrel = Dd - (QC - 1)
                if minrel >= 128:
                    act(e_out, s_ps, AF.Exp, [b_banks[ssl], b_cols], [b_E[esl]], bias=ea[0:kn, h:h + 1])
                else:
                    act(e_out, s_ps, AF.Exp, [b_banks[ssl]], [b_E[esl]])
                if maxrel <= -128 or minrel >= 128:
                    pass
                else:
                    i0 = 368 - Dd
                    assert 0 <= i0 <= TW - QC, (i0, Dd)
                    tb0 = Tb[0:kn, h, i0:i0 + QC]
                    tbb = bass.AP(tb0.tensor, tb0.offset, [list(tb0.ap[0]), [0, 2], list(tb0.ap[1])])
                    e3 = Et[esl][0:kn, :].rearrange("p (c n) -> p c n", c=2)
                    dve(lambda e, o=e3, b=tbb: e.tensor_tensor(out=o, in0=o, in1=b, op=ALU.mult),
                        [b_Tb, b_E[esl]], [b_E[esl]])
                if pos == 4 and cast_pieces:
                    o_, i_, bb_ = cast_pieces.pop(0)
                    dma_pool(o_, i_, [], [bb_])
                if pos == 8 and u + 9 < NU:
                    nU = units[u + 9]
                    emit_qb(nU["cq"], nU["h"], nU["sl"])
                if u + 2 < NU:
                    emit_qk(u + 2)
                if u >= 1:
                    pU = units[u - 1]
                    av(pU["cq"], pU["j"], (u - 1) % NE, pU["kn"], pU["sl"], pU["h"], pU["pos"])
                run_deferred(unit, 1)
            pU = units[NU - 1]
            unit = NU + 1
            av(pU["cq"], pU["j"], (NU - 1) % NE, pU["kn"], pU["sl"], pU["h"], pU["pos"])
            while deferred:
                run_deferred(10 ** 9, 10 ** 6)
            while cast_pieces:
                o_, i_, bb_ = cast_pieces.pop(0)
                dma_pool(o_, i_, [], [bb_])
            T.barrier()

            fT = bfv(0, 8 * 1024).rearrange("p (c n) -> p c n", c=8)
            aT = bfv(16384, 22 * 1024).rearrange("p (k n) -> p k n", k=22)
            Wdh = [bfv(61440 + i * 22528, 22 * 512).rearrange("p (k n) -> p k n", k=22) for i in range(2)]
            wgs = [bfv(106496 + i * 8192, 2048).rearrange("p (c n) -> p c n", c=8) for i in range(2)]
            wus = [bfv(106496 + i * 8192 + 4096, 2048).rearrange("p (c n) -> p c n", c=8) for i in range(2)]
            hn2 = [bfv(122880 + i * 2048, 1024) for i in range(2)]
            sgt = [bfv(126976 + i * 1024, 512) for i in range(2)]
            junkC = bfv(129024, 1024)
            b_fT = [B(f"fT{i}") for i in range(8)]
            b_aT = [B(f"aT{i}") for i in range(4)]
            b_Wdh = [B("Wdh0"), B("Wdh1")]
            b_wgs = [B("wgs0"), B("wgs1")]
            b_hn2 = [B("hn2_0"), B("hn2_1")]
            b_sgt = [B("sgt0"), B("sgt1")]
            b_junkC = B("junkC")
            b_st = B("stats")
            dve(lambda e: e.memset(junkC[:, 0:128], 1.0), [], [b_junkC])
            gain_tile(g2col, junkC[:, 0:128], b_junkC)
            b_rs2 = [B("rs2_0"), B("rs2_1")]
            b_st3t = [B(f"st3_{i}") for i in range(8)]

            def c0_stats(sc):
                rd_ss = [b_ss2[t] for t in range(sc * 8, sc * 8 + 8)]
                act(ln2[:, :], ss2[:, sc * 8:(sc + 1) * 8], AF.Ln, rd_ss + [b_cols], [b_rs2[sc]], bias=eps6, scale=1.0 / D)
                act(rs2x[:, sc * 8:(sc + 1) * 8], ln2[:, :], AF.Exp, [b_rs2[sc]], [b_rs2[sc]], scale=-0.5)

            def c0_copy(sc, i):
                t = sc * 8 + i
                hs = i % 2
                act(hn2[hs][:, :], h1[:, t * D:(t + 1) * D], AF.Copy, [b_h1[t], b_rs2[sc]], [b_hn2[hs]],
                    scale=rs2x[:, sc * 8 + i:sc * 8 + i + 1])

            def c0_tr(sc, i, pbase):
                hs = i % 2
                pb = pbase + hs
                for c in range(8):
                    tr(bankb(pb)[:, c * 128:(c + 1) * 128], hn2[hs][:, c * 128:(c + 1) * 128], 128,
                       [b_hn2[hs]], [b_banks[pb]], bf=True)
                src_ps = bankb(pb).rearrange("p (c n) -> p c n", c=8)
                dst = fT[:, :, i * 128:(i + 1) * 128]
                g = gbc[:, :].rearrange("p (c n) -> p c n", c=8)
                dve(lambda e, o=dst, i_=src_ps, g=g: e.tensor_tensor(out=o, in0=i_, in1=g, op=ALU.mult),
                    [b_banks[pb], b_gbc], [b_fT[i]])

            rs2x = sm[:, 88:104]
            c0_stats(0)
            c0_copy(0, 0)
            for i in range(8):
                if i + 1 < 8:
                    c0_copy(0, i + 1)
                c0_tr(0, i, 0)
            for sc in range(2):
                tiles = list(range(sc * 8, sc * 8 + 8))
                dma_sp(Wdh[0], wd_b.ap().rearrange("(k p) n -> p k n", p=128)[:, :, 0:512], [b_wdb], [b_Wdh[0]])
                gu = 0
                for j in range(11):
                    wsl = j % 2
                    dma_sp(wgs[wsl].rearrange("p c n -> p (c n)"), wg_b.ap()[j * 128:(j + 1) * 128, :],
                           [b_wgb], [b_wgs[wsl]])
                    dma_sp(wus[wsl].rearrange("p c n -> p (c n)"), wu_b.ap()[j * 128:(j + 1) * 128, :],
                           [b_wub], [b_wgs[wsl]])
                    for tc in range(2):
                        rdf = [b_fT[tc * 4 + k] for k in range(4)]
                        for sub in range(2):
                            gsl = gu % 2
                            gu += 1
                            bg, bu_ = 4 + 2 * gsl, 5 + 2 * gsl
                            for c in range(8):
                                mm(bank(bg), wgs[wsl][:, c, sub * 128:(sub + 1) * 128], fT[:, c, tc * 512:(tc + 1) * 512],
                                   c == 0, c == 7, [b_wgs[wsl]] + rdf, [b_banks[bg]])
                            for c in range(8):
                                mm(bank(bu_), wus[wsl][:, c, sub * 128:(sub + 1) * 128], fT[:, c, tc * 512:(tc + 1) * 512],
                                   c == 0, c == 7, [b_wgs[wsl]] + rdf, [b_banks[bu_]])
                            act(sgt[gsl][:, :], bank(bg), AF.Silu, [b_banks[bg]], [b_sgt[gsl]])
                            kf = 2 * j + sub
                            dve(lambda e, gsl=gsl, bu_=bu_, kf=kf, tc=tc:
                                e.tensor_tensor(out=aT[:, kf, tc * 512:(tc + 1) * 512], in0=bank(bu_), in1=sgt[gsl][:, :],
                                                op=ALU.mult), [b_banks[bu_], b_sgt[gsl]], [b_aT[tc * 2 + (kf % 2)]])
                dma_sp(Wdh[1], wd_b.ap().rearrange("(k p) n -> p k n", p=128)[:, :, 512:1024], [b_wdb], [b_Wdh[1]])
                dn = 0
                nxt = sc + 1 if sc + 1 < 2 else None
                if nxt is not None:
                    c0_stats(nxt)
                    c0_copy(nxt, 0)
                for half in range(2):
                    for i, t in enumerate(tiles):
                        bi = dn % 4
                        dn += 1
                        tc = i // 4
                        for k in range(22):
                            mm(bank(bi), aT[:, k, i * 128:(i + 1) * 128], Wdh[half][:, k, :], k == 0, k == 21,
                               [b_aT[tc * 2], b_aT[tc * 2 + 1], b_Wdh[half]], [b_banks[bi]])
                        hsl = h1[:, t * D + half * 512:t * D + (half + 1) * 512]
                        dve(lambda e, hsl=hsl, bi=bi: e.tensor_tensor(out=hsl, in0=bank(bi), in1=hsl, op=ALU.add),
                            [b_banks[bi], b_h1[t]], [b_h1[t]])
                        if nxt is not None and half == 0:
                            if i + 1 < 8:
                                c0_copy(nxt, i + 1)
                            c0_tr(nxt, i, 4)
                        if half == 1:
                            hfull = h1[:, t * D:(t + 1) * D]
                            act(junkC[:, :], hfull, AF.Square, [b_h1[t]], [b_junkC, b_st3t[i]], accum=ss3[:, i:i + 1])
                            act(ln3[:, i:i + 1], ss3[:, i:i + 1], AF.Ln, [b_st3t[i], b_cols], [b_st3t[i]], bias=eps6,
                                scale=1.0 / D)
                            act(rs3[:, i:i + 1], ln3[:, i:i + 1], AF.Exp, [b_st3t[i]], [b_st3t[i]], scale=-0.5)
                            dve(lambda e, hsl=hfull, i=i: e.scalar_tensor_tensor(out=hsl, in0=hsl, scalar=rs3[:, i:i + 1],
                                                                                in1=fgb[:, :], op0=ALU.mult, op1=ALU.mult),
                                [b_h1[t], b_st3t[i], b_fgb], [b_h1[t]])
                            dma_pool(out_d.ap()[s, t * 128:(t + 1) * 128, :], hfull, [b_h1[t]], [])
            T.barrier()

        T.finalize()
        with nc.Block() as block:
            @block.tensor
            def _(e):
                T.play("pe", e, sems, dsems)

            @block.scalar
            def _(e):
                T.play("act", e, sems, dsems)

            @block.vector
            def _(e):
                T.play("dve", e, sems, dsems)

            @block.gpsimd
            def _(e):
                T.play("pool", e, sems, dsems)

            @block.sync
            def _(e):
                T.play("sp", e, sems, dsems, final_wait=True)
    return nc


_CACHE = {}


def _host_layouts(inp):
    f = lambda a: np.ascontiguousarray(np.asarray(a, dtype=np.float32))
    w_in = f(inp["w_in"])[0]
    w_o = f(inp["w_o"])[0]
    w_gate = f(inp["w_gate"])[0]
    w_up = f(inp["w_up"])[0]
    w_down = f(inp["w_down"])[0]
    rel_bias = f(inp["rel_bias"])
    n = np.arange(NLUT)
    bucket = _t5_bucket_np(495 - n)
    onehot = np.zeros((32, NLUT), np.float32)
    onehot[bucket, n] = 1.0
    shared = {
        "meta": f(inp["meta_tokens"]),
        "w_in_l": np.ascontiguousarray(w_in.reshape(8, 128, 2048).transpose(1, 0, 2)).reshape(1024, 2048),
        "w_o_l": np.ascontiguousarray(w_o.reshape(8, 128, 1024).transpose(1, 0, 2)).reshape(1024, 1024),
        "w_gate_l": np.ascontiguousarray(w_gate.reshape(8, 128, 11, 256).transpose(2, 1, 0, 3)).reshape(1408, 2048),
        "w_up_l": np.ascontiguousarray(w_up.reshape(8, 128, 11, 256).transpose(2, 1, 0, 3)).reshape(1408, 2048),
        "w_down_l": w_down,
        "pool_w_l": np.ascontiguousarray(f(inp["pool_w"])[0].transpose(1, 0, 2)).reshape(128, 512),
        "colpack": np.ascontiguousarray(np.concatenate([
            f(inp["norm1_g"])[0].reshape(8, 128).T,
            f(inp["norm2_g"])[0].reshape(8, 128).T,
            f(inp["pool_scale"])[0].reshape(4, 128).T,
            f(inp["subln_g"])[0].reshape(128, 1),
            np.zeros((128, 3), np.float32),
            np.broadcast_to(np.concatenate([rel_bias[15], rel_bias[31]])[None, :], (128, 8)),
        ], axis=1)),
        "fgb": np.ascontiguousarray(np.broadcast_to(f(inp["final_g"])[None, :], (128, D))),
        "lamv": np.ascontiguousarray(np.broadcast_to(np.concatenate(
            [f(inp["lambda_q1"])[0], f(inp["lambda_k1"])[0], f(inp["lambda_q2"])[0], f(inp["lambda_k2"])[0]])[None, :],
            (128, 256))),
        "relb": rel_bias,
        "onehot": onehot,
        "ident": np.eye(128, dtype=np.float32),
        "aident": np.ascontiguousarray(np.eye(128, dtype=np.float32)[::-1]),
    }
    return shared


def kernel(**inputs):
    x = np.ascontiguousarray(np.asarray(inputs["x"], dtype=np.float32))
    shared = _host_layouts(inputs)
    if "nc" not in _CACHE:
        _CACHE["nc"] = build_program()
    nc = _CACHE["nc"]
    in_maps = []
    for c in range(NCORES):
        m = dict(shared)
        m["x"] = x[2 * c:2 * c + 2]
        in_maps.append(m)
    res = run_bass_kernel_spmd(nc, in_maps, core_ids=list(range(NCORES)))
    out = np.concatenate([np.asarray(r["out"], dtype=np.float32) for r in res.results], axis=0)
    return out
```

```python
import math
import numpy as np
import ml_dtypes
import concourse.bass as bass
import concourse.mybir as mybir
from concourse.bass_utils import run_bass_kernel_spmd

F32 = mybir.dt.float32
BF16 = mybir.dt.bfloat16
AF = mybir.ActivationFunctionType
ALU = mybir.AluOpType

NCORES = 8
SEQ = 2048
D = 1024
NMETA = 16
LPOS = SEQ + NMETA
DFF = 2816
NKT = 17
QC = 256
NQC = SEQ // QC
TW = 768
NLUT = 896
LAMBDA_INIT = 0.8 - 0.6 * math.exp(-0.3 * 0)
NS_DMA = 8
FILLER = 0


class Buf:
    __slots__ = ("name", "psum", "last_w", "readers")

    def __init__(self, name, psum=False):
        self.name = name
        self.psum = psum
        self.last_w = None
        self.readers = {}


class Op:
    __slots__ = ("eng", "fn", "deps", "inc", "seq", "dma", "semkey", "val", "prev")

    def __init__(self, eng, fn, deps, dma):
        self.eng = eng
        self.fn = fn
        self.deps = deps
        self.inc = False
        self.seq = 0
        self.dma = dma
        self.semkey = None
        self.val = 0
        self.prev = None


class Tracker:
    ENGS = ("pe", "act", "dve", "pool", "sp")

    def __init__(self):
        self.ops = {e: [] for e in self.ENGS}
        self.dma_cnt = {"sp": 0, "pool": 0}
        self.dma_uses = {}
        self.dma_last = {}
        self.pending = {e: set() for e in self.ENGS}
        self.recent_dma = []
        self.all_dma = []

    def emit(self, eng, fn, reads=(), writes=(), dma=False):
        deps = set()
        for b in reads:
            if b.psum:
                if b.last_w is not None:
                    deps.add(b.last_w)
                deps.update(b.readers.values())
            elif b.last_w is not None:
                deps.add(b.last_w)
        for b in writes:
            if b.last_w is not None:
                deps.add(b.last_w)
            deps.update(b.readers.values())
        if self.pending[eng]:
            deps.update(self.pending[eng])
            self.pending[eng] = set()
        if eng == "pe":
            deps = {d for d in deps if not (d.eng == "pe" and not d.dma)}
        op = Op(eng, fn, deps, dma)
        if dma:
            k = self.dma_cnt[eng] % NS_DMA
            self.dma_cnt[eng] += 1
            key = (eng, k)
            n = self.dma_uses.get(key, 0) + 1
            self.dma_uses[key] = n
            op.semkey = key
            op.val = 16 * n
            op.prev = self.dma_last.get(key)
            self.dma_last[key] = op
            self.recent_dma.append(op)
            self.all_dma.append(op)
        self.ops[eng].append(op)
        for b in reads:
            if b.psum:
                b.last_w = op
                b.readers = {}
            else:
                b.readers[(eng, id(op)) if dma else eng] = op
        for b in writes:
            b.last_w = op
            b.readers = {}
        return op

    def barrier(self, exclude=()):
        deps = set(self.recent_dma) - set(exclude)
        self.recent_dma = [d for d in self.recent_dma if d in set(exclude)]
        for e in self.ENGS:
            if self.ops[e]:
                last = self.ops[e][-1]
                if not last.dma:
                    deps.add(last)
                else:
                    for o in reversed(self.ops[e]):
                        if not o.dma:
                            deps.add(o)
                            break
        for e in self.ENGS:
            self.pending[e] = set(deps) | self.pending[e]

    def finalize(self):
        for e in self.ENGS:
            for op in self.ops[e]:
                for d in op.deps:
                    if not d.dma:
                        d.inc = True
        for e in self.ENGS:
            c = 0
            for op in self.ops[e]:
                if op.inc and not op.dma:
                    c += 1
                    op.seq = c

    def play(self, eng, handle, sems, dsems, final_wait=False):
        waited = {}

        def wait(key, semobj, val):
            if waited.get(key, 0) >= val:
                return
            handle.wait_ge(semobj, val)
            waited[key] = val

        for op in self.ops[eng]:
            for d in op.deps:
                if d.dma:
                    wait(d.semkey, dsems[d.semkey], d.val)
                else:
                    wait(d.eng, sems[d.eng], d.seq)
            if op.dma and op.prev is not None:
                wait(op.prev.semkey, dsems[op.prev.semkey], op.prev.val)
            inst = op.fn(handle)
            if op.dma:
                inst.then_inc(dsems[op.semkey], 16)
            elif op.inc:
                inst.then_inc(sems[eng], 1)
        if final_wait:
            for key, op in self.dma_last.items():
                wait(key, dsems[key], op.val)


def _t5_bucket_np(rel):
    rel = np.asarray(rel, dtype=np.int64)
    nb = 16
    ret = np.where(rel > 0, nb, 0)
    n = np.abs(rel)
    max_exact = 8
    nf = np.maximum(n, 1).astype(np.float32)
    large = max_exact + (np.log(nf / np.float32(max_exact)) / np.float32(math.log(128 / max_exact))
                         * np.float32(nb - max_exact)).astype(np.int32)
    large = np.minimum(large, nb - 1)
    return ret + np.where(n < max_exact, n, large)


def build_program():
    nc = bass.Bass("TRN2", target_bir_lowering=False)
    T = Tracker()

    def din(name, shape, dt=F32):
        return nc.dram_tensor(name, list(shape), dt, kind="ExternalInput")

    x_d = din("x", [2, SEQ, D])
    meta_d = din("meta", [NMETA, D])
    win_l = din("w_in_l", [1024, 2048])
    wo_l = din("w_o_l", [1024, 1024])
    wg_l = din("w_gate_l", [1408, 2048])
    wu_l = din("w_up_l", [1408, 2048])
    wd_l = din("w_down_l", [DFF, 1024])
    pw_l = din("pool_w_l", [128, 512])
    colpack_d = din("colpack", [128, 32])
    fgb_d = din("fgb", [128, D])
    lam_d = din("lamv", [128, 256])
    relb_d = din("relb", [32, 4])
    oh_d = din("onehot", [32, NLUT])
    ident_d = din("ident", [128, 128])
    aident_d = din("aident", [128, 128])
    out_d = nc.dram_tensor("out", [2, SEQ, D], F32, kind="ExternalOutput")

    win_b = nc.dram_tensor("win_b", [1024, 2048], BF16, kind="Internal")
    wo_b = nc.dram_tensor("wo_b", [1024, 1024], BF16, kind="Internal")
    wg_b = nc.dram_tensor("wg_b", [1408, 2048], BF16, kind="Internal")
    wu_b = nc.dram_tensor("wu_b", [1408, 2048], BF16, kind="Internal")
    wd_b = nc.dram_tensor("wd_b", [DFF, 1024], BF16, kind="Internal")
    lut_d = nc.dram_tensor("lut_d", [4, NLUT], F32, kind="Internal")

    ARENA = 131072
    ctxs = dict(
        arena=nc.sbuf_tensor("arena", [128, ARENA // 2], BF16),
        h1=nc.sbuf_tensor("h1", [128, 16 * D], F32),
        ident=nc.sbuf_tensor("ident_sb", [128, 128], F32),
        gbc=nc.sbuf_tensor("gbc", [128, D], F32),
        fgb=nc.sbuf_tensor("fgb_sb", [128, D], F32),
        pw=nc.sbuf_tensor("pw_sb", [128, 512], BF16),
        identb=nc.sbuf_tensor("identb_sb", [128, 128], BF16),
        jb=nc.sbuf_tensor("jb_sb", [128, 128], BF16),
        cols=nc.sbuf_tensor("cols", [128, 64], F32),
        lamv=nc.sbuf_tensor("lamv_sb", [128, 256], F32),
        sm=nc.sbuf_tensor("sm", [128, 128], F32),
        relb=nc.sbuf_tensor("relb_sb", [32, 4], F32),
        oh=nc.sbuf_tensor("oh_sb", [32, NLUT], F32),
        ps=nc.psum_tensor("ps", [128, 8 * 512], F32),
    )
    sem_names = ["pe", "act", "dve", "pool", "sp"]
    from contextlib import ExitStack
    with ExitStack() as es:
        tens = {k: es.enter_context(v) for k, v in ctxs.items()}
        sems = {e: es.enter_context(nc.semaphore("s_" + e)) for e in sem_names}
        dsems = {}
        for e in ("sp", "pool"):
            for k in range(NS_DMA):
                dsems[(e, k)] = es.enter_context(nc.semaphore(f"d_{e}{k}"))

        arena_bf = tens["arena"]
        arena_f = arena_bf.bitcast(F32)
        h1 = tens["h1"]
        ident = tens["ident"]
        gbc = tens["gbc"]
        fgb = tens["fgb"]
        pw = tens["pw"]
        cols = tens["cols"]
        lamv = tens["lamv"]
        sm = tens["sm"]
        relb = tens["relb"]
        ohsb = tens["oh"]
        ps = tens["ps"]
        psb = ps.bitcast(BF16)
        identb = tens["identb"]
        jb = tens["jb"]

        def bfv(off, n):
            return arena_bf[:, off // 2: off // 2 + n]

        def f32v(off, n):
            return arena_f[:, off // 4: off // 4 + n]

        def bank(i):
            return ps[:, i * 512:(i + 1) * 512]

        def bankb(i):
            return psb[:, i * 1024:(i + 1) * 1024]

        g1col = cols[:, 0:8]
        g2col = cols[:, 8:16]
        pscol = cols[:, 16:20]
        sgcol = cols[:, 20:21]
        sg08 = cols[:, 21:22]
        farb = cols[:, 24:32]
        neglam = cols[:, 32:33]
        eps6 = cols[:, 33:34]
        eps5 = cols[:, 34:35]
        lam_s = cols[:, 36:38]
        lam_e = cols[:, 38:40]
        ssA = sm[:, 0:2]
        lnA = sm[:, 2:4]
        rsA = sm[:, 4:6]
        ss2 = sm[:, 8:24]
        ln2 = sm[:, 24:32]
        rs2 = sm[:, 32:40]
        ss3 = sm[:, 40:48]
        ln3 = sm[:, 48:56]
        rs3 = sm[:, 56:64]
        rz = sm[:, 64:72]
        rz1n = sm[:, 72:76]
        ssb = sm[:, 76:80]
        lnb = sm[:, 80:84]
        rsb = sm[:, 84:88]

        B = lambda name, psum=False: Buf(name, psum)
        b_banks = [B(f"bank{i}", True) for i in range(8)]
        b_h1 = [B(f"h1_{i}") for i in range(16)]
        b_ident = B("ident")
        b_gbc = B("gbc")
        b_fgb = B("fgb")
        b_pw = B("pw")
        b_cols = B("cols")
        b_lamv = B("lamv")
        b_lams = B("lams")
        b_relb = B("relb")
        b_oh = B("oh")
        b_winb, b_wob, b_wgb, b_wub, b_wdb = B("winb"), B("wob"), B("wgb"), B("wub"), B("wdb")
        b_lutd = B("lutd")
        b_ss2 = [B(f"ss2_{i}") for i in range(16)]

        emit = T.emit

        def dma_sp(out, in_, reads, writes):
            return emit("sp", lambda e, o=out, i=in_: e.dma_start(out=o, in_=i), reads, writes, dma=True)

        def dma_pool(out, in_, reads, writes):
            return emit("pool", lambda e, o=out, i=in_: e.dma_start(out=o, in_=i), reads, writes, dma=True)

        def mm(out, lhsT, rhs, start, stop, reads, writes, skip=False):
            return emit("pe", lambda e, o=out, l=lhsT, r=rhs, s=start, t=stop, k=skip:
                        e.matmul(o, lhsT=l, rhs=r, start=s, stop=t, skip_group_check=k), reads, writes)

        def tr(out, in_, npart, reads, writes, bf=False):
            idn = (identb if bf else ident)[0:npart, 0:npart]
            return emit("pe", lambda e, o=out, i=in_, d=idn: e.transpose(o, i, d), list(reads) + [b_ident], writes)

        def act(out, in_, func, reads, writes, bias=None, scale=None, accum=None):
            def fn(e, o=out, i=in_, f=func, b=bias, s=scale, a=accum):
                kw = {}
                if b is not None:
                    kw["bias"] = b
                if s is not None:
                    kw["scale"] = s
                if a is not None:
                    kw["accum_out"] = a
                return e.activation(out=o, in_=i, func=f, **kw)
            return emit("act", fn, reads, writes)

        def dve(fn, reads, writes):
            return emit("dve", fn, reads, writes)

        def pool(fn, reads, writes):
            return emit("pool", fn, reads, writes)

        dma_sp(cols[:, 0:32], colpack_d.ap(), [], [b_cols])
        dma_sp(ident[:], ident_d.ap(), [], [b_ident])
        dma_sp(lamv[:], lam_d.ap(), [], [b_lamv])
        dma_sp(relb[:], relb_d.ap(), [], [b_relb])
        dma_sp(ohsb[:], oh_d.ap(), [], [b_oh])
        win0 = bfv(67600, 8 * 2048).rearrange("p (c n) -> p c n", c=8)
        b_win0 = B("win0")
        b_stg = [B(f"wstg{c}") for c in range(8)]
        for c in range(8):
            dma_sp(h1[:, c * 2048:(c + 1) * 2048],
                   win_l.ap().rearrange("(p a) n -> p a n", p=128)[:, c, :], [], [b_stg[c]])
        dma_sp(fgb[:], fgb_d.ap(), [], [b_fgb])
        b_win0c = [B(f"win0_{c}") for c in range(8)]
        for c in range(8):
            dve(lambda e, c=c: e.tensor_copy(out=win0[:, c, :], in_=h1[:, c * 2048:(c + 1) * 2048]),
                [b_stg[c]], [b_win0c[c]])
        early_ex = [dma_sp(win_b.ap().rearrange("(p a) n -> p (a n)", p=128), win0.rearrange("p c n -> p (c n)"),
                           b_win0c, [b_winb])]

        dve(lambda e: e.tensor_copy(out=identb[:], in_=ident[:]), [b_ident], [b_ident])
        b_jb = B("jb")
        jstage = f32v(8192, 128)
        b_jst = B("jstage")
        dma_sp(jstage, aident_d.ap(), [], [b_jst])
        dve(lambda e: e.tensor_copy(out=jb[:], in_=jstage), [b_jst], [b_jb])
        dve(lambda e: e.memset(cols[:, 33:34], 1e-6), [], [b_cols])
        dve(lambda e: e.memset(cols[:, 34:35], 1e-5), [], [b_cols])
        dve(lambda e: e.tensor_scalar(out=sg08, in0=sgcol, scalar1=float(1.0 - LAMBDA_INIT), scalar2=None,
                                      op0=ALU.mult), [b_cols], [b_cols])
        negbb = cols[:, 40:44]
        ea = cols[:, 44:48]
        dve(lambda e: e.tensor_scalar(out=negbb, in0=farb[:, 0:4], scalar1=-1.0, scalar2=None, op0=ALU.mult),
            [b_cols], [b_cols])
        dve(lambda e: e.tensor_tensor(out=ea, in0=farb[:, 4:8], in1=farb[:, 0:4], op=ALU.subtract),
            [b_cols], [b_cols])
        b_lamp = B("lamp")
        lamp = f32v(0, 128)
        lamj = bfv(1024, 128)
        dve(lambda e: e.tensor_tensor(out=lamp[:, 0:64], in0=lamv[:, 0:64], in1=lamv[:, 64:128], op=ALU.mult),
            [b_lamv], [b_lamp])
        dve(lambda e: e.tensor_tensor(out=lamp[:, 64:128], in0=lamv[:, 128:192], in1=lamv[:, 192:256], op=ALU.mult),
            [b_lamv], [b_lamp])
        b_lamj = B("lamj")
        act(lamj[:, 0:64], lamp[:, 0:64], AF.Copy, [b_lamp], [b_lamj, b_lams], accum=lam_s[:, 0:1])
        act(lamj[:, 0:64], lamp[:, 64:128], AF.Copy, [b_lamp], [b_lamj, b_lams], accum=lam_s[:, 1:2])
        act(lam_e, lam_s, AF.Exp, [b_lams], [b_lams])
        dve(lambda e: e.tensor_tensor(out=neglam, in0=lam_e[:, 1:2], in1=lam_e[:, 0:1], op=ALU.subtract),
            [b_lams], [b_cols])
        dve(lambda e: e.tensor_scalar(out=neglam, in0=neglam, scalar1=float(-LAMBDA_INIT), scalar2=None,
                                      op0=ALU.add), [b_cols], [b_cols])
        lutsb = f32v(4096, NLUT)
        b_lutsb = B("lutsb")
        for half in range(2):
            mm(bank(0)[0:4, 0:448], relb[:, :], ohsb[:, half * 448:(half + 1) * 448], True, True,
               [b_relb, b_oh], [b_banks[0]])
            dve(lambda e, hf=half: e.tensor_copy(out=lutsb[0:4, hf * 448:(hf + 1) * 448], in_=bank(0)[0:4, 0:448]),
                [b_banks[0]], [b_lutsb])
        dma_sp(lut_d.ap(), lutsb[0:4, :], [b_lutsb], [b_lutd])
        T.barrier(exclude=early_ex)

        O_QT, O_KT, O_U, O_VX, O_X = 0, 16640, 33280, 49920, 67600
        PW_ = 2080
        QT = bfv(O_QT, 4 * PW_).rearrange("p (h n) -> p h n", h=4)
        KT = bfv(O_KT, 4 * PW_).rearrange("p (h n) -> p h n", h=4)
        UU = bfv(O_U, 4 * PW_).rearrange("p (h n) -> p h n", h=4)
        VX = bfv(O_VX, NKT * 4 * 130).rearrange("p (k h n) -> p k h n", k=NKT, h=4)

        def gain_tile(gcol, ones_ap, b_ones):
            for c in range(8):
                dve(lambda e, c=c: e.tensor_scalar(out=gbc[:, c * 128:(c + 1) * 128], in0=ones_ap,
                                                   scalar1=gcol[:, c:c + 1], scalar2=None, op0=ALU.mult),
                    [b_cols, b_ones], [b_gbc])

        for s in range(2):
            b_QT = [B(f"QT{i}") for i in range(5)]
            b_KT = [B(f"KT{i}") for i in range(NKT)]
            b_U = [B(f"U{i}") for i in range(5)]
            b_Upad = B("Upad")
            b_VX = [B(f"VX{i}") for i in range(NKT)]
            win = bfv(O_X, 8 * 2048).rearrange("p (c n) -> p c n", c=8)
            uT = [bfv(O_X + 32768 + i * 8192, 8 * 512).rearrange("p (c n) -> p c n", c=8) for i in range(2)]
            xs = [f32v(O_X + 49152 + i * 4096, 1024) for i in range(2)]
            xb = [bfv(O_X + 57344 + i * 2048, 1024) for i in range(2)]
            b_win = B("win")
            b_uT = [B("uT0"), B("uT1")]
            b_xs = [B("xs0"), B("xs1")]
            b_xb = [B("xb0"), B("xb1")]
            b_ssA = [B("ssA0"), B("ssA1")]

            rd_win = b_win0c if s == 0 else [b_win]
            if s == 0:
                pass
            else:
                dma_sp(win.rearrange("p c n -> p (c n)"), win_b.ap().rearrange("(p a) n -> p (a n)", p=128),
                       [b_winb], [b_win])
            dve(lambda e: e.memset(xs[1][:, 0:128], 1.0), [], [b_xs[1]])
            gain_tile(g1col, xs[1][:, 0:128], b_xs[1])
            pool(lambda e: e.memset(UU[:, :, 2064:2080], 0.0), [], [b_Upad])
            pool(lambda e: e.memset(VX[:, :, :, 128:130], 1.0), [], b_VX)

            def load_tile(ti, slot):
                np_ = NMETA if ti < 0 else 128
                src = meta_d.ap() if ti < 0 else x_d.ap()[s, ti * 128:(ti + 1) * 128, :]
                dma_sp(xs[slot][0:np_, :], src, [], [b_xs[slot]])

            def norm_compute(ti, slot):
                np_ = NMETA if ti < 0 else 128
                xt = xs[slot]
                xbt = xb[slot]
                act(xbt[0:np_, :], xt[0:np_, :], AF.Square, [b_xs[slot]], [b_xb[slot], b_ssA[slot]],
                    accum=ssA[0:np_, slot:slot + 1])
                act(lnA[0:np_, slot:slot + 1], ssA[0:np_, slot:slot + 1], AF.Ln, [b_ssA[slot], b_cols], [b_ssA[slot]],
                    bias=eps6[0:np_, :], scale=1.0 / D)
                act(rsA[0:np_, slot:slot + 1], lnA[0:np_, slot:slot + 1], AF.Exp, [b_ssA[slot]], [b_ssA[slot]],
                    scale=-0.5)
                act(xbt[0:np_, :], xt[0:np_, :], AF.Copy, [b_ssA[slot], b_xs[slot]], [b_xb[slot]],
                    scale=rsA[0:np_, slot:slot + 1])

            def norm_transpose(ti, slot):
                np_ = NMETA if ti < 0 else 128
                xbt = xb[slot]
                pb = slot
                for c in range(8):
                    tr(bankb(pb)[:, c * 128:c * 128 + np_], xbt[0:np_, c * 128:(c + 1) * 128], np_,
                       [b_xb[slot]], [b_banks[pb]], bf=True)
                p0 = 0 if ti < 0 else NMETA + ti * 128
                a = p0
                while a < p0 + np_:
                    pc = a // 512
                    bnd = min(p0 + np_, (pc + 1) * 512)
                    n = bnd - a
                    off = a - p0
                    src_ps = bankb(pb).rearrange("p (c n) -> p c n", c=8)[:, :, off:off + n]
                    dst = uT[pc % 2][:, :, a - pc * 512:a - pc * 512 + n]
                    g = gbc[:, :].rearrange("p (c n) -> p c n", c=8)[:, :, 0:n]
                    dve(lambda e, o=dst, i=src_ps, g=g: e.tensor_tensor(out=o, in0=i, in1=g, op=ALU.mult),
                        [b_banks[pb], b_gbc], [b_uT[pc % 2]])
                    a = bnd

            kindc = [0]

            def inproj_groups(pc):
                n = 512 if pc < 4 else 16
                u = uT[pc % 2]
                bu = b_uT[pc % 2]
                pos0 = pc * 512
                groups = []
                for grp in range(3):
                    for h in range(4):
                        def g_(grp=grp, h=h):
                            bi = 2 + (kindc[0] % 6)
                            kindc[0] += 1
                            colbase = {0: 512, 1: 1024, 2: 0}[grp] + h * 128
                            for c in range(8):
                                mm(bank(bi)[:, 0:n], win[:, c, colbase:colbase + 128], u[:, c, 0:n], c == 0, c == 7,
                                   rd_win + [bu], [b_banks[bi]])
                            if grp == 0:
                                dve(lambda e, o=QT[:, h, pos0:pos0 + n], i=bank(bi)[:, 0:n]:
                                    e.tensor_scalar(out=o, in0=i, scalar1=0.125, scalar2=None, op0=ALU.mult),
                                    [b_banks[bi]], [b_QT[pc]])
                            elif grp == 1:
                                wr = [b_KT[k] for k in range(pos0 // 128, (pos0 + n - 1) // 128 + 1)]
                                dve(lambda e, o=KT[:, h, pos0:pos0 + n], i=bank(bi)[:, 0:n]:
                                    e.tensor_copy(out=o, in_=i), [b_banks[bi]], wr)
                            else:
                                act(UU[:, h, pos0:pos0 + n], bank(bi)[:, 0:n], AF.Copy, [b_banks[bi]], [b_U[pc]])
                        groups.append(g_)
                nt = (n + 127) // 128
                for kk in range(nt):
                    def gv(kk=kk):
                        m = min(128, n - kk * 128)
                        kt = pos0 // 128 + kk
                        bi = 2 + (kindc[0] % 6)
                        kindc[0] += 1
                        for c in range(8):
                            mm(bank(bi)[0:m, :], u[:, c, kk * 128:kk * 128 + m], win[:, c, 1536:2048], c == 0, c == 7,
                               rd_win + [bu], [b_banks[bi]])
                        src_ps = bank(bi)[0:m, :].rearrange("p (h n) -> p h n", h=4)
                        if kk % 2 == 0:
                            dve(lambda e, o=VX[0:m, kt, :, 0:128], sp_=src_ps: e.tensor_copy(out=o, in_=sp_),
                                [b_banks[bi]], [b_VX[kt]])
                        else:
                            act(VX[0:m, kt, :, 0:128], src_ps, AF.Copy, [b_banks[bi]], [b_VX[kt]])
                    groups.append(gv)
                return groups

            tidx = 0
            order = [-1] + list(range(16))
            load_tile(order[0], 0)
            load_tile(order[1], 1)
            ptr = {"c": 0, "t": 0}

            def do_compute():
                k = ptr["c"]
                norm_compute(order[k], k % 2)
                if k + 2 < len(order):
                    load_tile(order[k + 2], k % 2)
                ptr["c"] += 1

            def norm_tile(ti, slot):
                if ptr["c"] == ptr["t"]:
                    do_compute()
                if ptr["c"] < len(order) and ptr["c"] == ptr["t"] + 1:
                    do_compute()
                k = ptr["t"]
                assert order[k] == ti
                norm_transpose(ti, k % 2)
                ptr["t"] += 1
            for ti in order[0:5]:
                norm_tile(ti, tidx % 2)
                tidx += 1
            nxt_tile = 5
            if s == 0:
                gate = [b_uT[0], b_uT[1]]
                dma_pool(wo_b.ap(), wo_l.ap(), gate, [b_wob])
                dma_pool(pw[:], pw_l.ap(), gate, [b_pw])
            for pc in range(5):
                groups = inproj_groups(pc)
                ng = len(groups)
                for gi, g_ in enumerate(groups):
                    g_()
                    if pc < 3 and gi % 4 == 3 and nxt_tile < len(order) and nxt_tile < 5 + 4 * (pc + 1):
                        norm_tile(order[nxt_tile], tidx % 2)
                        tidx += 1
                        nxt_tile += 1
                while pc < 3 and nxt_tile < 5 + 4 * (pc + 1):
                    norm_tile(order[nxt_tile], tidx % 2)
                    tidx += 1
                    nxt_tile += 1
            T.barrier()

            from collections import deque
            wo = bfv(O_X, 8 * 1024).rearrange("p (c n) -> p c n", c=8)
            Tst = f32v(O_X + 16384, 4 * TW).rearrange("p (h n) -> p h n", h=4)
            o2 = O_X + 16384 + 12288
            NE = 4
            Et = [bfv(o2 + i * 1024, 512) for i in range(NE)]
            o2 += NE * 1024
            Tb = bfv(o2, 4 * TW).rearrange("p (h n) -> p h n", h=4)
            o2 += 4 * TW * 2
            yT_off = o2
            yT = [bfv(o2 + i * 4096, 8 * QC).rearrange("p (c n) -> p c n", c=8) for i in range(2)]
            o2 += 8192
            ot = [f32v(o2 + i * 1024, 256).rearrange("p (q n) -> p q n", q=2) for i in range(2)]
            o2 += 2048
            yt = [f32v(o2 + i * 1024, 256).rearrange("p (q n) -> p q n", q=2) for i in range(2)]
            ytb_all = [bfv(o2 + i * 1024, 256).rearrange("p (q n) -> p q n", q=2) for i in range(2)]
            o2 += 2048
            tA = f32v(o2, 272)
            tB = f32v(o2 + 1088, 272)
            o2 += 2176
            dT = [bfv(o2 + i * 2048, 4 * QC).rearrange("p (g n) -> p g n", g=4) for i in range(2)]
            o2 += 4096
            junkB = bfv(o2, 1024)
            o2 += 2048
            Qb = [bfv(o2 + i * 1024, 512) for i in range(2)]
            o2 += 2048
            assert o2 <= ARENA, o2
            b_wo = B("wo")
            b_E = [B(f"E{i}") for i in range(NE)]
            b_sb = [B(f"sb{i}") for i in range(2)]
            b_yT = [B(f"yT{i}") for i in range(2)]
            b_ot = [B(f"ot{i}") for i in range(2)]
            b_yt = [B(f"yt{i}") for i in range(2)]
            b_tA, b_tB = B("tA"), B("tB")
            b_dT = [B(f"dT{i}") for i in range(2)]
            b_junkB = B("junkB")
            b_rz = [B("rz0"), B("rz1")]
            b_ssb = [B("ssb0"), B("ssb1")]
            b_Qb = [B("Qb0"), B("Qb1")]

            dma_sp(wo.rearrange("p c n -> p (c n)"), wo_b.ap().rearrange("(p a) n -> p (a n)", p=128),
                   [b_wob], [b_wo])
            for i in range(2):
                pool(lambda e, i=i: e.memset(Qb[i][:, :], 0.0), [], [b_Qb[i]])
            b_Tst = [B(f"Tst{h}") for h in range(4)]
            b_Tb = B("Tb")
            Hb = bfv(yT_off, 4 * TW)
            for h in range(4):
                src = bass.AP(lut_d, h * NLUT, [[1, 128], [1, TW]])
                dma_sp(Tst[:, h, :], src, [b_lutd], [b_Tst[h]])
                act(Hb[:, h * TW:(h + 1) * TW], Tst[:, h, :], AF.Exp, [b_Tst[h], b_cols], b_yT, bias=negbb[:, h:h + 1])
            tb_banks = [7, 3, 4, 5]
            k = 0
            for h in range(4):
                for half in range(2):
                    bk = tb_banks[k % 4]
                    k += 1
                    mm(bank(bk)[:, 0:384], jb[:, :], Hb[:, h * TW + half * 384:h * TW + (half + 1) * 384], True, True,
                       b_yT + [b_jb], [b_banks[bk]])
                    dve(lambda e, o=Tb[:, h, half * 384:(half + 1) * 384], i=bank(bk)[:, 0:384]:
                        e.tensor_copy(out=o, in_=i), [b_banks[bk]], [b_Tb])

            import heapq
            deferred = []
            dseq = [0]

            def defer(at, fn):
                dseq[0] += 1
                heapq.heappush(deferred, (at, dseq[0], fn))

            def run_deferred(cur, nmax):
                k = 0
                while deferred and k < nmax and deferred[0][0] <= cur:
                    heapq.heappop(deferred)[2]()
                    k += 1

            def pool_stage(cq, g, w):
                q0 = NMETA + cq * QC
                ysl = cq % 2

                def zz(lo, n):
                    return UU[:, g, q0 + lo:q0 + lo + n]
                rdU = [b_U[min(4, (q0 - 8) // 512)], b_U[min(4, (q0 + 263) // 512)], b_Upad]

                def padd(o, a, b, reads, writes):
                    pool(lambda e, o=o, a=a, b=b: e.tensor_tensor(out=o, in0=a, in1=b, op=ALU.add), reads, writes)
                if w == 2:
                    padd(tA[:, 0:256], zz(0, 256), zz(-1, 256), rdU, [b_tA])
                    ws, wb, wsl = tA, b_tA, 0
                else:
                    padd(tA[:, 1:272], zz(-7, 271), zz(-8, 271), rdU, [b_tA])
                    padd(tB[:, 2:271], tA[:, 3:272], tA[:, 1:270], [b_tA], [b_tB])
                    ws, wb, wsl = tB, b_tB, 8
                    if w >= 8:
                        padd(tA[:, 4:269], tB[:, 6:271], tB[:, 2:267], [b_tB], [b_tA])
                        ws, wb = tA, b_tA
                    if w == 16:
                        padd(tB[:, 8:265], tA[:, 12:269], tA[:, 4:261], [b_tA], [b_tB])
                        ws, wb = tB, b_tB
                def dpart():
                    dve(lambda e, o=dT[ysl][:, g, :], a=ws[:, wsl:wsl + 256], sc_=1.0 / w, b=zz(0, 256):
                        e.scalar_tensor_tensor(out=o, in0=a, scalar=sc_, in1=b, op0=ALU.mult, op1=ALU.subtract),
                        [wb] + rdU, [b_dT[ysl]])
                    if cq == NQC - 1:
                        right = w - 1 - w // 2
                        for r in range(right):
                            cnt = w - (right - r)
                            col = 255 - r
                            dve(lambda e, o=dT[ysl][:, g, col:col + 1], a=ws[:, wsl + col:wsl + col + 1], sc_=1.0 / cnt,
                                b=zz(col, 1):
                                e.scalar_tensor_tensor(out=o, in0=a, scalar=sc_, in1=b, op0=ALU.mult, op1=ALU.subtract),
                                [wb] + rdU, [b_dT[ysl]])
                return dpart

            def poolmm_stage(cq, g):
                ysl = cq % 2
                mb = 7
                mm(bank(mb)[:, 0:QC], pw[:, g * 128:(g + 1) * 128], dT[ysl][:, g, :], True, True,
                   [b_pw, b_dT[ysl]], [b_banks[mb]])
                dve(lambda e, o=yT[ysl][:, g, :], i=bank(mb)[:, 0:QC], sc_=pscol[:, g:g + 1]:
                    e.tensor_scalar(out=o, in0=i, scalar1=sc_, scalar2=None, op0=ALU.mult),
                    [b_banks[mb], b_cols], [b_yT[ysl]])

            def epilogue_stages(cq, h, osl):
                ysl = cq % 2
                ba, bb = 3 + 2 * osl, 4 + 2 * osl
                rzs = rz[:, osl * 4:(osl + 1) * 4]
                r1n = rz1n[:, osl * 2:(osl + 1) * 2]
                sss = ssb[:, osl * 2:(osl + 1) * 2]
                lns = lnb[:, osl * 2:(osl + 1) * 2]
                rss = rsb[:, osl * 2:(osl + 1) * 2]
                mb = 7

                def st1():
                    for c, bk in ((0, ba), (1, bb)):
                        zsrc = bank(bk)[:, 0:258].rearrange("p (q n) -> p q n", n=129)[:, :, 128:129]
                        zdst = rzs[:, c * 2:(c + 1) * 2].rearrange("p (q o) -> p q o", o=1)
                        dve(lambda e, o=zdst, i=zsrc: e.reciprocal(out=o, in_=i), [b_banks[bk]], [b_rz[osl]])
                    dve(lambda e, o=r1n, i=rzs[:, 2:4]: e.tensor_scalar(out=o, in0=i, scalar1=neglam, scalar2=None,
                                                                         op0=ALU.mult), [b_rz[osl], b_cols], [b_rz[osl]])

                def st2():
                    for qs in range(2):
                        dve(lambda e, o=ot[osl][:, qs, :], i=bank(ba)[:, qs * 129:qs * 129 + 128], sc_=rzs[:, qs:qs + 1]:
                            e.tensor_scalar(out=o, in0=i, scalar1=sc_, scalar2=None, op0=ALU.mult),
                            [b_banks[ba], b_rz[osl]], [b_ot[osl]])

                def st3():
                    for qs in range(2):
                        dve(lambda e, o=ot[osl][:, qs, :], i=bank(bb)[:, qs * 129:qs * 129 + 128], sc_=r1n[:, qs:qs + 1]:
                            e.scalar_tensor_tensor(out=o, in0=i, scalar=sc_, in1=o, op0=ALU.mult, op1=ALU.add),
                            [b_banks[bb], b_rz[osl], b_ot[osl]], [b_ot[osl]])

                def st4():
                    for qs in range(2):
                        dve(lambda e, o=junkB[:, 0:128], i=ot[osl][:, qs, :], a=sss[:, qs:qs + 1]:
                            e.scalar_tensor_tensor(out=o, in0=i, scalar=1.0, in1=i, op0=ALU.mult, op1=ALU.mult,
                                                   accum_out=a),
                            [b_ot[osl]], [b_junkB, b_ssb[osl]])

                def st5():
                    act(lns, sss, AF.Ln, [b_ssb[osl], b_cols], [b_ssb[osl]], bias=eps5, scale=1.0 / 128)
                    act(rss, lns, AF.Exp, [b_ssb[osl]], [b_ssb[osl]], scale=-0.5)

                ytb = ytb_all[osl]

                def st6():
                    for qs in range(2):
                        pool(lambda e, o=ytb[:, qs, :], i=ot[osl][:, qs, :], sc_=rss[:, qs:qs + 1]:
                             e.tensor_scalar(out=o, in0=i, scalar1=sc_, scalar2=1.0, op0=ALU.mult, op1=ALU.mult),
                             [b_ot[osl], b_ssb[osl]], [b_yt[osl]])

                def st7():
                    for qs in range(2):
                        tr(bankb(mb)[:, qs * 128:(qs + 1) * 128], ytb[:, qs, :], 128, [b_yt[osl]], [b_banks[mb]], bf=True)

                def st8():
                    dve(lambda e, o=yT[ysl][:, 4 + h, :], i=bankb(mb)[:, 0:QC]:
                        e.tensor_scalar(out=o, in0=i, scalar1=sg08, scalar2=None, op0=ALU.mult),
                        [b_banks[mb], b_cols], [b_yT[ysl]])
                def st78():
                    st7()
                    st8()
                return [(0, st1), (2, st2), (4, st3), (6, st4), (11, st5), (15, st6), (19, st78)]

            def wo_stages(cq):
                ysl = cq % 2
                sts = []
                k = 0
                for t2 in range(2):
                    tile_i = cq * 2 + t2
                    for half in range(4):
                        def st(t2=t2, half=half, tile_i=tile_i):
                            mb = 7
                            for c in range(8):
                                mm(bank(mb)[:, 0:256], yT[ysl][:, c, t2 * 128:(t2 + 1) * 128],
                                   wo[:, c, half * 256:(half + 1) * 256],
                                   c == 0, c == 7, [b_yT[ysl], b_wo], [b_banks[mb]])
                            hsl = h1[:, tile_i * D + half * 256:tile_i * D + (half + 1) * 256]
                            dve(lambda e, hsl=hsl, mb=mb: e.tensor_tensor(out=hsl, in0=bank(mb)[:, 0:256], in1=hsl, op=ALU.add),
                                [b_banks[mb], b_h1[tile_i]], [b_h1[tile_i]])
                        sts.append((24 + 2 * k, st))
                        k += 1

                    def stq(tile_i=tile_i):
                        dve(lambda e, o=junkB[:, :], i=h1[:, tile_i * D:(tile_i + 1) * D], a=ss2[:, tile_i:tile_i + 1]:
                            e.scalar_tensor_tensor(out=o, in0=i, scalar=1.0, in1=i, op0=ALU.mult, op1=ALU.mult,
                                                   accum_out=a),
                            [b_h1[tile_i]], [b_junkB, b_ss2[tile_i]])
                    sts.append((25 + 2 * (k - 1), stq))
                return sts

            unit = 0

            def av(cqh, j, esl, kn, osl, h, pos):
                ba, bb = 3 + 2 * osl, 4 + 2 * osl
                for c, bk in ((0, ba), (1, bb)):
                    for qs in range(2):
                        mm(bank(bk)[:, qs * 129:(qs + 1) * 129],
                           Et[esl][0:kn, c * QC + qs * 128:c * QC + (qs + 1) * 128],
                           VX[0:kn, j, h, 0:129], (pos == 0 and qs == 0), pos == NKT - 1,
                           [b_E[esl], b_VX[j]], [b_banks[bk]], skip=True)
                if FILLER:
                    mm(bank(bb)[:, 258:258 + FILLER], Et[esl][0:kn, QC + 128:QC + 256], Qb[osl][0:kn, 0:FILLER],
                       False, False, [b_E[esl], b_Qb[osl]], [b_banks[bb]], skip=True)
                if pos == NKT - 1:
                    for off, st in epilogue_stages(cqh, h, osl):
                        defer(unit + off, st)
                    if h == 3:
                        for off, st in wo_stages(cqh):
                            defer(unit + off, st)

            def emit_qb(cq_, h_, slot):
                q0_ = NMETA + cq_ * QC
                for c in range(2):
                    pool(lambda e, o=Qb[slot][c * 64:(c + 1) * 64, c * QC:(c + 1) * QC],
                         i=QT[c * 64:(c + 1) * 64, h_, q0_:q0_ + QC]: e.tensor_copy(out=o, in_=i),
                         [b_QT[min(4, q0_ // 512)], b_QT[min(4, (q0_ + QC - 1) // 512)]], [b_Qb[slot]])

            units = []
            hq_ = 0
            def _is_near(cq, j):
                kn_ = 128 if j < NKT - 1 else LPOS - 128 * (NKT - 1)
                Dd_ = j * 128 - (NMETA + cq * QC)
                return not (Dd_ + kn_ - 1 <= -128 or Dd_ - (QC - 1) >= 128)
            for cq in range(NQC):
                if cq == 0:
                    jorder = [j for j in range(NKT) if not _is_near(cq, j)] + [j for j in range(NKT) if _is_near(cq, j)]
                else:
                    jorder = list(range(NKT))
                for h in range(4):
                    for pos, j in enumerate(jorder):
                        units.append(dict(cq=cq, h=h, j=j, pos=pos, sl=hq_ % 2,
                                          kn=128 if j < NKT - 1 else LPOS - 128 * (NKT - 1)))
                    hq_ += 1
            NU = len(units)

            def emit_qk(u):
                U_ = units[u]
                cq_, h_, j_, kn_, sl_ = U_["cq"], U_["h"], U_["j"], U_["kn"], U_["sl"]
                ssl_ = u % 3
                k0_ = j_ * 128
                mm(bank(ssl_)[0:kn_, :], KT[:, h_, k0_:k0_ + kn_], Qb[sl_][:, :], True, True,
                   [b_KT[j_], b_Qb[sl_]], [b_banks[ssl_]])

            emit_qb(0, 0, 0)
            emit_qk(0)
            emit_qk(1)
            cast_pieces = []
            if s == 0:
                for r in range(11):
                    cast_pieces.append((wg_b.ap()[r * 128:(r + 1) * 128, :], wg_l.ap()[r * 128:(r + 1) * 128, :], b_wgb))
                    cast_pieces.append((wu_b.ap()[r * 128:(r + 1) * 128, :], wu_l.ap()[r * 128:(r + 1) * 128, :], b_wub))
                for r in range(11):
                    cast_pieces.append((wd_b.ap()[r * 256:(r + 1) * 256, :], wd_l.ap()[r * 256:(r + 1) * 256, :], b_wdb))
            for u in range(NU):
                U_ = units[u]
                cq, h, j, kn, sl, pos = U_["cq"], U_["h"], U_["j"], U_["kn"], U_["sl"], U_["pos"]
                q0 = NMETA + cq * QC
                unit = u + 1
                if h == 0 and pos == 0:
                    for t2 in range(2):
                        tile_i = cq * 2 + t2
                        dma_sp(h1[:, tile_i * D:(tile_i + 1) * D], x_d.ap()[s, tile_i * 128:(tile_i + 1) * 128, :],
                               [], [b_h1[tile_i]])
                    for g, w in enumerate((2, 4, 8, 16)):
                        def pst(cq=cq, g=g, w=w, at=unit + 1 + 8 * g + 6):
                            dpart = pool_stage(cq, g, w)
                            defer(at, dpart)
                        defer(unit + 1 + 8 * g, pst)
                if h == 2 and pos == 0:
                    for g in range(4):
                        defer(unit + 3 * g, lambda cq=cq, g=g: poolmm_stage(cq, g))
                ssl = u % 3
                esl = u % NE
                k0 = j * 128
                Dd = k0 - q0
                s_ps = bank(ssl)[0:kn, :]
                e_out = Et[esl][0:kn, :]
                maxrel = Dd + kn - 1
                minrel = Dd - (QC - 1)
                if minrel >= 128:
                    act(e_out, s_ps, AF.Exp, [b_banks[ssl], b_cols], [b_E[esl]], bias=ea[0:kn, h:h + 1])
                else:
                    act(e_out, s_ps, AF.Exp, [b_banks[ssl]], [b_E[esl]])
                if maxrel <= -128 or minrel >= 128:
                    pass
                else:
                    i0 = 368 - Dd
                    assert 0 <= i0 <= TW - QC, (i0, Dd)
                    tb0 = Tb[0:kn, h, i0:i0 + QC]
                    tbb = bass.AP(tb0.tensor, tb0.offset, [list(tb0.ap[0]), [0, 2], list(tb0.ap[1])])
                    e3 = Et[esl][0:kn, :].rearrange("p (c n) -> p c n", c=2)
                    dve(lambda e, o=e3, b=tbb: e.tensor_tensor(out=o, in0=o, in1=b, op=ALU.mult),
                        [b_Tb, b_E[esl]], [b_E[esl]])
                if pos == 4 and cast_pieces:
                    o_, i_, bb_ = cast_pieces.pop(0)
                    dma_pool(o_, i_, [], [bb_])
                if pos == 8 and u + 9 < NU:
                    nU = units[u + 9]
                    emit_qb(nU["cq"], nU["h"], nU["sl"])
                if u + 2 < NU:
                    emit_qk(u + 2)
                if u >= 1:
                    pU = units[u - 1]
                    av(pU["cq"], pU["j"], (u - 1) % NE, pU["kn"], pU["sl"], pU["h"], pU["pos"])
                run_deferred(unit, 1)
            pU = units[NU - 1]
            unit = NU + 1
            av(pU["cq"], pU["j"], (NU - 1) % NE, pU["kn"], pU["sl"], pU["h"], pU["pos"])
            while deferred:
                run_deferred(10 ** 9, 10 ** 6)
            while cast_pieces:
                o_, i_, bb_ = cast_pieces.pop(0)
                dma_pool(o_, i_, [], [bb_])
            T.barrier()

            fT = bfv(0, 8 * 1024).rearrange("p (c n) -> p c n", c=8)
            aT = bfv(16384, 22 * 1024).rearrange("p (k n) -> p k n", k=22)
            Wdh = [bfv(61440 + i * 22528, 22 * 512).rearrange("p (k n) -> p k n", k=22) for i in range(2)]
            wgs = [bfv(106496 + i * 8192, 2048).rearrange("p (c n) -> p c n", c=8) for i in range(2)]
            wus = [bfv(106496 + i * 8192 + 4096, 2048).rearrange("p (c n) -> p c n", c=8) for i in range(2)]
            hn2 = [bfv(122880 + i * 2048, 1024) for i in range(2)]
            sgt = [bfv(126976 + i * 1024, 512) for i in range(2)]
            junkC = bfv(129024, 1024)
            b_fT = [B(f"fT{i}") for i in range(8)]
            b_aT = [B(f"aT{i}") for i in range(4)]
            b_Wdh = [B("Wdh0"), B("Wdh1")]
            b_wgs = [B("wgs0"), B("wgs1")]
            b_hn2 = [B("hn2_0"), B("hn2_1")]
            b_sgt = [B("sgt0"), B("sgt1")]
            b_junkC = B("junkC")
            b_st = B("stats")
            dve(lambda e: e.memset(junkC[:, 0:128], 1.0), [], [b_junkC])
            gain_tile(g2col, junkC[:, 0:128], b_junkC)
            b_rs2 = [B("rs2_0"), B("rs2_1")]
            b_st3t = [B(f"st3_{i}") for i in range(8)]

            def c0_stats(sc):
                rd_ss = [b_ss2[t] for t in range(sc * 8, sc * 8 + 8)]
                act(ln2[:, :], ss2[:, sc * 8:(sc + 1) * 8], AF.Ln, rd_ss + [b_cols], [b_rs2[sc]], bias=eps6, scale=1.0 / D)
                act(rs2x[:, sc * 8:(sc + 1) * 8], ln2[:, :], AF.Exp, [b_rs2[sc]], [b_rs2[sc]], scale=-0.5)

            def c0_copy(sc, i):
                t = sc * 8 + i
                hs = i % 2
                act(hn2[hs][:, :], h1[:, t * D:(t + 1) * D], AF.Copy, [b_h1[t], b_rs2[sc]], [b_hn2[hs]],
                    scale=rs2x[:, sc * 8 + i:sc * 8 + i + 1])

            def c0_tr(sc, i, pbase):
                hs = i % 2
                pb = pbase + hs
                for c in range(8):
                    tr(bankb(pb)[:, c * 128:(c + 1) * 128], hn2[hs][:, c * 128:(c + 1) * 128], 128,
                       [b_hn2[hs]], [b_banks[pb]], bf=True)
                src_ps = bankb(pb).rearrange("p (c n) -> p c n", c=8)
                dst = fT[:, :, i * 128:(i + 1) * 128]
                g = gbc[:, :].rearrange("p (c n) -> p c n", c=8)
                dve(lambda e, o=dst, i_=src_ps, g=g: e.tensor_tensor(out=o, in0=i_, in1=g, op=ALU.mult),
                    [b_banks[pb], b_gbc], [b_fT[i]])

            rs2x = sm[:, 88:104]
            c0_stats(0)
            c0_copy(0, 0)
            for i in range(8):
                if i + 1 < 8:
                    c0_copy(0, i + 1)
                c0_tr(0, i, 0)
            for sc in range(2):
                tiles = list(range(sc * 8, sc * 8 + 8))
                dma_sp(Wdh[0], wd_b.ap().rearrange("(k p) n -> p k n", p=128)[:, :, 0:512], [b_wdb], [b_Wdh[0]])
                gu = 0
                for j in range(11):
                    wsl = j % 2
                    dma_sp(wgs[wsl].rearrange("p c n -> p (c n)"), wg_b.ap()[j * 128:(j + 1) * 128, :],
                           [b_wgb], [b_wgs[wsl]])
                    dma_sp(wus[wsl].rearrange("p c n -> p (c n)"), wu_b.ap()[j * 128:(j + 1) * 128, :],
                           [b_wub], [b_wgs[wsl]])
                    for tc in range(2):
                        rdf = [b_fT[tc * 4 + k] for k in range(4)]
                        for sub in range(2):
                            gsl = gu % 2
                            gu += 1
                            bg, bu_ = 4 + 2 * gsl, 5 + 2 * gsl
                            for c in range(8):
                                mm(bank(bg), wgs[wsl][:, c, sub * 128:(sub + 1) * 128], fT[:, c, tc * 512:(tc + 1) * 512],
                                   c == 0, c == 7, [b_wgs[wsl]] + rdf, [b_banks[bg]])
                            for c in range(8):
                                mm(bank(bu_), wus[wsl][:, c, sub * 128:(sub + 1) * 128], fT[:, c, tc * 512:(tc + 1) * 512],
                                   c == 0, c == 7, [b_wgs[wsl]] + rdf, [b_banks[bu_]])
                            act(sgt[gsl][:, :], bank(bg), AF.Silu, [b_banks[bg]], [b_sgt[gsl]])
                            kf = 2 * j + sub
                            dve(lambda e, gsl=gsl, bu_=bu_, kf=kf, tc=tc:
                                e.tensor_tensor(out=aT[:, kf, tc * 512:(tc + 1) * 512], in0=bank(bu_), in1=sgt[gsl][:, :],
                                                op=ALU.mult), [b_banks[bu_], b_sgt[gsl]], [b_aT[tc * 2 + (kf % 2)]])
                dma_sp(Wdh[1], wd_b.ap().rearrange("(k p) n -> p k n", p=128)[:, :, 512:1024], [b_wdb], [b_Wdh[1]])
                dn = 0
                nxt = sc + 1 if sc + 1 < 2 else None
                if nxt is not None:
                    c0_stats(nxt)
                    c0_copy(nxt, 0)
                for half in range(2):
                    for i, t in enumerate(tiles):
                        bi = dn % 4
                        dn += 1
                        tc = i // 4
                        for k in range(22):
                            mm(bank(bi), aT[:, k, i * 128:(i + 1) * 128], Wdh[half][:, k, :], k == 0, k == 21,
                               [b_aT[tc * 2], b_aT[tc * 2 + 1], b_Wdh[half]], [b_banks[bi]])
                        hsl = h1[:, t * D + half * 512:t * D + (half + 1) * 512]
                        dve(lambda e, hsl=hsl, bi=bi: e.tensor_tensor(out=hsl, in0=bank(bi), in1=hsl, op=ALU.add),
                            [b_banks[bi], b_h1[t]], [b_h1[t]])
                        if nxt is not None and half == 0:
                            if i + 1 < 8:
                                c0_copy(nxt, i + 1)
                            c0_tr(nxt, i, 4)
                        if half == 1:
                            hfull = h1[:, t * D:(t + 1) * D]
                            act(junkC[:, :], hfull, AF.Square, [b_h1[t]], [b_junkC, b_st3t[i]], accum=ss3[:, i:i + 1])
                            act(ln3[:, i:i + 1], ss3[:, i:i + 1], AF.Ln, [b_st3t[i], b_cols], [b_st3t[i]], bias=eps6,
                                scale=1.0 / D)
                            act(rs3[:, i:i + 1], ln3[:, i:i + 1], AF.Exp, [b_st3t[i]], [b_st3t[i]], scale=-0.5)
                            dve(lambda e, hsl=hfull, i=i: e.scalar_tensor_tensor(out=hsl, in0=hsl, scalar=rs3[:, i:i + 1],
                                                                                in1=fgb[:, :], op0=ALU.mult, op1=ALU.mult),
                                [b_h1[t], b_st3t[i], b_fgb], [b_h1[t]])
                            dma_pool(out_d.ap()[s, t * 128:(t + 1) * 128, :], hfull, [b_h1[t]], [])
            T.barrier()

        T.finalize()
        with nc.Block() as block:
            @block.tensor
            def _(e):
                T.play("pe", e, sems, dsems)

            @block.scalar
            def _(e):
                T.play("act", e, sems, dsems)

            @block.vector
            def _(e):
                T.play("dve", e, sems, dsems)

            @block.gpsimd
            def _(e):
                T.play("pool", e, sems, dsems)

            @block.sync
            def _(e):
                T.play("sp", e, sems, dsems, final_wait=True)
    return nc


_CACHE = {}


def _host_layouts(inp):
    f = lambda a: np.ascontiguousarray(np.asarray(a, dtype=np.float32))
    w_in = f(inp["w_in"])[0]
    w_o = f(inp["w_o"])[0]
    w_gate = f(inp["w_gate"])[0]
    w_up = f(inp["w_up"])[0]
    w_down = f(inp["w_down"])[0]
    rel_bias = f(inp["rel_bias"])
    n = np.arange(NLUT)
    bucket = _t5_bucket_np(495 - n)
    onehot = np.zeros((32, NLUT), np.float32)
    onehot[bucket, n] = 1.0
    shared = {
        "meta": f(inp["meta_tokens"]),
        "w_in_l": np.ascontiguousarray(w_in.reshape(8, 128, 2048).transpose(1, 0, 2)).reshape(1024, 2048),
        "w_o_l": np.ascontiguousarray(w_o.reshape(8, 128, 1024).transpose(1, 0, 2)).reshape(1024, 1024),
        "w_gate_l": np.ascontiguousarray(w_gate.reshape(8, 128, 11, 256).transpose(2, 1, 0, 3)).reshape(1408, 2048),
        "w_up_l": np.ascontiguousarray(w_up.reshape(8, 128, 11, 256).transpose(2, 1, 0, 3)).reshape(1408, 2048),
        "w_down_l": w_down,
        "pool_w_l": np.ascontiguousarray(f(inp["pool_w"])[0].transpose(1, 0, 2)).reshape(128, 512),
        "colpack": np.ascontiguousarray(np.concatenate([
            f(inp["norm1_g"])[0].reshape(8, 128).T,
            f(inp["norm2_g"])[0].reshape(8, 128).T,
            f(inp["pool_scale"])[0].reshape(4, 128).T,
            f(inp["subln_g"])[0].reshape(128, 1),
            np.zeros((128, 3), np.float32),
            np.broadcast_to(np.concatenate([rel_bias[15], rel_bias[31]])[None, :], (128, 8)),
        ], axis=1)),
        "fgb": np.ascontiguousarray(np.broadcast_to(f(inp["final_g"])[None, :], (128, D))),
        "lamv": np.ascontiguousarray(np.broadcast_to(np.concatenate(
            [f(inp["lambda_q1"])[0], f(inp["lambda_k1"])[0], f(inp["lambda_q2"])[0], f(inp["lambda_k2"])[0]])[None, :],
            (128, 256))),
        "relb": rel_bias,
        "onehot": onehot,
        "ident": np.eye(128, dtype=np.float32),
        "aident": np.ascontiguousarray(np.eye(128, dtype=np.float32)[::-1]),
    }
    return shared


def kernel(**inputs):
    x = np.ascontiguousarray(np.asarray(inputs["x"], dtype=np.float32))
    shared = _host_layouts(inputs)
    if "nc" not in _CACHE:
        _CACHE["nc"] = build_program()
    nc = _CACHE["nc"]
    in_maps = []
    for c in range(NCORES):
        m = dict(shared)
        m["x"] = x[2 * c:2 * c + 2]
        in_maps.append(m)
    res = run_bass_kernel_spmd(nc, in_maps, core_ids=list(range(NCORES)))
    out = np.concatenate([np.asarray(r["out"], dtype=np.float32) for r in res.results], axis=0)
    return out
```
